# Optimizing a Trainium2 kernel written in Bass

```python
import math
import jax, jax.numpy as jnp
from jax import lax
import numpy as np

D_MODEL = 2048
BATCH = 1
SEQ = 8192
DEPTH = 2
DEC_BATCH = 16
DEC_SEQ = 2048
PAST_LEN = 128

D_FNET = D_MODEL // 2
FNET_GROUPS = 4
FNET_GROUP_DIM = D_FNET // FNET_GROUPS
D_HGRN = D_MODEL // 2
HGRN_HEAD_DIM = 128
HGRN_HEADS = D_HGRN // HGRN_HEAD_DIM
HGRN_CHUNK = 64
EVEN_WIDTHS = (D_FNET, D_FNET, D_HGRN, D_HGRN, D_HGRN, D_HGRN, D_HGRN)
D_EVEN_IN = sum(EVEN_WIDTHS)
D_EVEN_MIX = D_FNET + D_HGRN
DIFF_HEADS = 8
DIFF_HEAD_DIM = D_MODEL // (2 * DIFF_HEADS)
DIFF_V_DIM = 2 * DIFF_HEAD_DIM
D_ATTN = DIFF_HEADS * DIFF_V_DIM
D_ODD_IN = 4 * D_ATTN
Q_BLOCK = 128
N_EVEN = (DEPTH + 1) // 2
N_ODD = DEPTH // 2
RMS_EPS = 1e-6

kernel_name = "hybrid_fnet_hgrn2_diffattn_encoder"


def rmsnorm(x, g):
    x32 = x.astype(jnp.float32)
    inv = lax.rsqrt(jnp.mean(x32 * x32, axis=-1, keepdims=True) + RMS_EPS)
    return (x32 * inv * g.astype(jnp.float32)).astype(x.dtype)


def split_cols(t, widths):
    idx = [int(v) for v in np.cumsum(widths)[:-1]]
    return jnp.split(t, idx, axis=-1)


def fnet_mix(u):
    b, l, _ = u.shape
    ug = u.astype(jnp.float32).reshape(b, l, FNET_GROUPS, FNET_GROUP_DIM)
    y = jnp.fft.fft2(ug, axes=(1, 3), norm="ortho").real
    return y.reshape(b, l, D_FNET)


def hgrn2_scan(q, k, v, log_f):
    b, l, h, dk = q.shape
    dv = v.shape[-1]
    n = l // HGRN_CHUNK

    def to_chunks(t):
        return jnp.moveaxis(t.reshape(b, n, HGRN_CHUNK, h, t.shape[-1]), 1, 0)

    xs = tuple(to_chunks(t) for t in (q, k, v, log_f))
    lower = jnp.tril(jnp.ones((HGRN_CHUNK, HGRN_CHUNK), dtype=bool))

    def step(state, chunk):
        qn, kn, vn, gn = chunk
        cum = jnp.cumsum(gn, axis=1)
        last = cum[:, -1:]
        q_dec = qn * jnp.exp(cum)
        k_dec = kn * jnp.exp(-cum)
        scores = jnp.einsum("bthk,bshk->bhts", q_dec, k_dec)
        scores = jnp.where(lower, scores, 0.0)
        o = (jnp.einsum("bhts,bshv->bthv", scores, vn)
             + jnp.einsum("bthk,bhkv->bthv", q_dec, state))
        k_to_end = kn * jnp.exp(last - cum)
        state = (state * jnp.exp(last[:, 0])[..., None]
                 + jnp.einsum("bshk,bshv->bhkv", k_to_end, vn))
        return state, o

    s0 = jnp.zeros((b, h, dk, dv), jnp.float32)
    _, o = lax.scan(step, s0, xs)
    return jnp.moveaxis(o, 0, 1).reshape(b, l, h, dv)


def hgrn2_mix(q_raw, i_raw, f_fwd_raw, f_bwd_raw, lb_fwd, lb_bwd, g_norm):
    b, l, _ = q_raw.shape

    def heads(t):
        return t.astype(jnp.float32).reshape(b, l, HGRN_HEADS, HGRN_HEAD_DIM)

    q = jax.nn.silu(heads(q_raw))
    v = heads(i_raw)

    def gates(f_raw, lb):
        lb = lb.astype(jnp.float32).reshape(HGRN_HEADS, HGRN_HEAD_DIM)
        f = lb + (1.0 - lb) * jax.nn.sigmoid(heads(f_raw))
        return 1.0 - f, jnp.log(f)

    k_f, g_f = gates(f_fwd_raw, lb_fwd)
    k_b, g_b = gates(f_bwd_raw, lb_bwd)
    o_fwd = hgrn2_scan(q, k_f, v, g_f)
    rev = lambda t: jnp.flip(t, axis=1)
    o_bwd = rev(hgrn2_scan(rev(q), rev(k_b), rev(v), rev(g_b)))
    o = rmsnorm(o_fwd + o_bwd, g_norm)
    return o.reshape(b, l, D_HGRN)


def alibi_slopes(n_heads):
    return jnp.exp2(-8.0 * (jnp.arange(n_heads, dtype=jnp.float32) + 1.0) / n_heads)


def diff_attention(q, k, v, lam, subln_g, lambda_init):
    b, l = q.shape[:2]
    nb = l // Q_BLOCK
    key_pos = jnp.arange(l)
    slopes = alibi_slopes(DIFF_HEADS)
    scale = DIFF_HEAD_DIM ** -0.5
    q_blocks = jnp.moveaxis(q.reshape(b, nb, Q_BLOCK, DIFF_HEADS, 2, DIFF_HEAD_DIM), 1, 0)
    starts = jnp.arange(nb) * Q_BLOCK

    def block(args):
        q_blk, start = args
        q_pos = start + jnp.arange(Q_BLOCK)
        dist = jnp.abs(q_pos[:, None] - key_pos[None, :]).astype(jnp.float32)
        bias = -slopes[:, None, None] * dist
        s = jnp.einsum("bqhcd,bkhcd->bhcqk", q_blk, k) * scale + bias[None, :, None]
        p = jax.nn.softmax(s, axis=-1)
        attn = p[:, :, 0] - lam * p[:, :, 1]
        return jnp.einsum("bhqk,bkhv->bqhv", attn, v)

    o = lax.map(block, (q_blocks, starts))
    o = jnp.moveaxis(o, 0, 1).reshape(b, l, DIFF_HEADS, DIFF_V_DIM)
    o = rmsnorm(o, subln_g) * (1.0 - lambda_init)
    return o.reshape(b, l, D_ATTN)


def even_layer(x, w_in, w_out, g_pre, g_post, lb_fwd, lb_bwd, g_hgrn):
    h = rmsnorm(x, g_pre)
    proj = jnp.einsum("bld,de->ble", h, w_in)
    u_a, gate_a, q, i, f_fwd, f_bwd, gate_b = split_cols(proj, EVEN_WIDTHS)
    y_a = fnet_mix(u_a).astype(x.dtype) * jax.nn.silu(gate_a)
    y_b = hgrn2_mix(q, i, f_fwd, f_bwd, lb_fwd, lb_bwd, g_hgrn).astype(x.dtype) * jax.nn.silu(gate_b)
    y = jnp.einsum("ble,ed->bld", jnp.concatenate([y_a, y_b], axis=-1), w_out)
    return x + rmsnorm(y, g_post)


def odd_layer(x, w_in, w_out, g_pre, g_post, lq1, lk1, lq2, lk2, subln_g, lambda_init):
    b, l, _ = x.shape
    h = rmsnorm(x, g_pre)
    proj = jnp.einsum("bld,de->ble", h, w_in)
    q, k, v, gate = split_cols(proj, (D_ATTN, D_ATTN, D_ATTN, D_ATTN))
    q = q.astype(jnp.float32).reshape(b, l, DIFF_HEADS, 2, DIFF_HEAD_DIM)
    k = k.astype(jnp.float32).reshape(b, l, DIFF_HEADS, 2, DIFF_HEAD_DIM)
    v = v.astype(jnp.float32).reshape(b, l, DIFF_HEADS, DIFF_V_DIM)
    f32 = lambda t: t.astype(jnp.float32)
    lam = (jnp.exp(jnp.sum(f32(lq1) * f32(lk1))) - jnp.exp(jnp.sum(f32(lq2) * f32(lk2)))
           + lambda_init)
    o = diff_attention(q, k, v, lam, subln_g, lambda_init).astype(x.dtype) * jax.nn.silu(gate)
    y = jnp.einsum("ble,ed->bld", o, w_out)
    return x + rmsnorm(y, g_post)


def trunk(x, ev_w_in, ev_w_out, ev_norm_pre, ev_norm_post, hgrn_lb_logits, hgrn_norm,
          od_w_in, od_w_out, od_norm_pre, od_norm_post,
          lambda_q1, lambda_k1, lambda_q2, lambda_k2, subln):
    lb = jnp.cumsum(jax.nn.softmax(hgrn_lb_logits.astype(jnp.float32), axis=1), axis=1)
    for layer in range(DEPTH):
        if layer % 2 == 0:
            e = layer // 2
            x = even_layer(x, ev_w_in[e], ev_w_out[e], ev_norm_pre[e], ev_norm_post[e],
                           lb[0, layer], lb[1, layer], hgrn_norm[e])
        else:
            o = layer // 2
            lambda_init = 0.8 - 0.6 * math.exp(-0.3 * layer)
            x = odd_layer(x, od_w_in[o], od_w_out[o], od_norm_pre[o], od_norm_post[o],
                          lambda_q1[o], lambda_k1[o], lambda_q2[o], lambda_k2[o], subln[o],
                          lambda_init)
    return x


def setup_inputs(seed: int = 0) -> dict:
    key = jax.random.key(seed)
    ks = jax.random.split(key, 20)
    nrm = lambda k, shape, s: jax.random.normal(k, shape, jnp.float32) * s
    gain = lambda k, shape: 1.0 + 0.05 * jax.random.normal(k, shape, jnp.float32)
    return {
        "x_prompt": nrm(ks[0], (BATCH, SEQ, D_MODEL), 1.0),
        "x_sample": nrm(ks[1], (DEC_BATCH, DEC_SEQ, D_MODEL), 1.0),
        "ev_w_in": nrm(ks[2], (N_EVEN, D_MODEL, D_EVEN_IN), D_MODEL ** -0.5),
        "ev_w_out": nrm(ks[3], (N_EVEN, D_EVEN_MIX, D_MODEL), D_EVEN_MIX ** -0.5),
        "ev_norm_pre": gain(ks[4], (N_EVEN, D_MODEL)),
        "ev_norm_post": gain(ks[5], (N_EVEN, D_MODEL)),
        "hgrn_lb_logits": nrm(ks[6], (2, DEPTH + 1, D_HGRN), 0.1),
        "hgrn_norm": gain(ks[7], (N_EVEN, HGRN_HEAD_DIM)),
        "od_w_in": nrm(ks[8], (N_ODD, D_MODEL, D_ODD_IN), D_MODEL ** -0.5),
        "od_w_out": nrm(ks[9], (N_ODD, D_ATTN, D_MODEL), D_ATTN ** -0.5),
        "od_norm_pre": gain(ks[10], (N_ODD, D_MODEL)),
        "od_norm_post": gain(ks[11], (N_ODD, D_MODEL)),
        "lambda_q1": nrm(ks[12], (N_ODD, DIFF_HEAD_DIM), 0.1),
        "lambda_k1": nrm(ks[13], (N_ODD, DIFF_HEAD_DIM), 0.1),
        "lambda_q2": nrm(ks[14], (N_ODD, DIFF_HEAD_DIM), 0.1),
        "lambda_k2": nrm(ks[15], (N_ODD, DIFF_HEAD_DIM), 0.1),
        "subln": gain(ks[16], (N_ODD, DIFF_V_DIM)),
    }


def reference(x_prompt, x_sample, ev_w_in, ev_w_out, ev_norm_pre, ev_norm_post, hgrn_lb_logits,
              hgrn_norm, od_w_in, od_w_out, od_norm_pre, od_norm_post,
              lambda_q1, lambda_k1, lambda_q2, lambda_k2, subln):
    y_prompt = trunk(x_prompt, ev_w_in, ev_w_out, ev_norm_pre, ev_norm_post, hgrn_lb_logits,
                     hgrn_norm, od_w_in, od_w_out, od_norm_pre, od_norm_post,
                     lambda_q1, lambda_k1, lambda_q2, lambda_k2, subln)
    y_sample = trunk(x_sample, ev_w_in, ev_w_out, ev_norm_pre, ev_norm_post, hgrn_lb_logits,
                     hgrn_norm, od_w_in, od_w_out, od_norm_pre, od_norm_post,
                     lambda_q1, lambda_k1, lambda_q2, lambda_k2, subln)
    return (y_prompt, y_sample)
```

```python
import contextlib, math
import numpy as np
import ml_dtypes
import concourse.bass as bass
import concourse.mybir as mybir
from concourse.bass_utils import run_bass_kernel_spmd

F32, BF16, I32 = mybir.dt.float32, mybir.dt.bfloat16, mybir.dt.int32
AF = mybir.ActivationFunctionType
ALU = mybir.AluOpType
D = 2048
KC = 16
EPS = 1e-6
LAMBDA_INIT = 0.8 - 0.6 * math.exp(-0.3 * 1)
SLOPES = [2.0 ** (-8.0 * (h + 1) / 8) for h in range(8)]
NPBF = ml_dtypes.bfloat16


class StopBuild(Exception):
    pass


class Eng:
    def __init__(self, e, sem):
        self.e, self.sem, self.n, self.seen = e, sem, 0, {}

    def mark(self, ins):
        ins.then_inc(self.sem, 1)
        self.n += 1
        return (self, self.n)

    def wait(self, *toks):
        for tok in toks:
            if tok is None:
                continue
            src, n = tok
            if self.seen.get(src, 0) >= n:
                continue
            self.e.wait_ge(src.sem, n)
            self.seen[src] = n


class DSem:
    def __init__(self, sem):
        self.sem, self.n = sem, 0


class P:
    def __init__(self, cfg):
        self.cfg = cfg
        self.es = contextlib.ExitStack()
        nc = self.nc = bass.Bass("TRN2", target_bir_lowering=False)
        mk = lambda nm: self.es.enter_context(nc.semaphore(nm))
        self.pe = Eng(nc.tensor, mk("s_pe"))
        self.act = Eng(nc.scalar, mk("s_act"))
        self.dve = Eng(nc.vector, mk("s_dve"))
        self.pool = Eng(nc.gpsimd, mk("s_pool"))
        self.sp = Eng(nc.sync, mk("s_sp"))
        self.engs = [self.pe, self.act, self.dve, self.pool, self.sp]
        self.dsems = {}
        self.pending = []
        self.ndram = 0
        self.ps = [self.es.enter_context(nc.psum_tensor(f"ps{i}", [128, 512], F32)) for i in range(6)]
        self.pb = [self.es.enter_context(nc.psum_tensor(f"pb{i}", [128, 1024], BF16)) for i in range(2)]
        self.dummy = self.es.enter_context(nc.sbuf_tensor("dummy_sb", [128, 2], F32))
        self.pool.mark(nc.gpsimd.memset(self.dummy[:], 0.0))

    def sb(self, es, name, shape, dt):
        self.nsb = getattr(self, "nsb", 0) + 1
        return es.enter_context(self.nc.sbuf_tensor(f"{name}_{self.nsb}", shape, dt))

    def dram(self, name, shape, dt, kind="Internal"):
        t = self.nc.dram_tensor(name, list(shape), dt, kind=kind)
        if not hasattr(self, "named"):
            self.named = {}
        self.named[name] = (t, list(shape), dt)
        return t

    def dump(self):
        self.barrier()
        for name in self.cfg.get("dump", []):
            if name not in self.named:
                continue
            t, shape, dt = self.named[name]
            o = self.nc.dram_tensor("dbg_" + name, shape, dt, kind="ExternalOutput")
            self.ld(o.ap(), t.ap(), "dump")
        self.barrier()

    def dsem(self, key):
        if key not in self.dsems:
            self.dsems[key] = DSem(self.es.enter_context(self.nc.semaphore("d_" + key)))
        return self.dsems[key]

    def dma(self, q, out, in_, key, deps=(), slow=False):
        q.wait(*deps)
        ds = self.dsem(key)
        kw = {"allow_slow_non_contiguous": True} if slow else {}
        q.e.dma_start(out=out, in_=in_, **kw).then_inc(ds.sem, 16)
        ds.n += 16
        tok = (ds, ds.n)
        self.pending.append(tok)
        return tok

    def ld(self, out, in_, key, deps=(), slow=False):
        return self.dma(self.sp, out, in_, key, deps, slow)

    def st(self, out, in_, key, deps=()):
        return self.dma(self.pool, out, in_, key, deps)

    def barrier(self):
        toks = [(e, e.n) for e in self.engs if e.n > 0] + self.pending
        for e in self.engs:
            e.wait(*toks)
        self.pending = []

    def A(self, deps, *a, **k):
        self.act.wait(*deps)
        tok = self.act.mark(self.act.e.activation(*a, **k))
        if k.get("accum_out") is not None:
            tok = self.act.mark(self.act.e.activation(out=self.dummy[:, 1:2], in_=self.dummy[:, 0:1], func=AF.Copy))
        return tok

    def V(self, deps, fn, *a, **k):
        self.dve.wait(*deps)
        return self.dve.mark(getattr(self.dve.e, fn)(*a, **k))

    def G(self, deps, fn, *a, **k):
        self.pool.wait(*deps)
        return self.pool.mark(getattr(self.pool.e, fn)(*a, **k))

    def X(self, eng, deps, fn, *a, **k):
        eng.wait(*deps)
        return eng.mark(getattr(eng.e, fn)(*a, **k))

    def rsqrt(self, deps, out, in_, mul, add):
        a = self.A(deps, out=out, in_=in_, func=AF.Ln, scale=float(mul), bias=float(add))
        return self.A([a], out=out, in_=out, func=AF.Exp, scale=-0.5)

    def mm(self, deps, out, lhsT, rhs, start, stop, mark=False):
        self.pe.wait(*deps)
        ins = self.pe.e.matmul(out, lhsT, rhs, start=start, stop=stop)
        return self.pe.mark(ins) if mark else None

    def tr(self, deps, out, in_, ident, mark=False):
        self.pe.wait(*deps)
        ins = self.pe.e.transpose(out, in_, ident)
        return self.pe.mark(ins) if mark else None


def bcast_rows(ap_dram_1d, n, parts=128):
    return bass.AP(ap_dram_1d.tensor, ap_dram_1d.offset, [[0, parts], [1, n]])


def build(cfg):
    p = P(cfg)
    try:
        _build(cfg, p)
    except StopBuild:
        p.dump()
        return p.nc
    p.dump()
    p.es.close()
    return p.nc


def _build(cfg, p):
    NS, LS, LP, OWN = cfg["NS"], cfg["LS"], cfg["LP"], cfg["OWN"]

    def done(tag):
        if cfg.get("stop") == tag:
            raise StopBuild()
    nc = p.nc
    inp = lambda name, shape, dt=F32: nc.dram_tensor(name, list(shape), dt, kind="ExternalInput")
    xs = inp("xs", [NS * LS, D])
    xp = inp("xp", [LP, D])
    w0i, w0o = inp("w0i", [D, 7168]), inp("w0o", [D, D])
    w1i, w1o = inp("w1i", [D, 8192]), inp("w1o", [D, D])
    g0pre, g0post = inp("g0pre", [D]), inp("g0post", [D])
    g1pre, g1post = inp("g1pre", [D]), inp("g1post", [D])
    lbl = inp("lbl", [2, 3, 1024])
    ghg = inp("ghg", [128])
    lam4 = inp("lam4", [4, 128])
    gsub = inp("gsub", [256])
    ident_d = inp("ident", [128, 128], BF16)
    c128_d = inp("c128", [3, 128, 128], BF16)
    cs256_d = inp("cs256", [256, 512], BF16)
    tw_d = {L: inp(f"tw{L}", [3, 128, L // 128]) for L in sorted({LS, LP})}
    cm_d = {L: inp(f"cm{L}", [2, L // 128, L // 128], BF16) for L in sorted({LS, LP})}
    hmask_d = inp("hmask", [2, 64, 64])
    delta_d = inp("delta", [128, 256])
    dtabs_d = inp("dtabs", [128, (LS // 128) * (LS // 256)])
    dtabp_d = inp("dtabp", [128, (LP // 128) * (OWN // 256)])
    ownidx_d = inp("ownidx", [128, OWN // 128], I32)
    ys = nc.dram_tensor("ys", [NS * LS, D], F32, kind="ExternalOutput")
    yp = nc.dram_tensor("yp", [OWN, D], F32, kind="ExternalOutput")

    ges = p.es
    ident = p.sb(ges, "ident_sb", [128, 128], BF16)
    toks = [p.ld(ident[:], ident_d.ap(), "c0")]
    gpost_sb = [p.sb(ges, f"gpost{i}", [128, D], F32) for i in range(2)]
    toks.append(p.ld(gpost_sb[0][:], bcast_rows(g0post.ap(), D), "c0"))
    toks.append(p.ld(gpost_sb[1][:], bcast_rows(g1post.ap(), D), "c0"))
    gpre_sb = [p.sb(ges, f"gpre{i}", [128, KC], F32) for i in range(2)]
    toks.append(p.ld(gpre_sb[0][:], g0pre.ap().rearrange("(c p) -> p c", p=128), "c0", slow=True))
    toks.append(p.ld(gpre_sb[1][:], g1pre.ap().rearrange("(c p) -> p c", p=128), "c0", slow=True))
    p.barrier()

    def prep_w(w, ncols, gcol, name):
        wb = p.dram(name, [D, ncols], BF16)
        with contextlib.ExitStack() as es:
            CW = 1024
            wf = [p.sb(es, f"wf{i}", [128, CW], F32) for i in range(2)]
            wo = [p.sb(es, f"wo{i}", [128, CW], BF16) for i in range(2)]
            cons = [None, None]
            sts = [None, None]
            i = 0
            for kc in range(KC):
                for c0 in range(0, ncols, CW):
                    b = i % 2
                    t = p.ld(wf[b][:], w.ap()[kc * 128:(kc + 1) * 128, c0:c0 + CW], f"wl{b}", deps=[cons[b]])
                    eng = p.dve if b == 0 else p.pool
                    if gcol is None:
                        cons[b] = p.X(eng, [t, sts[b]], "tensor_copy", wo[b][:], wf[b][:])
                    else:
                        cons[b] = p.X(eng, [t, sts[b]], "tensor_scalar", wo[b][:], wf[b][:],
                                      gcol[:, kc:kc + 1], None, ALU.mult)
                    sts[b] = p.dma(p.act, wb.ap()[kc * 128:(kc + 1) * 128, c0:c0 + CW], wo[b][:], f"ws{b}",
                                   deps=[cons[b]])
                    i += 1
        p.barrier()
        return wb

    wb0i = prep_w(w0i, 7168, gpre_sb[0], "wb0i")
    wb0o = prep_w(w0o, D, None, "wb0o")
    wb1i = prep_w(w1i, 8192, gpre_sb[1], "wb1i")
    wb1o = prep_w(w1o, D, None, "wb1o")
    done("prep")

    def norm_T(x_rows, L, hT):
        with contextlib.ExitStack() as es:
            xt = [p.sb(es, f"nx{i}", [128, D], F32) for i in range(2)]
            junk = p.sb(es, "njunk", [128, D], BF16)
            hb = [p.sb(es, f"nhb{i}", [128, D], BF16) for i in range(2)]
            ho = [p.sb(es, f"nho{i}", [128, KC, 128], BF16) for i in range(2)]
            st_ = [p.sb(es, f"nst{i}", [128, 2], F32) for i in range(2)]
            rd = [None, None]
            hbr = [None, None]
            hor = [None, None]
            for ti in range(L // 128):
                b = ti % 2
                t = p.ld(xt[b][:], x_rows[ti * 128:(ti + 1) * 128, :], f"nl{b}", deps=[rd[b]])
                a1 = p.A([t], out=junk[:], in_=xt[b][:], func=AF.Square, accum_out=st_[b][:, 0:1])
                v2 = p.rsqrt([a1], st_[b][:, 1:2], st_[b][:, 0:1], 1.0 / D, EPS)
                v3 = p.G([v2, t, hbr[b]], "tensor_scalar", hb[b][:], xt[b][:], st_[b][:, 1:2], None, ALU.mult)
                rd[b] = v3
                for kc in range(KC):
                    tk = p.tr([v3, hor[b]] if kc == 0 else [], p.pb[kc // 8][:, (kc % 8) * 128:(kc % 8 + 1) * 128],
                              hb[b][:, kc * 128:(kc + 1) * 128], ident[:], mark=(kc == KC - 1))
                hbr[b] = tk
                c1 = p.A([tk, hor[b]], out=ho[b][:, 0:8, :], in_=p.pb[0][:].rearrange("p (k t) -> p k t", k=8),
                         func=AF.Copy)
                c2 = p.V([tk, hor[b]], "tensor_copy", ho[b][:, 8:16, :],
                         p.pb[1][:].rearrange("p (k t) -> p k t", k=8))
                p.pe.wait(c1, c2)
                hor[b] = p.st(hT.ap().rearrange("(k p) t -> p k t", p=128)[:, :, ti * 128:(ti + 1) * 128],
                              ho[b][:], f"ns{b}", deps=[c1, c2])
        p.barrier()

    def proj(hT, L, wb, ncols_total, jobs):
        TB = min(L, 1024)
        with contextlib.ExitStack() as es:
            hblk = p.sb(es, "pj_h", [128, KC, TB], BF16)
            wblk = [p.sb(es, f"pj_w{i}", [128, KC, 512], BF16) for i in range(2)]
            osb = {F32: [p.sb(es, f"pj_of{i}", [128, 512], F32) for i in range(2)],
                   BF16: [p.sb(es, f"pj_ob{i}", [128, 512], BF16) for i in range(2)]}
            wread = [None, None]
            ost = {F32: [None, None], BF16: [None, None]}
            psr = [None] * 4
            wi = 0
            oi = 0
            pi = 0
            hread = None
            for tb in range(L // TB):
                th = p.ld(hblk[:], hT.ap().rearrange("(k p) t -> p k t", p=128)[:, :, tb * TB:(tb + 1) * TB],
                          "pjh", deps=[hread])
                cbs = [(j, c) for j in jobs for c in range(0, j[1], 512)]
                for (job, c) in cbs:
                    col0, ncols, mode, od, o0, scale, odt = job
                    b = wi % 2
                    wi += 1
                    tw = p.ld(wblk[b][:], wb.ap().rearrange("(k p) n -> p k n", p=128)[:, :, col0 + c:col0 + c + 512],
                              f"pjw{b}", deps=[wread[b]])
                    last = None
                    TW = min(512, TB)
                    if mode == "F":
                        subs = [(s4, t5) for s4 in range(4) for t5 in range(TB // TW)]
                    else:
                        subs = [(s4, 0) for s4 in range(TB // 128)]
                    for (s4, t5) in subs:
                        pk = pi % 4
                        pi += 1
                        W_ = TW if mode == "F" else 512
                        ps = p.ps[pk][:, 0:W_]
                        for kc in range(KC):
                            if mode == "F":
                                lhsT, rhs = wblk[b][:, kc, s4 * 128:(s4 + 1) * 128], hblk[:, kc, t5 * TW:(t5 + 1) * TW]
                            else:
                                lhsT, rhs = hblk[:, kc, s4 * 128:(s4 + 1) * 128], wblk[b][:, kc, :]
                            tk = p.mm([th, tw, psr[pk]] if kc == 0 else [], ps, lhsT, rhs, kc == 0, kc == KC - 1,
                                      mark=(kc == KC - 1))
                        last = tk
                        ob = oi % 2
                        oi += 1
                        o = osb[odt][ob][:, 0:W_]
                        if oi % 2 == 0:
                            ev = p.A([tk, ost[odt][ob]], out=o, in_=ps, func=AF.Copy, scale=float(scale))
                        else:
                            ev = p.V([tk, ost[odt][ob]], "tensor_scalar", o, ps, float(scale), None, ALU.mult)
                        psr[pk] = ev
                        if mode == "F":
                            dst = od.ap()[o0 + c + s4 * 128:o0 + c + (s4 + 1) * 128,
                                          tb * TB + t5 * TW:tb * TB + (t5 + 1) * TW]
                        else:
                            dst = od.ap()[tb * TB + s4 * 128:tb * TB + (s4 + 1) * 128, o0 + c:o0 + c + 512]
                        ost[odt][ob] = p.st(dst, o, f"pjs{ob}{'f' if odt == F32 else 'b'}", deps=[ev])
                    wread[b] = last
                    hread = last
        p.barrier()

    def fnet(uT, L, ya, tabs):
        M = L // 128
        c128, cs256, tw, cm = tabs
        Bd = p.dram(f"fn_B{p.ndram}", [128, M, 512], BF16)
        p.ndram += 1
        CP = 32
        with contextlib.ExitStack() as es:
            ug = p.sb(es, "fn_u", [128, 2, L], BF16)
            MB = min(M, 32)
            V = p.sb(es, "fn_V", [128, MB, 512], BF16)
            Bs = p.sb(es, "fn_Bs", [128, MB, 512], BF16)
            tmp = [p.sb(es, f"fn_t{i}", [128, 2, 256], F32) for i in range(2)]
            Bt = p.sb(es, "fn_Bt", [M, CP, 512], BF16)
            Y = [p.sb(es, f"fn_Y{i}", [M, 2, 256], F32) for i in range(2)]
            for g in range(4):
                tu = p.ld(ug[:], uT.ap()[g * 256:(g + 1) * 256, :].rearrange("(c p) t -> p c t", p=128), "fnu")
                ts_all = []
                for bh in range(M // MB):
                    evs = []
                    prev = [None, None]
                    for bl in range(MB):
                        b = bh * MB + bl
                        pk = b % 2
                        for ch in range(2):
                            lhsT = bass.AP(ug, ch * L + b, [[2 * L, 128], [M, 128]])
                            tk = p.mm([tu, prev[pk]] if ch == 0 else [], p.ps[pk][:], lhsT, cs256[:, ch, :],
                                      ch == 0, ch == 1, mark=(ch == 1))
                        if b % 2 == 0:
                            ev = p.A([tk], out=V[:, bl, :], in_=p.ps[pk][:], func=AF.Copy)
                        else:
                            ev = p.V([tk], "tensor_copy", V[:, bl, :], p.ps[pk][:])
                        prev[pk] = ev
                        evs.append(ev)
                    prevr = [None, None]
                    tw_tok = []
                    for bp in range(MB // 2):
                        pk = 2 + (bp % 2) * 2
                        Ar, Ai = p.ps[pk], p.ps[pk + 1]
                        b0 = 2 * bp
                        dep = [evs[b0], evs[b0 + 1], prevr[bp % 2]]
                        ar3 = Ar[:].rearrange("p (b f) -> p b f", b=2)
                        ai3 = Ai[:].rearrange("p (b f) -> p b f", b=2)
                        p.mm(dep, ar3, c128[:, 0, :], V[:, b0:b0 + 2, 0:256], True, False)
                        p.mm([], ar3, c128[:, 1, :], V[:, b0:b0 + 2, 256:512], False, True)
                        p.mm([], ai3, c128[:, 0, :], V[:, b0:b0 + 2, 256:512], True, False)
                        tk = p.mm([], ai3, c128[:, 2, :], V[:, b0:b0 + 2, 0:256], False, True, mark=True)
                        last = []
                        for j in range(2):
                            bl = b0 + j
                            b = bh * MB + bl
                            t1 = p.V([tk], "tensor_scalar", tmp[0][:, j, :], ar3[:, j, :], tw[:, 0, b:b + 1], None, ALU.mult)
                            t3 = p.V([tk], "tensor_scalar", tmp[1][:, j, :], ai3[:, j, :], tw[:, 0, b:b + 1], None, ALU.mult)
                            r1 = p.V([t1, tk], "scalar_tensor_tensor", Bs[:, bl, 0:256], ai3[:, j, :], tw[:, 1, b:b + 1],
                                     tmp[0][:, j, :], ALU.mult, ALU.add)
                            r2 = p.V([t3, tk], "scalar_tensor_tensor", Bs[:, bl, 256:512], ar3[:, j, :], tw[:, 2, b:b + 1],
                                     tmp[1][:, j, :], ALU.mult, ALU.add)
                            last = [r1, r2]
                        p.act.wait(*last)
                        prevr[bp % 2] = last[1]
                        tw_tok = last
                    ts_all.append(p.st(Bd.ap()[:, bh * MB:(bh + 1) * MB, :], Bs[:], "fnb", deps=tw_tok))
                    p.barrier()
                if True:
                    ts_ = ts_all[-1]
                    yst = [None, None]
                    bt_read = None
                    for cp in range(128 // CP):
                        tl = p.ld(Bt[:], Bd.ap()[cp * CP:(cp + 1) * CP, :, :].rearrange("c b f -> b c f"), "fnbt",
                                  deps=[ts_, bt_read])
                        for c2 in range(CP // 2):
                            pk = c2 % 2
                            ps3 = p.ps[pk][0:M, :].rearrange("p (c f) -> p c f", c=2)
                            p.mm([tl, yst[pk]], ps3, cm[0:M, 0, :], Bt[:, 2 * c2:2 * c2 + 2, 0:256], True, False)
                            tk = p.mm([], ps3, cm[0:M, 1, :], Bt[:, 2 * c2:2 * c2 + 2, 256:512], False, True, mark=True)
                            if c2 % 2 == 0:
                                ev = p.A([tk], out=Y[pk][:], in_=ps3, func=AF.Copy)
                            else:
                                ev = p.V([tk], "tensor_copy", Y[pk][:], ps3)
                            c_abs = cp * CP + 2 * c2
                            dst = ya.ap().rearrange("(d c) f -> d c f", c=128)[:, c_abs:c_abs + 2, g * 256:(g + 1) * 256]
                            yst[pk] = p.st(dst, Y[pk][:], f"fny{pk}", deps=[ev])
                            p.pe.wait(ev)
                            bt_read = tk
                    p.barrier()
        p.barrier()

    def hgrn(qrT, ffT, fbT, vtok, gates, L, ymix, lbt, hm, ghg_sb):
        SEG = min(L, 2048)
        NCH = SEG // 64
        nseg = L // SEG
        NT = L // 64
        dS = p.dram(f"hg_dS{p.ndram}", [2, NT, 128, 128], F32)
        Sb = p.dram(f"hg_Sb{p.ndram}", [2, NT, 128, 128], BF16)
        qd = p.dram(f"hg_qd{p.ndram}", [2, 128, L], BF16)
        scd = p.dram(f"hg_sc{p.ndram}", [64, NT, 64], BF16)
        eld = p.dram(f"hg_el{p.ndram}", [2, 128, NT], F32)
        p.ndram += 1
        for h in range(8):
            with contextlib.ExitStack() as es:
                qr = p.sb(es, "h_qr", [128, SEG], F32)
                fr = p.sb(es, "h_fr", [128, SEG], F32)
                f_ = p.sb(es, "h_f", [128, SEG], F32)
                g_ = p.sb(es, "h_g", [128, SEG], F32)
                k_ = p.sb(es, "h_k", [128, SEG], F32)
                cum = p.sb(es, "h_cum", [128, SEG], F32)
                cb = p.sb(es, "h_cb", [128, SEG], F32)
                ex = p.sb(es, "h_ex", [128, SEG], F32)
                rmask = p.sb(es, "h_rm", [128, SEG], F32)
                qdec = [p.sb(es, f"h_qd{i}", [128, SEG], BF16) for i in range(2)]
                kdec = p.sb(es, "h_kd", [128, SEG], BF16)
                kend = p.sb(es, "h_ke", [128, SEG], BF16)
                kendT = p.sb(es, "h_keT", [64, NCH, 128], BF16)
                vt = p.sb(es, "h_v", [64, NCH, 128], BF16)
                sct = p.sb(es, "h_sc", [64, NCH, 64], BF16)
                sc1 = p.sb(es, "h_sc1", [64, NCH, 64], F32)
                el = p.sb(es, "h_el", [128, 2, NCH], F32)
                dSs = [p.sb(es, f"h_dS{i}", [128, 128], F32) for i in range(2)]
                m1 = p.G([], "memset", rmask[:], 1.0)
                m2 = p.G([], "memset", rmask[:].rearrange("p (n s) -> p n s", s=64)[:, :, 0:1], 0.0)
                p.barrier()
                for sg in range(nseg):
                    t0 = sg * SEG
                    tq = p.ld(qr[:], qrT.ap()[h * 128:(h + 1) * 128, t0:t0 + SEG], "hq")
                    tv = p.ld(vt[:], vtok.ap()[t0:t0 + SEG, h * 128:(h + 1) * 128].rearrange("(n s) v -> s n v", s=64),
                              "hv")
                    aq = p.A([tq], out=qr[:], in_=qr[:], func=AF.Silu)
                    for d in range(2):
                        src = ffT if d == 0 else fbT
                        tf = p.ld(fr[:], src.ap()[h * 128:(h + 1) * 128, t0:t0 + SEG], "hf")
                        a1 = p.A([tf], out=f_[:], in_=fr[:], func=AF.Sigmoid)
                        v1 = p.V([a1], "tensor_scalar", f_[:], f_[:], lbt[:, 2 + d, h:h + 1], lbt[:, d, h:h + 1],
                                 ALU.mult, ALU.add)
                        a2 = p.A([v1], out=g_[:], in_=f_[:], func=AF.Ln)
                        g1 = p.G([v1], "tensor_scalar", k_[:], f_[:], -1.0, 1.0, ALU.mult, ALU.add)
                        v2 = p.V([a2], "tensor_tensor_scan", cum[:], rmask[:], g_[:], 0.0, ALU.mult, ALU.add)
                        cum3 = cum[:].rearrange("p (n s) -> p n s", s=64)
                        lastb = bass.AP(cum, 63, [[SEG, 128], [64, NCH], [0, 64]])
                        if d == 0:
                            cc = cum
                            v3 = p.V([v2], "tensor_tensor", cb[:].rearrange("p (n s) -> p n s", s=64), lastb, cum3,
                                     ALU.subtract)
                            dl = cb
                        else:
                            v3a = p.V([v2], "tensor_tensor", cb[:].rearrange("p (n s) -> p n s", s=64), lastb, cum3,
                                      ALU.subtract)
                            v3b = p.V([v3a], "tensor_tensor", cb[:], cb[:], g_[:], ALU.add)
                            cc = cb
                            v3 = p.V([v3b], "tensor_tensor", g_[:], cum[:], g_[:], ALU.subtract)
                            dl = g_
                        a3 = p.A([v3], out=ex[:], in_=cc[:], func=AF.Exp)
                        g2 = p.G([a3, aq], "tensor_tensor", qdec[d][:], qr[:], ex[:], ALU.mult)
                        a4 = p.A([g2], out=ex[:], in_=cc[:], func=AF.Exp, scale=-1.0)
                        g3 = p.G([a4, g1], "tensor_tensor", kdec[:], k_[:], ex[:], ALU.mult)
                        a5 = p.A([g3], out=ex[:], in_=dl[:], func=AF.Exp)
                        g4 = p.G([a5], "tensor_tensor", kend[:], k_[:], ex[:], ALU.mult)
                        a6 = p.A([v2], out=el[:, d, :], in_=cum3[:, :, 63], func=AF.Exp)
                        sq = p.st(qd.ap()[d, :, t0:t0 + SEG], qdec[d][:], "hsq", deps=[g2])
                        se = p.st(eld.ap()[d, :, sg * NCH:(sg + 1) * NCH], el[:, d, :], "hse", deps=[a6])
                        prevc = [None, None]
                        prevs = [None, None]
                        sdt = [None, None]
                        for n in range(NCH):
                            sl = slice(n * 64, (n + 1) * 64)
                            pk = n % 2
                            tk = p.mm([g2, g3, prevc[pk]], p.ps[pk][0:64, 0:64], kdec[:, sl], qdec[d][:, sl], True, True,
                                      mark=True)
                            if d == 0:
                                ev = p.V([tk], "tensor_tensor", sc1[:, n, :], p.ps[pk][0:64, 0:64], hm[:, 0, :], ALU.mult)
                            else:
                                ev0 = p.V([tk], "tensor_tensor", sct[:, n, :], p.ps[pk][0:64, 0:64], hm[:, 1, :],
                                          ALU.mult)
                                ev = p.V([ev0], "tensor_tensor", sct[:, n, :], sct[:, n, :], sc1[:, n, :], ALU.add)
                            prevc[pk] = ev
                            tk2 = p.tr([g4, prevs[pk]], p.pb[pk][0:64, 0:128], kend[:, sl], ident[:], mark=True)
                            ev2 = p.A([tk2], out=kendT[:, n, :], in_=p.pb[pk][0:64, 0:128], func=AF.Copy)
                            prevs[pk] = ev2
                            pk2 = 2 + pk
                            tk3 = p.mm([ev2, tv, sdt[pk]], p.ps[pk2][:, 0:128], kendT[:, n, :], vt[:, n, :], True, True,
                                       mark=True)
                            ev3 = p.V([tk3], "tensor_copy", dSs[pk][:], p.ps[pk2][:, 0:128])
                            sdt[pk] = p.st(dS.ap()[d, sg * NCH + n, :, :], dSs[pk][:], f"hds{pk}", deps=[ev3])
                            p.pe.wait(ev3)
                        if d == 1:
                            p.st(scd.ap()[:, sg * NCH:(sg + 1) * NCH, :], sct[:], "hsc", deps=[ev])
                        p.barrier()
            with contextlib.ExitStack() as es:
                G = min(NT, 32)
                dsl = p.sb(es, "h2_ds", [128, G, 128], F32)
                sbo = p.sb(es, "h2_sb", [128, G, 128], BF16)
                state = p.sb(es, "h2_st", [128, 128], F32)
                ela = p.sb(es, "h2_el", [128, 2, NT], F32)
                te = p.ld(ela[:], eld.ap().rearrange("d p n -> p d n"), "h2e")
                for d in range(2):
                    z = p.V([], "memset", state[:], 0.0)
                    groups = list(range(NT // G))
                    if d == 1:
                        groups = groups[::-1]
                    for gi in groups:
                        tl = p.ld(dsl[:], dS.ap()[d, gi * G:(gi + 1) * G, :, :].rearrange("n p v -> p n v"), "h2l")
                        order = list(range(G)) if d == 0 else list(range(G))[::-1]
                        lastv = None
                        for j in order:
                            n = gi * G + j
                            c = p.G([z, lastv], "tensor_copy", sbo[:, j, :], state[:])
                            lastv = p.V([tl, te, c], "scalar_tensor_tensor", state[:], state[:], ela[:, d, n:n + 1],
                                        dsl[:, j, :], ALU.mult, ALU.add)
                        p.st(Sb.ap()[d, gi * G:(gi + 1) * G, :, :].rearrange("n p v -> p n v"), sbo[:], "h2s",
                             deps=[c])
                        p.barrier()
            with contextlib.ExitStack() as es:
                G = min(NT, 32)
                qd3 = p.sb(es, "h3_qd", [128, 2, G * 64], BF16)
                sc3 = p.sb(es, "h3_sc", [64, G, 64], BF16)
                v3_ = p.sb(es, "h3_v", [64, G, 128], BF16)
                gt3 = p.sb(es, "h3_g", [64, G, 128], F32)
                s3 = p.sb(es, "h3_s", [128, 2, G, 128], BF16)
                o3 = p.sb(es, "h3_o", [64, G, 128], F32)
                ob3 = p.sb(es, "h3_ob", [64, G, 128], BF16)
                jk = p.sb(es, "h3_j", [64, 128], F32)
                ss = p.sb(es, "h3_ss", [64, G], F32)
                for gi in range(NT // G):
                    t0 = gi * G * 64
                    tl = [p.ld(qd3[:], qd.ap()[:, :, t0:t0 + G * 64].rearrange("d p t -> p d t"), "h3a"),
                          p.ld(sc3[:], scd.ap()[:, gi * G:(gi + 1) * G, :], "h3a"),
                          p.ld(v3_[:], vtok.ap()[t0:t0 + G * 64, h * 128:(h + 1) * 128].rearrange("(n s) v -> s n v", s=64), "h3a"),
                          p.ld(gt3[:], gates.ap()[t0:t0 + G * 64, 1024 + h * 128:1024 + (h + 1) * 128].rearrange("(n s) v -> s n v", s=64), "h3a"),
                          p.ld(s3[:, 0], Sb.ap()[0, gi * G:(gi + 1) * G, :, :].rearrange("n p v -> p n v"), "h3a"),
                          p.ld(s3[:, 1], Sb.ap()[1, gi * G:(gi + 1) * G, :, :].rearrange("n p v -> p n v"), "h3a")]
                    ag = p.A([tl[3]], out=gt3[:], in_=gt3[:], func=AF.Silu)
                    prev = [None, None]
                    evs = []
                    for j in range(G):
                        pk = j % 2
                        ps = p.ps[pk][0:64, 0:128]
                        sl = slice(j * 64, (j + 1) * 64)
                        p.mm(tl + [prev[pk]], ps, sc3[:, j, :], v3_[:, j, :], True, False)
                        p.mm([], ps, qd3[:, 0, sl], s3[:, 0, j, :], False, False)
                        tk = p.mm([], ps, qd3[:, 1, sl], s3[:, 1, j, :], False, True, mark=True)
                        ev = p.V([tk], "tensor_copy", o3[:, j, :], ps)
                        ev2 = p.A([ev], out=jk[:], in_=o3[:, j, :], func=AF.Square, accum_out=ss[:, j:j + 1])
                        prev[pk] = ev
                        evs = [ev, ev2]
                    r2 = p.rsqrt(evs, ss[:], ss[:], 1.0 / 128, EPS)
                    ssb = bass.AP(ss, 0, [[G, 64], [1, G], [0, 128]])
                    r3 = p.V([r2], "tensor_tensor", o3[:], o3[:], ssb, ALU.mult)
                    ghb = bass.AP(ghg_sb, 0, [[128, 64], [0, G], [1, 128]])
                    r4 = p.V([r3], "tensor_tensor", o3[:], o3[:], ghb, ALU.mult)
                    r5 = p.V([r4, ag], "tensor_tensor", ob3[:], o3[:], gt3[:], ALU.mult)
                    p.st(ymix.ap()[t0:t0 + G * 64, 1024 + h * 128:1024 + (h + 1) * 128].rearrange("(n s) v -> s n v", s=64),
                         ob3[:], "h3s", deps=[r5])
                    p.barrier()
        p.barrier()

    def outproj(L, ymix, ya, gates, wbo, gpost, xres, xout, hT_next):
        with contextlib.ExitStack() as es:
            wsb = p.sb(es, "op_w", [128, KC, D], BF16)
            tw = p.ld(wsb[:], wbo.ap().rearrange("(k p) n -> p k n", p=128), "opw")
            ym = [p.sb(es, f"op_ym{i}", [128, D], BF16) for i in range(2)]
            yaf = p.sb(es, "op_ya", [128, 1024], F32) if ya is not None else None
            gaf = p.sb(es, "op_ga", [128, 1024], F32) if ya is not None else None
            ymT = p.sb(es, "op_ymT", [128, KC, 128], BF16)
            xr = [p.sb(es, f"op_x{i}", [128, D], F32) for i in range(2)]
            yo = p.sb(es, "op_y", [128, D], F32)
            junk = p.sb(es, "op_j", [128, D], BF16)
            st_ = p.sb(es, "op_st", [128, 4], F32)
            hb = p.sb(es, "op_hb", [128, D], BF16)
            ho = p.sb(es, "op_ho", [128, KC, 128], BF16)
            for ti in range(L // 128):
                b = ti % 2
                rows = slice(ti * 128, (ti + 1) * 128)
                tx = p.ld(xr[b][:], xres[rows, :], f"opx{b}")
                if ya is not None:
                    t1 = p.ld(ym[b][:, 1024:2048], ymix.ap()[rows, 1024:2048], f"opm{b}")
                    t2 = p.ld(yaf[:], ya.ap()[rows, :], "opa")
                    t3 = p.ld(gaf[:], gates.ap()[rows, 0:1024], "opa")
                    a1 = p.A([t3], out=gaf[:], in_=gaf[:], func=AF.Silu)
                    v1 = p.V([a1, t2], "tensor_tensor", ym[b][:, 0:1024], yaf[:], gaf[:], ALU.mult)
                    rdy = [t1, v1]
                else:
                    rdy = [p.ld(ym[b][:], ymix.ap()[rows, :], f"opm{b}")]
                for kc in range(KC):
                    tk = p.tr(rdy if kc == 0 else [], p.pb[kc // 8][:, (kc % 8) * 128:(kc % 8 + 1) * 128],
                              ym[b][:, kc * 128:(kc + 1) * 128], ident[:], mark=(kc == KC - 1))
                c1 = p.A([tk], out=ymT[:, 0:8, :], in_=p.pb[0][:].rearrange("p (k t) -> p k t", k=8), func=AF.Copy)
                c2 = p.V([tk], "tensor_copy", ymT[:, 8:16, :], p.pb[1][:].rearrange("p (k t) -> p k t", k=8))
                for cb in range(4):
                    for kc in range(KC):
                        tk = p.mm([c1, c2, tw] if kc == 0 else [], p.ps[cb][:], ymT[:, kc, :],
                                  wsb[:, kc, cb * 512:(cb + 1) * 512], kc == 0, kc == KC - 1, mark=(kc == KC - 1))
                evs = []
                for cb in range(4):
                    if cb % 2 == 0:
                        evs.append(p.A([tk], out=yo[:, cb * 512:(cb + 1) * 512], in_=p.ps[cb][:], func=AF.Copy))
                    else:
                        evs.append(p.V([tk], "tensor_copy", yo[:, cb * 512:(cb + 1) * 512], p.ps[cb][:]))
                a2 = p.A(evs, out=junk[:], in_=yo[:], func=AF.Square, accum_out=st_[:, 0:1])
                v3 = p.rsqrt([a2], st_[:, 1:2], st_[:, 0:1], 1.0 / D, EPS)
                v4 = p.V([v3], "scalar_tensor_tensor", yo[:], yo[:], st_[:, 1:2], gpost[:], ALU.mult, ALU.mult)
                v5 = p.V([v4, tx], "tensor_tensor", yo[:], yo[:], xr[b][:], ALU.add)
                so = p.st(xout[rows, :], yo[:], "opo", deps=[v5])
                last = [so]
                if hT_next is not None:
                    a3 = p.A([v5], out=junk[:], in_=yo[:], func=AF.Square, accum_out=st_[:, 2:3])
                    v7 = p.rsqrt([a3], st_[:, 3:4], st_[:, 2:3], 1.0 / D, EPS)
                    v8 = p.V([v7], "tensor_scalar", hb[:], yo[:], st_[:, 3:4], None, ALU.mult)
                    for kc in range(KC):
                        tk = p.tr([v8] if kc == 0 else [], p.pb[kc // 8][:, (kc % 8) * 128:(kc % 8 + 1) * 128],
                                  hb[:, kc * 128:(kc + 1) * 128], ident[:], mark=(kc == KC - 1))
                    c1 = p.A([tk], out=ho[:, 0:8, :], in_=p.pb[0][:].rearrange("p (k t) -> p k t", k=8), func=AF.Copy)
                    c2 = p.V([tk], "tensor_copy", ho[:, 8:16, :], p.pb[1][:].rearrange("p (k t) -> p k t", k=8))
                    last.append(p.st(hT_next.ap().rearrange("(k p) t -> p k t", p=128)[:, :, rows], ho[:], "oph",
                                     deps=[c1, c2]))
                for e in (p.pe, p.act, p.dve, p.pool, p.sp):
                    e.wait(*last, (p.dve, p.dve.n), (p.act, p.act.n))
        p.barrier()

    def attention(qT, kT, vtok, gates, Lq, Lk, og, dtab, lam_sb, gsub_sb, delta):
        NKT, NQB = Lk // 128, Lq // 256
        with contextlib.ExitStack() as es:
            qs = p.sb(es, "at_q", [128, 2, Lq], BF16)
            ks = p.sb(es, "at_k", [128, 2, Lk], BF16)
            vs = p.sb(es, "at_v", [128, NKT, 257], BF16)
            absd = [p.sb(es, f"at_ad{i}", [128, 256], F32) for i in range(2)]
            sb_ = [p.sb(es, f"at_s{i}", [128, 256], F32) for i in range(2)]
            pt = [p.sb(es, f"at_p{i}", [128, 256], BF16) for i in range(2)]
            gt = p.sb(es, "at_g", [128, 2, 256], F32)
            o0 = p.sb(es, "at_o0", [128, 2, 256], F32)
            o1 = p.sb(es, "at_o1", [128, 2, 256], F32)
            rs = p.sb(es, "at_rs", [128, 8], F32)
            ob = p.sb(es, "at_ob", [128, 2, 256], BF16)
            jk = p.sb(es, "at_j", [128, 256], F32)
            for h in range(8):
                slope = SLOPES[h]
                t_in = [p.ld(qs[:], qT.ap()[h * 256:(h + 1) * 256, :].rearrange("(c p) t -> p c t", p=128), "atq"),
                        p.ld(ks[:], kT.ap()[h * 256:(h + 1) * 256, :].rearrange("(c p) t -> p c t", p=128), "atq"),
                        p.ld(vs[:, :, 0:256], vtok.ap()[:, h * 256:(h + 1) * 256].rearrange("(n p) v -> p n v", p=128),
                             "atq")]
                t_in.append(p.G([], "memset", vs[:, :, 256:257], 1.0))
                for qb in range(NQB):
                    q0 = qb * 256
                    tg = p.ld(gt[:], gates.ap()[q0:q0 + 256, h * 256:(h + 1) * 256].rearrange("(j p) v -> p j v", p=128),
                              "atg")
                    acc = [[p.ps[2 + 2 * c + j] for j in range(2)] for c in range(2)]
                    u = 0
                    rd_ad = [None, None]
                    rd_s = [None, None]
                    rd_p = [None, None]
                    rd_ps = [None, None]
                    lastpv = None
                    for kt in range(NKT):
                        idx = kt * NQB + qb
                        ab = kt % 2
                        g1 = p.A([rd_ad[ab]], out=absd[ab][:], in_=delta[:], func=AF.Abs, bias=dtab[:, idx:idx + 1])
                        for c in range(2):
                            b = u % 2
                            u += 1
                            tk = p.mm(t_in + [rd_ps[b]], p.ps[b][:, 0:256], ks[:, c, kt * 128:(kt + 1) * 128],
                                      qs[:, c, q0:q0 + 256], True, True, mark=True)
                            v1 = p.V([tk, g1, rd_s[b]], "scalar_tensor_tensor", sb_[b][:], absd[ab][:], -slope,
                                     p.ps[b][:, 0:256], ALU.mult, ALU.add)
                            rd_ad[ab] = v1
                            rd_ps[b] = v1
                            a1 = p.A([v1, rd_p[b]], out=pt[b][:], in_=sb_[b][:], func=AF.Exp)
                            rd_s[b] = a1
                            for j in range(2):
                                lastpv = p.mm([a1] if j == 0 else [], acc[c][j][:, 0:257], pt[b][:, j * 128:(j + 1) * 128],
                                              vs[:, kt, :], kt == 0, kt == NKT - 1, mark=(j == 1))
                            rd_p[b] = lastpv
                    evs = []
                    for c in range(2):
                        for j in range(2):
                            dst = (o0 if c == 0 else o1)
                            e1 = p.V([lastpv], "reciprocal", rs[:, 2 * c + j:2 * c + j + 1], acc[c][j][:, 256:257])
                            if c == 0:
                                evs.append(p.V([e1], "tensor_scalar", dst[:, j, :], acc[c][j][:, 0:256],
                                               rs[:, 2 * c + j:2 * c + j + 1], None, ALU.mult))
                            else:
                                evs.append(p.V([e1], "tensor_scalar", dst[:, j, :], acc[c][j][:, 0:256],
                                               rs[:, 2 * c + j:2 * c + j + 1], lam_sb[:, 1:2], ALU.mult, ALU.mult))
                    p.pe.wait(*evs)
                    f1 = p.V(evs, "tensor_tensor", o0[:], o0[:], o1[:], ALU.add)
                    for j in range(2):
                        p.A([f1], out=jk[:], in_=o0[:, j, :], func=AF.Square, accum_out=rs[:, 4 + j:5 + j])
                    f2 = p.rsqrt([(p.act, p.act.n)], rs[:, 4:6], rs[:, 4:6], 1.0 / 256, EPS)
                    f3 = p.V([f2], "tensor_scalar", rs[:, 4:6], rs[:, 4:6], (1.0 - LAMBDA_INIT), None, ALU.mult)
                    ag = p.A([tg], out=gt[:], in_=gt[:], func=AF.Silu)
                    for j in range(2):
                        f4 = p.V([f3], "scalar_tensor_tensor", o0[:, j, :], o0[:, j, :], rs[:, 4 + j:5 + j], gsub_sb[:],
                                 ALU.mult, ALU.mult)
                    f5 = p.V([f4, ag], "tensor_tensor", ob[:], o0[:], gt[:], ALU.mult)
                    so = p.st(og.ap()[q0:q0 + 256, h * 256:(h + 1) * 256].rearrange("(j p) v -> p j v", p=128), ob[:],
                              "ato", deps=[f5])
                    for e in (p.act, p.dve, p.pool, p.sp, p.pe):
                        e.wait(so, f5)
                p.barrier()
        p.barrier()

    c128 = p.sb(ges, "c128_sb", [128, 3, 128], BF16)
    p.ld(c128[:], c128_d.ap().rearrange("k p c -> p k c"), "c0")
    cs256 = p.sb(ges, "cs256_sb", [128, 2, 512], BF16)
    p.ld(cs256[:], cs256_d.ap().rearrange("(c p) n -> p c n", p=128), "c0")
    tw_sb, cm_sb = {}, {}
    for L in tw_d:
        M = L // 128
        tw_sb[L] = p.sb(ges, f"tw{L}_sb", [128, 3, M], F32)
        p.ld(tw_sb[L][:], tw_d[L].ap().rearrange("k p m -> p k m"), "c0")
        cm_sb[L] = p.sb(ges, f"cm{L}_sb", [M, 2, M], BF16)
        p.ld(cm_sb[L][:], cm_d[L].ap().rearrange("k b d -> b k d"), "c0")
    hm = p.sb(ges, "hm_sb", [64, 2, 64], F32)
    p.ld(hm[:], hmask_d.ap().rearrange("k s t -> s k t"), "c0")
    delta = p.sb(ges, "delta_sb", [128, 256], F32)
    p.ld(delta[:], delta_d.ap(), "c0")
    dtabs = p.sb(ges, "dtabs_sb", [128, (LS // 128) * (LS // 256)], F32)
    p.ld(dtabs[:], dtabs_d.ap(), "c0")
    dtabp = p.sb(ges, "dtabp_sb", [128, (LP // 128) * (OWN // 256)], F32)
    p.ld(dtabp[:], dtabp_d.ap(), "c0")
    ownidx = p.sb(ges, "ownidx_sb", [128, OWN // 128], I32)
    p.ld(ownidx[:], ownidx_d.ap(), "c0")
    ghg_sb = p.sb(ges, "ghg_sb", [64, 128], F32)
    p.ld(ghg_sb[:], bcast_rows(ghg.ap(), 128, 64), "c0")
    gsub_sb = p.sb(ges, "gsub_sb", [128, 256], F32)
    p.ld(gsub_sb[:], bcast_rows(gsub.ap(), 256), "c0")
    lraw = p.sb(ges, "lraw_sb", [128, 2, 3, 8], F32)
    p.ld(lraw[:], lbl.ap().rearrange("d s (h k) -> k d s h", k=128), "c0", slow=True)
    lbt = p.sb(ges, "lbt_sb", [128, 4, 8], F32)
    lsum = p.sb(ges, "lsum_sb", [128, 2, 8], F32)
    lamv = p.sb(ges, "lamv_sb", [128, 4, 128], F32)
    p.ld(lamv[:], bass.AP(lam4.ap().tensor, 0, [[0, 128], [128, 4], [1, 128]]), "c0")
    lam_sb = p.sb(ges, "lam_sb", [128, 4], F32)
    ljunk = p.sb(ges, "ljunk_sb", [128, 128], F32)
    p.barrier()
    a = p.A([], out=lraw[:], in_=lraw[:], func=AF.Exp)
    v = p.V([a], "tensor_tensor", lsum[:], lraw[:, :, 0, :], lraw[:, :, 1, :], ALU.add)
    v = p.V([v], "tensor_tensor", lsum[:], lsum[:], lraw[:, :, 2, :], ALU.add)
    v = p.V([v], "reciprocal", lsum[:], lsum[:])
    v = p.V([v], "tensor_tensor", lbt[:, 0:2, :], lraw[:, :, 0, :], lsum[:], ALU.mult)
    v = p.V([v], "tensor_scalar", lbt[:, 2:4, :], lbt[:, 0:2, :], -1.0, 1.0, ALU.mult, ALU.add)
    v = p.V([v], "tensor_tensor", ljunk[:], lamv[:, 0, :], lamv[:, 1, :], ALU.mult)
    v = p.V([v], "tensor_reduce", lam_sb[:, 2:3], ljunk[:], mybir.AxisListType.X, ALU.add)
    v = p.V([v], "tensor_tensor", ljunk[:], lamv[:, 2, :], lamv[:, 3, :], ALU.mult)
    v = p.V([v], "tensor_reduce", lam_sb[:, 3:4], ljunk[:], mybir.AxisListType.X, ALU.add)
    a = p.A([v], out=lam_sb[:, 2:4], in_=lam_sb[:, 2:4], func=AF.Exp)
    v = p.V([a], "tensor_tensor", lam_sb[:, 0:1], lam_sb[:, 2:3], lam_sb[:, 3:4], ALU.subtract)
    v = p.V([v], "tensor_scalar", lam_sb[:, 1:2], lam_sb[:, 0:1], LAMBDA_INIT, -1.0, ALU.add, ALU.mult)
    p.barrier()

    seqs = [("s%d" % i, LS, xs.ap()[i * LS:(i + 1) * LS, :], ys.ap()[i * LS:(i + 1) * LS, :]) for i in range(NS)]
    seqs.append(("p", LP, xp.ap(), None))
    scale_q = 128 ** -0.5
    for (nm, L, xin, yout) in seqs:
        isP = yout is None
        hT = p.dram(f"hT_{nm}", [D, L], BF16)
        norm_T(xin, L, hT)
        done("norm")
        uT = p.dram(f"uT_{nm}", [1024, L], BF16)
        gat0 = p.dram(f"g0_{nm}", [L, 2048], F32)
        qrT = p.dram(f"qr_{nm}", [1024, L], F32)
        ffT = p.dram(f"ff_{nm}", [1024, L], F32)
        fbT = p.dram(f"fb_{nm}", [1024, L], F32)
        vtk = p.dram(f"vt_{nm}", [L, 1024], BF16)
        proj(hT, L, wb0i, 7168, [
            (0, 1024, "F", uT, 0, 1.0, BF16),
            (1024, 1024, "T", gat0, 0, 1.0, F32),
            (2048, 1024, "F", qrT, 0, 1.0, F32),
            (3072, 1024, "T", vtk, 0, 1.0, BF16),
            (4096, 1024, "F", ffT, 0, 1.0, F32),
            (5120, 1024, "F", fbT, 0, 1.0, F32),
            (6144, 1024, "T", gat0, 1024, 1.0, F32),
        ])
        done("proj0")
        ya = p.dram(f"ya_{nm}", [L, 1024], F32)
        fnet(uT, L, ya, (c128, cs256, tw_sb[L], cm_sb[L]))
        done("fnet")
        ymix = p.dram(f"ym_{nm}", [L, 2048], BF16)
        hgrn(qrT, ffT, fbT, vtk, gat0, L, ymix, lbt, hm, ghg_sb)
        done("hgrn")
        x1 = p.dram(f"x1_{nm}", [L, D], F32)
        h1T = p.dram(f"h1T_{nm}", [D, L], BF16)
        outproj(L, ymix, ya, gat0, wb0o, gpost_sb[0], xin, x1.ap(), h1T)
        done("out0")
        kT = p.dram(f"kT_{nm}", [2048, L], BF16)
        v1t = p.dram(f"v1_{nm}", [L, 2048], BF16)
        if not isP:
            Lq = L
            qT = p.dram(f"qT_{nm}", [2048, Lq], BF16)
            gat1 = p.dram(f"g1_{nm}", [Lq, 2048], F32)
            proj(h1T, L, wb1i, 8192, [
                (0, 2048, "F", qT, 0, scale_q, BF16),
                (2048, 2048, "F", kT, 0, 1.0, BF16),
                (4096, 2048, "T", v1t, 0, 1.0, BF16),
                (6144, 2048, "T", gat1, 0, 1.0, F32),
            ])
            xres1 = x1.ap()
            dtab = dtabs
        else:
            Lq = OWN
            proj(h1T, L, wb1i, 8192, [
                (2048, 2048, "F", kT, 0, 1.0, BF16),
                (4096, 2048, "T", v1t, 0, 1.0, BF16),
            ])
            x1own = p.dram("x1own", [OWN, D], F32)
            with contextlib.ExitStack() as es:
                gx = p.sb(es, "gx", [128, D], F32)
                prev = None
                for ti in range(OWN // 128):
                    p.pool.wait(prev)
                    ds = p.dsem("gath")
                    p.nc.gpsimd.indirect_dma_start(
                        out=gx[:], out_offset=None, in_=x1.ap(),
                        in_offset=bass.IndirectOffsetOnAxis(ap=ownidx[:, ti:ti + 1], axis=0),
                    ).then_inc(ds.sem, 16)
                    ds.n += 16
                    tok = (ds, ds.n)
                    p.pending.append(tok)
                    prev = p.st(x1own.ap()[ti * 128:(ti + 1) * 128, :], gx[:], "gaths", deps=[tok])
            p.barrier()
            h1To = p.dram("h1To", [D, OWN], BF16)
            norm_T(x1own.ap(), OWN, h1To)
            qT = p.dram(f"qT_{nm}", [2048, Lq], BF16)
            gat1 = p.dram(f"g1_{nm}", [Lq, 2048], F32)
            proj(h1To, OWN, wb1i, 8192, [
                (0, 2048, "F", qT, 0, scale_q, BF16),
                (6144, 2048, "T", gat1, 0, 1.0, F32),
            ])
            xres1 = x1own.ap()
            yout = yp.ap()
            dtab = dtabp
        done("proj1")
        og = p.dram(f"og_{nm}", [Lq, 2048], BF16)
        attention(qT, kT, v1t, gat1, Lq, L, og, dtab, lam_sb, gsub_sb, delta)
        done("attn")
        outproj(Lq, og, None, None, wb1o, gpost_sb[1], xres1, yout, None)
        done("seq0")


def dft_cs(n):
    j = np.arange(n)
    ang = 2 * np.pi * np.outer(j, j) / n
    return np.cos(ang), np.sin(ang)


def host_tables(cfg, core):
    NS, LS, LP, OWN = cfg["NS"], cfg["LS"], cfg["LP"], cfg["OWN"]
    t = {}
    t["ident"] = np.eye(128, dtype=np.float32).astype(NPBF)
    c, s = dft_cs(128)
    t["c128"] = np.stack([c, s, -s]).astype(np.float32).astype(NPBF)
    c, s = dft_cs(256)
    t["cs256"] = np.concatenate([c, -s], axis=1).astype(np.float32).astype(NPBF)
    for L in sorted({LS, LP}):
        M = L // 128
        ang = 2 * np.pi * np.outer(np.arange(128), np.arange(M)) / L
        t[f"tw{L}"] = np.stack([np.cos(ang), np.sin(ang), -np.sin(ang)]).astype(np.float32)
        cm, sm = dft_cs(M)
        sc = 1.0 / math.sqrt(L * 256)
        t[f"cm{L}"] = np.stack([cm * sc, sm * sc]).astype(np.float32).astype(NPBF)
    s_, t_ = np.meshgrid(np.arange(64), np.arange(64), indexing="ij")
    t["hmask"] = np.stack([(s_ <= t_), (s_ >= t_)]).astype(np.float32)
    t["delta"] = (np.arange(128)[:, None] - np.arange(256)[None, :]).astype(np.float32)
    kt, qb = np.meshgrid(np.arange(LS // 128), np.arange(LS // 256), indexing="ij")
    t["dtabs"] = np.broadcast_to((kt * 128 - qb * 256).reshape(1, -1), (128, kt.size)).astype(np.float32).copy()
    kt, qb = np.meshgrid(np.arange(LP // 128), np.arange(OWN // 256), indexing="ij")
    t["dtabp"] = np.broadcast_to((kt * 128 - (core * OWN + qb * 256)).reshape(1, -1), (128, kt.size)).astype(np.float32).copy()
    t["ownidx"] = (core * OWN + np.arange(OWN)).reshape(OWN // 128, 128).T.astype(np.int32).copy()
    return t


def run(inputs, cfg, ncores):
    NS, LS, LP, OWN = cfg["NS"], cfg["LS"], cfg["LP"], cfg["OWN"]
    f = lambda a: np.ascontiguousarray(np.asarray(a, dtype=np.float32))
    xsamp = f(inputs["x_sample"])
    shared = {
        "xp": f(inputs["x_prompt"])[0],
        "w0i": f(inputs["ev_w_in"])[0], "w0o": f(inputs["ev_w_out"])[0],
        "w1i": f(inputs["od_w_in"])[0], "w1o": f(inputs["od_w_out"])[0],
        "g0pre": f(inputs["ev_norm_pre"])[0], "g0post": f(inputs["ev_norm_post"])[0],
        "g1pre": f(inputs["od_norm_pre"])[0], "g1post": f(inputs["od_norm_post"])[0],
        "lbl": f(inputs["hgrn_lb_logits"]), "ghg": f(inputs["hgrn_norm"])[0],
        "lam4": np.stack([f(inputs["lambda_q1"])[0], f(inputs["lambda_k1"])[0],
                          f(inputs["lambda_q2"])[0], f(inputs["lambda_k2"])[0]]),
        "gsub": f(inputs["subln"])[0],
    }
    nc = build(cfg)
    in_maps = []
    for c in range(ncores):
        m = dict(shared)
        m["xs"] = xsamp[c * NS:(c + 1) * NS].reshape(NS * LS, D)
        m.update(host_tables(cfg, c))
        in_maps.append(m)
    res = run_bass_kernel_spmd(nc, in_maps, core_ids=list(range(ncores)))
    global LAST_RES
    LAST_RES = res.results
    y_s = np.concatenate([r["ys"].reshape(NS, LS, D) for r in res.results], axis=0)
    y_p = np.concatenate([r["yp"] for r in res.results], axis=0)[None]
    return y_p.astype(np.float32), y_s.astype(np.float32)


def kernel(**inputs):
    cfg = {"NS": 2, "LS": 2048, "LP": 8192, "OWN": 1024}
    return run(inputs, cfg, 8)
```

```python
import contextlib, math
import numpy as np
import ml_dtypes
import concourse.bass as bass
import concourse.mybir as mybir
from concourse.bass_utils import run_bass_kernel_spmd

F32, BF16, I32 = mybir.dt.float32, mybir.dt.bfloat16, mybir.dt.int32
AF = mybir.ActivationFunctionType
ALU = mybir.AluOpType
D = 2048
KC = 16
EPS = 1e-6
LAMBDA_INIT = 0.8 - 0.6 * math.exp(-0.3 * 1)
SLOPES = [2.0 ** (-8.0 * (h + 1) / 8) for h in range(8)]
NPBF = ml_dtypes.bfloat16


class StopBuild(Exception):
    pass


class Eng:
    def __init__(self, e, sem):
        self.e, self.sem, self.n, self.seen = e, sem, 0, {}

    def mark(self, ins):
        ins.then_inc(self.sem, 1)
        self.n += 1
        return (self, self.n)

    def wait(self, *toks):
        for tok in toks:
            if tok is None:
                continue
            src, n = tok
            if self.seen.get(src, 0) >= n:
                continue
            self.e.wait_ge(src.sem, n)
            self.seen[src] = n


class DSem:
    def __init__(self, sem):
        self.sem, self.n = sem, 0


class P:
    def __init__(self, cfg):
        self.cfg = cfg
        self.es = contextlib.ExitStack()
        nc = self.nc = bass.Bass("TRN2", target_bir_lowering=False)
        mk = lambda nm: self.es.enter_context(nc.semaphore(nm))
        self.pe = Eng(nc.tensor, mk("s_pe"))
        self.act = Eng(nc.scalar, mk("s_act"))
        self.dve = Eng(nc.vector, mk("s_dve"))
        self.pool = Eng(nc.gpsimd, mk("s_pool"))
        self.sp = Eng(nc.sync, mk("s_sp"))
        self.engs = [self.pe, self.act, self.dve, self.pool, self.sp]
        self.dsems = {}
        self.pending = []
        self.ndram = 0
        self.ps = [self.es.enter_context(nc.psum_tensor(f"ps{i}", [128, 512], F32)) for i in range(6)]
        self.pb = [self.es.enter_context(nc.psum_tensor(f"pb{i}", [128, 1024], BF16)) for i in range(2)]
        self.dummy = self.es.enter_context(nc.sbuf_tensor("dummy_sb", [128, 2], F32))
        self.pool.mark(nc.gpsimd.memset(self.dummy[:], 0.0))

    def sb(self, es, name, shape, dt):
        self.nsb = getattr(self, "nsb", 0) + 1
        return es.enter_context(self.nc.sbuf_tensor(f"{name}_{self.nsb}", shape, dt))

    def dram(self, name, shape, dt, kind="Internal"):
        t = self.nc.dram_tensor(name, list(shape), dt, kind=kind)
        if not hasattr(self, "named"):
            self.named = {}
        self.named[name] = (t, list(shape), dt)
        return t

    def dump(self):
        self.barrier()
        for name in self.cfg.get("dump", []):
            if name not in self.named:
                continue
            t, shape, dt = self.named[name]
            o = self.nc.dram_tensor("dbg_" + name, shape, dt, kind="ExternalOutput")
            self.ld(o.ap(), t.ap(), "dump")
        self.barrier()

    def dsem(self, key):
        if key not in self.dsems:
            self.dsems[key] = DSem(self.es.enter_context(self.nc.semaphore("d_" + key)))
        return self.dsems[key]

    def dma(self, q, out, in_, key, deps=(), slow=False):
        q.wait(*deps)
        ds = self.dsem(key)
        kw = {"allow_slow_non_contiguous": True} if slow else {}
        q.e.dma_start(out=out, in_=in_, **kw).then_inc(ds.sem, 16)
        ds.n += 16
        tok = (ds, ds.n)
        self.pending.append(tok)
        return tok

    def ld(self, out, in_, key, deps=(), slow=False):
        return self.dma(self.sp, out, in_, key, deps, slow)

    def st(self, out, in_, key, deps=()):
        return self.dma(self.pool, out, in_, key, deps)

    def barrier(self):
        toks = [(e, e.n) for e in self.engs if e.n > 0] + self.pending
        for e in self.engs:
            e.wait(*toks)
        self.pending = []

    def A(self, deps, *a, **k):
        self.act.wait(*deps)
        tok = self.act.mark(self.act.e.activation(*a, **k))
        if k.get("accum_out") is not None:
            tok = self.act.mark(self.act.e.activation(out=self.dummy[:, 1:2], in_=self.dummy[:, 0:1], func=AF.Copy))
        return tok

    def V(self, deps, fn, *a, **k):
        self.dve.wait(*deps)
        return self.dve.mark(getattr(self.dve.e, fn)(*a, **k))

    def G(self, deps, fn, *a, **k):
        self.pool.wait(*deps)
        return self.pool.mark(getattr(self.pool.e, fn)(*a, **k))

    def X(self, eng, deps, fn, *a, **k):
        eng.wait(*deps)
        return eng.mark(getattr(eng.e, fn)(*a, **k))

    def rsqrt(self, deps, out, in_, mul, add):
        a = self.A(deps, out=out, in_=in_, func=AF.Ln, scale=float(mul), bias=float(add))
        return self.A([a], out=out, in_=out, func=AF.Exp, scale=-0.5)

    def mm(self, deps, out, lhsT, rhs, start, stop, mark=False):
        self.pe.wait(*deps)
        ins = self.pe.e.matmul(out, lhsT, rhs, start=start, stop=stop)
        return self.pe.mark(ins) if mark else None

    def tr(self, deps, out, in_, ident, mark=False):
        self.pe.wait(*deps)
        ins = self.pe.e.transpose(out, in_, ident)
        return self.pe.mark(ins) if mark else None


def bcast_rows(ap_dram_1d, n, parts=128):
    return bass.AP(ap_dram_1d.tensor, ap_dram_1d.offset, [[0, parts], [1, n]])


def build(cfg):
    p = P(cfg)
    try:
        _build(cfg, p)
    except StopBuild:
        p.dump()
        return p.nc
    p.dump()
    p.es.close()
    return p.nc


def _build(cfg, p):
    NS, LS, LP, OWN = cfg["NS"], cfg["LS"], cfg["LP"], cfg["OWN"]

    def done(tag):
        if cfg.get("stop") == tag:
            raise StopBuild()
    nc = p.nc
    inp = lambda name, shape, dt=F32: nc.dram_tensor(name, list(shape), dt, kind="ExternalInput")
    xs = inp("xs", [NS * LS, D])
    xp = inp("xp", [LP, D])
    w0i, w0o = inp("w0i", [D, 7168]), inp("w0o", [D, D])
    w1i, w1o = inp("w1i", [D, 8192]), inp("w1o", [D, D])
    g0pre, g0post = inp("g0pre", [D]), inp("g0post", [D])
    g1pre, g1post = inp("g1pre", [D]), inp("g1post", [D])
    lbl = inp("lbl", [2, 3, 1024])
    ghg = inp("ghg", [128])
    lam4 = inp("lam4", [4, 128])
    gsub = inp("gsub", [256])
    ident_d = inp("ident", [128, 128], BF16)
    c128_d = inp("c128", [3, 128, 128], BF16)
    cs256_d = inp("cs256", [256, 512], BF16)
    tw_d = {L: inp(f"tw{L}", [3, 128, L // 128]) for L in sorted({LS, LP})}
    cm_d = {L: inp(f"cm{L}", [2, L // 128, L // 128], BF16) for L in sorted({LS, LP})}
    hmask_d = inp("hmask", [2, 64, 64])
    delta_d = inp("delta", [128, 256])
    dtabs_d = inp("dtabs", [128, (LS // 128) * (LS // 256)])
    dtabp_d = inp("dtabp", [128, (LP // 128) * (OWN // 256)])
    ownidx_d = inp("ownidx", [128, OWN // 128], I32)
    ys = nc.dram_tensor("ys", [NS * LS, D], F32, kind="ExternalOutput")
    yp = nc.dram_tensor("yp", [OWN, D], F32, kind="ExternalOutput")

    ges = p.es
    ident = p.sb(ges, "ident_sb", [128, 128], BF16)
    toks = [p.ld(ident[:], ident_d.ap(), "c0")]
    gpost_sb = [p.sb(ges, f"gpost{i}", [128, D], F32) for i in range(2)]
    toks.append(p.ld(gpost_sb[0][:], bcast_rows(g0post.ap(), D), "c0"))
    toks.append(p.ld(gpost_sb[1][:], bcast_rows(g1post.ap(), D), "c0"))
    gpre_sb = [p.sb(ges, f"gpre{i}", [128, KC], F32) for i in range(2)]
    toks.append(p.ld(gpre_sb[0][:], g0pre.ap().rearrange("(c p) -> p c", p=128), "c0", slow=True))
    toks.append(p.ld(gpre_sb[1][:], g1pre.ap().rearrange("(c p) -> p c", p=128), "c0", slow=True))
    p.barrier()

    def prep_w(w, ncols, gcol, name):
        wb = p.dram(name, [D, ncols], BF16)
        with contextlib.ExitStack() as es:
            CW = 1024
            wf = [p.sb(es, f"wf{i}", [128, CW], F32) for i in range(2)]
            wo = [p.sb(es, f"wo{i}", [128, CW], BF16) for i in range(2)]
            cons = [None, None]
            sts = [None, None]
            i = 0
            for kc in range(KC):
                for c0 in range(0, ncols, CW):
                    b = i % 2
                    t = p.ld(wf[b][:], w.ap()[kc * 128:(kc + 1) * 128, c0:c0 + CW], f"wl{b}", deps=[cons[b]])
                    eng = p.dve if b == 0 else p.pool
                    if gcol is None:
                        cons[b] = p.X(eng, [t, sts[b]], "tensor_copy", wo[b][:], wf[b][:])
                    else:
                        cons[b] = p.X(eng, [t, sts[b]], "tensor_scalar", wo[b][:], wf[b][:],
                                      gcol[:, kc:kc + 1], None, ALU.mult)
                    sts[b] = p.dma(p.act, wb.ap()[kc * 128:(kc + 1) * 128, c0:c0 + CW], wo[b][:], f"ws{b}",
                                   deps=[cons[b]])
                    i += 1
        p.barrier()
        return wb

    wb0i = prep_w(w0i, 7168, gpre_sb[0], "wb0i")
    wb0o = prep_w(w0o, D, None, "wb0o")
    wb1i = prep_w(w1i, 8192, gpre_sb[1], "wb1i")
    wb1o = prep_w(w1o, D, None, "wb1o")
    done("prep")

    def norm_T(x_rows, L, hT):
        with contextlib.ExitStack() as es:
            xt = [p.sb(es, f"nx{i}", [128, D], F32) for i in range(2)]
            junk = p.sb(es, "njunk", [128, D], BF16)
            hb = [p.sb(es, f"nhb{i}", [128, D], BF16) for i in range(2)]
            ho = [p.sb(es, f"nho{i}", [128, KC, 128], BF16) for i in range(2)]
            st_ = [p.sb(es, f"nst{i}", [128, 2], F32) for i in range(2)]
            rd = [None, None]
            hbr = [None, None]
            hor = [None, None]
            for ti in range(L // 128):
                b = ti % 2
                t = p.ld(xt[b][:], x_rows[ti * 128:(ti + 1) * 128, :], f"nl{b}", deps=[rd[b]])
                a1 = p.A([t], out=junk[:], in_=xt[b][:], func=AF.Square, accum_out=st_[b][:, 0:1])
                v2 = p.rsqrt([a1], st_[b][:, 1:2], st_[b][:, 0:1], 1.0 / D, EPS)
                v3 = p.G([v2, t, hbr[b]], "tensor_scalar", hb[b][:], xt[b][:], st_[b][:, 1:2], None, ALU.mult)
                rd[b] = v3
                for kc in range(KC):
                    tk = p.tr([v3, hor[b]] if kc == 0 else [], p.pb[kc // 8][:, (kc % 8) * 128:(kc % 8 + 1) * 128],
                              hb[b][:, kc * 128:(kc + 1) * 128], ident[:], mark=(kc == KC - 1))
                hbr[b] = tk
                c1 = p.A([tk, hor[b]], out=ho[b][:, 0:8, :], in_=p.pb[0][:].rearrange("p (k t) -> p k t", k=8),
                         func=AF.Copy)
                c2 = p.V([tk, hor[b]], "tensor_copy", ho[b][:, 8:16, :],
                         p.pb[1][:].rearrange("p (k t) -> p k t", k=8))
                p.pe.wait(c1, c2)
                hor[b] = p.st(hT.ap().rearrange("(k p) t -> p k t", p=128)[:, :, ti * 128:(ti + 1) * 128],
                              ho[b][:], f"ns{b}", deps=[c1, c2])
        p.barrier()

    def proj(hT, L, wb, ncols_total, jobs):
        TB = min(L, 1024)
        with contextlib.ExitStack() as es:
            hblk = p.sb(es, "pj_h", [128, KC, TB], BF16)
            wblk = [p.sb(es, f"pj_w{i}", [128, KC, 512], BF16) for i in range(2)]
            osb = {F32: [p.sb(es, f"pj_of{i}", [128, 512], F32) for i in range(2)],
                   BF16: [p.sb(es, f"pj_ob{i}", [128, 512], BF16) for i in range(2)]}
            wread = [None, None]
            ost = {F32: [None, None], BF16: [None, None]}
            psr = [None] * 4
            wi = 0
            oi = 0
            pi = 0
            hread = None
            for tb in range(L // TB):
                th = p.ld(hblk[:], hT.ap().rearrange("(k p) t -> p k t", p=128)[:, :, tb * TB:(tb + 1) * TB],
                          "pjh", deps=[hread])
                cbs = [(j, c) for j in jobs for c in range(0, j[1], 512)]
                for (job, c) in cbs:
                    col0, ncols, mode, od, o0, scale, odt = job
                    b = wi % 2
                    wi += 1
                    tw = p.ld(wblk[b][:], wb.ap().rearrange("(k p) n -> p k n", p=128)[:, :, col0 + c:col0 + c + 512],
                              f"pjw{b}", deps=[wread[b]])
                    last = None
                    TW = min(512, TB)
                    if mode == "F":
                        subs = [(s4, t5) for s4 in range(4) for t5 in range(TB // TW)]
                    else:
                        subs = [(s4, 0) for s4 in range(TB // 128)]
                    for (s4, t5) in subs:
                        pk = pi % 4
                        pi += 1
                        W_ = TW if mode == "F" else 512
                        ps = p.ps[pk][:, 0:W_]
                        for kc in range(KC):
                            if mode == "F":
                                lhsT, rhs = wblk[b][:, kc, s4 * 128:(s4 + 1) * 128], hblk[:, kc, t5 * TW:(t5 + 1) * TW]
                            else:
                                lhsT, rhs = hblk[:, kc, s4 * 128:(s4 + 1) * 128], wblk[b][:, kc, :]
                            tk = p.mm([th, tw, psr[pk]] if kc == 0 else [], ps, lhsT, rhs, kc == 0, kc == KC - 1,
                                      mark=(kc == KC - 1))
                        last = tk
                        ob = oi % 2
                        oi += 1
                        o = osb[odt][ob][:, 0:W_]
                        if oi % 2 == 0:
                            ev = p.A([tk, ost[odt][ob]], out=o, in_=ps, func=AF.Copy, scale=float(scale))
                        else:
                            ev = p.V([tk, ost[odt][ob]], "tensor_scalar", o, ps, float(scale), None, ALU.mult)
                        psr[pk] = ev
                        if mode == "F":
                            dst = od.ap()[o0 + c + s4 * 128:o0 + c + (s4 + 1) * 128,
                                          tb * TB + t5 * TW:tb * TB + (t5 + 1) * TW]
                        else:
                            dst = od.ap()[tb * TB + s4 * 128:tb * TB + (s4 + 1) * 128, o0 + c:o0 + c + 512]
                        ost[odt][ob] = p.st(dst, o, f"pjs{ob}{'f' if odt == F32 else 'b'}", deps=[ev])
                    wread[b] = last
                    hread = last
        p.barrier()

    def fnet(uT, L, ya, tabs):
        M = L // 128
        c128, cs256, tw, cm = tabs
        Bd = p.dram(f"fn_B{p.ndram}", [128, M, 512], BF16)
        p.ndram += 1
        CP = 32
        with contextlib.ExitStack() as es:
            ug = p.sb(es, "fn_u", [128, 2, L], BF16)
            MB = min(M, 32)
            V = p.sb(es, "fn_V", [128, MB, 512], BF16)
            Bs = p.sb(es, "fn_Bs", [128, MB, 512], BF16)
            tmp = [p.sb(es, f"fn_t{i}", [128, 2, 256], F32) for i in range(2)]
            Bt = p.sb(es, "fn_Bt", [M, CP, 512], BF16)
            Y = [p.sb(es, f"fn_Y{i}", [M, 2, 256], F32) for i in range(2)]
            for g in range(4):
                tu = p.ld(ug[:], uT.ap()[g * 256:(g + 1) * 256, :].rearrange("(c p) t -> p c t", p=128), "fnu")
                ts_all = []
                for bh in range(M // MB):
                    evs = []
                    prev = [None, None]
                    for bl in range(MB):
                        b = bh * MB + bl
                        pk = b % 2
                        for ch in range(2):
                            lhsT = bass.AP(ug, ch * L + b, [[2 * L, 128], [M, 128]])
                            tk = p.mm([tu, prev[pk]] if ch == 0 else [], p.ps[pk][:], lhsT, cs256[:, ch, :],
                                      ch == 0, ch == 1, mark=(ch == 1))
                        if b % 2 == 0:
                            ev = p.A([tk], out=V[:, bl, :], in_=p.ps[pk][:], func=AF.Copy)
                        else:
                            ev = p.V([tk], "tensor_copy", V[:, bl, :], p.ps[pk][:])
                        prev[pk] = ev
                        evs.append(ev)
                    prevr = [None, None]
                    tw_tok = []
                    for bp in range(MB // 2):
                        pk = 2 + (bp % 2) * 2
                        Ar, Ai = p.ps[pk], p.ps[pk + 1]
                        b0 = 2 * bp
                        dep = [evs[b0], evs[b0 + 1], prevr[bp % 2]]
                        ar3 = Ar[:].rearrange("p (b f) -> p b f", b=2)
                        ai3 = Ai[:].rearrange("p (b f) -> p b f", b=2)
                        p.mm(dep, ar3, c128[:, 0, :], V[:, b0:b0 + 2, 0:256], True, False)
                        p.mm([], ar3, c128[:, 1, :], V[:, b0:b0 + 2, 256:512], False, True)
                        p.mm([], ai3, c128[:, 0, :], V[:, b0:b0 + 2, 256:512], True, False)
                        tk = p.mm([], ai3, c128[:, 2, :], V[:, b0:b0 + 2, 0:256], False, True, mark=True)
                        last = []
                        for j in range(2):
                            bl = b0 + j
                            b = bh * MB + bl
                            t1 = p.V([tk], "tensor_scalar", tmp[0][:, j, :], ar3[:, j, :], tw[:, 0, b:b + 1], None, ALU.mult)
                            t3 = p.V([tk], "tensor_scalar", tmp[1][:, j, :], ai3[:, j, :], tw[:, 0, b:b + 1], None, ALU.mult)
                            r1 = p.V([t1, tk], "scalar_tensor_tensor", Bs[:, bl, 0:256], ai3[:, j, :], tw[:, 1, b:b + 1],
                                     tmp[0][:, j, :], ALU.mult, ALU.add)
                            r2 = p.V([t3, tk], "scalar_tensor_tensor", Bs[:, bl, 256:512], ar3[:, j, :], tw[:, 2, b:b + 1],
                                     tmp[1][:, j, :], ALU.mult, ALU.add)
                            last = [r1, r2]
                        p.act.wait(*last)
                        prevr[bp % 2] = last[1]
                        tw_tok = last
                    ts_all.append(p.st(Bd.ap()[:, bh * MB:(bh + 1) * MB, :], Bs[:], "fnb", deps=tw_tok))
                    p.barrier()
                if True:
                    ts_ = ts_all[-1]
                    yst = [None, None]
                    bt_read = None
                    for cp in range(128 // CP):
                        tl = p.ld(Bt[:], Bd.ap()[cp * CP:(cp + 1) * CP, :, :].rearrange("c b f -> b c f"), "fnbt",
                                  deps=[ts_, bt_read])
                        for c2 in range(CP // 2):
                            pk = c2 % 2
                            ps3 = p.ps[pk][0:M, :].rearrange("p (c f) -> p c f", c=2)
                            p.mm([tl, yst[pk]], ps3, cm[0:M, 0, :], Bt[:, 2 * c2:2 * c2 + 2, 0:256], True, False)
                            tk = p.mm([], ps3, cm[0:M, 1, :], Bt[:, 2 * c2:2 * c2 + 2, 256:512], False, True, mark=True)
                            if c2 % 2 == 0:
                                ev = p.A([tk], out=Y[pk][:], in_=ps3, func=AF.Copy)
                            else:
                                ev = p.V([tk], "tensor_copy", Y[pk][:], ps3)
                            c_abs = cp * CP + 2 * c2
                            dst = ya.ap().rearrange("(d c) f -> d c f", c=128)[:, c_abs:c_abs + 2, g * 256:(g + 1) * 256]
                            yst[pk] = p.st(dst, Y[pk][:], f"fny{pk}", deps=[ev])
                            p.pe.wait(ev)
                            bt_read = tk
                    p.barrier()
        p.barrier()

    def hgrn(qrT, ffT, fbT, vtok, gates, L, ymix, lbt, hm, ghg_sb):
        SEG = min(L, 2048)
        NCH = SEG // 64
        nseg = L // SEG
        NT = L // 64
        dS = p.dram(f"hg_dS{p.ndram}", [2, NT, 128, 128], F32)
        Sb = p.dram(f"hg_Sb{p.ndram}", [2, NT, 128, 128], BF16)
        qd = p.dram(f"hg_qd{p.ndram}", [2, 128, L], BF16)
        scd = p.dram(f"hg_sc{p.ndram}", [64, NT, 64], BF16)
        eld = p.dram(f"hg_el{p.ndram}", [2, 128, NT], F32)
        p.ndram += 1
        for h in range(8):
            with contextlib.ExitStack() as es:
                qr = p.sb(es, "h_qr", [128, SEG], F32)
                fr = p.sb(es, "h_fr", [128, SEG], F32)
                f_ = p.sb(es, "h_f", [128, SEG], F32)
                g_ = p.sb(es, "h_g", [128, SEG], F32)
                k_ = p.sb(es, "h_k", [128, SEG], F32)
                cum = p.sb(es, "h_cum", [128, SEG], F32)
                cb = p.sb(es, "h_cb", [128, SEG], F32)
                ex = p.sb(es, "h_ex", [128, SEG], F32)
                rmask = p.sb(es, "h_rm", [128, SEG], F32)
                qdec = [p.sb(es, f"h_qd{i}", [128, SEG], BF16) for i in range(2)]
                kdec = p.sb(es, "h_kd", [128, SEG], BF16)
                kend = p.sb(es, "h_ke", [128, SEG], BF16)
                kendT = p.sb(es, "h_keT", [64, NCH, 128], BF16)
                vt = p.sb(es, "h_v", [64, NCH, 128], BF16)
                sct = p.sb(es, "h_sc", [64, NCH, 64], BF16)
                sc1 = p.sb(es, "h_sc1", [64, NCH, 64], F32)
                el = p.sb(es, "h_el", [128, 2, NCH], F32)
                dSs = [[p.sb(es, f"h_dS{i}{j}", [128, 4, 128], F32) for j in range(2)] for i in range(2)]
                hz = {}
                m1 = p.G([], "memset", rmask[:], 1.0)
                m2 = p.G([], "memset", rmask[:].rearrange("p (n s) -> p n s", s=64)[:, :, 0:1], 0.0)
                p.barrier()
                for sg in range(nseg):
                    t0 = sg * SEG
                    tq = p.ld(qr[:], qrT.ap()[h * 128:(h + 1) * 128, t0:t0 + SEG], "hq")
                    tv = p.ld(vt[:], vtok.ap()[t0:t0 + SEG, h * 128:(h + 1) * 128].rearrange("(n s) v -> s n v", s=64),
                              "hv")
                    aq = p.A([tq], out=qr[:], in_=qr[:], func=AF.Silu)
                    for d in range(2):
                        src = ffT if d == 0 else fbT
                        tf = p.ld(fr[:], src.ap()[h * 128:(h + 1) * 128, t0:t0 + SEG], "hf")
                        a1 = p.A([tf], out=f_[:], in_=fr[:], func=AF.Sigmoid)
                        v1 = p.V([a1], "tensor_scalar", f_[:], f_[:], lbt[:, 2 + d, h:h + 1], lbt[:, d, h:h + 1],
                                 ALU.mult, ALU.add)
                        a2 = p.A([v1], out=g_[:], in_=f_[:], func=AF.Ln)
                        g1 = p.G([v1], "tensor_scalar", k_[:], f_[:], -1.0, 1.0, ALU.mult, ALU.add)
                        v2 = p.V([a2], "tensor_tensor_scan", cum[:], rmask[:], g_[:], 0.0, ALU.mult, ALU.add)
                        cum3 = cum[:].rearrange("p (n s) -> p n s", s=64)
                        lastb = bass.AP(cum, 63, [[SEG, 128], [64, NCH], [0, 64]])
                        if d == 0:
                            cc = cum
                            v3 = p.V([v2], "tensor_tensor", cb[:].rearrange("p (n s) -> p n s", s=64), lastb, cum3,
                                     ALU.subtract)
                            dl = cb
                        else:
                            v3a = p.V([v2], "tensor_tensor", cb[:].rearrange("p (n s) -> p n s", s=64), lastb, cum3,
                                      ALU.subtract)
                            v3b = p.V([v3a], "tensor_tensor", cb[:], cb[:], g_[:], ALU.add)
                            cc = cb
                            v3 = p.V([v3b], "tensor_tensor", g_[:], cum[:], g_[:], ALU.subtract)
                            dl = g_
                        a3 = p.A([v3], out=ex[:], in_=cc[:], func=AF.Exp)
                        g2 = p.G([a3, aq], "tensor_tensor", qdec[d][:], qr[:], ex[:], ALU.mult)
                        a4 = p.A([g2], out=ex[:], in_=cc[:], func=AF.Exp, scale=-1.0)
                        g3 = p.G([a4, g1], "tensor_tensor", kdec[:], k_[:], ex[:], ALU.mult)
                        a5 = p.A([g3], out=ex[:], in_=dl[:], func=AF.Exp)
                        g4 = p.G([a5], "tensor_tensor", kend[:], k_[:], ex[:], ALU.mult)
                        a6 = p.A([v2], out=el[:, d, :], in_=cum3[:, :, 63], func=AF.Exp)
                        sq = p.st(qd.ap()[d, :, t0:t0 + SEG], qdec[d][:], "hsq", deps=[g2])
                        se = p.st(eld.ap()[d, :, sg * NCH:(sg + 1) * NCH], el[:, d, :], "hse", deps=[a6])
                        nb = min(8, NCH)
                        NGR = NCH // nb
                        mk_ap = bass.AP(hm, d * 64, [[128, 64], [0, nb], [1, 64]])
                        for gq in range(NGR + 1):
                            if gq < NGR:
                                n0 = gq * nb
                                par = gq % 2
                                key = ("sc", par)
                                for j in range(nb):
                                    sl = slice((n0 + j) * 64, (n0 + j + 1) * 64)
                                    tk = p.mm([g2, g3, hz.get(key)] if j == 0 else [], p.ps[par][0:64, j * 64:(j + 1) * 64],
                                              kdec[:, sl], qdec[d][:, sl], True, True, mark=(j == nb - 1))
                                psv = p.ps[par][0:64, 0:nb * 64].rearrange("p (n s) -> p n s", s=64)
                                if d == 0:
                                    ev = p.V([tk], "tensor_tensor", sc1[:, n0:n0 + nb, :], psv, mk_ap, ALU.mult)
                                else:
                                    ev0 = p.V([tk], "tensor_tensor", sct[:, n0:n0 + nb, :], psv, mk_ap, ALU.mult)
                                    ev = p.V([ev0], "tensor_tensor", sct[:, n0:n0 + nb, :], sct[:, n0:n0 + nb, :],
                                             sc1[:, n0:n0 + nb, :], ALU.add)
                                hz[key] = ev
                                key = ("tr", par)
                                for j in range(nb):
                                    sl = slice((n0 + j) * 64, (n0 + j + 1) * 64)
                                    tk2 = p.tr([g4, hz.get(key)] if j == 0 else [], p.pb[par][0:64, j * 128:(j + 1) * 128],
                                               kend[:, sl], ident[:], mark=(j == nb - 1))
                                ev2 = p.A([tk2, hz.get(("ds_mm", par))], out=kendT[:, n0:n0 + nb, :],
                                          in_=p.pb[par][0:64, 0:nb * 128].rearrange("p (n k) -> p n k", k=128), func=AF.Copy)
                                hz[key] = ev2
                                hz[("kT", par)] = ev2
                            if gq >= 1:
                                gprev = gq - 1
                                n0 = gprev * nb
                                par = gprev % 2
                                nbk = (nb + 3) // 4
                                for j in range(nb):
                                    bank = p.ps[2 + 2 * par + j // 4]
                                    tk3 = p.mm([hz[("kT", par)], tv, hz.get(("dsb", par, j // 4))] if j % 4 == 0 else [],
                                               bank[:, (j % 4) * 128:(j % 4 + 1) * 128], kendT[:, n0 + j, :], vt[:, n0 + j, :],
                                               True, True, mark=(j % 4 == 3 or j == nb - 1))
                                    if j % 4 == 3 or j == nb - 1:
                                        i4 = j // 4
                                        w4 = j % 4 + 1
                                        dst = dSs[par][i4][:, 0:w4, :]
                                        src = bank[:, 0:w4 * 128].rearrange("p (n v) -> p n v", v=128)
                                        if i4 == 0:
                                            ev3 = p.A([tk3, hz.get(("dss", par, i4))], out=dst, in_=src, func=AF.Copy)
                                        else:
                                            ev3 = p.V([tk3, hz.get(("dss", par, i4))], "tensor_copy", dst, src)
                                        hz[("dsb", par, i4)] = ev3
                                        c0 = sg * NCH + n0 + i4 * 4
                                        hz[("dss", par, i4)] = p.st(
                                            dS.ap()[d, c0:c0 + w4, :, :].rearrange("n p v -> p n v"), dst, f"hds{par}{i4}",
                                            deps=[ev3])
                                hz[("ds_mm", par)] = tk3
                        if d == 1:
                            p.st(scd.ap()[:, sg * NCH:(sg + 1) * NCH, :], sct[:], "hsc", deps=[ev])
                        p.barrier()
                        hz.clear()
            with contextlib.ExitStack() as es:
                G = min(NT, 32)
                NG = NT // G
                dsl = [p.sb(es, f"h2_ds{i}", [128, G, 128], F32) for i in range(2)]
                sall = [p.sb(es, f"h2_sa{i}", [128, G + 1, 128], F32) for i in range(2)]
                sbo = [p.sb(es, f"h2_sb{i}", [128, G, 128], BF16) for i in range(2)]
                ela = p.sb(es, "h2_el", [128, 2, NT], F32)
                te = p.ld(ela[:], eld.ap().rearrange("d p n -> p d n"), "h2e")
                z0 = p.V([], "memset", sall[0][:, 0, :], 0.0)
                z1 = p.V([], "memset", sall[1][:, G, :], 0.0)
                last = [z0, z1]
                for gi in range(NG):
                    gf, gb = gi, NG - 1 - gi
                    tl = [p.ld(dsl[0][:], dS.ap()[0, gf * G:(gf + 1) * G, :, :].rearrange("n p v -> p n v"), "h2l0"),
                          p.ld(dsl[1][:], dS.ap()[1, gb * G:(gb + 1) * G, :, :].rearrange("n p v -> p n v"), "h2l1")]
                    for i in range(G):
                        nf = gf * G + i
                        last[0] = p.V([tl[0], te, last[0]], "scalar_tensor_tensor", sall[0][:, i + 1, :], sall[0][:, i, :],
                                      ela[:, 0, nf:nf + 1], dsl[0][:, i, :], ALU.mult, ALU.add)
                        j = G - 1 - i
                        nbk = gb * G + j
                        last[1] = p.V([tl[1], te, last[1]], "scalar_tensor_tensor", sall[1][:, j, :], sall[1][:, j + 1, :],
                                      ela[:, 1, nbk:nbk + 1], dsl[1][:, j, :], ALU.mult, ALU.add)
                    c0 = p.G(last, "tensor_copy", sbo[0][:], sall[0][:, 0:G, :])
                    c1 = p.A(last, out=sbo[1][:], in_=sall[1][:, 1:G + 1, :], func=AF.Copy)
                    p.st(Sb.ap()[0, gf * G:(gf + 1) * G, :, :].rearrange("n p v -> p n v"), sbo[0][:], "h2s0", deps=[c0])
                    p.st(Sb.ap()[1, gb * G:(gb + 1) * G, :, :].rearrange("n p v -> p n v"), sbo[1][:], "h2s1", deps=[c1])
                    last[0] = p.V([c0, c1] + last, "tensor_copy", sall[0][:, 0, :], sall[0][:, G, :])
                    last[1] = p.V([last[0]], "tensor_copy", sall[1][:, G, :], sall[1][:, 0, :])
                    p.barrier()
            with contextlib.ExitStack() as es:
                G = min(NT, 32)
                qd3 = p.sb(es, "h3_qd", [128, 2, G * 64], BF16)
                sc3 = p.sb(es, "h3_sc", [64, G, 64], BF16)
                v3_ = p.sb(es, "h3_v", [64, G, 128], BF16)
                gt3 = p.sb(es, "h3_g", [64, G, 128], F32)
                s3 = p.sb(es, "h3_s", [128, 2, G, 128], BF16)
                o3 = p.sb(es, "h3_o", [64, G, 128], F32)
                ob3 = p.sb(es, "h3_ob", [64, G, 128], BF16)
                sq3 = p.sb(es, "h3_sq", [64, G, 128], F32)
                ss = p.sb(es, "h3_ss", [64, G], F32)
                for gi in range(NT // G):
                    t0 = gi * G * 64
                    tl = [p.ld(qd3[:], qd.ap()[:, :, t0:t0 + G * 64].rearrange("d p t -> p d t"), "h3a"),
                          p.ld(sc3[:], scd.ap()[:, gi * G:(gi + 1) * G, :], "h3a"),
                          p.ld(v3_[:], vtok.ap()[t0:t0 + G * 64, h * 128:(h + 1) * 128].rearrange("(n s) v -> s n v", s=64), "h3a"),
                          p.ld(gt3[:], gates.ap()[t0:t0 + G * 64, 1024 + h * 128:1024 + (h + 1) * 128].rearrange("(n s) v -> s n v", s=64), "h3a"),
                          p.ld(s3[:, 0], Sb.ap()[0, gi * G:(gi + 1) * G, :, :].rearrange("n p v -> p n v"), "h3a"),
                          p.ld(s3[:, 1], Sb.ap()[1, gi * G:(gi + 1) * G, :, :].rearrange("n p v -> p n v"), "h3a")]
                    ag = p.A([tl[3]], out=gt3[:], in_=gt3[:], func=AF.Silu)
                    prev = [None, None]
                    nb3 = min(4, G)
                    for j0 in range(0, G, nb3):
                        pk = (j0 // nb3) % 2
                        for jj in range(nb3):
                            j = j0 + jj
                            ps = p.ps[pk][0:64, jj * 128:(jj + 1) * 128]
                            sl = slice(j * 64, (j + 1) * 64)
                            p.mm(tl + [prev[pk]] if jj == 0 else [], ps, sc3[:, j, :], v3_[:, j, :], True, False)
                            p.mm([], ps, qd3[:, 0, sl], s3[:, 0, j, :], False, False)
                            tk = p.mm([], ps, qd3[:, 1, sl], s3[:, 1, j, :], False, True, mark=(jj == nb3 - 1))
                        ev = p.V([tk], "tensor_copy", o3[:, j0:j0 + nb3, :],
                                 p.ps[pk][0:64, 0:nb3 * 128].rearrange("p (n v) -> p n v", v=128))
                        prev[pk] = ev
                    e2 = p.A([ev], out=sq3[:], in_=o3[:], func=AF.Square)
                    e3 = p.V([e2], "tensor_reduce", ss[:], sq3[:], mybir.AxisListType.X, ALU.add)
                    r2 = p.rsqrt([e3], ss[:], ss[:], 1.0 / 128, EPS)
                    ssb = bass.AP(ss, 0, [[G, 64], [1, G], [0, 128]])
                    r3 = p.V([r2], "tensor_tensor", o3[:], o3[:], ssb, ALU.mult)
                    ghb = bass.AP(ghg_sb, 0, [[128, 64], [0, G], [1, 128]])
                    r4 = p.V([r3], "tensor_tensor", o3[:], o3[:], ghb, ALU.mult)
                    r5 = p.V([r4, ag], "tensor_tensor", ob3[:], o3[:], gt3[:], ALU.mult)
                    p.st(ymix.ap()[t0:t0 + G * 64, 1024 + h * 128:1024 + (h + 1) * 128].rearrange("(n s) v -> s n v", s=64),
                         ob3[:], "h3s", deps=[r5])
                    p.barrier()
        p.barrier()

    def outproj(L, ymix, ya, gates, wbo, gpost, xres, xout, hT_next):
        with contextlib.ExitStack() as es:
            wsb = p.sb(es, "op_w", [128, KC, D], BF16)
            tw = p.ld(wsb[:], wbo.ap().rearrange("(k p) n -> p k n", p=128), "opw")
            ym = [p.sb(es, f"op_ym{i}", [128, D], BF16) for i in range(2)]
            yaf = p.sb(es, "op_ya", [128, 1024], F32) if ya is not None else None
            gaf = p.sb(es, "op_ga", [128, 1024], F32) if ya is not None else None
            ymT = p.sb(es, "op_ymT", [128, KC, 128], BF16)
            xr = [p.sb(es, f"op_x{i}", [128, D], F32) for i in range(2)]
            yo = p.sb(es, "op_y", [128, D], F32)
            junk = p.sb(es, "op_j", [128, D], BF16)
            st_ = p.sb(es, "op_st", [128, 4], F32)
            hb = p.sb(es, "op_hb", [128, D], BF16)
            ho = p.sb(es, "op_ho", [128, KC, 128], BF16)
            for ti in range(L // 128):
                b = ti % 2
                rows = slice(ti * 128, (ti + 1) * 128)
                tx = p.ld(xr[b][:], xres[rows, :], f"opx{b}")
                if ya is not None:
                    t1 = p.ld(ym[b][:, 1024:2048], ymix.ap()[rows, 1024:2048], f"opm{b}")
                    t2 = p.ld(yaf[:], ya.ap()[rows, :], "opa")
                    t3 = p.ld(gaf[:], gates.ap()[rows, 0:1024], "opa")
                    a1 = p.A([t3], out=gaf[:], in_=gaf[:], func=AF.Silu)
                    v1 = p.V([a1, t2], "tensor_tensor", ym[b][:, 0:1024], yaf[:], gaf[:], ALU.mult)
                    rdy = [t1, v1]
                else:
                    rdy = [p.ld(ym[b][:], ymix.ap()[rows, :], f"opm{b}")]
                for kc in range(KC):
                    tk = p.tr(rdy if kc == 0 else [], p.pb[kc // 8][:, (kc % 8) * 128:(kc % 8 + 1) * 128],
                              ym[b][:, kc * 128:(kc + 1) * 128], ident[:], mark=(kc == KC - 1))
                c1 = p.A([tk], out=ymT[:, 0:8, :], in_=p.pb[0][:].rearrange("p (k t) -> p k t", k=8), func=AF.Copy)
                c2 = p.V([tk], "tensor_copy", ymT[:, 8:16, :], p.pb[1][:].rearrange("p (k t) -> p k t", k=8))
                for cb in range(4):
                    for kc in range(KC):
                        tk = p.mm([c1, c2, tw] if kc == 0 else [], p.ps[cb][:], ymT[:, kc, :],
                                  wsb[:, kc, cb * 512:(cb + 1) * 512], kc == 0, kc == KC - 1, mark=(kc == KC - 1))
                evs = []
                for cb in range(4):
                    if cb % 2 == 0:
                        evs.append(p.A([tk], out=yo[:, cb * 512:(cb + 1) * 512], in_=p.ps[cb][:], func=AF.Copy))
                    else:
                        evs.append(p.V([tk], "tensor_copy", yo[:, cb * 512:(cb + 1) * 512], p.ps[cb][:]))
                a2 = p.A(evs, out=junk[:], in_=yo[:], func=AF.Square, accum_out=st_[:, 0:1])
                v3 = p.rsqrt([a2], st_[:, 1:2], st_[:, 0:1], 1.0 / D, EPS)
                v4 = p.V([v3], "scalar_tensor_tensor", yo[:], yo[:], st_[:, 1:2], gpost[:], ALU.mult, ALU.mult)
                v5 = p.V([v4, tx], "tensor_tensor", yo[:], yo[:], xr[b][:], ALU.add)
                so = p.st(xout[rows, :], yo[:], "opo", deps=[v5])
                last = [so]
                if hT_next is not None:
                    a3 = p.A([v5], out=junk[:], in_=yo[:], func=AF.Square, accum_out=st_[:, 2:3])
                    v7 = p.rsqrt([a3], st_[:, 3:4], st_[:, 2:3], 1.0 / D, EPS)
                    v8 = p.V([v7], "tensor_scalar", hb[:], yo[:], st_[:, 3:4], None, ALU.mult)
                    for kc in range(KC):
                        tk = p.tr([v8] if kc == 0 else [], p.pb[kc // 8][:, (kc % 8) * 128:(kc % 8 + 1) * 128],
                                  hb[:, kc * 128:(kc + 1) * 128], ident[:], mark=(kc == KC - 1))
                    c1 = p.A([tk], out=ho[:, 0:8, :], in_=p.pb[0][:].rearrange("p (k t) -> p k t", k=8), func=AF.Copy)
                    c2 = p.V([tk], "tensor_copy", ho[:, 8:16, :], p.pb[1][:].rearrange("p (k t) -> p k t", k=8))
                    last.append(p.st(hT_next.ap().rearrange("(k p) t -> p k t", p=128)[:, :, rows], ho[:], "oph",
                                     deps=[c1, c2]))
                for e in (p.pe, p.act, p.dve, p.pool, p.sp):
                    e.wait(*last, (p.dve, p.dve.n), (p.act, p.act.n))
        p.barrier()

    def attention(qT, kT, vtok, gates, Lq, Lk, og, dtab, lam_sb, gsub_sb, delta):
        NKT, NQB = Lk // 128, Lq // 256
        with contextlib.ExitStack() as es:
            qs = p.sb(es, "at_q", [128, 2, Lq], BF16)
            ks = p.sb(es, "at_k", [128, 2, Lk], BF16)
            vs = p.sb(es, "at_v", [128, NKT, 257], BF16)
            absd = [p.sb(es, f"at_ad{i}", [128, 256], F32) for i in range(2)]
            sb_ = [p.sb(es, f"at_s{i}", [128, 256], F32) for i in range(2)]
            pt = [p.sb(es, f"at_p{i}", [128, 256], BF16) for i in range(2)]
            gt = p.sb(es, "at_g", [128, 2, 256], F32)
            o0 = p.sb(es, "at_o0", [128, 2, 256], F32)
            o1 = p.sb(es, "at_o1", [128, 2, 256], F32)
            rs = p.sb(es, "at_rs", [128, 8], F32)
            ob = p.sb(es, "at_ob", [128, 2, 256], BF16)
            jk = p.sb(es, "at_j", [128, 256], F32)
            for h in range(8):
                slope = SLOPES[h]
                t_in = [p.ld(qs[:], qT.ap()[h * 256:(h + 1) * 256, :].rearrange("(c p) t -> p c t", p=128), "atq"),
                        p.ld(ks[:], kT.ap()[h * 256:(h + 1) * 256, :].rearrange("(c p) t -> p c t", p=128), "atq"),
                        p.ld(vs[:, :, 0:256], vtok.ap()[:, h * 256:(h + 1) * 256].rearrange("(n p) v -> p n v", p=128),
                             "atq")]
                t_in.append(p.G([], "memset", vs[:, :, 256:257], 1.0))
                for qb in range(NQB):
                    q0 = qb * 256
                    tg = p.ld(gt[:], gates.ap()[q0:q0 + 256, h * 256:(h + 1) * 256].rearrange("(j p) v -> p j v", p=128),
                              "atg")
                    acc = [[p.ps[2 + 2 * c + j] for j in range(2)] for c in range(2)]
                    u = 0
                    rd_ad = [None, None]
                    rd_s = [None, None]
                    rd_p = [None, None]
                    rd_ps = [None, None]
                    lastpv = None
                    for kt in range(NKT):
                        idx = kt * NQB + qb
                        ab = kt % 2
                        g1 = p.A([rd_ad[ab]], out=absd[ab][:], in_=delta[:], func=AF.Abs, bias=dtab[:, idx:idx + 1])
                        for c in range(2):
                            b = u % 2
                            u += 1
                            tk = p.mm(t_in + [rd_ps[b]], p.ps[b][:, 0:256], ks[:, c, kt * 128:(kt + 1) * 128],
                                      qs[:, c, q0:q0 + 256], True, True, mark=True)
                            v1 = p.V([tk, g1, rd_s[b]], "scalar_tensor_tensor", sb_[b][:], absd[ab][:], -slope,
                                     p.ps[b][:, 0:256], ALU.mult, ALU.add)
                            rd_ad[ab] = v1
                            rd_ps[b] = v1
                            a1 = p.A([v1, rd_p[b]], out=pt[b][:], in_=sb_[b][:], func=AF.Exp)
                            rd_s[b] = a1
                            for j in range(2):
                                lastpv = p.mm([a1] if j == 0 else [], acc[c][j][:, 0:257], pt[b][:, j * 128:(j + 1) * 128],
                                              vs[:, kt, :], kt == 0, kt == NKT - 1, mark=(j == 1))
                            rd_p[b] = lastpv
                    evs = []
                    for c in range(2):
                        for j in range(2):
                            dst = (o0 if c == 0 else o1)
                            e1 = p.V([lastpv], "reciprocal", rs[:, 2 * c + j:2 * c + j + 1], acc[c][j][:, 256:257])
                            if c == 0:
                                evs.append(p.V([e1], "tensor_scalar", dst[:, j, :], acc[c][j][:, 0:256],
                                               rs[:, 2 * c + j:2 * c + j + 1], None, ALU.mult))
                            else:
                                evs.append(p.V([e1], "tensor_scalar", dst[:, j, :], acc[c][j][:, 0:256],
                                               rs[:, 2 * c + j:2 * c + j + 1], lam_sb[:, 1:2], ALU.mult, ALU.mult))
                    p.pe.wait(*evs)
                    f1 = p.V(evs, "tensor_tensor", o0[:], o0[:], o1[:], ALU.add)
                    for j in range(2):
                        p.A([f1], out=jk[:], in_=o0[:, j, :], func=AF.Square, accum_out=rs[:, 4 + j:5 + j])
                    f2 = p.rsqrt([(p.act, p.act.n)], rs[:, 4:6], rs[:, 4:6], 1.0 / 256, EPS)
                    f3 = p.V([f2], "tensor_scalar", rs[:, 4:6], rs[:, 4:6], (1.0 - LAMBDA_INIT), None, ALU.mult)
                    ag = p.A([tg], out=gt[:], in_=gt[:], func=AF.Silu)
                    for j in range(2):
                        f4 = p.V([f3], "scalar_tensor_tensor", o0[:, j, :], o0[:, j, :], rs[:, 4 + j:5 + j], gsub_sb[:],
                                 ALU.mult, ALU.mult)
                    f5 = p.V([f4, ag], "tensor_tensor", ob[:], o0[:], gt[:], ALU.mult)
                    so = p.st(og.ap()[q0:q0 + 256, h * 256:(h + 1) * 256].rearrange("(j p) v -> p j v", p=128), ob[:],
                              "ato", deps=[f5])
                    for e in (p.act, p.dve, p.pool, p.sp, p.pe):
                        e.wait(so, f5)
                p.barrier()
        p.barrier()

    c128 = p.sb(ges, "c128_sb", [128, 3, 128], BF16)
    p.ld(c128[:], c128_d.ap().rearrange("k p c -> p k c"), "c0")
    cs256 = p.sb(ges, "cs256_sb", [128, 2, 512], BF16)
    p.ld(cs256[:], cs256_d.ap().rearrange("(c p) n -> p c n", p=128), "c0")
    tw_sb, cm_sb = {}, {}
    for L in tw_d:
        M = L // 128
        tw_sb[L] = p.sb(ges, f"tw{L}_sb", [128, 3, M], F32)
        p.ld(tw_sb[L][:], tw_d[L].ap().rearrange("k p m -> p k m"), "c0")
        cm_sb[L] = p.sb(ges, f"cm{L}_sb", [M, 2, M], BF16)
        p.ld(cm_sb[L][:], cm_d[L].ap().rearrange("k b d -> b k d"), "c0")
    hm = p.sb(ges, "hm_sb", [64, 2, 64], F32)
    p.ld(hm[:], hmask_d.ap().rearrange("k s t -> s k t"), "c0")
    delta = p.sb(ges, "delta_sb", [128, 256], F32)
    p.ld(delta[:], delta_d.ap(), "c0")
    dtabs = p.sb(ges, "dtabs_sb", [128, (LS // 128) * (LS // 256)], F32)
    p.ld(dtabs[:], dtabs_d.ap(), "c0")
    dtabp = p.sb(ges, "dtabp_sb", [128, (LP // 128) * (OWN // 256)], F32)
    p.ld(dtabp[:], dtabp_d.ap(), "c0")
    ownidx = p.sb(ges, "ownidx_sb", [128, OWN // 128], I32)
    p.ld(ownidx[:], ownidx_d.ap(), "c0")
    ghg_sb = p.sb(ges, "ghg_sb", [64, 128], F32)
    p.ld(ghg_sb[:], bcast_rows(ghg.ap(), 128, 64), "c0")
    gsub_sb = p.sb(ges, "gsub_sb", [128, 256], F32)
    p.ld(gsub_sb[:], bcast_rows(gsub.ap(), 256), "c0")
    lraw = p.sb(ges, "lraw_sb", [128, 2, 3, 8], F32)
    p.ld(lraw[:], lbl.ap().rearrange("d s (h k) -> k d s h", k=128), "c0", slow=True)
    lbt = p.sb(ges, "lbt_sb", [128, 4, 8], F32)
    lsum = p.sb(ges, "lsum_sb", [128, 2, 8], F32)
    lamv = p.sb(ges, "lamv_sb", [128, 4, 128], F32)
    p.ld(lamv[:], bass.AP(lam4.ap().tensor, 0, [[0, 128], [128, 4], [1, 128]]), "c0")
    lam_sb = p.sb(ges, "lam_sb", [128, 4], F32)
    ljunk = p.sb(ges, "ljunk_sb", [128, 128], F32)
    p.barrier()
    a = p.A([], out=lraw[:], in_=lraw[:], func=AF.Exp)
    v = p.V([a], "tensor_tensor", lsum[:], lraw[:, :, 0, :], lraw[:, :, 1, :], ALU.add)
    v = p.V([v], "tensor_tensor", lsum[:], lsum[:], lraw[:, :, 2, :], ALU.add)
    v = p.V([v], "reciprocal", lsum[:], lsum[:])
    v = p.V([v], "tensor_tensor", lbt[:, 0:2, :], lraw[:, :, 0, :], lsum[:], ALU.mult)
    v = p.V([v], "tensor_scalar", lbt[:, 2:4, :], lbt[:, 0:2, :], -1.0, 1.0, ALU.mult, ALU.add)
    v = p.V([v], "tensor_tensor", ljunk[:], lamv[:, 0, :], lamv[:, 1, :], ALU.mult)
    v = p.V([v], "tensor_reduce", lam_sb[:, 2:3], ljunk[:], mybir.AxisListType.X, ALU.add)
    v = p.V([v], "tensor_tensor", ljunk[:], lamv[:, 2, :], lamv[:, 3, :], ALU.mult)
    v = p.V([v], "tensor_reduce", lam_sb[:, 3:4], ljunk[:], mybir.AxisListType.X, ALU.add)
    a = p.A([v], out=lam_sb[:, 2:4], in_=lam_sb[:, 2:4], func=AF.Exp)
    v = p.V([a], "tensor_tensor", lam_sb[:, 0:1], lam_sb[:, 2:3], lam_sb[:, 3:4], ALU.subtract)
    v = p.V([v], "tensor_scalar", lam_sb[:, 1:2], lam_sb[:, 0:1], LAMBDA_INIT, -1.0, ALU.add, ALU.mult)
    p.barrier()

    seqs = [("s%d" % i, LS, xs.ap()[i * LS:(i + 1) * LS, :], ys.ap()[i * LS:(i + 1) * LS, :]) for i in range(NS)]
    seqs.append(("p", LP, xp.ap(), None))
    scale_q = 128 ** -0.5
    for (nm, L, xin, yout) in seqs:
        isP = yout is None
        hT = p.dram(f"hT_{nm}", [D, L], BF16)
        norm_T(xin, L, hT)
        done("norm")
        uT = p.dram(f"uT_{nm}", [1024, L], BF16)
        gat0 = p.dram(f"g0_{nm}", [L, 2048], F32)
        qrT = p.dram(f"qr_{nm}", [1024, L], F32)
        ffT = p.dram(f"ff_{nm}", [1024, L], F32)
        fbT = p.dram(f"fb_{nm}", [1024, L], F32)
        vtk = p.dram(f"vt_{nm}", [L, 1024], BF16)
        proj(hT, L, wb0i, 7168, [
            (0, 1024, "F", uT, 0, 1.0, BF16),
            (1024, 1024, "T", gat0, 0, 1.0, F32),
            (2048, 1024, "F", qrT, 0, 1.0, F32),
            (3072, 1024, "T", vtk, 0, 1.0, BF16),
            (4096, 1024, "F", ffT, 0, 1.0, F32),
            (5120, 1024, "F", fbT, 0, 1.0, F32),
            (6144, 1024, "T", gat0, 1024, 1.0, F32),
        ])
        done("proj0")
        ya = p.dram(f"ya_{nm}", [L, 1024], F32)
        fnet(uT, L, ya, (c128, cs256, tw_sb[L], cm_sb[L]))
        done("fnet")
        ymix = p.dram(f"ym_{nm}", [L, 2048], BF16)
        hgrn(qrT, ffT, fbT, vtk, gat0, L, ymix, lbt, hm, ghg_sb)
        done("hgrn")
        x1 = p.dram(f"x1_{nm}", [L, D], F32)
        h1T = p.dram(f"h1T_{nm}", [D, L], BF16)
        outproj(L, ymix, ya, gat0, wb0o, gpost_sb[0], xin, x1.ap(), h1T)
        done("out0")
        kT = p.dram(f"kT_{nm}", [2048, L], BF16)
        v1t = p.dram(f"v1_{nm}", [L, 2048], BF16)
        if not isP:
            Lq = L
            qT = p.dram(f"qT_{nm}", [2048, Lq], BF16)
            gat1 = p.dram(f"g1_{nm}", [Lq, 2048], F32)
            proj(h1T, L, wb1i, 8192, [
                (0, 2048, "F", qT, 0, scale_q, BF16),
                (2048, 2048, "F", kT, 0, 1.0, BF16),
                (4096, 2048, "T", v1t, 0, 1.0, BF16),
                (6144, 2048, "T", gat1, 0, 1.0, F32),
            ])
            xres1 = x1.ap()
            dtab = dtabs
        else:
            Lq = OWN
            proj(h1T, L, wb1i, 8192, [
                (2048, 2048, "F", kT, 0, 1.0, BF16),
                (4096, 2048, "T", v1t, 0, 1.0, BF16),
            ])
            x1own = p.dram("x1own", [OWN, D], F32)
            with contextlib.ExitStack() as es:
                gx = p.sb(es, "gx", [128, D], F32)
                prev = None
                for ti in range(OWN // 128):
                    p.pool.wait(prev)
                    ds = p.dsem("gath")
                    p.nc.gpsimd.indirect_dma_start(
                        out=gx[:], out_offset=None, in_=x1.ap(),
                        in_offset=bass.IndirectOffsetOnAxis(ap=ownidx[:, ti:ti + 1], axis=0),
                    ).then_inc(ds.sem, 16)
                    ds.n += 16
                    tok = (ds, ds.n)
                    p.pending.append(tok)
                    prev = p.st(x1own.ap()[ti * 128:(ti + 1) * 128, :], gx[:], "gaths", deps=[tok])
            p.barrier()
            h1To = p.dram("h1To", [D, OWN], BF16)
            norm_T(x1own.ap(), OWN, h1To)
            qT = p.dram(f"qT_{nm}", [2048, Lq], BF16)
            gat1 = p.dram(f"g1_{nm}", [Lq, 2048], F32)
            proj(h1To, OWN, wb1i, 8192, [
                (0, 2048, "F", qT, 0, scale_q, BF16),
                (6144, 2048, "T", gat1, 0, 1.0, F32),
            ])
            xres1 = x1own.ap()
            yout = yp.ap()
            dtab = dtabp
        done("proj1")
        og = p.dram(f"og_{nm}", [Lq, 2048], BF16)
        attention(qT, kT, v1t, gat1, Lq, L, og, dtab, lam_sb, gsub_sb, delta)
        done("attn")
        outproj(Lq, og, None, None, wb1o, gpost_sb[1], xres1, yout, None)
        done("seq0")


def dft_cs(n):
    j = np.arange(n)
    ang = 2 * np.pi * np.outer(j, j) / n
    return np.cos(ang), np.sin(ang)


def host_tables(cfg, core):
    NS, LS, LP, OWN = cfg["NS"], cfg["LS"], cfg["LP"], cfg["OWN"]
    t = {}
    t["ident"] = np.eye(128, dtype=np.float32).astype(NPBF)
    c, s = dft_cs(128)
    t["c128"] = np.stack([c, s, -s]).astype(np.float32).astype(NPBF)
    c, s = dft_cs(256)
    t["cs256"] = np.concatenate([c, -s], axis=1).astype(np.float32).astype(NPBF)
    for L in sorted({LS, LP}):
        M = L // 128
        ang = 2 * np.pi * np.outer(np.arange(128), np.arange(M)) / L
        t[f"tw{L}"] = np.stack([np.cos(ang), np.sin(ang), -np.sin(ang)]).astype(np.float32)
        cm, sm = dft_cs(M)
        sc = 1.0 / math.sqrt(L * 256)
        t[f"cm{L}"] = np.stack([cm * sc, sm * sc]).astype(np.float32).astype(NPBF)
    s_, t_ = np.meshgrid(np.arange(64), np.arange(64), indexing="ij")
    t["hmask"] = np.stack([(s_ <= t_), (s_ >= t_)]).astype(np.float32)
    t["delta"] = (np.arange(128)[:, None] - np.arange(256)[None, :]).astype(np.float32)
    kt, qb = np.meshgrid(np.arange(LS // 128), np.arange(LS // 256), indexing="ij")
    t["dtabs"] = np.broadcast_to((kt * 128 - qb * 256).reshape(1, -1), (128, kt.size)).astype(np.float32).copy()
    kt, qb = np.meshgrid(np.arange(LP // 128), np.arange(OWN // 256), indexing="ij")
    t["dtabp"] = np.broadcast_to((kt * 128 - (core * OWN + qb * 256)).reshape(1, -1), (128, kt.size)).astype(np.float32).copy()
    t["ownidx"] = (core * OWN + np.arange(OWN)).reshape(OWN // 128, 128).T.astype(np.int32).copy()
    return t


def run(inputs, cfg, ncores):
    NS, LS, LP, OWN = cfg["NS"], cfg["LS"], cfg["LP"], cfg["OWN"]
    f = lambda a: np.ascontiguousarray(np.asarray(a, dtype=np.float32))
    xsamp = f(inputs["x_sample"])
    shared = {
        "xp": f(inputs["x_prompt"])[0],
        "w0i": f(inputs["ev_w_in"])[0], "w0o": f(inputs["ev_w_out"])[0],
        "w1i": f(inputs["od_w_in"])[0], "w1o": f(inputs["od_w_out"])[0],
        "g0pre": f(inputs["ev_norm_pre"])[0], "g0post": f(inputs["ev_norm_post"])[0],
        "g1pre": f(inputs["od_norm_pre"])[0], "g1post": f(inputs["od_norm_post"])[0],
        "lbl": f(inputs["hgrn_lb_logits"]), "ghg": f(inputs["hgrn_norm"])[0],
        "lam4": np.stack([f(inputs["lambda_q1"])[0], f(inputs["lambda_k1"])[0],
                          f(inputs["lambda_q2"])[0], f(inputs["lambda_k2"])[0]]),
        "gsub": f(inputs["subln"])[0],
    }
    nc = build(cfg)
    in_maps = []
    for c in range(ncores):
        m = dict(shared)
        m["xs"] = xsamp[c * NS:(c + 1) * NS].reshape(NS * LS, D)
        m.update(host_tables(cfg, c))
        in_maps.append(m)
    res = run_bass_kernel_spmd(nc, in_maps, core_ids=list(range(ncores)))
    global LAST_RES
    LAST_RES = res.results
    y_s = np.concatenate([r["ys"].reshape(NS, LS, D) for r in res.results], axis=0)
    y_p = np.concatenate([r["yp"] for r in res.results], axis=0)[None]
    return y_p.astype(np.float32), y_s.astype(np.float32)


def kernel(**inputs):
    cfg = {"NS": 2, "LS": 2048, "LP": 8192, "OWN": 1024}
    return run(inputs, cfg, 8)
```

```python
import contextlib, math
import numpy as np
import ml_dtypes
import concourse.bass as bass
import concourse.mybir as mybir
from concourse.bass_utils import run_bass_kernel_spmd

F32, BF16, I32 = mybir.dt.float32, mybir.dt.bfloat16, mybir.dt.int32
AF = mybir.ActivationFunctionType
ALU = mybir.AluOpType
D = 2048
KC = 16
EPS = 1e-6
LAMBDA_INIT = 0.8 - 0.6 * math.exp(-0.3 * 1)
SLOPES = [2.0 ** (-8.0 * (h + 1) / 8) for h in range(8)]
NPBF = ml_dtypes.bfloat16


class StopBuild(Exception):
    pass


class Eng:
    def __init__(self, e, sem):
        self.e, self.sem, self.n, self.seen = e, sem, 0, {}

    def mark(self, ins):
        ins.then_inc(self.sem, 1)
        self.n += 1
        return (self, self.n)

    def wait(self, *toks):
        for tok in toks:
            if tok is None:
                continue
            src, n = tok
            if self.seen.get(src, 0) >= n:
                continue
            self.e.wait_ge(src.sem, n)
            self.seen[src] = n


class DSem:
    def __init__(self, sem):
        self.sem, self.n = sem, 0


class P:
    def __init__(self, cfg):
        self.cfg = cfg
        self.es = contextlib.ExitStack()
        nc = self.nc = bass.Bass("TRN2", target_bir_lowering=False)
        mk = lambda nm: self.es.enter_context(nc.semaphore(nm))
        self.pe = Eng(nc.tensor, mk("s_pe"))
        self.act = Eng(nc.scalar, mk("s_act"))
        self.dve = Eng(nc.vector, mk("s_dve"))
        self.pool = Eng(nc.gpsimd, mk("s_pool"))
        self.sp = Eng(nc.sync, mk("s_sp"))
        self.engs = [self.pe, self.act, self.dve, self.pool, self.sp]
        self.dsems = {}
        self.pending = []
        self.ndram = 0
        self.ps = [self.es.enter_context(nc.psum_tensor(f"ps{i}", [128, 512], F32)) for i in range(6)]
        self.pb = [self.es.enter_context(nc.psum_tensor(f"pb{i}", [128, 1024], BF16)) for i in range(2)]
        self.dummy = self.es.enter_context(nc.sbuf_tensor("dummy_sb", [128, 2], F32))
        self.pool.mark(nc.gpsimd.memset(self.dummy[:], 0.0))

    def sb(self, es, name, shape, dt):
        self.nsb = getattr(self, "nsb", 0) + 1
        return es.enter_context(self.nc.sbuf_tensor(f"{name}_{self.nsb}", shape, dt))

    def dram(self, name, shape, dt, kind="Internal"):
        t = self.nc.dram_tensor(name, list(shape), dt, kind=kind)
        if not hasattr(self, "named"):
            self.named = {}
        self.named[name] = (t, list(shape), dt)
        return t

    def dump(self):
        self.barrier()
        for name in self.cfg.get("dump", []):
            if name not in self.named:
                continue
            t, shape, dt = self.named[name]
            o = self.nc.dram_tensor("dbg_" + name, shape, dt, kind="ExternalOutput")
            self.ld(o.ap(), t.ap(), "dump")
        self.barrier()

    def dsem(self, key):
        if key not in self.dsems:
            self.dsems[key] = DSem(self.es.enter_context(self.nc.semaphore("d_" + key)))
        return self.dsems[key]

    def dma(self, q, out, in_, key, deps=(), slow=False):
        q.wait(*deps)
        ds = self.dsem(key)
        kw = {"allow_slow_non_contiguous": True} if slow else {}
        q.e.dma_start(out=out, in_=in_, **kw).then_inc(ds.sem, 16)
        ds.n += 16
        tok = (ds, ds.n)
        self.pending.append(tok)
        return tok

    def ld(self, out, in_, key, deps=(), slow=False):
        return self.dma(self.sp, out, in_, key, deps, slow)

    def st(self, out, in_, key, deps=()):
        return self.dma(self.pool, out, in_, key, deps)

    def barrier(self):
        toks = [(e, e.n) for e in self.engs if e.n > 0] + self.pending
        for e in self.engs:
            e.wait(*toks)
        self.pending = []

    def A(self, deps, *a, **k):
        self.act.wait(*deps)
        tok = self.act.mark(self.act.e.activation(*a, **k))
        if k.get("accum_out") is not None:
            tok = self.act.mark(self.act.e.activation(out=self.dummy[:, 1:2], in_=self.dummy[:, 0:1], func=AF.Copy))
        return tok

    def V(self, deps, fn, *a, **k):
        self.dve.wait(*deps)
        return self.dve.mark(getattr(self.dve.e, fn)(*a, **k))

    def G(self, deps, fn, *a, **k):
        self.pool.wait(*deps)
        return self.pool.mark(getattr(self.pool.e, fn)(*a, **k))

    def X(self, eng, deps, fn, *a, **k):
        eng.wait(*deps)
        return eng.mark(getattr(eng.e, fn)(*a, **k))

    def rsqrt(self, deps, out, in_, mul, add):
        a = self.A(deps, out=out, in_=in_, func=AF.Ln, scale=float(mul), bias=float(add))
        return self.A([a], out=out, in_=out, func=AF.Exp, scale=-0.5)

    def mm(self, deps, out, lhsT, rhs, start, stop, mark=False):
        self.pe.wait(*deps)
        ins = self.pe.e.matmul(out, lhsT, rhs, start=start, stop=stop)
        return self.pe.mark(ins) if mark else None

    def tr(self, deps, out, in_, ident, mark=False):
        self.pe.wait(*deps)
        ins = self.pe.e.transpose(out, in_, ident)
        return self.pe.mark(ins) if mark else None


def bcast_rows(ap_dram_1d, n, parts=128):
    return bass.AP(ap_dram_1d.tensor, ap_dram_1d.offset, [[0, parts], [1, n]])


def build(cfg):
    p = P(cfg)
    try:
        _build(cfg, p)
    except StopBuild:
        p.dump()
        return p.nc
    p.dump()
    p.es.close()
    return p.nc


def _build(cfg, p):
    NS, LS, LP, OWN = cfg["NS"], cfg["LS"], cfg["LP"], cfg["OWN"]

    def done(tag):
        if cfg.get("stop") == tag:
            raise StopBuild()
    nc = p.nc
    inp = lambda name, shape, dt=F32: nc.dram_tensor(name, list(shape), dt, kind="ExternalInput")
    xs = inp("xs", [NS * LS, D])
    xp = inp("xp", [LP, D])
    w0i, w0o = inp("w0i", [D, 7168]), inp("w0o", [D, D])
    w1i, w1o = inp("w1i", [D, 8192]), inp("w1o", [D, D])
    g0pre, g0post = inp("g0pre", [D]), inp("g0post", [D])
    g1pre, g1post = inp("g1pre", [D]), inp("g1post", [D])
    lbl = inp("lbl", [2, 3, 1024])
    ghg = inp("ghg", [128])
    lam4 = inp("lam4", [4, 128])
    gsub = inp("gsub", [256])
    ident_d = inp("ident", [128, 128], BF16)
    c128_d = inp("c128", [3, 128, 128], BF16)
    cs256_d = inp("cs256", [256, 512], BF16)
    tw_d = {L: inp(f"tw{L}", [3, 128, L // 128]) for L in sorted({LS, LP})}
    cm_d = {L: inp(f"cm{L}", [2, L // 128, L // 128], BF16) for L in sorted({LS, LP})}
    hmask_d = inp("hmask", [2, 64, 64])
    delta_d = inp("delta", [128, 256])
    dtabs_d = inp("dtabs", [128, (LS // 128) * (LS // 256)])
    dtabp_d = inp("dtabp", [128, (LP // 128) * (OWN // 256)])
    ownidx_d = inp("ownidx", [128, OWN // 128], I32)
    ys = nc.dram_tensor("ys", [NS * LS, D], F32, kind="ExternalOutput")
    yp = nc.dram_tensor("yp", [OWN, D], F32, kind="ExternalOutput")

    ges = p.es
    ident = p.sb(ges, "ident_sb", [128, 128], BF16)
    toks = [p.ld(ident[:], ident_d.ap(), "c0")]
    gpost_sb = [p.sb(ges, f"gpost{i}", [128, D], F32) for i in range(2)]
    toks.append(p.ld(gpost_sb[0][:], bcast_rows(g0post.ap(), D), "c0"))
    toks.append(p.ld(gpost_sb[1][:], bcast_rows(g1post.ap(), D), "c0"))
    gpre_sb = [p.sb(ges, f"gpre{i}", [128, KC], F32) for i in range(2)]
    toks.append(p.ld(gpre_sb[0][:], g0pre.ap().rearrange("(c p) -> p c", p=128), "c0", slow=True))
    toks.append(p.ld(gpre_sb[1][:], g1pre.ap().rearrange("(c p) -> p c", p=128), "c0", slow=True))
    p.barrier()

    def prep_w(w, ncols, gcol, name):
        wb = p.dram(name, [D, ncols], BF16)
        with contextlib.ExitStack() as es:
            CW = 1024
            wf = [p.sb(es, f"wf{i}", [128, CW], F32) for i in range(2)]
            wo = [p.sb(es, f"wo{i}", [128, CW], BF16) for i in range(2)]
            cons = [None, None]
            sts = [None, None]
            i = 0
            for kc in range(KC):
                for c0 in range(0, ncols, CW):
                    b = i % 2
                    t = p.ld(wf[b][:], w.ap()[kc * 128:(kc + 1) * 128, c0:c0 + CW], f"wl{b}", deps=[cons[b]])
                    eng = p.dve if b == 0 else p.pool
                    if gcol is None:
                        cons[b] = p.X(eng, [t, sts[b]], "tensor_copy", wo[b][:], wf[b][:])
                    else:
                        cons[b] = p.X(eng, [t, sts[b]], "tensor_scalar", wo[b][:], wf[b][:],
                                      gcol[:, kc:kc + 1], None, ALU.mult)
                    sts[b] = p.dma(p.act, wb.ap()[kc * 128:(kc + 1) * 128, c0:c0 + CW], wo[b][:], f"ws{b}",
                                   deps=[cons[b]])
                    i += 1
        p.barrier()
        return wb

    wb0i = prep_w(w0i, 7168, gpre_sb[0], "wb0i")
    wb0o = prep_w(w0o, D, None, "wb0o")
    wb1i = prep_w(w1i, 8192, gpre_sb[1], "wb1i")
    wb1o = prep_w(w1o, D, None, "wb1o")
    done("prep")

    def norm_T(x_rows, L, hT):
        with contextlib.ExitStack() as es:
            xt = [p.sb(es, f"nx{i}", [128, D], F32) for i in range(2)]
            junk = p.sb(es, "njunk", [128, D], BF16)
            hb = [p.sb(es, f"nhb{i}", [128, D], BF16) for i in range(2)]
            ho = [p.sb(es, f"nho{i}", [128, KC, 128], BF16) for i in range(2)]
            st_ = [p.sb(es, f"nst{i}", [128, 2], F32) for i in range(2)]
            rd = [None, None]
            hbr = [None, None]
            hor = [None, None]
            for ti in range(L // 128):
                b = ti % 2
                t = p.ld(xt[b][:], x_rows[ti * 128:(ti + 1) * 128, :], f"nl{b}", deps=[rd[b]])
                a1 = p.A([t], out=junk[:], in_=xt[b][:], func=AF.Square, accum_out=st_[b][:, 0:1])
                v2 = p.rsqrt([a1], st_[b][:, 1:2], st_[b][:, 0:1], 1.0 / D, EPS)
                v3 = p.G([v2, t, hbr[b]], "tensor_scalar", hb[b][:], xt[b][:], st_[b][:, 1:2], None, ALU.mult)
                rd[b] = v3
                for kc in range(KC):
                    tk = p.tr([v3, hor[b]] if kc == 0 else [], p.pb[kc // 8][:, (kc % 8) * 128:(kc % 8 + 1) * 128],
                              hb[b][:, kc * 128:(kc + 1) * 128], ident[:], mark=(kc == KC - 1))
                hbr[b] = tk
                c1 = p.A([tk, hor[b]], out=ho[b][:, 0:8, :], in_=p.pb[0][:].rearrange("p (k t) -> p k t", k=8),
                         func=AF.Copy)
                c2 = p.V([tk, hor[b]], "tensor_copy", ho[b][:, 8:16, :],
                         p.pb[1][:].rearrange("p (k t) -> p k t", k=8))
                p.pe.wait(c1, c2)
                hor[b] = p.st(hT.ap().rearrange("(k p) t -> p k t", p=128)[:, :, ti * 128:(ti + 1) * 128],
                              ho[b][:], f"ns{b}", deps=[c1, c2])
        p.barrier()

    def proj(hT, L, wb, ncols_total, jobs):
        TB = min(L, 1024)
        with contextlib.ExitStack() as es:
            hblk = p.sb(es, "pj_h", [128, KC, TB], BF16)
            wblk = [p.sb(es, f"pj_w{i}", [128, KC, 512], BF16) for i in range(2)]
            osb = {F32: [p.sb(es, f"pj_of{i}", [128, 512], F32) for i in range(2)],
                   BF16: [p.sb(es, f"pj_ob{i}", [128, 512], BF16) for i in range(2)]}
            wread = [None, None]
            ost = {F32: [None, None], BF16: [None, None]}
            psr = [None] * 4
            wi = 0
            oi = 0
            pi = 0
            hread = None
            for tb in range(L // TB):
                th = p.ld(hblk[:], hT.ap().rearrange("(k p) t -> p k t", p=128)[:, :, tb * TB:(tb + 1) * TB],
                          "pjh", deps=[hread])
                cbs = [(j, c) for j in jobs for c in range(0, j[1], 512)]
                for (job, c) in cbs:
                    col0, ncols, mode, od, o0, scale, odt = job
                    b = wi % 2
                    wi += 1
                    tw = p.ld(wblk[b][:], wb.ap().rearrange("(k p) n -> p k n", p=128)[:, :, col0 + c:col0 + c + 512],
                              f"pjw{b}", deps=[wread[b]])
                    last = None
                    TW = min(512, TB)
                    if mode == "F":
                        subs = [(s4, t5) for s4 in range(4) for t5 in range(TB // TW)]
                    else:
                        subs = [(s4, 0) for s4 in range(TB // 128)]
                    for (s4, t5) in subs:
                        pk = pi % 4
                        pi += 1
                        W_ = TW if mode == "F" else 512
                        ps = p.ps[pk][:, 0:W_]
                        for kc in range(KC):
                            if mode == "F":
                                lhsT, rhs = wblk[b][:, kc, s4 * 128:(s4 + 1) * 128], hblk[:, kc, t5 * TW:(t5 + 1) * TW]
                            else:
                                lhsT, rhs = hblk[:, kc, s4 * 128:(s4 + 1) * 128], wblk[b][:, kc, :]
                            tk = p.mm([th, tw, psr[pk]] if kc == 0 else [], ps, lhsT, rhs, kc == 0, kc == KC - 1,
                                      mark=(kc == KC - 1))
                        last = tk
                        ob = oi % 2
                        oi += 1
                        o = osb[odt][ob][:, 0:W_]
                        if oi % 2 == 0:
                            ev = p.A([tk, ost[odt][ob]], out=o, in_=ps, func=AF.Copy, scale=float(scale))
                        else:
                            ev = p.V([tk, ost[odt][ob]], "tensor_scalar", o, ps, float(scale), None, ALU.mult)
                        psr[pk] = ev
                        if mode == "F":
                            dst = od.ap()[o0 + c + s4 * 128:o0 + c + (s4 + 1) * 128,
                                          tb * TB + t5 * TW:tb * TB + (t5 + 1) * TW]
                        else:
                            dst = od.ap()[tb * TB + s4 * 128:tb * TB + (s4 + 1) * 128, o0 + c:o0 + c + 512]
                        ost[odt][ob] = p.st(dst, o, f"pjs{ob}{'f' if odt == F32 else 'b'}", deps=[ev])
                    wread[b] = last
                    hread = last
        p.barrier()

    def fnet(uT, L, ya, tabs):
        M = L // 128
        c128, cs256, tw, cm = tabs
        Bd = p.dram(f"fn_B{p.ndram}", [128, M, 512], BF16)
        p.ndram += 1
        CP = 32
        with contextlib.ExitStack() as es:
            ug = p.sb(es, "fn_u", [128, 2, L], BF16)
            MB = min(M, 32)
            V = p.sb(es, "fn_V", [128, MB, 512], BF16)
            Bs = p.sb(es, "fn_Bs", [128, MB, 512], BF16)
            tmp = [p.sb(es, f"fn_t{i}", [128, 2, 256], F32) for i in range(2)]
            Bt = p.sb(es, "fn_Bt", [M, CP, 512], BF16)
            Y = [p.sb(es, f"fn_Y{i}", [M, 2, 256], F32) for i in range(2)]
            for g in range(4):
                tu = p.ld(ug[:], uT.ap()[g * 256:(g + 1) * 256, :].rearrange("(c p) t -> p c t", p=128), "fnu")
                ts_all = []
                for bh in range(M // MB):
                    evs = []
                    prev = [None, None]
                    for bl in range(MB):
                        b = bh * MB + bl
                        pk = b % 2
                        for ch in range(2):
                            lhsT = bass.AP(ug, ch * L + b, [[2 * L, 128], [M, 128]])
                            tk = p.mm([tu, prev[pk]] if ch == 0 else [], p.ps[pk][:], lhsT, cs256[:, ch, :],
                                      ch == 0, ch == 1, mark=(ch == 1))
                        if b % 2 == 0:
                            ev = p.A([tk], out=V[:, bl, :], in_=p.ps[pk][:], func=AF.Copy)
                        else:
                            ev = p.V([tk], "tensor_copy", V[:, bl, :], p.ps[pk][:])
                        prev[pk] = ev
                        evs.append(ev)
                    prevr = [None, None]
                    tw_tok = []
                    for bp in range(MB // 2):
                        pk = 2 + (bp % 2) * 2
                        Ar, Ai = p.ps[pk], p.ps[pk + 1]
                        b0 = 2 * bp
                        dep = [evs[b0], evs[b0 + 1], prevr[bp % 2]]
                        ar3 = Ar[:].rearrange("p (b f) -> p b f", b=2)
                        ai3 = Ai[:].rearrange("p (b f) -> p b f", b=2)
                        p.mm(dep, ar3, c128[:, 0, :], V[:, b0:b0 + 2, 0:256], True, False)
                        p.mm([], ar3, c128[:, 1, :], V[:, b0:b0 + 2, 256:512], False, True)
                        p.mm([], ai3, c128[:, 0, :], V[:, b0:b0 + 2, 256:512], True, False)
                        tk = p.mm([], ai3, c128[:, 2, :], V[:, b0:b0 + 2, 0:256], False, True, mark=True)
                        last = []
                        for j in range(2):
                            bl = b0 + j
                            b = bh * MB + bl
                            t1 = p.V([tk], "tensor_scalar", tmp[0][:, j, :], ar3[:, j, :], tw[:, 0, b:b + 1], None, ALU.mult)
                            t3 = p.V([tk], "tensor_scalar", tmp[1][:, j, :], ai3[:, j, :], tw[:, 0, b:b + 1], None, ALU.mult)
                            r1 = p.V([t1, tk], "scalar_tensor_tensor", Bs[:, bl, 0:256], ai3[:, j, :], tw[:, 1, b:b + 1],
                                     tmp[0][:, j, :], ALU.mult, ALU.add)
                            r2 = p.V([t3, tk], "scalar_tensor_tensor", Bs[:, bl, 256:512], ar3[:, j, :], tw[:, 2, b:b + 1],
                                     tmp[1][:, j, :], ALU.mult, ALU.add)
                            last = [r1, r2]
                        p.act.wait(*last)
                        prevr[bp % 2] = last[1]
                        tw_tok = last
                    ts_all.append(p.st(Bd.ap()[:, bh * MB:(bh + 1) * MB, :], Bs[:], "fnb", deps=tw_tok))
                    p.barrier()
                if True:
                    ts_ = ts_all[-1]
                    yst = [None, None]
                    bt_read = None
                    for cp in range(128 // CP):
                        tl = p.ld(Bt[:], Bd.ap()[cp * CP:(cp + 1) * CP, :, :].rearrange("c b f -> b c f"), "fnbt",
                                  deps=[ts_, bt_read])
                        for c2 in range(CP // 2):
                            pk = c2 % 2
                            ps3 = p.ps[pk][0:M, :].rearrange("p (c f) -> p c f", c=2)
                            p.mm([tl, yst[pk]], ps3, cm[0:M, 0, :], Bt[:, 2 * c2:2 * c2 + 2, 0:256], True, False)
                            tk = p.mm([], ps3, cm[0:M, 1, :], Bt[:, 2 * c2:2 * c2 + 2, 256:512], False, True, mark=True)
                            if c2 % 2 == 0:
                                ev = p.A([tk], out=Y[pk][:], in_=ps3, func=AF.Copy)
                            else:
                                ev = p.V([tk], "tensor_copy", Y[pk][:], ps3)
                            c_abs = cp * CP + 2 * c2
                            dst = ya.ap().rearrange("(d c) f -> d c f", c=128)[:, c_abs:c_abs + 2, g * 256:(g + 1) * 256]
                            yst[pk] = p.st(dst, Y[pk][:], f"fny{pk}", deps=[ev])
                            p.pe.wait(ev)
                            bt_read = tk
                    p.barrier()
        p.barrier()

    def hgrn(qrT, ffT, fbT, vtok, gates, L, ymix, lbt, hm, ghg_sb):
        SEG = min(L, 2048)
        NCH = SEG // 64
        nseg = L // SEG
        NT = L // 64
        dS = p.dram(f"hg_dS{p.ndram}", [2, NT, 128, 128], F32)
        Sb = p.dram(f"hg_Sb{p.ndram}", [2, NT, 128, 128], BF16)
        qd = p.dram(f"hg_qd{p.ndram}", [2, 128, L], BF16)
        scd = p.dram(f"hg_sc{p.ndram}", [64, NT, 64], BF16)
        eld = p.dram(f"hg_el{p.ndram}", [2, 128, NT], F32)
        p.ndram += 1
        for h in range(8):
            with contextlib.ExitStack() as es:
                qr = p.sb(es, "h_qr", [128, SEG], F32)
                fr = p.sb(es, "h_fr", [128, SEG], F32)
                f_ = p.sb(es, "h_f", [128, SEG], F32)
                g_ = p.sb(es, "h_g", [128, SEG], F32)
                k_ = p.sb(es, "h_k", [128, SEG], F32)
                cum = p.sb(es, "h_cum", [128, SEG], F32)
                cb = p.sb(es, "h_cb", [128, SEG], F32)
                ex = p.sb(es, "h_ex", [128, SEG], F32)
                rmask = p.sb(es, "h_rm", [128, SEG], F32)
                qdec = [p.sb(es, f"h_qd{i}", [128, SEG], BF16) for i in range(2)]
                kdec = p.sb(es, "h_kd", [128, SEG], BF16)
                kend = p.sb(es, "h_ke", [128, SEG], BF16)
                kendT = p.sb(es, "h_keT", [64, NCH, 128], BF16)
                vt = p.sb(es, "h_v", [64, NCH, 128], BF16)
                sct = p.sb(es, "h_sc", [64, NCH, 64], BF16)
                sc1 = p.sb(es, "h_sc1", [64, NCH, 64], F32)
                el = p.sb(es, "h_el", [128, 2, NCH], F32)
                dSs = [[p.sb(es, f"h_dS{i}{j}", [128, 4, 128], F32) for j in range(2)] for i in range(2)]
                hz = {}
                m1 = p.G([], "memset", rmask[:], 1.0)
                m2 = p.G([], "memset", rmask[:].rearrange("p (n s) -> p n s", s=64)[:, :, 0:1], 0.0)
                p.barrier()
                for sg in range(nseg):
                    t0 = sg * SEG
                    tq = p.ld(qr[:], qrT.ap()[h * 128:(h + 1) * 128, t0:t0 + SEG], "hq")
                    tv = p.ld(vt[:], vtok.ap()[t0:t0 + SEG, h * 128:(h + 1) * 128].rearrange("(n s) v -> s n v", s=64),
                              "hv")
                    aq = p.A([tq], out=qr[:], in_=qr[:], func=AF.Silu)
                    for d in range(2):
                        src = ffT if d == 0 else fbT
                        tf = p.ld(fr[:], src.ap()[h * 128:(h + 1) * 128, t0:t0 + SEG], "hf")
                        a1 = p.A([tf], out=f_[:], in_=fr[:], func=AF.Sigmoid)
                        v1 = p.V([a1], "tensor_scalar", f_[:], f_[:], lbt[:, 2 + d, h:h + 1], lbt[:, d, h:h + 1],
                                 ALU.mult, ALU.add)
                        a2 = p.A([v1], out=g_[:], in_=f_[:], func=AF.Ln)
                        g1 = p.G([v1], "tensor_scalar", k_[:], f_[:], -1.0, 1.0, ALU.mult, ALU.add)
                        v2 = p.V([a2], "tensor_tensor_scan", cum[:], rmask[:], g_[:], 0.0, ALU.mult, ALU.add)
                        cum3 = cum[:].rearrange("p (n s) -> p n s", s=64)
                        lastb = bass.AP(cum, 63, [[SEG, 128], [64, NCH], [0, 64]])
                        if d == 0:
                            cc = cum
                            v3 = p.V([v2], "tensor_tensor", cb[:].rearrange("p (n s) -> p n s", s=64), lastb, cum3,
                                     ALU.subtract)
                            dl = cb
                        else:
                            v3a = p.V([v2], "tensor_tensor", cb[:].rearrange("p (n s) -> p n s", s=64), lastb, cum3,
                                      ALU.subtract)
                            v3b = p.V([v3a], "tensor_tensor", cb[:], cb[:], g_[:], ALU.add)
                            cc = cb
                            v3 = p.V([v3b], "tensor_tensor", g_[:], cum[:], g_[:], ALU.subtract)
                            dl = g_
                        a3 = p.A([v3], out=ex[:], in_=cc[:], func=AF.Exp)
                        g2 = p.G([a3, aq], "tensor_tensor", qdec[d][:], qr[:], ex[:], ALU.mult)
                        a4 = p.A([g2], out=ex[:], in_=cc[:], func=AF.Exp, scale=-1.0)
                        g3 = p.G([a4, g1], "tensor_tensor", kdec[:], k_[:], ex[:], ALU.mult)
                        a5 = p.A([g3], out=ex[:], in_=dl[:], func=AF.Exp)
                        g4 = p.G([a5], "tensor_tensor", kend[:], k_[:], ex[:], ALU.mult)
                        a6 = p.A([v2], out=el[:, d, :], in_=cum3[:, :, 63], func=AF.Exp)
                        sq = p.st(qd.ap()[d, :, t0:t0 + SEG], qdec[d][:], "hsq", deps=[g2])
                        se = p.st(eld.ap()[d, :, sg * NCH:(sg + 1) * NCH], el[:, d, :], "hse", deps=[a6])
                        nb = min(8, NCH)
                        NGR = NCH // nb
                        mk_ap = bass.AP(hm, d * 64, [[128, 64], [0, nb], [1, 64]])
                        for gq in range(NGR + 1):
                            if gq < NGR:
                                n0 = gq * nb
                                par = gq % 2
                                key = ("sc", par)
                                for j in range(nb):
                                    sl = slice((n0 + j) * 64, (n0 + j + 1) * 64)
                                    tk = p.mm([g2, g3, hz.get(key)] if j == 0 else [], p.ps[par][0:64, j * 64:(j + 1) * 64],
                                              kdec[:, sl], qdec[d][:, sl], True, True, mark=(j == nb - 1))
                                psv = p.ps[par][0:64, 0:nb * 64].rearrange("p (n s) -> p n s", s=64)
                                if d == 0:
                                    ev = p.V([tk], "tensor_tensor", sc1[:, n0:n0 + nb, :], psv, mk_ap, ALU.mult)
                                else:
                                    ev0 = p.V([tk], "tensor_tensor", sct[:, n0:n0 + nb, :], psv, mk_ap, ALU.mult)
                                    ev = p.V([ev0], "tensor_tensor", sct[:, n0:n0 + nb, :], sct[:, n0:n0 + nb, :],
                                             sc1[:, n0:n0 + nb, :], ALU.add)
                                hz[key] = ev
                                key = ("tr", par)
                                for j in range(nb):
                                    sl = slice((n0 + j) * 64, (n0 + j + 1) * 64)
                                    tk2 = p.tr([g4, hz.get(key)] if j == 0 else [], p.pb[par][0:64, j * 128:(j + 1) * 128],
                                               kend[:, sl], ident[:], mark=(j == nb - 1))
                                ev2 = p.A([tk2, hz.get(("ds_mm", par))], out=kendT[:, n0:n0 + nb, :],
                                          in_=p.pb[par][0:64, 0:nb * 128].rearrange("p (n k) -> p n k", k=128), func=AF.Copy)
                                hz[key] = ev2
                                hz[("kT", par)] = ev2
                            if gq >= 1:
                                gprev = gq - 1
                                n0 = gprev * nb
                                par = gprev % 2
                                nbk = (nb + 3) // 4
                                for j in range(nb):
                                    bank = p.ps[2 + 2 * par + j // 4]
                                    tk3 = p.mm([hz[("kT", par)], tv, hz.get(("dsb", par, j // 4))] if j % 4 == 0 else [],
                                               bank[:, (j % 4) * 128:(j % 4 + 1) * 128], kendT[:, n0 + j, :], vt[:, n0 + j, :],
                                               True, True, mark=(j % 4 == 3 or j == nb - 1))
                                    if j % 4 == 3 or j == nb - 1:
                                        i4 = j // 4
                                        w4 = j % 4 + 1
                                        dst = dSs[par][i4][:, 0:w4, :]
                                        src = bank[:, 0:w4 * 128].rearrange("p (n v) -> p n v", v=128)
                                        if i4 == 0:
                                            ev3 = p.A([tk3, hz.get(("dss", par, i4))], out=dst, in_=src, func=AF.Copy)
                                        else:
                                            ev3 = p.V([tk3, hz.get(("dss", par, i4))], "tensor_copy", dst, src)
                                        hz[("dsb", par, i4)] = ev3
                                        c0 = sg * NCH + n0 + i4 * 4
                                        hz[("dss", par, i4)] = p.st(
                                            dS.ap()[d, c0:c0 + w4, :, :].rearrange("n p v -> p n v"), dst, f"hds{par}{i4}",
                                            deps=[ev3])
                                hz[("ds_mm", par)] = tk3
                        if d == 1:
                            p.st(scd.ap()[:, sg * NCH:(sg + 1) * NCH, :], sct[:], "hsc", deps=[ev])
                        p.barrier()
                        hz.clear()
            with contextlib.ExitStack() as es:
                G = min(NT, 32)
                NG = NT // G
                dsl = [p.sb(es, f"h2_ds{i}", [128, G, 128], F32) for i in range(2)]
                sall = [p.sb(es, f"h2_sa{i}", [128, G + 1, 128], F32) for i in range(2)]
                sbo = [p.sb(es, f"h2_sb{i}", [128, G, 128], BF16) for i in range(2)]
                ela = p.sb(es, "h2_el", [128, 2, NT], F32)
                te = p.ld(ela[:], eld.ap().rearrange("d p n -> p d n"), "h2e")
                z0 = p.V([], "memset", sall[0][:, 0, :], 0.0)
                z1 = p.V([], "memset", sall[1][:, G, :], 0.0)
                last = [z0, z1]
                for gi in range(NG):
                    gf, gb = gi, NG - 1 - gi
                    tl = [p.ld(dsl[0][:], dS.ap()[0, gf * G:(gf + 1) * G, :, :].rearrange("n p v -> p n v"), "h2l0"),
                          p.ld(dsl[1][:], dS.ap()[1, gb * G:(gb + 1) * G, :, :].rearrange("n p v -> p n v"), "h2l1")]
                    for i in range(G):
                        nf = gf * G + i
                        last[0] = p.V([tl[0], te, last[0]], "scalar_tensor_tensor", sall[0][:, i + 1, :], sall[0][:, i, :],
                                      ela[:, 0, nf:nf + 1], dsl[0][:, i, :], ALU.mult, ALU.add)
                        j = G - 1 - i
                        nbk = gb * G + j
                        last[1] = p.V([tl[1], te, last[1]], "scalar_tensor_tensor", sall[1][:, j, :], sall[1][:, j + 1, :],
                                      ela[:, 1, nbk:nbk + 1], dsl[1][:, j, :], ALU.mult, ALU.add)
                    c0 = p.G(last, "tensor_copy", sbo[0][:], sall[0][:, 0:G, :])
                    c1 = p.A(last, out=sbo[1][:], in_=sall[1][:, 1:G + 1, :], func=AF.Copy)
                    p.st(Sb.ap()[0, gf * G:(gf + 1) * G, :, :].rearrange("n p v -> p n v"), sbo[0][:], "h2s0", deps=[c0])
                    p.st(Sb.ap()[1, gb * G:(gb + 1) * G, :, :].rearrange("n p v -> p n v"), sbo[1][:], "h2s1", deps=[c1])
                    last[0] = p.V([c0, c1] + last, "tensor_copy", sall[0][:, 0, :], sall[0][:, G, :])
                    last[1] = p.V([last[0]], "tensor_copy", sall[1][:, G, :], sall[1][:, 0, :])
                    p.barrier()
            with contextlib.ExitStack() as es:
                G = min(NT, 32)
                qd3 = p.sb(es, "h3_qd", [128, 2, G * 64], BF16)
                sc3 = p.sb(es, "h3_sc", [64, G, 64], BF16)
                v3_ = p.sb(es, "h3_v", [64, G, 128], BF16)
                gt3 = p.sb(es, "h3_g", [64, G, 128], F32)
                s3 = p.sb(es, "h3_s", [128, 2, G, 128], BF16)
                o3 = p.sb(es, "h3_o", [64, G, 128], F32)
                ob3 = p.sb(es, "h3_ob", [64, G, 128], BF16)
                sq3 = p.sb(es, "h3_sq", [64, G, 128], F32)
                ss = p.sb(es, "h3_ss", [64, G], F32)
                for gi in range(NT // G):
                    t0 = gi * G * 64
                    tl = [p.ld(qd3[:], qd.ap()[:, :, t0:t0 + G * 64].rearrange("d p t -> p d t"), "h3a"),
                          p.ld(sc3[:], scd.ap()[:, gi * G:(gi + 1) * G, :], "h3a"),
                          p.ld(v3_[:], vtok.ap()[t0:t0 + G * 64, h * 128:(h + 1) * 128].rearrange("(n s) v -> s n v", s=64), "h3a"),
                          p.ld(gt3[:], gates.ap()[t0:t0 + G * 64, 1024 + h * 128:1024 + (h + 1) * 128].rearrange("(n s) v -> s n v", s=64), "h3a"),
                          p.ld(s3[:, 0], Sb.ap()[0, gi * G:(gi + 1) * G, :, :].rearrange("n p v -> p n v"), "h3a"),
                          p.ld(s3[:, 1], Sb.ap()[1, gi * G:(gi + 1) * G, :, :].rearrange("n p v -> p n v"), "h3a")]
                    ag = p.A([tl[3]], out=gt3[:], in_=gt3[:], func=AF.Silu)
                    prev = [None, None]
                    nb3 = min(4, G)
                    for j0 in range(0, G, nb3):
                        pk = (j0 // nb3) % 2
                        for jj in range(nb3):
                            j = j0 + jj
                            ps = p.ps[pk][0:64, jj * 128:(jj + 1) * 128]
                            sl = slice(j * 64, (j + 1) * 64)
                            p.mm(tl + [prev[pk]] if jj == 0 else [], ps, sc3[:, j, :], v3_[:, j, :], True, False)
                            p.mm([], ps, qd3[:, 0, sl], s3[:, 0, j, :], False, False)
                            tk = p.mm([], ps, qd3[:, 1, sl], s3[:, 1, j, :], False, True, mark=(jj == nb3 - 1))
                        ev = p.V([tk], "tensor_copy", o3[:, j0:j0 + nb3, :],
                                 p.ps[pk][0:64, 0:nb3 * 128].rearrange("p (n v) -> p n v", v=128))
                        prev[pk] = ev
                    e2 = p.A([ev], out=sq3[:], in_=o3[:], func=AF.Square)
                    e3 = p.V([e2], "tensor_reduce", ss[:], sq3[:], mybir.AxisListType.X, ALU.add)
                    r2 = p.rsqrt([e3], ss[:], ss[:], 1.0 / 128, EPS)
                    ssb = bass.AP(ss, 0, [[G, 64], [1, G], [0, 128]])
                    r3 = p.V([r2], "tensor_tensor", o3[:], o3[:], ssb, ALU.mult)
                    ghb = bass.AP(ghg_sb, 0, [[128, 64], [0, G], [1, 128]])
                    r4 = p.V([r3], "tensor_tensor", o3[:], o3[:], ghb, ALU.mult)
                    r5 = p.V([r4, ag], "tensor_tensor", ob3[:], o3[:], gt3[:], ALU.mult)
                    p.st(ymix.ap()[t0:t0 + G * 64, 1024 + h * 128:1024 + (h + 1) * 128].rearrange("(n s) v -> s n v", s=64),
                         ob3[:], "h3s", deps=[r5])
                    p.barrier()
        p.barrier()

    def outproj(L, ymix, ya, gates, wbo, gpost, xres, xout, hT_next):
        with contextlib.ExitStack() as es:
            wsb = p.sb(es, "op_w", [128, KC, D], BF16)
            tw = p.ld(wsb[:], wbo.ap().rearrange("(k p) n -> p k n", p=128), "opw")
            ym = [p.sb(es, f"op_ym{i}", [128, D], BF16) for i in range(2)]
            yaf = p.sb(es, "op_ya", [128, 1024], F32) if ya is not None else None
            gaf = p.sb(es, "op_ga", [128, 1024], F32) if ya is not None else None
            ymT = p.sb(es, "op_ymT", [128, KC, 128], BF16)
            xr = [p.sb(es, f"op_x{i}", [128, D], F32) for i in range(2)]
            yo = p.sb(es, "op_y", [128, D], F32)
            junk = p.sb(es, "op_j", [128, D], BF16)
            st_ = p.sb(es, "op_st", [128, 4], F32)
            hb = p.sb(es, "op_hb", [128, D], BF16)
            ho = p.sb(es, "op_ho", [128, KC, 128], BF16)
            for ti in range(L // 128):
                b = ti % 2
                rows = slice(ti * 128, (ti + 1) * 128)
                tx = p.ld(xr[b][:], xres[rows, :], f"opx{b}")
                if ya is not None:
                    t1 = p.ld(ym[b][:, 1024:2048], ymix.ap()[rows, 1024:2048], f"opm{b}")
                    t2 = p.ld(yaf[:], ya.ap()[rows, :], "opa")
                    t3 = p.ld(gaf[:], gates.ap()[rows, 0:1024], "opa")
                    a1 = p.A([t3], out=gaf[:], in_=gaf[:], func=AF.Silu)
                    v1 = p.V([a1, t2], "tensor_tensor", ym[b][:, 0:1024], yaf[:], gaf[:], ALU.mult)
                    rdy = [t1, v1]
                else:
                    rdy = [p.ld(ym[b][:], ymix.ap()[rows, :], f"opm{b}")]
                for kc in range(KC):
                    tk = p.tr(rdy if kc == 0 else [], p.pb[kc // 8][:, (kc % 8) * 128:(kc % 8 + 1) * 128],
                              ym[b][:, kc * 128:(kc + 1) * 128], ident[:], mark=(kc == KC - 1))
                c1 = p.A([tk], out=ymT[:, 0:8, :], in_=p.pb[0][:].rearrange("p (k t) -> p k t", k=8), func=AF.Copy)
                c2 = p.V([tk], "tensor_copy", ymT[:, 8:16, :], p.pb[1][:].rearrange("p (k t) -> p k t", k=8))
                for cb in range(4):
                    for kc in range(KC):
                        tk = p.mm([c1, c2, tw] if kc == 0 else [], p.ps[cb][:], ymT[:, kc, :],
                                  wsb[:, kc, cb * 512:(cb + 1) * 512], kc == 0, kc == KC - 1, mark=(kc == KC - 1))
                evs = []
                for cb in range(4):
                    if cb % 2 == 0:
                        evs.append(p.A([tk], out=yo[:, cb * 512:(cb + 1) * 512], in_=p.ps[cb][:], func=AF.Copy))
                    else:
                        evs.append(p.V([tk], "tensor_copy", yo[:, cb * 512:(cb + 1) * 512], p.ps[cb][:]))
                a2 = p.A(evs, out=junk[:], in_=yo[:], func=AF.Square, accum_out=st_[:, 0:1])
                v3 = p.rsqrt([a2], st_[:, 1:2], st_[:, 0:1], 1.0 / D, EPS)
                v4 = p.V([v3], "scalar_tensor_tensor", yo[:], yo[:], st_[:, 1:2], gpost[:], ALU.mult, ALU.mult)
                v5 = p.V([v4, tx], "tensor_tensor", yo[:], yo[:], xr[b][:], ALU.add)
                so = p.st(xout[rows, :], yo[:], "opo", deps=[v5])
                last = [so]
                if hT_next is not None:
                    a3 = p.A([v5], out=junk[:], in_=yo[:], func=AF.Square, accum_out=st_[:, 2:3])
                    v7 = p.rsqrt([a3], st_[:, 3:4], st_[:, 2:3], 1.0 / D, EPS)
                    v8 = p.V([v7], "tensor_scalar", hb[:], yo[:], st_[:, 3:4], None, ALU.mult)
                    for kc in range(KC):
                        tk = p.tr([v8] if kc == 0 else [], p.pb[kc // 8][:, (kc % 8) * 128:(kc % 8 + 1) * 128],
                                  hb[:, kc * 128:(kc + 1) * 128], ident[:], mark=(kc == KC - 1))
                    c1 = p.A([tk], out=ho[:, 0:8, :], in_=p.pb[0][:].rearrange("p (k t) -> p k t", k=8), func=AF.Copy)
                    c2 = p.V([tk], "tensor_copy", ho[:, 8:16, :], p.pb[1][:].rearrange("p (k t) -> p k t", k=8))
                    last.append(p.st(hT_next.ap().rearrange("(k p) t -> p k t", p=128)[:, :, rows], ho[:], "oph",
                                     deps=[c1, c2]))
                for e in (p.pe, p.act, p.dve, p.pool, p.sp):
                    e.wait(*last, (p.dve, p.dve.n), (p.act, p.act.n))
        p.barrier()

    def attention(qT, kT, vtok, gates, Lq, Lk, og, dtab, lam_sb, gsub_sb, delta):
        NKT, NQB = Lk // 128, Lq // 256
        SK = 2
        with contextlib.ExitStack() as es:
            qs = p.sb(es, "at_q", [128, 2, Lq], BF16)
            ks = p.sb(es, "at_k", [128, 2, Lk], BF16)
            vs = p.sb(es, "at_v", [128, NKT, 257], BF16)
            absd = [p.sb(es, f"at_ad{i}", [128, 256], F32) for i in range(3)]
            sb_ = [p.sb(es, f"at_s{i}", [128, 256], F32) for i in range(4)]
            pt = [p.sb(es, f"at_p{i}", [128, 256], BF16) for i in range(4)]
            gt = [p.sb(es, f"at_g{i}", [128, 2, 256], F32) for i in range(2)]
            o0 = [p.sb(es, f"at_o0{i}", [128, 2, 256], F32) for i in range(2)]
            o1 = [p.sb(es, f"at_o1{i}", [128, 2, 256], F32) for i in range(2)]
            rs = [p.sb(es, f"at_rs{i}", [128, 8], F32) for i in range(2)]
            ob = [p.sb(es, f"at_ob{i}", [128, 2, 256], BF16) for i in range(2)]
            jk = p.sb(es, "at_j", [128, 256], F32)
            Sps = [p.ps[i][:, 0:256] for i in range(2)]
            acc = [[p.ps[2 + 2 * c + j] for j in range(2)] for c in range(2)]
            qbi = 0
            gt_free = [[], []]
            ob_free = [None, None]
            for h in range(8):
                slope = SLOPES[h]
                t_in = [p.ld(qs[:], qT.ap()[h * 256:(h + 1) * 256, :].rearrange("(c p) t -> p c t", p=128), "atq"),
                        p.ld(ks[:], kT.ap()[h * 256:(h + 1) * 256, :].rearrange("(c p) t -> p c t", p=128), "atq"),
                        p.ld(vs[:, :, 0:256], vtok.ap()[:, h * 256:(h + 1) * 256].rearrange("(n p) v -> p n v", p=128),
                             "atq")]
                t_in.append(p.G([], "memset", vs[:, :, 256:257], 1.0))
                acc_free = []
                for qb in range(NQB):
                    par = qbi % 2
                    qbi += 1
                    q0 = qb * 256
                    tg = p.ld(gt[par][:],
                              gates.ap()[q0:q0 + 256, h * 256:(h + 1) * 256].rearrange("(j p) v -> p j v", p=128),
                              f"atg{par}", deps=gt_free[par])
                    units = [(kt, c) for kt in range(NKT) for c in range(2)]
                    NU = len(units)
                    rd_ad = [None] * 3
                    rd_s = [None] * 4
                    rd_p = [None] * 4
                    rd_ps = [None] * 2
                    absT = {}
                    expT = {}
                    state = {"lastpv": None}

                    def do_abs(kt):
                        ab = kt % 3
                        idx = kt * NQB + qb
                        absT[kt] = p.A([rd_ad[ab]], out=absd[ab][:], in_=delta[:], func=AF.Abs, bias=dtab[:, idx:idx + 1])

                    def front(u):
                        kt, c = units[u]
                        b = u % 4
                        if c == 0:
                            if kt == 0:
                                do_abs(0)
                            if kt + 1 < NKT:
                                do_abs(kt + 1)
                        b2 = u % 2
                        tk = p.mm(t_in + [rd_ps[b2]], Sps[b2], ks[:, c, kt * 128:(kt + 1) * 128], qs[:, c, q0:q0 + 256],
                                  True, True, mark=True)
                        v1 = p.V([tk, absT[kt], rd_s[b]], "scalar_tensor_tensor", sb_[b][:], absd[kt % 3][:], -slope,
                                 Sps[b2], ALU.mult, ALU.add)
                        rd_ps[b2] = v1
                        rd_ad[kt % 3] = v1
                        a1 = p.A([v1, rd_p[b]], out=pt[b][:], in_=sb_[b][:], func=AF.Exp)
                        rd_s[b] = a1
                        expT[u] = a1

                    def back(u):
                        kt, c = units[u]
                        b = u % 4
                        for j in range(2):
                            deps = [expT[u]] if j == 0 else []
                            if kt == 0:
                                deps = deps + acc_free
                            state["lastpv"] = p.mm(deps, acc[c][j][:, 0:257], pt[b][:, j * 128:(j + 1) * 128],
                                                   vs[:, kt, :], kt == 0, kt == NKT - 1, mark=(j == 1))
                        rd_p[b] = state["lastpv"]

                    for u in range(NU + SK):
                        if u < NU:
                            front(u)
                        if u >= SK:
                            back(u - SK)
                    lastpv = state["lastpv"]
                    evs = []
                    for c in range(2):
                        for j in range(2):
                            dst = (o0[par] if c == 0 else o1[par])
                            e1 = p.V([lastpv], "reciprocal", rs[par][:, 2 * c + j:2 * c + j + 1], acc[c][j][:, 256:257])
                            if c == 0:
                                evs.append(p.V([e1], "tensor_scalar", dst[:, j, :], acc[c][j][:, 0:256],
                                               rs[par][:, 2 * c + j:2 * c + j + 1], None, ALU.mult))
                            else:
                                evs.append(p.V([e1], "tensor_scalar", dst[:, j, :], acc[c][j][:, 0:256],
                                               rs[par][:, 2 * c + j:2 * c + j + 1], lam_sb[:, 1:2], ALU.mult, ALU.mult))
                    acc_free = [evs[-1]]
                    f1 = p.V(evs, "tensor_tensor", o0[par][:], o0[par][:], o1[par][:], ALU.add)
                    for j in range(2):
                        sqt = p.A([f1], out=jk[:], in_=o0[par][:, j, :], func=AF.Square, accum_out=rs[par][:, 4 + j:5 + j])
                    f2 = p.rsqrt([sqt], rs[par][:, 4:6], rs[par][:, 4:6], 1.0 / 256, EPS)
                    f3 = p.V([f2], "tensor_scalar", rs[par][:, 4:6], rs[par][:, 4:6], (1.0 - LAMBDA_INIT), None, ALU.mult)
                    ag = p.A([tg], out=gt[par][:], in_=gt[par][:], func=AF.Silu)
                    for j in range(2):
                        f4 = p.V([f3], "scalar_tensor_tensor", o0[par][:, j, :], o0[par][:, j, :], rs[par][:, 4 + j:5 + j],
                                 gsub_sb[:], ALU.mult, ALU.mult)
                    f5 = p.V([f4, ag, ob_free[par]], "tensor_tensor", ob[par][:], o0[par][:], gt[par][:], ALU.mult)
                    ob_free[par] = p.st(og.ap()[q0:q0 + 256, h * 256:(h + 1) * 256].rearrange("(j p) v -> p j v", p=128),
                                        ob[par][:], f"ato{par}", deps=[f5])
                    gt_free[par] = [f5, ag]
                p.barrier()
                gt_free = [[], []]
                ob_free = [None, None]
        p.barrier()

    c128 = p.sb(ges, "c128_sb", [128, 3, 128], BF16)
    p.ld(c128[:], c128_d.ap().rearrange("k p c -> p k c"), "c0")
    cs256 = p.sb(ges, "cs256_sb", [128, 2, 512], BF16)
    p.ld(cs256[:], cs256_d.ap().rearrange("(c p) n -> p c n", p=128), "c0")
    tw_sb, cm_sb = {}, {}
    for L in tw_d:
        M = L // 128
        tw_sb[L] = p.sb(ges, f"tw{L}_sb", [128, 3, M], F32)
        p.ld(tw_sb[L][:], tw_d[L].ap().rearrange("k p m -> p k m"), "c0")
        cm_sb[L] = p.sb(ges, f"cm{L}_sb", [M, 2, M], BF16)
        p.ld(cm_sb[L][:], cm_d[L].ap().rearrange("k b d -> b k d"), "c0")
    hm = p.sb(ges, "hm_sb", [64, 2, 64], F32)
    p.ld(hm[:], hmask_d.ap().rearrange("k s t -> s k t"), "c0")
    delta = p.sb(ges, "delta_sb", [128, 256], F32)
    p.ld(delta[:], delta_d.ap(), "c0")
    dtabs = p.sb(ges, "dtabs_sb", [128, (LS // 128) * (LS // 256)], F32)
    p.ld(dtabs[:], dtabs_d.ap(), "c0")
    dtabp = p.sb(ges, "dtabp_sb", [128, (LP // 128) * (OWN // 256)], F32)
    p.ld(dtabp[:], dtabp_d.ap(), "c0")
    ownidx = p.sb(ges, "ownidx_sb", [128, OWN // 128], I32)
    p.ld(ownidx[:], ownidx_d.ap(), "c0")
    ghg_sb = p.sb(ges, "ghg_sb", [64, 128], F32)
    p.ld(ghg_sb[:], bcast_rows(ghg.ap(), 128, 64), "c0")
    gsub_sb = p.sb(ges, "gsub_sb", [128, 256], F32)
    p.ld(gsub_sb[:], bcast_rows(gsub.ap(), 256), "c0")
    lraw = p.sb(ges, "lraw_sb", [128, 2, 3, 8], F32)
    p.ld(lraw[:], lbl.ap().rearrange("d s (h k) -> k d s h", k=128), "c0", slow=True)
    lbt = p.sb(ges, "lbt_sb", [128, 4, 8], F32)
    lsum = p.sb(ges, "lsum_sb", [128, 2, 8], F32)
    lamv = p.sb(ges, "lamv_sb", [128, 4, 128], F32)
    p.ld(lamv[:], bass.AP(lam4.ap().tensor, 0, [[0, 128], [128, 4], [1, 128]]), "c0")
    lam_sb = p.sb(ges, "lam_sb", [128, 4], F32)
    ljunk = p.sb(ges, "ljunk_sb", [128, 128], F32)
    p.barrier()
    a = p.A([], out=lraw[:], in_=lraw[:], func=AF.Exp)
    v = p.V([a], "tensor_tensor", lsum[:], lraw[:, :, 0, :], lraw[:, :, 1, :], ALU.add)
    v = p.V([v], "tensor_tensor", lsum[:], lsum[:], lraw[:, :, 2, :], ALU.add)
    v = p.V([v], "reciprocal", lsum[:], lsum[:])
    v = p.V([v], "tensor_tensor", lbt[:, 0:2, :], lraw[:, :, 0, :], lsum[:], ALU.mult)
    v = p.V([v], "tensor_scalar", lbt[:, 2:4, :], lbt[:, 0:2, :], -1.0, 1.0, ALU.mult, ALU.add)
    v = p.V([v], "tensor_tensor", ljunk[:], lamv[:, 0, :], lamv[:, 1, :], ALU.mult)
    v = p.V([v], "tensor_reduce", lam_sb[:, 2:3], ljunk[:], mybir.AxisListType.X, ALU.add)
    v = p.V([v], "tensor_tensor", ljunk[:], lamv[:, 2, :], lamv[:, 3, :], ALU.mult)
    v = p.V([v], "tensor_reduce", lam_sb[:, 3:4], ljunk[:], mybir.AxisListType.X, ALU.add)
    a = p.A([v], out=lam_sb[:, 2:4], in_=lam_sb[:, 2:4], func=AF.Exp)
    v = p.V([a], "tensor_tensor", lam_sb[:, 0:1], lam_sb[:, 2:3], lam_sb[:, 3:4], ALU.subtract)
    v = p.V([v], "tensor_scalar", lam_sb[:, 1:2], lam_sb[:, 0:1], LAMBDA_INIT, -1.0, ALU.add, ALU.mult)
    p.barrier()

    seqs = [("s%d" % i, LS, xs.ap()[i * LS:(i + 1) * LS, :], ys.ap()[i * LS:(i + 1) * LS, :]) for i in range(NS)]
    seqs.append(("p", LP, xp.ap(), None))
    scale_q = 128 ** -0.5
    for (nm, L, xin, yout) in seqs:
        isP = yout is None
        hT = p.dram(f"hT_{nm}", [D, L], BF16)
        norm_T(xin, L, hT)
        done("norm")
        uT = p.dram(f"uT_{nm}", [1024, L], BF16)
        gat0 = p.dram(f"g0_{nm}", [L, 2048], F32)
        qrT = p.dram(f"qr_{nm}", [1024, L], F32)
        ffT = p.dram(f"ff_{nm}", [1024, L], F32)
        fbT = p.dram(f"fb_{nm}", [1024, L], F32)
        vtk = p.dram(f"vt_{nm}", [L, 1024], BF16)
        proj(hT, L, wb0i, 7168, [
            (0, 1024, "F", uT, 0, 1.0, BF16),
            (1024, 1024, "T", gat0, 0, 1.0, F32),
            (2048, 1024, "F", qrT, 0, 1.0, F32),
            (3072, 1024, "T", vtk, 0, 1.0, BF16),
            (4096, 1024, "F", ffT, 0, 1.0, F32),
            (5120, 1024, "F", fbT, 0, 1.0, F32),
            (6144, 1024, "T", gat0, 1024, 1.0, F32),
        ])
        done("proj0")
        ya = p.dram(f"ya_{nm}", [L, 1024], F32)
        fnet(uT, L, ya, (c128, cs256, tw_sb[L], cm_sb[L]))
        done("fnet")
        ymix = p.dram(f"ym_{nm}", [L, 2048], BF16)
        hgrn(qrT, ffT, fbT, vtk, gat0, L, ymix, lbt, hm, ghg_sb)
        done("hgrn")
        x1 = p.dram(f"x1_{nm}", [L, D], F32)
        h1T = p.dram(f"h1T_{nm}", [D, L], BF16)
        outproj(L, ymix, ya, gat0, wb0o, gpost_sb[0], xin, x1.ap(), h1T)
        done("out0")
        kT = p.dram(f"kT_{nm}", [2048, L], BF16)
        v1t = p.dram(f"v1_{nm}", [L, 2048], BF16)
        if not isP:
            Lq = L
            qT = p.dram(f"qT_{nm}", [2048, Lq], BF16)
            gat1 = p.dram(f"g1_{nm}", [Lq, 2048], F32)
            proj(h1T, L, wb1i, 8192, [
                (0, 2048, "F", qT, 0, scale_q, BF16),
                (2048, 2048, "F", kT, 0, 1.0, BF16),
                (4096, 2048, "T", v1t, 0, 1.0, BF16),
                (6144, 2048, "T", gat1, 0, 1.0, F32),
            ])
            xres1 = x1.ap()
            dtab = dtabs
        else:
            Lq = OWN
            proj(h1T, L, wb1i, 8192, [
                (2048, 2048, "F", kT, 0, 1.0, BF16),
                (4096, 2048, "T", v1t, 0, 1.0, BF16),
            ])
            x1own = p.dram("x1own", [OWN, D], F32)
            with contextlib.ExitStack() as es:
                gx = p.sb(es, "gx", [128, D], F32)
                prev = None
                for ti in range(OWN // 128):
                    p.pool.wait(prev)
                    ds = p.dsem("gath")
                    p.nc.gpsimd.indirect_dma_start(
                        out=gx[:], out_offset=None, in_=x1.ap(),
                        in_offset=bass.IndirectOffsetOnAxis(ap=ownidx[:, ti:ti + 1], axis=0),
                    ).then_inc(ds.sem, 16)
                    ds.n += 16
                    tok = (ds, ds.n)
                    p.pending.append(tok)
                    prev = p.st(x1own.ap()[ti * 128:(ti + 1) * 128, :], gx[:], "gaths", deps=[tok])
            p.barrier()
            h1To = p.dram("h1To", [D, OWN], BF16)
            norm_T(x1own.ap(), OWN, h1To)
            qT = p.dram(f"qT_{nm}", [2048, Lq], BF16)
            gat1 = p.dram(f"g1_{nm}", [Lq, 2048], F32)
            proj(h1To, OWN, wb1i, 8192, [
                (0, 2048, "F", qT, 0, scale_q, BF16),
                (6144, 2048, "T", gat1, 0, 1.0, F32),
            ])
            xres1 = x1own.ap()
            yout = yp.ap()
            dtab = dtabp
        done("proj1")
        og = p.dram(f"og_{nm}", [Lq, 2048], BF16)
        attention(qT, kT, v1t, gat1, Lq, L, og, dtab, lam_sb, gsub_sb, delta)
        done("attn")
        outproj(Lq, og, None, None, wb1o, gpost_sb[1], xres1, yout, None)
        done("seq0")


def dft_cs(n):
    j = np.arange(n)
    ang = 2 * np.pi * np.outer(j, j) / n
    return np.cos(ang), np.sin(ang)


def host_tables(cfg, core):
    NS, LS, LP, OWN = cfg["NS"], cfg["LS"], cfg["LP"], cfg["OWN"]
    t = {}
    t["ident"] = np.eye(128, dtype=np.float32).astype(NPBF)
    c, s = dft_cs(128)
    t["c128"] = np.stack([c, s, -s]).astype(np.float32).astype(NPBF)
    c, s = dft_cs(256)
    t["cs256"] = np.concatenate([c, -s], axis=1).astype(np.float32).astype(NPBF)
    for L in sorted({LS, LP}):
        M = L // 128
        ang = 2 * np.pi * np.outer(np.arange(128), np.arange(M)) / L
        t[f"tw{L}"] = np.stack([np.cos(ang), np.sin(ang), -np.sin(ang)]).astype(np.float32)
        cm, sm = dft_cs(M)
        sc = 1.0 / math.sqrt(L * 256)
        t[f"cm{L}"] = np.stack([cm * sc, sm * sc]).astype(np.float32).astype(NPBF)
    s_, t_ = np.meshgrid(np.arange(64), np.arange(64), indexing="ij")
    t["hmask"] = np.stack([(s_ <= t_), (s_ >= t_)]).astype(np.float32)
    t["delta"] = (np.arange(128)[:, None] - np.arange(256)[None, :]).astype(np.float32)
    kt, qb = np.meshgrid(np.arange(LS // 128), np.arange(LS // 256), indexing="ij")
    t["dtabs"] = np.broadcast_to((kt * 128 - qb * 256).reshape(1, -1), (128, kt.size)).astype(np.float32).copy()
    kt, qb = np.meshgrid(np.arange(LP // 128), np.arange(OWN // 256), indexing="ij")
    t["dtabp"] = np.broadcast_to((kt * 128 - (core * OWN + qb * 256)).reshape(1, -1), (128, kt.size)).astype(np.float32).copy()
    t["ownidx"] = (core * OWN + np.arange(OWN)).reshape(OWN // 128, 128).T.astype(np.int32).copy()
    return t


def run(inputs, cfg, ncores):
    NS, LS, LP, OWN = cfg["NS"], cfg["LS"], cfg["LP"], cfg["OWN"]
    f = lambda a: np.ascontiguousarray(np.asarray(a, dtype=np.float32))
    xsamp = f(inputs["x_sample"])
    shared = {
        "xp": f(inputs["x_prompt"])[0],
        "w0i": f(inputs["ev_w_in"])[0], "w0o": f(inputs["ev_w_out"])[0],
        "w1i": f(inputs["od_w_in"])[0], "w1o": f(inputs["od_w_out"])[0],
        "g0pre": f(inputs["ev_norm_pre"])[0], "g0post": f(inputs["ev_norm_post"])[0],
        "g1pre": f(inputs["od_norm_pre"])[0], "g1post": f(inputs["od_norm_post"])[0],
        "lbl": f(inputs["hgrn_lb_logits"]), "ghg": f(inputs["hgrn_norm"])[0],
        "lam4": np.stack([f(inputs["lambda_q1"])[0], f(inputs["lambda_k1"])[0],
                          f(inputs["lambda_q2"])[0], f(inputs["lambda_k2"])[0]]),
        "gsub": f(inputs["subln"])[0],
    }
    nc = build(cfg)
    in_maps = []
    for c in range(ncores):
        m = dict(shared)
        m["xs"] = xsamp[c * NS:(c + 1) * NS].reshape(NS * LS, D)
        m.update(host_tables(cfg, c))
        in_maps.append(m)
    res = run_bass_kernel_spmd(nc, in_maps, core_ids=list(range(ncores)))
    global LAST_RES
    LAST_RES = res.results
    y_s = np.concatenate([r["ys"].reshape(NS, LS, D) for r in res.results], axis=0)
    y_p = np.concatenate([r["yp"] for r in res.results], axis=0)[None]
    return y_p.astype(np.float32), y_s.astype(np.float32)


def kernel(**inputs):
    import os
    cfg = {"NS": 2, "LS": 2048, "LP": 8192, "OWN": 1024}
    if os.environ.get("K_STOP"):
        cfg["stop"] = os.environ["K_STOP"]
    return run(inputs, cfg, 8)
```

```python
import contextlib, math
import numpy as np
import ml_dtypes
import concourse.bass as bass
import concourse.mybir as mybir
from concourse.bass_utils import run_bass_kernel_spmd

F32, BF16, I32 = mybir.dt.float32, mybir.dt.bfloat16, mybir.dt.int32
AF = mybir.ActivationFunctionType
ALU = mybir.AluOpType
D = 2048
KC = 16
EPS = 1e-6
LAMBDA_INIT = 0.8 - 0.6 * math.exp(-0.3 * 1)
SLOPES = [2.0 ** (-8.0 * (h + 1) / 8) for h in range(8)]
NPBF = ml_dtypes.bfloat16


class StopBuild(Exception):
    pass


class Eng:
    def __init__(self, e, sem):
        self.e, self.sem, self.n, self.seen = e, sem, 0, {}

    def mark(self, ins):
        ins.then_inc(self.sem, 1)
        self.n += 1
        return (self, self.n)

    def wait(self, *toks):
        for tok in toks:
            if tok is None:
                continue
            src, n = tok
            if self.seen.get(src, 0) >= n:
                continue
            self.e.wait_ge(src.sem, n)
            self.seen[src] = n


class DSem:
    def __init__(self, sem):
        self.sem, self.n = sem, 0


class P:
    def __init__(self, cfg):
        self.cfg = cfg
        self.es = contextlib.ExitStack()
        nc = self.nc = bass.Bass("TRN2", target_bir_lowering=False)
        mk = lambda nm: self.es.enter_context(nc.semaphore(nm))
        self.pe = Eng(nc.tensor, mk("s_pe"))
        self.act = Eng(nc.scalar, mk("s_act"))
        self.dve = Eng(nc.vector, mk("s_dve"))
        self.pool = Eng(nc.gpsimd, mk("s_pool"))
        self.sp = Eng(nc.sync, mk("s_sp"))
        self.engs = [self.pe, self.act, self.dve, self.pool, self.sp]
        self.dsems = {}
        self.pending = []
        self.ndram = 0
        self.ps = [self.es.enter_context(nc.psum_tensor(f"ps{i}", [128, 512], F32)) for i in range(6)]
        self.pb = [self.es.enter_context(nc.psum_tensor(f"pb{i}", [128, 1024], BF16)) for i in range(2)]
        self.dummy = self.es.enter_context(nc.sbuf_tensor("dummy_sb", [128, 2], F32))
        self.pool.mark(nc.gpsimd.memset(self.dummy[:], 0.0))

    def sb(self, es, name, shape, dt):
        self.nsb = getattr(self, "nsb", 0) + 1
        return es.enter_context(self.nc.sbuf_tensor(f"{name}_{self.nsb}", shape, dt))

    def dram(self, name, shape, dt, kind="Internal"):
        t = self.nc.dram_tensor(name, list(shape), dt, kind=kind)
        if not hasattr(self, "named"):
            self.named = {}
        self.named[name] = (t, list(shape), dt)
        return t

    def dump(self):
        self.barrier()
        for name in self.cfg.get("dump", []):
            if name not in self.named:
                continue
            t, shape, dt = self.named[name]
            o = self.nc.dram_tensor("dbg_" + name, shape, dt, kind="ExternalOutput")
            self.ld(o.ap(), t.ap(), "dump")
        self.barrier()

    def dsem(self, key):
        if key not in self.dsems:
            self.dsems[key] = DSem(self.es.enter_context(self.nc.semaphore("d_" + key)))
        return self.dsems[key]

    def dma(self, q, out, in_, key, deps=(), slow=False):
        q.wait(*deps)
        ds = self.dsem(key)
        kw = {"allow_slow_non_contiguous": True} if slow else {}
        q.e.dma_start(out=out, in_=in_, **kw).then_inc(ds.sem, 16)
        ds.n += 16
        tok = (ds, ds.n)
        self.pending.append(tok)
        return tok

    def ld(self, out, in_, key, deps=(), slow=False):
        return self.dma(self.sp, out, in_, key, deps, slow)

    def st(self, out, in_, key, deps=()):
        return self.dma(self.pool, out, in_, key, deps)

    def barrier(self):
        toks = [(e, e.n) for e in self.engs if e.n > 0] + self.pending
        for e in self.engs:
            e.wait(*toks)
        self.pending = []

    def A(self, deps, *a, **k):
        self.act.wait(*deps)
        tok = self.act.mark(self.act.e.activation(*a, **k))
        if k.get("accum_out") is not None:
            tok = self.act.mark(self.act.e.activation(out=self.dummy[:, 1:2], in_=self.dummy[:, 0:1], func=AF.Copy))
        return tok

    def V(self, deps, fn, *a, **k):
        self.dve.wait(*deps)
        return self.dve.mark(getattr(self.dve.e, fn)(*a, **k))

    def G(self, deps, fn, *a, **k):
        self.pool.wait(*deps)
        return self.pool.mark(getattr(self.pool.e, fn)(*a, **k))

    def X(self, eng, deps, fn, *a, **k):
        eng.wait(*deps)
        return eng.mark(getattr(eng.e, fn)(*a, **k))

    def rsqrt(self, deps, out, in_, mul, add):
        a = self.A(deps, out=out, in_=in_, func=AF.Ln, scale=float(mul), bias=float(add))
        return self.A([a], out=out, in_=out, func=AF.Exp, scale=-0.5)

    def mm(self, deps, out, lhsT, rhs, start, stop, mark=False):
        self.pe.wait(*deps)
        ins = self.pe.e.matmul(out, lhsT, rhs, start=start, stop=stop)
        return self.pe.mark(ins) if mark else None

    def tr(self, deps, out, in_, ident, mark=False):
        self.pe.wait(*deps)
        ins = self.pe.e.transpose(out, in_, ident)
        return self.pe.mark(ins) if mark else None


def bcast_rows(ap_dram_1d, n, parts=128):
    return bass.AP(ap_dram_1d.tensor, ap_dram_1d.offset, [[0, parts], [1, n]])


def build(cfg):
    p = P(cfg)
    try:
        _build(cfg, p)
    except StopBuild:
        p.dump()
        return p.nc
    p.dump()
    p.es.close()
    return p.nc


def _build(cfg, p):
    NS, LS, LP, OWN = cfg["NS"], cfg["LS"], cfg["LP"], cfg["OWN"]

    def done(tag):
        if cfg.get("stop") == tag:
            raise StopBuild()
    nc = p.nc
    inp = lambda name, shape, dt=F32: nc.dram_tensor(name, list(shape), dt, kind="ExternalInput")
    xs = inp("xs", [NS * LS, D])
    xp = inp("xp", [LP, D])
    w0i, w0o = inp("w0i", [D, 7168]), inp("w0o", [D, D])
    w1i, w1o = inp("w1i", [D, 8192]), inp("w1o", [D, D])
    g0pre, g0post = inp("g0pre", [D]), inp("g0post", [D])
    g1pre, g1post = inp("g1pre", [D]), inp("g1post", [D])
    lbl = inp("lbl", [2, 3, 1024])
    ghg = inp("ghg", [128])
    lam4 = inp("lam4", [4, 128])
    gsub = inp("gsub", [256])
    ident_d = inp("ident", [128, 128], BF16)
    c128_d = inp("c128", [3, 128, 128], BF16)
    cs256_d = inp("cs256", [256, 512], BF16)
    tw_d = {L: inp(f"tw{L}", [3, 128, L // 128]) for L in sorted({LS, LP})}
    cm_d = {L: inp(f"cm{L}", [2, L // 128, L // 128], BF16) for L in sorted({LS, LP})}
    hmask_d = inp("hmask", [2, 64, 64])
    delta_d = inp("delta", [128, 256])
    dtabs_d = inp("dtabs", [128, (LS // 128) * (LS // 256)])
    dtabp_d = inp("dtabp", [128, (LP // 128) * (OWN // 256)])
    ownidx_d = inp("ownidx", [128, OWN // 128], I32)
    ys = nc.dram_tensor("ys", [NS * LS, D], F32, kind="ExternalOutput")
    yp = nc.dram_tensor("yp", [OWN, D], F32, kind="ExternalOutput")

    ges = p.es
    ident = p.sb(ges, "ident_sb", [128, 128], BF16)
    toks = [p.ld(ident[:], ident_d.ap(), "c0")]
    gpost_sb = [p.sb(ges, f"gpost{i}", [128, D], F32) for i in range(2)]
    toks.append(p.ld(gpost_sb[0][:], bcast_rows(g0post.ap(), D), "c0"))
    toks.append(p.ld(gpost_sb[1][:], bcast_rows(g1post.ap(), D), "c0"))
    gpre_sb = [p.sb(ges, f"gpre{i}", [128, KC], F32) for i in range(2)]
    toks.append(p.ld(gpre_sb[0][:], g0pre.ap().rearrange("(c p) -> p c", p=128), "c0", slow=True))
    toks.append(p.ld(gpre_sb[1][:], g1pre.ap().rearrange("(c p) -> p c", p=128), "c0", slow=True))
    p.barrier()

    def prep_w(w, ncols, gcol, name):
        wb = p.dram(name, [D, ncols], BF16)
        with contextlib.ExitStack() as es:
            CW = 1024
            wf = [p.sb(es, f"wf{i}", [128, CW], F32) for i in range(2)]
            wo = [p.sb(es, f"wo{i}", [128, CW], BF16) for i in range(2)]
            cons = [None, None]
            sts = [None, None]
            i = 0
            for kc in range(KC):
                for c0 in range(0, ncols, CW):
                    b = i % 2
                    t = p.ld(wf[b][:], w.ap()[kc * 128:(kc + 1) * 128, c0:c0 + CW], f"wl{b}", deps=[cons[b]])
                    eng = p.dve if b == 0 else p.pool
                    if gcol is None:
                        cons[b] = p.X(eng, [t, sts[b]], "tensor_copy", wo[b][:], wf[b][:])
                    else:
                        cons[b] = p.X(eng, [t, sts[b]], "tensor_scalar", wo[b][:], wf[b][:],
                                      gcol[:, kc:kc + 1], None, ALU.mult)
                    sts[b] = p.dma(p.act, wb.ap()[kc * 128:(kc + 1) * 128, c0:c0 + CW], wo[b][:], f"ws{b}",
                                   deps=[cons[b]])
                    i += 1
        p.barrier()
        return wb

    wb0i = prep_w(w0i, 7168, gpre_sb[0], "wb0i")
    wb0o = prep_w(w0o, D, None, "wb0o")
    wb1i = prep_w(w1i, 8192, gpre_sb[1], "wb1i")
    wb1o = prep_w(w1o, D, None, "wb1o")
    done("prep")

    def norm_T(x_rows, L, hT):
        with contextlib.ExitStack() as es:
            xt = [p.sb(es, f"nx{i}", [128, D], F32) for i in range(2)]
            junk = p.sb(es, "njunk", [128, D], BF16)
            hb = [p.sb(es, f"nhb{i}", [128, D], BF16) for i in range(2)]
            ho = [p.sb(es, f"nho{i}", [128, KC, 128], BF16) for i in range(2)]
            st_ = [p.sb(es, f"nst{i}", [128, 2], F32) for i in range(2)]
            rd = [None, None]
            hbr = [None, None]
            hor = [None, None]
            for ti in range(L // 128):
                b = ti % 2
                t = p.ld(xt[b][:], x_rows[ti * 128:(ti + 1) * 128, :], f"nl{b}", deps=[rd[b]])
                a1 = p.A([t], out=junk[:], in_=xt[b][:], func=AF.Square, accum_out=st_[b][:, 0:1])
                v2 = p.rsqrt([a1], st_[b][:, 1:2], st_[b][:, 0:1], 1.0 / D, EPS)
                v3 = p.V([v2, t, hbr[b]], "tensor_scalar", hb[b][:], xt[b][:], st_[b][:, 1:2], None, ALU.mult)
                rd[b] = v3
                for kc in range(KC):
                    tk = p.tr([v3, hor[b]] if kc == 0 else [], p.pb[kc // 8][:, (kc % 8) * 128:(kc % 8 + 1) * 128],
                              hb[b][:, kc * 128:(kc + 1) * 128], ident[:], mark=(kc == KC - 1))
                hbr[b] = tk
                c1 = p.A([tk, hor[b]], out=ho[b][:, 0:8, :], in_=p.pb[0][:].rearrange("p (k t) -> p k t", k=8),
                         func=AF.Copy)
                c2 = p.V([tk, hor[b]], "tensor_copy", ho[b][:, 8:16, :],
                         p.pb[1][:].rearrange("p (k t) -> p k t", k=8))
                p.pe.wait(c1, c2)
                hor[b] = p.st(hT.ap().rearrange("(k p) t -> p k t", p=128)[:, :, ti * 128:(ti + 1) * 128],
                              ho[b][:], f"ns{b}", deps=[c1, c2])
        p.barrier()

    def proj(hT, L, wb, ncols_total, jobs):
        TB = min(L, 1024)
        with contextlib.ExitStack() as es:
            hblk = p.sb(es, "pj_h", [128, KC, TB], BF16)
            wblk = [p.sb(es, f"pj_w{i}", [128, KC, 512], BF16) for i in range(2)]
            osb = {F32: [p.sb(es, f"pj_of{i}", [128, 512], F32) for i in range(2)],
                   BF16: [p.sb(es, f"pj_ob{i}", [128, 512], BF16) for i in range(2)]}
            wread = [None, None]
            ost = {F32: [None, None], BF16: [None, None]}
            psr = [None] * 4
            wi = 0
            oi = 0
            pi = 0
            hread = None
            for tb in range(L // TB):
                th = p.ld(hblk[:], hT.ap().rearrange("(k p) t -> p k t", p=128)[:, :, tb * TB:(tb + 1) * TB],
                          "pjh", deps=[hread])
                cbs = [(j, c) for j in jobs for c in range(0, j[1], 512)]
                for (job, c) in cbs:
                    col0, ncols, mode, od, o0, scale, odt = job
                    b = wi % 2
                    wi += 1
                    tw = p.ld(wblk[b][:], wb.ap().rearrange("(k p) n -> p k n", p=128)[:, :, col0 + c:col0 + c + 512],
                              f"pjw{b}", deps=[wread[b]])
                    last = None
                    TW = min(512, TB)
                    if mode == "F":
                        subs = [(s4, t5) for s4 in range(4) for t5 in range(TB // TW)]
                    else:
                        subs = [(s4, 0) for s4 in range(TB // 128)]
                    for (s4, t5) in subs:
                        pk = pi % 4
                        pi += 1
                        W_ = TW if mode == "F" else 512
                        ps = p.ps[pk][:, 0:W_]
                        for kc in range(KC):
                            if mode == "F":
                                lhsT, rhs = wblk[b][:, kc, s4 * 128:(s4 + 1) * 128], hblk[:, kc, t5 * TW:(t5 + 1) * TW]
                            else:
                                lhsT, rhs = hblk[:, kc, s4 * 128:(s4 + 1) * 128], wblk[b][:, kc, :]
                            tk = p.mm([th, tw, psr[pk]] if kc == 0 else [], ps, lhsT, rhs, kc == 0, kc == KC - 1,
                                      mark=(kc == KC - 1))
                        last = tk
                        ob = oi % 2
                        oi += 1
                        o = osb[odt][ob][:, 0:W_]
                        if oi % 2 == 0:
                            ev = p.A([tk, ost[odt][ob]], out=o, in_=ps, func=AF.Copy, scale=float(scale))
                        else:
                            ev = p.V([tk, ost[odt][ob]], "tensor_scalar", o, ps, float(scale), None, ALU.mult)
                        psr[pk] = ev
                        if mode == "F":
                            dst = od.ap()[o0 + c + s4 * 128:o0 + c + (s4 + 1) * 128,
                                          tb * TB + t5 * TW:tb * TB + (t5 + 1) * TW]
                        else:
                            dst = od.ap()[tb * TB + s4 * 128:tb * TB + (s4 + 1) * 128, o0 + c:o0 + c + 512]
                        ost[odt][ob] = p.st(dst, o, f"pjs{ob}{'f' if odt == F32 else 'b'}", deps=[ev])
                    wread[b] = last
                    hread = last
        p.barrier()

    def fnet(uT, L, ya, tabs):
        M = L // 128
        c128, cs256, tw, cm = tabs
        Bd = p.dram(f"fn_B{p.ndram}", [128, M, 512], BF16)
        p.ndram += 1
        CP = 32
        with contextlib.ExitStack() as es:
            ug = p.sb(es, "fn_u", [128, 2, L], BF16)
            MB = min(M, 32)
            V = p.sb(es, "fn_V", [128, MB, 512], BF16)
            Bs = p.sb(es, "fn_Bs", [128, MB, 512], BF16)
            tmp = [p.sb(es, f"fn_t{i}", [128, 2, 256], F32) for i in range(2)]
            Bt = p.sb(es, "fn_Bt", [M, CP, 512], BF16)
            Y = [p.sb(es, f"fn_Y{i}", [M, 2, 256], F32) for i in range(2)]
            for g in range(4):
                tu = p.ld(ug[:], uT.ap()[g * 256:(g + 1) * 256, :].rearrange("(c p) t -> p c t", p=128), "fnu")
                ts_all = []
                for bh in range(M // MB):
                    evs = []
                    prev = [None, None]
                    for bl in range(MB):
                        b = bh * MB + bl
                        pk = b % 2
                        for ch in range(2):
                            lhsT = bass.AP(ug, ch * L + b, [[2 * L, 128], [M, 128]])
                            tk = p.mm([tu, prev[pk]] if ch == 0 else [], p.ps[pk][:], lhsT, cs256[:, ch, :],
                                      ch == 0, ch == 1, mark=(ch == 1))
                        if b % 2 == 0:
                            ev = p.A([tk], out=V[:, bl, :], in_=p.ps[pk][:], func=AF.Copy)
                        else:
                            ev = p.V([tk], "tensor_copy", V[:, bl, :], p.ps[pk][:])
                        prev[pk] = ev
                        evs.append(ev)
                    prevr = [None, None]
                    tw_tok = []
                    for bp in range(MB // 2):
                        pk = 2 + (bp % 2) * 2
                        Ar, Ai = p.ps[pk], p.ps[pk + 1]
                        b0 = 2 * bp
                        dep = [evs[b0], evs[b0 + 1], prevr[bp % 2]]
                        ar3 = Ar[:].rearrange("p (b f) -> p b f", b=2)
                        ai3 = Ai[:].rearrange("p (b f) -> p b f", b=2)
                        p.mm(dep, ar3, c128[:, 0, :], V[:, b0:b0 + 2, 0:256], True, False)
                        p.mm([], ar3, c128[:, 1, :], V[:, b0:b0 + 2, 256:512], False, True)
                        p.mm([], ai3, c128[:, 0, :], V[:, b0:b0 + 2, 256:512], True, False)
                        tk = p.mm([], ai3, c128[:, 2, :], V[:, b0:b0 + 2, 0:256], False, True, mark=True)
                        last = []
                        for j in range(2):
                            bl = b0 + j
                            b = bh * MB + bl
                            t1 = p.V([tk], "tensor_scalar", tmp[0][:, j, :], ar3[:, j, :], tw[:, 0, b:b + 1], None, ALU.mult)
                            t3 = p.V([tk], "tensor_scalar", tmp[1][:, j, :], ai3[:, j, :], tw[:, 0, b:b + 1], None, ALU.mult)
                            r1 = p.V([t1, tk], "scalar_tensor_tensor", Bs[:, bl, 0:256], ai3[:, j, :], tw[:, 1, b:b + 1],
                                     tmp[0][:, j, :], ALU.mult, ALU.add)
                            r2 = p.V([t3, tk], "scalar_tensor_tensor", Bs[:, bl, 256:512], ar3[:, j, :], tw[:, 2, b:b + 1],
                                     tmp[1][:, j, :], ALU.mult, ALU.add)
                            last = [r1, r2]
                        p.act.wait(*last)
                        prevr[bp % 2] = last[1]
                        tw_tok = last
                    ts_all.append(p.st(Bd.ap()[:, bh * MB:(bh + 1) * MB, :], Bs[:], "fnb", deps=tw_tok))
                    p.barrier()
                if True:
                    ts_ = ts_all[-1]
                    yst = [None, None]
                    evp = [None, None]
                    bt_read = None
                    for cp in range(128 // CP):
                        tl = p.ld(Bt[:], Bd.ap()[cp * CP:(cp + 1) * CP, :, :].rearrange("c b f -> b c f"), "fnbt",
                                  deps=[ts_, bt_read])
                        for c2 in range(CP // 2):
                            pk = c2 % 2
                            ps3 = p.ps[pk][0:M, :].rearrange("p (c f) -> p c f", c=2)
                            p.mm([tl, evp[pk]], ps3, cm[0:M, 0, :], Bt[:, 2 * c2:2 * c2 + 2, 0:256], True, False)
                            tk = p.mm([], ps3, cm[0:M, 1, :], Bt[:, 2 * c2:2 * c2 + 2, 256:512], False, True, mark=True)
                            if c2 % 2 == 0:
                                ev = p.A([tk, yst[pk]], out=Y[pk][:], in_=ps3, func=AF.Copy)
                            else:
                                ev = p.V([tk, yst[pk]], "tensor_copy", Y[pk][:], ps3)
                            c_abs = cp * CP + 2 * c2
                            dst = ya.ap().rearrange("(d c) f -> d c f", c=128)[:, c_abs:c_abs + 2, g * 256:(g + 1) * 256]
                            yst[pk] = p.st(dst, Y[pk][:], f"fny{pk}", deps=[ev])
                            evp[pk] = ev
                            bt_read = tk
                    p.barrier()
        p.barrier()

    def hgrn(qrT, ffT, fbT, vtok, gates, L, ymix, lbt, hm, ghg_sb):
        SEG = min(L, 2048)
        NCH = SEG // 64
        nseg = L // SEG
        NT = L // 64
        dS = p.dram(f"hg_dS{p.ndram}", [2, NT, 128, 128], F32)
        Sb = p.dram(f"hg_Sb{p.ndram}", [2, NT, 128, 128], BF16)
        qd = p.dram(f"hg_qd{p.ndram}", [2, 128, L], BF16)
        scd = p.dram(f"hg_sc{p.ndram}", [64, NT, 64], BF16)
        eld = p.dram(f"hg_el{p.ndram}", [2, 128, NT], F32)
        p.ndram += 1
        for h in range(8):
            with contextlib.ExitStack() as es:
                qr = p.sb(es, "h_qr", [128, SEG], F32)
                fr = p.sb(es, "h_fr", [128, SEG], F32)
                f_ = p.sb(es, "h_f", [128, SEG], F32)
                g_ = p.sb(es, "h_g", [128, SEG], F32)
                k_ = p.sb(es, "h_k", [128, SEG], F32)
                cum = p.sb(es, "h_cum", [128, SEG], F32)
                cb = p.sb(es, "h_cb", [128, SEG], F32)
                ex = p.sb(es, "h_ex", [128, SEG], F32)
                ex2 = p.sb(es, "h_ex2", [128, SEG], F32)
                ex3 = p.sb(es, "h_ex3", [128, SEG], F32)
                rmask = p.sb(es, "h_rm", [128, SEG], F32)
                qdec = [p.sb(es, f"h_qd{i}", [128, SEG], BF16) for i in range(2)]
                kdec = p.sb(es, "h_kd", [128, SEG], BF16)
                kend = p.sb(es, "h_ke", [128, SEG], BF16)
                kendT = p.sb(es, "h_keT", [64, NCH, 128], BF16)
                vt = p.sb(es, "h_v", [64, NCH, 128], BF16)
                sct = p.sb(es, "h_sc", [64, NCH, 64], BF16)
                sc1 = p.sb(es, "h_sc1", [64, NCH, 64], F32)
                el = p.sb(es, "h_el", [128, 2, NCH], F32)
                dSs = [[p.sb(es, f"h_dS{i}{j}", [128, 4, 128], F32) for j in range(2)] for i in range(2)]
                hz = {}
                m1 = p.G([], "memset", rmask[:], 1.0)
                m2 = p.G([], "memset", rmask[:].rearrange("p (n s) -> p n s", s=64)[:, :, 0:1], 0.0)
                p.barrier()
                for sg in range(nseg):
                    t0 = sg * SEG
                    tq = p.ld(qr[:], qrT.ap()[h * 128:(h + 1) * 128, t0:t0 + SEG], "hq")
                    tv = p.ld(vt[:], vtok.ap()[t0:t0 + SEG, h * 128:(h + 1) * 128].rearrange("(n s) v -> s n v", s=64),
                              "hv")
                    aq = p.A([tq], out=qr[:], in_=qr[:], func=AF.Silu)
                    for d in range(2):
                        src = ffT if d == 0 else fbT
                        tf = p.ld(fr[:], src.ap()[h * 128:(h + 1) * 128, t0:t0 + SEG], "hf")
                        a1 = p.A([tf], out=f_[:], in_=fr[:], func=AF.Sigmoid)
                        v1 = p.V([a1], "tensor_scalar", f_[:], f_[:], lbt[:, 2 + d, h:h + 1], lbt[:, d, h:h + 1],
                                 ALU.mult, ALU.add)
                        a2 = p.A([v1], out=g_[:], in_=f_[:], func=AF.Ln)
                        g1 = p.G([v1], "tensor_scalar", k_[:], f_[:], -1.0, 1.0, ALU.mult, ALU.add)
                        v2 = p.V([a2], "tensor_tensor_scan", cum[:], rmask[:], g_[:], 0.0, ALU.mult, ALU.add)
                        cum3 = cum[:].rearrange("p (n s) -> p n s", s=64)
                        lastb = bass.AP(cum, 63, [[SEG, 128], [64, NCH], [0, 64]])
                        if d == 0:
                            cc = cum
                            v3 = p.V([v2], "tensor_tensor", cb[:].rearrange("p (n s) -> p n s", s=64), lastb, cum3,
                                     ALU.subtract)
                            dl = cb
                        else:
                            v3a = p.V([v2], "tensor_tensor", cb[:].rearrange("p (n s) -> p n s", s=64), lastb, cum3,
                                      ALU.subtract)
                            v3b = p.V([v3a], "tensor_tensor", cb[:], cb[:], g_[:], ALU.add)
                            cc = cb
                            v3 = p.V([v3b], "tensor_tensor", g_[:], cum[:], g_[:], ALU.subtract)
                            dl = g_
                        a3 = p.A([v3], out=ex[:], in_=cc[:], func=AF.Exp)
                        a4 = p.A([v3], out=ex2[:], in_=cc[:], func=AF.Exp, scale=-1.0)
                        a5 = p.A([v3], out=ex3[:], in_=dl[:], func=AF.Exp)
                        g2 = p.V([a3, aq], "tensor_tensor", qdec[d][:], qr[:], ex[:], ALU.mult)
                        g3 = p.G([a4, g1], "tensor_tensor", kdec[:], k_[:], ex2[:], ALU.mult)
                        g4 = p.V([a5, g1], "tensor_tensor", kend[:], k_[:], ex3[:], ALU.mult)
                        a6 = p.A([v2], out=el[:, d, :], in_=cum3[:, :, 63], func=AF.Exp)
                        sq = p.st(qd.ap()[d, :, t0:t0 + SEG], qdec[d][:], "hsq", deps=[g2])
                        se = p.st(eld.ap()[d, :, sg * NCH:(sg + 1) * NCH], el[:, d, :], "hse", deps=[a6])
                        nb = min(8, NCH)
                        NGR = NCH // nb
                        mk_ap = bass.AP(hm, d * 64, [[128, 64], [0, nb], [1, 64]])
                        for gq in range(NGR + 1):
                            if gq < NGR:
                                n0 = gq * nb
                                par = gq % 2
                                key = ("sc", par)
                                for j in range(nb):
                                    sl = slice((n0 + j) * 64, (n0 + j + 1) * 64)
                                    tk = p.mm([g2, g3, hz.get(key)] if j == 0 else [], p.ps[par][0:64, j * 64:(j + 1) * 64],
                                              kdec[:, sl], qdec[d][:, sl], True, True, mark=(j == nb - 1))
                                psv = p.ps[par][0:64, 0:nb * 64].rearrange("p (n s) -> p n s", s=64)
                                if d == 0:
                                    ev = p.V([tk], "tensor_tensor", sc1[:, n0:n0 + nb, :], psv, mk_ap, ALU.mult)
                                else:
                                    ev0 = p.V([tk], "tensor_tensor", sct[:, n0:n0 + nb, :], psv, mk_ap, ALU.mult)
                                    ev = p.V([ev0], "tensor_tensor", sct[:, n0:n0 + nb, :], sct[:, n0:n0 + nb, :],
                                             sc1[:, n0:n0 + nb, :], ALU.add)
                                hz[key] = ev
                                key = ("tr", par)
                                for j in range(nb):
                                    sl = slice((n0 + j) * 64, (n0 + j + 1) * 64)
                                    tk2 = p.tr([g4, hz.get(key)] if j == 0 else [], p.pb[par][0:64, j * 128:(j + 1) * 128],
                                               kend[:, sl], ident[:], mark=(j == nb - 1))
                                ev2 = p.A([tk2, hz.get(("ds_mm", par))], out=kendT[:, n0:n0 + nb, :],
                                          in_=p.pb[par][0:64, 0:nb * 128].rearrange("p (n k) -> p n k", k=128), func=AF.Copy)
                                hz[key] = ev2
                                hz[("kT", par)] = ev2
                            if gq >= 1:
                                gprev = gq - 1
                                n0 = gprev * nb
                                par = gprev % 2
                                nbk = (nb + 3) // 4
                                for j in range(nb):
                                    bank = p.ps[2 + 2 * par + j // 4]
                                    tk3 = p.mm([hz[("kT", par)], tv, hz.get(("dsb", par, j // 4))] if j % 4 == 0 else [],
                                               bank[:, (j % 4) * 128:(j % 4 + 1) * 128], kendT[:, n0 + j, :], vt[:, n0 + j, :],
                                               True, True, mark=(j % 4 == 3 or j == nb - 1))
                                    if j % 4 == 3 or j == nb - 1:
                                        i4 = j // 4
                                        w4 = j % 4 + 1
                                        dst = dSs[par][i4][:, 0:w4, :]
                                        src = bank[:, 0:w4 * 128].rearrange("p (n v) -> p n v", v=128)
                                        if i4 == 0:
                                            ev3 = p.A([tk3, hz.get(("dss", par, i4))], out=dst, in_=src, func=AF.Copy)
                                        else:
                                            ev3 = p.V([tk3, hz.get(("dss", par, i4))], "tensor_copy", dst, src)
                                        hz[("dsb", par, i4)] = ev3
                                        c0 = sg * NCH + n0 + i4 * 4
                                        hz[("dss", par, i4)] = p.st(
                                            dS.ap()[d, c0:c0 + w4, :, :].rearrange("n p v -> p n v"), dst, f"hds{par}{i4}",
                                            deps=[ev3])
                                hz[("ds_mm", par)] = tk3
                        if d == 1:
                            p.st(scd.ap()[:, sg * NCH:(sg + 1) * NCH, :], sct[:], "hsc", deps=[ev])
                        p.barrier()
                        hz.clear()
            with contextlib.ExitStack() as es:
                G = min(NT, 32)
                NG = NT // G
                dsl = [p.sb(es, f"h2_ds{i}", [128, G, 128], F32) for i in range(2)]
                sall = [p.sb(es, f"h2_sa{i}", [128, G + 1, 128], F32) for i in range(2)]
                sbo = [p.sb(es, f"h2_sb{i}", [128, G, 128], BF16) for i in range(2)]
                ela = p.sb(es, "h2_el", [128, 2, NT], F32)
                te = p.ld(ela[:], eld.ap().rearrange("d p n -> p d n"), "h2e")
                z0 = p.V([], "memset", sall[0][:, 0, :], 0.0)
                z1 = p.V([], "memset", sall[1][:, G, :], 0.0)
                last = [z0, z1]
                for gi in range(NG):
                    gf, gb = gi, NG - 1 - gi
                    tl = [p.ld(dsl[0][:], dS.ap()[0, gf * G:(gf + 1) * G, :, :].rearrange("n p v -> p n v"), "h2l0"),
                          p.ld(dsl[1][:], dS.ap()[1, gb * G:(gb + 1) * G, :, :].rearrange("n p v -> p n v"), "h2l1")]
                    for i in range(G):
                        nf = gf * G + i
                        last[0] = p.V([tl[0], te, last[0]], "scalar_tensor_tensor", sall[0][:, i + 1, :], sall[0][:, i, :],
                                      ela[:, 0, nf:nf + 1], dsl[0][:, i, :], ALU.mult, ALU.add)
                        j = G - 1 - i
                        nbk = gb * G + j
                        last[1] = p.V([tl[1], te, last[1]], "scalar_tensor_tensor", sall[1][:, j, :], sall[1][:, j + 1, :],
                                      ela[:, 1, nbk:nbk + 1], dsl[1][:, j, :], ALU.mult, ALU.add)
                    c0 = p.G(last, "tensor_copy", sbo[0][:], sall[0][:, 0:G, :])
                    c1 = p.A(last, out=sbo[1][:], in_=sall[1][:, 1:G + 1, :], func=AF.Copy)
                    p.st(Sb.ap()[0, gf * G:(gf + 1) * G, :, :].rearrange("n p v -> p n v"), sbo[0][:], "h2s0", deps=[c0])
                    p.st(Sb.ap()[1, gb * G:(gb + 1) * G, :, :].rearrange("n p v -> p n v"), sbo[1][:], "h2s1", deps=[c1])
                    last[0] = p.V([c0, c1] + last, "tensor_copy", sall[0][:, 0, :], sall[0][:, G, :])
                    last[1] = p.V([last[0]], "tensor_copy", sall[1][:, G, :], sall[1][:, 0, :])
                    p.barrier()
            with contextlib.ExitStack() as es:
                G = min(NT, 32)
                qd3 = p.sb(es, "h3_qd", [128, 2, G * 64], BF16)
                sc3 = p.sb(es, "h3_sc", [64, G, 64], BF16)
                v3_ = p.sb(es, "h3_v", [64, G, 128], BF16)
                gt3 = p.sb(es, "h3_g", [64, G, 128], F32)
                s3 = p.sb(es, "h3_s", [128, 2, G, 128], BF16)
                o3 = p.sb(es, "h3_o", [64, G, 128], F32)
                ob3 = p.sb(es, "h3_ob", [64, G, 128], BF16)
                sq3 = p.sb(es, "h3_sq", [64, G, 128], F32)
                ss = p.sb(es, "h3_ss", [64, G], F32)
                for gi in range(NT // G):
                    t0 = gi * G * 64
                    tl = [p.ld(qd3[:], qd.ap()[:, :, t0:t0 + G * 64].rearrange("d p t -> p d t"), "h3a"),
                          p.ld(sc3[:], scd.ap()[:, gi * G:(gi + 1) * G, :], "h3a"),
                          p.ld(v3_[:], vtok.ap()[t0:t0 + G * 64, h * 128:(h + 1) * 128].rearrange("(n s) v -> s n v", s=64), "h3a"),
                          p.ld(gt3[:], gates.ap()[t0:t0 + G * 64, 1024 + h * 128:1024 + (h + 1) * 128].rearrange("(n s) v -> s n v", s=64), "h3a"),
                          p.ld(s3[:, 0], Sb.ap()[0, gi * G:(gi + 1) * G, :, :].rearrange("n p v -> p n v"), "h3a"),
                          p.ld(s3[:, 1], Sb.ap()[1, gi * G:(gi + 1) * G, :, :].rearrange("n p v -> p n v"), "h3a")]
                    ag = p.A([tl[3]], out=gt3[:], in_=gt3[:], func=AF.Silu)
                    prev = [None, None]
                    nb3 = min(4, G)
                    for j0 in range(0, G, nb3):
                        pk = (j0 // nb3) % 2
                        for jj in range(nb3):
                            j = j0 + jj
                            ps = p.ps[pk][0:64, jj * 128:(jj + 1) * 128]
                            sl = slice(j * 64, (j + 1) * 64)
                            p.mm(tl + [prev[pk]] if jj == 0 else [], ps, sc3[:, j, :], v3_[:, j, :], True, False)
                            p.mm([], ps, qd3[:, 0, sl], s3[:, 0, j, :], False, False)
                            tk = p.mm([], ps, qd3[:, 1, sl], s3[:, 1, j, :], False, True, mark=(jj == nb3 - 1))
                        ev = p.V([tk], "tensor_copy", o3[:, j0:j0 + nb3, :],
                                 p.ps[pk][0:64, 0:nb3 * 128].rearrange("p (n v) -> p n v", v=128))
                        prev[pk] = ev
                    e2 = p.A([ev], out=sq3[:], in_=o3[:], func=AF.Square)
                    e3 = p.V([e2], "tensor_reduce", ss[:], sq3[:], mybir.AxisListType.X, ALU.add)
                    r2 = p.rsqrt([e3], ss[:], ss[:], 1.0 / 128, EPS)
                    ssb = bass.AP(ss, 0, [[G, 64], [1, G], [0, 128]])
                    r3 = p.V([r2], "tensor_tensor", o3[:], o3[:], ssb, ALU.mult)
                    ghb = bass.AP(ghg_sb, 0, [[128, 64], [0, G], [1, 128]])
                    r4 = p.V([r3], "tensor_tensor", o3[:], o3[:], ghb, ALU.mult)
                    r5 = p.V([r4, ag], "tensor_tensor", ob3[:], o3[:], gt3[:], ALU.mult)
                    p.st(ymix.ap()[t0:t0 + G * 64, 1024 + h * 128:1024 + (h + 1) * 128].rearrange("(n s) v -> s n v", s=64),
                         ob3[:], "h3s", deps=[r5])
                    p.barrier()
        p.barrier()

    def outproj(L, ymix, ya, gates, wbo, gpost, xres, xout, hT_next):
        with contextlib.ExitStack() as es:
            wsb = p.sb(es, "op_w", [128, KC, D], BF16)
            tw = p.ld(wsb[:], wbo.ap().rearrange("(k p) n -> p k n", p=128), "opw")
            ym = [p.sb(es, f"op_ym{i}", [128, D], BF16) for i in range(2)]
            yaf = p.sb(es, "op_ya", [128, 1024], F32) if ya is not None else None
            gaf = p.sb(es, "op_ga", [128, 1024], F32) if ya is not None else None
            ymT = p.sb(es, "op_ymT", [128, KC, 128], BF16)
            xr = [p.sb(es, f"op_x{i}", [128, D], F32) for i in range(2)]
            yo = p.sb(es, "op_y", [128, D], F32)
            junk = p.sb(es, "op_j", [128, D], BF16)
            st_ = p.sb(es, "op_st", [128, 4], F32)
            hb = p.sb(es, "op_hb", [128, D], BF16)
            ho = p.sb(es, "op_ho", [128, KC, 128], BF16)
            for ti in range(L // 128):
                b = ti % 2
                rows = slice(ti * 128, (ti + 1) * 128)
                tx = p.ld(xr[b][:], xres[rows, :], f"opx{b}")
                if ya is not None:
                    t1 = p.ld(ym[b][:, 1024:2048], ymix.ap()[rows, 1024:2048], f"opm{b}")
                    t2 = p.ld(yaf[:], ya.ap()[rows, :], "opa")
                    t3 = p.ld(gaf[:], gates.ap()[rows, 0:1024], "opa")
                    a1 = p.A([t3], out=gaf[:], in_=gaf[:], func=AF.Silu)
                    v1 = p.V([a1, t2], "tensor_tensor", ym[b][:, 0:1024], yaf[:], gaf[:], ALU.mult)
                    rdy = [t1, v1]
                else:
                    rdy = [p.ld(ym[b][:], ymix.ap()[rows, :], f"opm{b}")]
                for kc in range(KC):
                    tk = p.tr(rdy if kc == 0 else [], p.pb[kc // 8][:, (kc % 8) * 128:(kc % 8 + 1) * 128],
                              ym[b][:, kc * 128:(kc + 1) * 128], ident[:], mark=(kc == KC - 1))
                c1 = p.A([tk], out=ymT[:, 0:8, :], in_=p.pb[0][:].rearrange("p (k t) -> p k t", k=8), func=AF.Copy)
                c2 = p.V([tk], "tensor_copy", ymT[:, 8:16, :], p.pb[1][:].rearrange("p (k t) -> p k t", k=8))
                for cb in range(4):
                    for kc in range(KC):
                        tk = p.mm([c1, c2, tw] if kc == 0 else [], p.ps[cb][:], ymT[:, kc, :],
                                  wsb[:, kc, cb * 512:(cb + 1) * 512], kc == 0, kc == KC - 1, mark=(kc == KC - 1))
                evs = []
                for cb in range(4):
                    if cb % 2 == 0:
                        evs.append(p.A([tk], out=yo[:, cb * 512:(cb + 1) * 512], in_=p.ps[cb][:], func=AF.Copy))
                    else:
                        evs.append(p.V([tk], "tensor_copy", yo[:, cb * 512:(cb + 1) * 512], p.ps[cb][:]))
                a2 = p.A(evs, out=junk[:], in_=yo[:], func=AF.Square, accum_out=st_[:, 0:1])
                v3 = p.rsqrt([a2], st_[:, 1:2], st_[:, 0:1], 1.0 / D, EPS)
                v4 = p.V([v3], "scalar_tensor_tensor", yo[:], yo[:], st_[:, 1:2], gpost[:], ALU.mult, ALU.mult)
                v5 = p.V([v4, tx], "tensor_tensor", yo[:], yo[:], xr[b][:], ALU.add)
                so = p.st(xout[rows, :], yo[:], "opo", deps=[v5])
                last = [so]
                if hT_next is not None:
                    a3 = p.A([v5], out=junk[:], in_=yo[:], func=AF.Square, accum_out=st_[:, 2:3])
                    v7 = p.rsqrt([a3], st_[:, 3:4], st_[:, 2:3], 1.0 / D, EPS)
                    v8 = p.V([v7], "tensor_scalar", hb[:], yo[:], st_[:, 3:4], None, ALU.mult)
                    for kc in range(KC):
                        tk = p.tr([v8] if kc == 0 else [], p.pb[kc // 8][:, (kc % 8) * 128:(kc % 8 + 1) * 128],
                                  hb[:, kc * 128:(kc + 1) * 128], ident[:], mark=(kc == KC - 1))
                    c1 = p.A([tk], out=ho[:, 0:8, :], in_=p.pb[0][:].rearrange("p (k t) -> p k t", k=8), func=AF.Copy)
                    c2 = p.V([tk], "tensor_copy", ho[:, 8:16, :], p.pb[1][:].rearrange("p (k t) -> p k t", k=8))
                    last.append(p.st(hT_next.ap().rearrange("(k p) t -> p k t", p=128)[:, :, rows], ho[:], "oph",
                                     deps=[c1, c2]))
                for e in (p.pe, p.act, p.dve, p.pool, p.sp):
                    e.wait(*last, (p.dve, p.dve.n), (p.act, p.act.n))
        p.barrier()

    def attention(qT, kT, vtok, gates, Lq, Lk, og, dtab, lam_sb, gsub_sb, delta):
        NKT, NQB = Lk // 128, Lq // 256
        SK = 2
        with contextlib.ExitStack() as es:
            qs = p.sb(es, "at_q", [128, 2, Lq], BF16)
            ks = p.sb(es, "at_k", [128, 2, Lk], BF16)
            vs = p.sb(es, "at_v", [128, NKT, 257], BF16)
            absd = [p.sb(es, f"at_ad{i}", [128, 256], F32) for i in range(3)]
            sb_ = [p.sb(es, f"at_s{i}", [128, 256], F32) for i in range(4)]
            pt = [p.sb(es, f"at_p{i}", [128, 256], BF16) for i in range(4)]
            gt = [p.sb(es, f"at_g{i}", [128, 2, 256], F32) for i in range(2)]
            o0 = [p.sb(es, f"at_o0{i}", [128, 2, 256], F32) for i in range(2)]
            o1 = [p.sb(es, f"at_o1{i}", [128, 2, 256], F32) for i in range(2)]
            rs = [p.sb(es, f"at_rs{i}", [128, 8], F32) for i in range(2)]
            ob = [p.sb(es, f"at_ob{i}", [128, 2, 256], BF16) for i in range(2)]
            jk = p.sb(es, "at_j", [128, 256], F32)
            Sps = [p.ps[i][:, 0:256] for i in range(2)]
            acc = [[p.ps[2 + 2 * c + j] for j in range(2)] for c in range(2)]
            qbi = 0
            gt_free = [[], []]
            ob_free = [None, None]
            for h in range(8):
                slope = SLOPES[h]
                t_in = [p.ld(qs[:], qT.ap()[h * 256:(h + 1) * 256, :].rearrange("(c p) t -> p c t", p=128), "atq"),
                        p.ld(ks[:], kT.ap()[h * 256:(h + 1) * 256, :].rearrange("(c p) t -> p c t", p=128), "atq"),
                        p.ld(vs[:, :, 0:256], vtok.ap()[:, h * 256:(h + 1) * 256].rearrange("(n p) v -> p n v", p=128),
                             "atq")]
                t_in.append(p.G([], "memset", vs[:, :, 256:257], 1.0))
                acc_free = []
                for qb in range(NQB):
                    par = qbi % 2
                    qbi += 1
                    q0 = qb * 256
                    tg = p.ld(gt[par][:],
                              gates.ap()[q0:q0 + 256, h * 256:(h + 1) * 256].rearrange("(j p) v -> p j v", p=128),
                              f"atg{par}", deps=gt_free[par])
                    units = [(kt, c) for kt in range(NKT) for c in range(2)]
                    NU = len(units)
                    rd_ad = [None] * 3
                    rd_s = [None] * 4
                    rd_p = [None] * 4
                    rd_ps = [None] * 2
                    absT = {}
                    expT = {}
                    state = {"lastpv": None}

                    def do_abs(kt):
                        ab = kt % 3
                        idx = kt * NQB + qb
                        absT[kt] = p.A([rd_ad[ab]], out=absd[ab][:], in_=delta[:], func=AF.Abs, bias=dtab[:, idx:idx + 1])

                    def front(u):
                        kt, c = units[u]
                        b = u % 4
                        if c == 0:
                            if kt == 0:
                                do_abs(0)
                            if kt + 1 < NKT:
                                do_abs(kt + 1)
                        b2 = u % 2
                        tk = p.mm(t_in + [rd_ps[b2]], Sps[b2], ks[:, c, kt * 128:(kt + 1) * 128], qs[:, c, q0:q0 + 256],
                                  True, True, mark=True)
                        v1 = p.V([tk, absT[kt], rd_s[b]], "scalar_tensor_tensor", sb_[b][:], absd[kt % 3][:], -slope,
                                 Sps[b2], ALU.mult, ALU.add)
                        rd_ps[b2] = v1
                        rd_ad[kt % 3] = v1
                        a1 = p.A([v1, rd_p[b]], out=pt[b][:], in_=sb_[b][:], func=AF.Exp)
                        rd_s[b] = a1
                        expT[u] = a1

                    def back(u):
                        kt, c = units[u]
                        b = u % 4
                        for j in range(2):
                            deps = [expT[u]] if j == 0 else []
                            if kt == 0:
                                deps = deps + acc_free
                            state["lastpv"] = p.mm(deps, acc[c][j][:, 0:257], pt[b][:, j * 128:(j + 1) * 128],
                                                   vs[:, kt, :], kt == 0, kt == NKT - 1, mark=(j == 1))
                        rd_p[b] = state["lastpv"]

                    for u in range(NU + SK):
                        if u < NU:
                            front(u)
                        if u >= SK:
                            back(u - SK)
                    lastpv = state["lastpv"]
                    evs = []
                    for c in range(2):
                        for j in range(2):
                            dst = (o0[par] if c == 0 else o1[par])
                            e1 = p.V([lastpv], "reciprocal", rs[par][:, 2 * c + j:2 * c + j + 1], acc[c][j][:, 256:257])
                            if c == 0:
                                evs.append(p.V([e1], "tensor_scalar", dst[:, j, :], acc[c][j][:, 0:256],
                                               rs[par][:, 2 * c + j:2 * c + j + 1], None, ALU.mult))
                            else:
                                evs.append(p.V([e1], "tensor_scalar", dst[:, j, :], acc[c][j][:, 0:256],
                                               rs[par][:, 2 * c + j:2 * c + j + 1], lam_sb[:, 1:2], ALU.mult, ALU.mult))
                    acc_free = [evs[-1]]
                    f1 = p.V(evs, "tensor_tensor", o0[par][:], o0[par][:], o1[par][:], ALU.add)
                    for j in range(2):
                        sqt = p.A([f1], out=jk[:], in_=o0[par][:, j, :], func=AF.Square, accum_out=rs[par][:, 4 + j:5 + j])
                    f2 = p.rsqrt([sqt], rs[par][:, 4:6], rs[par][:, 4:6], 1.0 / 256, EPS)
                    f3 = p.V([f2], "tensor_scalar", rs[par][:, 4:6], rs[par][:, 4:6], (1.0 - LAMBDA_INIT), None, ALU.mult)
                    ag = p.A([tg], out=gt[par][:], in_=gt[par][:], func=AF.Silu)
                    for j in range(2):
                        f4 = p.V([f3], "scalar_tensor_tensor", o0[par][:, j, :], o0[par][:, j, :], rs[par][:, 4 + j:5 + j],
                                 gsub_sb[:], ALU.mult, ALU.mult)
                    f5 = p.V([f4, ag, ob_free[par]], "tensor_tensor", ob[par][:], o0[par][:], gt[par][:], ALU.mult)
                    ob_free[par] = p.st(og.ap()[q0:q0 + 256, h * 256:(h + 1) * 256].rearrange("(j p) v -> p j v", p=128),
                                        ob[par][:], f"ato{par}", deps=[f5])
                    gt_free[par] = [f5, ag]
                p.barrier()
                gt_free = [[], []]
                ob_free = [None, None]
        p.barrier()

    c128 = p.sb(ges, "c128_sb", [128, 3, 128], BF16)
    p.ld(c128[:], c128_d.ap().rearrange("k p c -> p k c"), "c0")
    cs256 = p.sb(ges, "cs256_sb", [128, 2, 512], BF16)
    p.ld(cs256[:], cs256_d.ap().rearrange("(c p) n -> p c n", p=128), "c0")
    tw_sb, cm_sb = {}, {}
    for L in tw_d:
        M = L // 128
        tw_sb[L] = p.sb(ges, f"tw{L}_sb", [128, 3, M], F32)
        p.ld(tw_sb[L][:], tw_d[L].ap().rearrange("k p m -> p k m"), "c0")
        cm_sb[L] = p.sb(ges, f"cm{L}_sb", [M, 2, M], BF16)
        p.ld(cm_sb[L][:], cm_d[L].ap().rearrange("k b d -> b k d"), "c0")
    hm = p.sb(ges, "hm_sb", [64, 2, 64], F32)
    p.ld(hm[:], hmask_d.ap().rearrange("k s t -> s k t"), "c0")
    delta = p.sb(ges, "delta_sb", [128, 256], F32)
    p.ld(delta[:], delta_d.ap(), "c0")
    dtabs = p.sb(ges, "dtabs_sb", [128, (LS // 128) * (LS // 256)], F32)
    p.ld(dtabs[:], dtabs_d.ap(), "c0")
    dtabp = p.sb(ges, "dtabp_sb", [128, (LP // 128) * (OWN // 256)], F32)
    p.ld(dtabp[:], dtabp_d.ap(), "c0")
    ownidx = p.sb(ges, "ownidx_sb", [128, OWN // 128], I32)
    p.ld(ownidx[:], ownidx_d.ap(), "c0")
    ghg_sb = p.sb(ges, "ghg_sb", [64, 128], F32)
    p.ld(ghg_sb[:], bcast_rows(ghg.ap(), 128, 64), "c0")
    gsub_sb = p.sb(ges, "gsub_sb", [128, 256], F32)
    p.ld(gsub_sb[:], bcast_rows(gsub.ap(), 256), "c0")
    lraw = p.sb(ges, "lraw_sb", [128, 2, 3, 8], F32)
    p.ld(lraw[:], lbl.ap().rearrange("d s (h k) -> k d s h", k=128), "c0", slow=True)
    lbt = p.sb(ges, "lbt_sb", [128, 4, 8], F32)
    lsum = p.sb(ges, "lsum_sb", [128, 2, 8], F32)
    lamv = p.sb(ges, "lamv_sb", [128, 4, 128], F32)
    p.ld(lamv[:], bass.AP(lam4.ap().tensor, 0, [[0, 128], [128, 4], [1, 128]]), "c0")
    lam_sb = p.sb(ges, "lam_sb", [128, 4], F32)
    ljunk = p.sb(ges, "ljunk_sb", [128, 128], F32)
    p.barrier()
    a = p.A([], out=lraw[:], in_=lraw[:], func=AF.Exp)
    v = p.V([a], "tensor_tensor", lsum[:], lraw[:, :, 0, :], lraw[:, :, 1, :], ALU.add)
    v = p.V([v], "tensor_tensor", lsum[:], lsum[:], lraw[:, :, 2, :], ALU.add)
    v = p.V([v], "reciprocal", lsum[:], lsum[:])
    v = p.V([v], "tensor_tensor", lbt[:, 0:2, :], lraw[:, :, 0, :], lsum[:], ALU.mult)
    v = p.V([v], "tensor_scalar", lbt[:, 2:4, :], lbt[:, 0:2, :], -1.0, 1.0, ALU.mult, ALU.add)
    v = p.V([v], "tensor_tensor", ljunk[:], lamv[:, 0, :], lamv[:, 1, :], ALU.mult)
    v = p.V([v], "tensor_reduce", lam_sb[:, 2:3], ljunk[:], mybir.AxisListType.X, ALU.add)
    v = p.V([v], "tensor_tensor", ljunk[:], lamv[:, 2, :], lamv[:, 3, :], ALU.mult)
    v = p.V([v], "tensor_reduce", lam_sb[:, 3:4], ljunk[:], mybir.AxisListType.X, ALU.add)
    a = p.A([v], out=lam_sb[:, 2:4], in_=lam_sb[:, 2:4], func=AF.Exp)
    v = p.V([a], "tensor_tensor", lam_sb[:, 0:1], lam_sb[:, 2:3], lam_sb[:, 3:4], ALU.subtract)
    v = p.V([v], "tensor_scalar", lam_sb[:, 1:2], lam_sb[:, 0:1], LAMBDA_INIT, -1.0, ALU.add, ALU.mult)
    p.barrier()

    seqs = [("s%d" % i, LS, xs.ap()[i * LS:(i + 1) * LS, :], ys.ap()[i * LS:(i + 1) * LS, :]) for i in range(NS)]
    seqs.append(("p", LP, xp.ap(), None))
    scale_q = 128 ** -0.5
    for (nm, L, xin, yout) in seqs:
        isP = yout is None
        hT = p.dram(f"hT_{nm}", [D, L], BF16)
        norm_T(xin, L, hT)
        done("norm")
        uT = p.dram(f"uT_{nm}", [1024, L], BF16)
        gat0 = p.dram(f"g0_{nm}", [L, 2048], F32)
        qrT = p.dram(f"qr_{nm}", [1024, L], F32)
        ffT = p.dram(f"ff_{nm}", [1024, L], F32)
        fbT = p.dram(f"fb_{nm}", [1024, L], F32)
        vtk = p.dram(f"vt_{nm}", [L, 1024], BF16)
        proj(hT, L, wb0i, 7168, [
            (0, 1024, "F", uT, 0, 1.0, BF16),
            (1024, 1024, "T", gat0, 0, 1.0, F32),
            (2048, 1024, "F", qrT, 0, 1.0, F32),
            (3072, 1024, "T", vtk, 0, 1.0, BF16),
            (4096, 1024, "F", ffT, 0, 1.0, F32),
            (5120, 1024, "F", fbT, 0, 1.0, F32),
            (6144, 1024, "T", gat0, 1024, 1.0, F32),
        ])
        done("proj0")
        ya = p.dram(f"ya_{nm}", [L, 1024], F32)
        fnet(uT, L, ya, (c128, cs256, tw_sb[L], cm_sb[L]))
        done("fnet")
        ymix = p.dram(f"ym_{nm}", [L, 2048], BF16)
        hgrn(qrT, ffT, fbT, vtk, gat0, L, ymix, lbt, hm, ghg_sb)
        done("hgrn")
        x1 = p.dram(f"x1_{nm}", [L, D], F32)
        h1T = p.dram(f"h1T_{nm}", [D, L], BF16)
        outproj(L, ymix, ya, gat0, wb0o, gpost_sb[0], xin, x1.ap(), h1T)
        done("out0")
        kT = p.dram(f"kT_{nm}", [2048, L], BF16)
        v1t = p.dram(f"v1_{nm}", [L, 2048], BF16)
        if not isP:
            Lq = L
            qT = p.dram(f"qT_{nm}", [2048, Lq], BF16)
            gat1 = p.dram(f"g1_{nm}", [Lq, 2048], F32)
            proj(h1T, L, wb1i, 8192, [
                (0, 2048, "F", qT, 0, scale_q, BF16),
                (2048, 2048, "F", kT, 0, 1.0, BF16),
                (4096, 2048, "T", v1t, 0, 1.0, BF16),
                (6144, 2048, "T", gat1, 0, 1.0, F32),
            ])
            xres1 = x1.ap()
            dtab = dtabs
        else:
            Lq = OWN
            proj(h1T, L, wb1i, 8192, [
                (2048, 2048, "F", kT, 0, 1.0, BF16),
                (4096, 2048, "T", v1t, 0, 1.0, BF16),
            ])
            x1own = p.dram("x1own", [OWN, D], F32)
            with contextlib.ExitStack() as es:
                gx = p.sb(es, "gx", [128, D], F32)
                prev = None
                for ti in range(OWN // 128):
                    p.pool.wait(prev)
                    ds = p.dsem("gath")
                    p.nc.gpsimd.indirect_dma_start(
                        out=gx[:], out_offset=None, in_=x1.ap(),
                        in_offset=bass.IndirectOffsetOnAxis(ap=ownidx[:, ti:ti + 1], axis=0),
                    ).then_inc(ds.sem, 16)
                    ds.n += 16
                    tok = (ds, ds.n)
                    p.pending.append(tok)
                    prev = p.st(x1own.ap()[ti * 128:(ti + 1) * 128, :], gx[:], "gaths", deps=[tok])
            p.barrier()
            h1To = p.dram("h1To", [D, OWN], BF16)
            norm_T(x1own.ap(), OWN, h1To)
            qT = p.dram(f"qT_{nm}", [2048, Lq], BF16)
            gat1 = p.dram(f"g1_{nm}", [Lq, 2048], F32)
            proj(h1To, OWN, wb1i, 8192, [
                (0, 2048, "F", qT, 0, scale_q, BF16),
                (6144, 2048, "T", gat1, 0, 1.0, F32),
            ])
            xres1 = x1own.ap()
            yout = yp.ap()
            dtab = dtabp
        done("proj1")
        og = p.dram(f"og_{nm}", [Lq, 2048], BF16)
        attention(qT, kT, v1t, gat1, Lq, L, og, dtab, lam_sb, gsub_sb, delta)
        done("attn")
        outproj(Lq, og, None, None, wb1o, gpost_sb[1], xres1, yout, None)
        done("seq0")


def dft_cs(n):
    j = np.arange(n)
    ang = 2 * np.pi * np.outer(j, j) / n
    return np.cos(ang), np.sin(ang)


def host_tables(cfg, core):
    NS, LS, LP, OWN = cfg["NS"], cfg["LS"], cfg["LP"], cfg["OWN"]
    t = {}
    t["ident"] = np.eye(128, dtype=np.float32).astype(NPBF)
    c, s = dft_cs(128)
    t["c128"] = np.stack([c, s, -s]).astype(np.float32).astype(NPBF)
    c, s = dft_cs(256)
    t["cs256"] = np.concatenate([c, -s], axis=1).astype(np.float32).astype(NPBF)
    for L in sorted({LS, LP}):
        M = L // 128
        ang = 2 * np.pi * np.outer(np.arange(128), np.arange(M)) / L
        t[f"tw{L}"] = np.stack([np.cos(ang), np.sin(ang), -np.sin(ang)]).astype(np.float32)
        cm, sm = dft_cs(M)
        sc = 1.0 / math.sqrt(L * 256)
        t[f"cm{L}"] = np.stack([cm * sc, sm * sc]).astype(np.float32).astype(NPBF)
    s_, t_ = np.meshgrid(np.arange(64), np.arange(64), indexing="ij")
    t["hmask"] = np.stack([(s_ <= t_), (s_ >= t_)]).astype(np.float32)
    t["delta"] = (np.arange(128)[:, None] - np.arange(256)[None, :]).astype(np.float32)
    kt, qb = np.meshgrid(np.arange(LS // 128), np.arange(LS // 256), indexing="ij")
    t["dtabs"] = np.broadcast_to((kt * 128 - qb * 256).reshape(1, -1), (128, kt.size)).astype(np.float32).copy()
    kt, qb = np.meshgrid(np.arange(LP // 128), np.arange(OWN // 256), indexing="ij")
    t["dtabp"] = np.broadcast_to((kt * 128 - (core * OWN + qb * 256)).reshape(1, -1), (128, kt.size)).astype(np.float32).copy()
    t["ownidx"] = (core * OWN + np.arange(OWN)).reshape(OWN // 128, 128).T.astype(np.int32).copy()
    return t


def run(inputs, cfg, ncores):
    NS, LS, LP, OWN = cfg["NS"], cfg["LS"], cfg["LP"], cfg["OWN"]
    f = lambda a: np.ascontiguousarray(np.asarray(a, dtype=np.float32))
    xsamp = f(inputs["x_sample"])
    shared = {
        "xp": f(inputs["x_prompt"])[0],
        "w0i": f(inputs["ev_w_in"])[0], "w0o": f(inputs["ev_w_out"])[0],
        "w1i": f(inputs["od_w_in"])[0], "w1o": f(inputs["od_w_out"])[0],
        "g0pre": f(inputs["ev_norm_pre"])[0], "g0post": f(inputs["ev_norm_post"])[0],
        "g1pre": f(inputs["od_norm_pre"])[0], "g1post": f(inputs["od_norm_post"])[0],
        "lbl": f(inputs["hgrn_lb_logits"]), "ghg": f(inputs["hgrn_norm"])[0],
        "lam4": np.stack([f(inputs["lambda_q1"])[0], f(inputs["lambda_k1"])[0],
                          f(inputs["lambda_q2"])[0], f(inputs["lambda_k2"])[0]]),
        "gsub": f(inputs["subln"])[0],
    }
    nc = build(cfg)
    in_maps = []
    for c in range(ncores):
        m = dict(shared)
        m["xs"] = xsamp[c * NS:(c + 1) * NS].reshape(NS * LS, D)
        m.update(host_tables(cfg, c))
        in_maps.append(m)
    res = run_bass_kernel_spmd(nc, in_maps, core_ids=list(range(ncores)))
    global LAST_RES
    LAST_RES = res.results
    y_s = np.concatenate([r["ys"].reshape(NS, LS, D) for r in res.results], axis=0)
    y_p = np.concatenate([r["yp"] for r in res.results], axis=0)[None]
    return y_p.astype(np.float32), y_s.astype(np.float32)


def kernel(**inputs):
    import os
    cfg = {"NS": 2, "LS": 2048, "LP": 8192, "OWN": 1024}
    if os.environ.get("K_STOP"):
        cfg["stop"] = os.environ["K_STOP"]
    return run(inputs, cfg, 8)
```

```python
import contextlib, math
import numpy as np
import ml_dtypes
import concourse.bass as bass
import concourse.mybir as mybir
from concourse.bass_utils import run_bass_kernel_spmd

F32, BF16, I32 = mybir.dt.float32, mybir.dt.bfloat16, mybir.dt.int32
AF = mybir.ActivationFunctionType
ALU = mybir.AluOpType
D = 2048
KC = 16
EPS = 1e-6
LAMBDA_INIT = 0.8 - 0.6 * math.exp(-0.3 * 1)
SLOPES = [2.0 ** (-8.0 * (h + 1) / 8) for h in range(8)]
NPBF = ml_dtypes.bfloat16


class StopBuild(Exception):
    pass


class Eng:
    def __init__(self, e, sem):
        self.e, self.sem, self.n, self.seen = e, sem, 0, {}

    def mark(self, ins):
        ins.then_inc(self.sem, 1)
        self.n += 1
        return (self, self.n)

    def wait(self, *toks):
        for tok in toks:
            if tok is None:
                continue
            src, n = tok
            if self.seen.get(src, 0) >= n:
                continue
            self.e.wait_ge(src.sem, n)
            self.seen[src] = n


class DSem:
    def __init__(self, sem):
        self.sem, self.n = sem, 0


class P:
    def __init__(self, cfg):
        self.cfg = cfg
        self.es = contextlib.ExitStack()
        nc = self.nc = bass.Bass("TRN2", target_bir_lowering=False)
        mk = lambda nm: self.es.enter_context(nc.semaphore(nm))
        self.pe = Eng(nc.tensor, mk("s_pe"))
        self.act = Eng(nc.scalar, mk("s_act"))
        self.dve = Eng(nc.vector, mk("s_dve"))
        self.pool = Eng(nc.gpsimd, mk("s_pool"))
        self.sp = Eng(nc.sync, mk("s_sp"))
        self.engs = [self.pe, self.act, self.dve, self.pool, self.sp]
        self.dsems = {}
        self.pending = []
        self.ndram = 0
        self.ps = [self.es.enter_context(nc.psum_tensor(f"ps{i}", [128, 512], F32)) for i in range(6)]
        self.pb = [self.es.enter_context(nc.psum_tensor(f"pb{i}", [128, 1024], BF16)) for i in range(2)]
        self.dummy = self.es.enter_context(nc.sbuf_tensor("dummy_sb", [128, 2], F32))
        self.pool.mark(nc.gpsimd.memset(self.dummy[:], 0.0))

    def sb(self, es, name, shape, dt):
        self.nsb = getattr(self, "nsb", 0) + 1
        return es.enter_context(self.nc.sbuf_tensor(f"{name}_{self.nsb}", shape, dt))

    def dram(self, name, shape, dt, kind="Internal"):
        t = self.nc.dram_tensor(name, list(shape), dt, kind=kind)
        if not hasattr(self, "named"):
            self.named = {}
        self.named[name] = (t, list(shape), dt)
        return t

    def dump(self):
        self.barrier()
        for name in self.cfg.get("dump", []):
            if name not in self.named:
                continue
            t, shape, dt = self.named[name]
            o = self.nc.dram_tensor("dbg_" + name, shape, dt, kind="ExternalOutput")
            self.ld(o.ap(), t.ap(), "dump")
        self.barrier()

    def dsem(self, key):
        if key not in self.dsems:
            self.dsems[key] = DSem(self.es.enter_context(self.nc.semaphore("d_" + key)))
        return self.dsems[key]

    def dma(self, q, out, in_, key, deps=(), slow=False):
        q.wait(*deps)
        ds = self.dsem(key)
        kw = {"allow_slow_non_contiguous": True} if slow else {}
        q.e.dma_start(out=out, in_=in_, **kw).then_inc(ds.sem, 16)
        ds.n += 16
        tok = (ds, ds.n)
        self.pending.append(tok)
        return tok

    def ld(self, out, in_, key, deps=(), slow=False):
        return self.dma(self.sp, out, in_, key, deps, slow)

    def st(self, out, in_, key, deps=()):
        return self.dma(self.pool, out, in_, key, deps)

    def barrier(self):
        toks = [(e, e.n) for e in self.engs if e.n > 0] + self.pending
        for e in self.engs:
            e.wait(*toks)
        self.pending = []

    def A(self, deps, *a, **k):
        self.act.wait(*deps)
        tok = self.act.mark(self.act.e.activation(*a, **k))
        if k.get("accum_out") is not None:
            tok = self.act.mark(self.act.e.activation(out=self.dummy[:, 1:2], in_=self.dummy[:, 0:1], func=AF.Copy))
        return tok

    def V(self, deps, fn, *a, **k):
        self.dve.wait(*deps)
        return self.dve.mark(getattr(self.dve.e, fn)(*a, **k))

    def G(self, deps, fn, *a, **k):
        self.pool.wait(*deps)
        return self.pool.mark(getattr(self.pool.e, fn)(*a, **k))

    def X(self, eng, deps, fn, *a, **k):
        eng.wait(*deps)
        return eng.mark(getattr(eng.e, fn)(*a, **k))

    def rsqrt(self, deps, out, in_, mul, add):
        a = self.A(deps, out=out, in_=in_, func=AF.Ln, scale=float(mul), bias=float(add))
        return self.A([a], out=out, in_=out, func=AF.Exp, scale=-0.5)

    def mm(self, deps, out, lhsT, rhs, start, stop, mark=False):
        self.pe.wait(*deps)
        ins = self.pe.e.matmul(out, lhsT, rhs, start=start, stop=stop)
        return self.pe.mark(ins) if mark else None

    def tr(self, deps, out, in_, ident, mark=False):
        self.pe.wait(*deps)
        ins = self.pe.e.transpose(out, in_, ident)
        return self.pe.mark(ins) if mark else None


def bcast_rows(ap_dram_1d, n, parts=128):
    return bass.AP(ap_dram_1d.tensor, ap_dram_1d.offset, [[0, parts], [1, n]])


def build(cfg):
    p = P(cfg)
    try:
        _build(cfg, p)
    except StopBuild:
        p.dump()
        return p.nc
    p.dump()
    p.es.close()
    return p.nc


def _build(cfg, p):
    NS, LS, LP, OWN = cfg["NS"], cfg["LS"], cfg["LP"], cfg["OWN"]

    def done(tag):
        if cfg.get("stop") == tag:
            raise StopBuild()
    nc = p.nc
    inp = lambda name, shape, dt=F32: nc.dram_tensor(name, list(shape), dt, kind="ExternalInput")
    xs = inp("xs", [NS * LS, D])
    xp = inp("xp", [LP, D])
    w0i, w0o = inp("w0i", [D, 7168]), inp("w0o", [D, D])
    w1i, w1o = inp("w1i", [D, 8192]), inp("w1o", [D, D])
    g0pre, g0post = inp("g0pre", [D]), inp("g0post", [D])
    g1pre, g1post = inp("g1pre", [D]), inp("g1post", [D])
    lbl = inp("lbl", [2, 3, 1024])
    ghg = inp("ghg", [128])
    lam4 = inp("lam4", [4, 128])
    gsub = inp("gsub", [256])
    ident_d = inp("ident", [128, 128], BF16)
    c128_d = inp("c128", [3, 128, 128], BF16)
    cs256_d = inp("cs256", [256, 512], BF16)
    tw_d = {L: inp(f"tw{L}", [3, 128, L // 128]) for L in sorted({LS, LP})}
    cm_d = {L: inp(f"cm{L}", [2, L // 128, L // 128], BF16) for L in sorted({LS, LP})}
    hmask_d = inp("hmask", [2, 64, 64])
    delta_d = inp("delta", [128, 256])
    dtabs_d = inp("dtabs", [128, (LS // 128) * (LS // 256)])
    dtabp_d = inp("dtabp", [128, (LP // 128) * (OWN // 256)])
    ownidx_d = inp("ownidx", [128, OWN // 128], I32)
    ys = nc.dram_tensor("ys", [NS * LS, D], F32, kind="ExternalOutput")
    yp = nc.dram_tensor("yp", [OWN, D], F32, kind="ExternalOutput")

    ges = p.es
    ident = p.sb(ges, "ident_sb", [128, 128], BF16)
    toks = [p.ld(ident[:], ident_d.ap(), "c0")]
    gpost_sb = [p.sb(ges, f"gpost{i}", [128, D], F32) for i in range(2)]
    toks.append(p.ld(gpost_sb[0][:], bcast_rows(g0post.ap(), D), "c0"))
    toks.append(p.ld(gpost_sb[1][:], bcast_rows(g1post.ap(), D), "c0"))
    gpre_sb = [p.sb(ges, f"gpre{i}", [128, KC], F32) for i in range(2)]
    toks.append(p.ld(gpre_sb[0][:], g0pre.ap().rearrange("(c p) -> p c", p=128), "c0", slow=True))
    toks.append(p.ld(gpre_sb[1][:], g1pre.ap().rearrange("(c p) -> p c", p=128), "c0", slow=True))
    p.barrier()

    def prep_w(w, ncols, gcol, name):
        wb = p.dram(name, [D, ncols], BF16)
        with contextlib.ExitStack() as es:
            CW = 1024
            wf = [p.sb(es, f"wf{i}", [128, CW], F32) for i in range(2)]
            wo = [p.sb(es, f"wo{i}", [128, CW], BF16) for i in range(2)]
            cons = [None, None]
            sts = [None, None]
            i = 0
            for kc in range(KC):
                for c0 in range(0, ncols, CW):
                    b = i % 2
                    t = p.ld(wf[b][:], w.ap()[kc * 128:(kc + 1) * 128, c0:c0 + CW], f"wl{b}", deps=[cons[b]])
                    eng = p.dve if b == 0 else p.pool
                    if gcol is None:
                        cons[b] = p.X(eng, [t, sts[b]], "tensor_copy", wo[b][:], wf[b][:])
                    else:
                        cons[b] = p.X(eng, [t, sts[b]], "tensor_scalar", wo[b][:], wf[b][:],
                                      gcol[:, kc:kc + 1], None, ALU.mult)
                    sts[b] = p.dma(p.act, wb.ap()[kc * 128:(kc + 1) * 128, c0:c0 + CW], wo[b][:], f"ws{b}",
                                   deps=[cons[b]])
                    i += 1
        p.barrier()
        return wb

    wb0i = prep_w(w0i, 7168, gpre_sb[0], "wb0i")
    wb0o = prep_w(w0o, D, None, "wb0o")
    wb1i = prep_w(w1i, 8192, gpre_sb[1], "wb1i")
    wb1o = prep_w(w1o, D, None, "wb1o")
    done("prep")

    def norm_T(x_rows, L, hT):
        with contextlib.ExitStack() as es:
            xt = [p.sb(es, f"nx{i}", [128, D], F32) for i in range(2)]
            junk = p.sb(es, "njunk", [128, D], BF16)
            hb = [p.sb(es, f"nhb{i}", [128, D], BF16) for i in range(2)]
            ho = [p.sb(es, f"nho{i}", [128, KC, 128], BF16) for i in range(2)]
            st_ = [p.sb(es, f"nst{i}", [128, 2], F32) for i in range(2)]
            rd = [None, None]
            hbr = [None, None]
            hor = [None, None]
            for ti in range(L // 128):
                b = ti % 2
                t = p.ld(xt[b][:], x_rows[ti * 128:(ti + 1) * 128, :], f"nl{b}", deps=[rd[b]])
                a1 = p.A([t], out=junk[:], in_=xt[b][:], func=AF.Square, accum_out=st_[b][:, 0:1])
                v2 = p.rsqrt([a1], st_[b][:, 1:2], st_[b][:, 0:1], 1.0 / D, EPS)
                v3 = p.V([v2, t, hbr[b]], "tensor_scalar", hb[b][:], xt[b][:], st_[b][:, 1:2], None, ALU.mult)
                rd[b] = v3
                for kc in range(KC):
                    tk = p.tr([v3, hor[b]] if kc == 0 else [], p.pb[kc // 8][:, (kc % 8) * 128:(kc % 8 + 1) * 128],
                              hb[b][:, kc * 128:(kc + 1) * 128], ident[:], mark=(kc == KC - 1))
                hbr[b] = tk
                c1 = p.A([tk, hor[b]], out=ho[b][:, 0:8, :], in_=p.pb[0][:].rearrange("p (k t) -> p k t", k=8),
                         func=AF.Copy)
                c2 = p.V([tk, hor[b]], "tensor_copy", ho[b][:, 8:16, :],
                         p.pb[1][:].rearrange("p (k t) -> p k t", k=8))
                p.pe.wait(c1, c2)
                hor[b] = p.st(hT.ap().rearrange("(k p) t -> p k t", p=128)[:, :, ti * 128:(ti + 1) * 128],
                              ho[b][:], f"ns{b}", deps=[c1, c2])
        p.barrier()

    def proj(hT, L, wb, ncols_total, jobs):
        TB = min(L, 1024)
        with contextlib.ExitStack() as es:
            hblk = p.sb(es, "pj_h", [128, KC, TB], BF16)
            wblk = [p.sb(es, f"pj_w{i}", [128, KC, 512], BF16) for i in range(2)]
            osb = {F32: [p.sb(es, f"pj_of{i}", [128, 512], F32) for i in range(2)],
                   BF16: [p.sb(es, f"pj_ob{i}", [128, 512], BF16) for i in range(2)]}
            wread = [None, None]
            ost = {F32: [None, None], BF16: [None, None]}
            psr = [None] * 4
            wi = 0
            oi = 0
            pi = 0
            hread = None
            for tb in range(L // TB):
                th = p.ld(hblk[:], hT.ap().rearrange("(k p) t -> p k t", p=128)[:, :, tb * TB:(tb + 1) * TB],
                          "pjh", deps=[hread])
                cbs = [(j, c) for j in jobs for c in range(0, j[1], 512)]
                for (job, c) in cbs:
                    col0, ncols, mode, od, o0, scale, odt = job
                    b = wi % 2
                    wi += 1
                    tw = p.ld(wblk[b][:], wb.ap().rearrange("(k p) n -> p k n", p=128)[:, :, col0 + c:col0 + c + 512],
                              f"pjw{b}", deps=[wread[b]])
                    last = None
                    TW = min(512, TB)
                    if mode == "F":
                        subs = [(s4, t5) for s4 in range(4) for t5 in range(TB // TW)]
                    else:
                        subs = [(s4, 0) for s4 in range(TB // 128)]
                    for (s4, t5) in subs:
                        pk = pi % 4
                        pi += 1
                        W_ = TW if mode == "F" else 512
                        ps = p.ps[pk][:, 0:W_]
                        for kc in range(KC):
                            if mode == "F":
                                lhsT, rhs = wblk[b][:, kc, s4 * 128:(s4 + 1) * 128], hblk[:, kc, t5 * TW:(t5 + 1) * TW]
                            else:
                                lhsT, rhs = hblk[:, kc, s4 * 128:(s4 + 1) * 128], wblk[b][:, kc, :]
                            tk = p.mm([th, tw, psr[pk]] if kc == 0 else [], ps, lhsT, rhs, kc == 0, kc == KC - 1,
                                      mark=(kc == KC - 1))
                        last = tk
                        ob = oi % 2
                        oi += 1
                        o = osb[odt][ob][:, 0:W_]
                        if oi % 2 == 0:
                            ev = p.A([tk, ost[odt][ob]], out=o, in_=ps, func=AF.Copy, scale=float(scale))
                        else:
                            ev = p.V([tk, ost[odt][ob]], "tensor_scalar", o, ps, float(scale), None, ALU.mult)
                        psr[pk] = ev
                        if mode == "F":
                            dst = od.ap()[o0 + c + s4 * 128:o0 + c + (s4 + 1) * 128,
                                          tb * TB + t5 * TW:tb * TB + (t5 + 1) * TW]
                        else:
                            dst = od.ap()[tb * TB + s4 * 128:tb * TB + (s4 + 1) * 128, o0 + c:o0 + c + 512]
                        ost[odt][ob] = p.st(dst, o, f"pjs{ob}{'f' if odt == F32 else 'b'}", deps=[ev])
                    wread[b] = last
                    hread = last
        p.barrier()

    def fnet(uT, L, ya, tabs):
        M = L // 128
        c128, cs256, tw, cm = tabs
        Bd = p.dram(f"fn_B{p.ndram}", [128, M, 512], BF16)
        p.ndram += 1
        CP = 32
        with contextlib.ExitStack() as es:
            ug = p.sb(es, "fn_u", [128, 2, L], BF16)
            MB = min(M, 32)
            V = p.sb(es, "fn_V", [128, MB, 512], BF16)
            Bs = p.sb(es, "fn_Bs", [128, MB, 512], BF16)
            tmp = [p.sb(es, f"fn_t{i}", [128, 2, 256], F32) for i in range(2)]
            Bt = p.sb(es, "fn_Bt", [M, CP, 512], BF16)
            Y = [p.sb(es, f"fn_Y{i}", [M, 2, 256], F32) for i in range(2)]
            for g in range(4):
                tu = p.ld(ug[:], uT.ap()[g * 256:(g + 1) * 256, :].rearrange("(c p) t -> p c t", p=128), "fnu")
                ts_all = []
                for bh in range(M // MB):
                    evs = []
                    prev = [None, None]
                    for bl in range(MB):
                        b = bh * MB + bl
                        pk = b % 2
                        for ch in range(2):
                            lhsT = bass.AP(ug, ch * L + b, [[2 * L, 128], [M, 128]])
                            tk = p.mm([tu, prev[pk]] if ch == 0 else [], p.ps[pk][:], lhsT, cs256[:, ch, :],
                                      ch == 0, ch == 1, mark=(ch == 1))
                        if b % 2 == 0:
                            ev = p.A([tk], out=V[:, bl, :], in_=p.ps[pk][:], func=AF.Copy)
                        else:
                            ev = p.V([tk], "tensor_copy", V[:, bl, :], p.ps[pk][:])
                        prev[pk] = ev
                        evs.append(ev)
                    prevr = [None, None]
                    tw_tok = []
                    for bp in range(MB // 2):
                        pk = 2 + (bp % 2) * 2
                        Ar, Ai = p.ps[pk], p.ps[pk + 1]
                        b0 = 2 * bp
                        dep = [evs[b0], evs[b0 + 1], prevr[bp % 2]]
                        ar3 = Ar[:].rearrange("p (b f) -> p b f", b=2)
                        ai3 = Ai[:].rearrange("p (b f) -> p b f", b=2)
                        p.mm(dep, ar3, c128[:, 0, :], V[:, b0:b0 + 2, 0:256], True, False)
                        p.mm([], ar3, c128[:, 1, :], V[:, b0:b0 + 2, 256:512], False, True)
                        p.mm([], ai3, c128[:, 0, :], V[:, b0:b0 + 2, 256:512], True, False)
                        tk = p.mm([], ai3, c128[:, 2, :], V[:, b0:b0 + 2, 0:256], False, True, mark=True)
                        last = []
                        for j in range(2):
                            bl = b0 + j
                            b = bh * MB + bl
                            t1 = p.V([tk], "tensor_scalar", tmp[0][:, j, :], ar3[:, j, :], tw[:, 0, b:b + 1], None, ALU.mult)
                            t3 = p.V([tk], "tensor_scalar", tmp[1][:, j, :], ai3[:, j, :], tw[:, 0, b:b + 1], None, ALU.mult)
                            r1 = p.V([t1, tk], "scalar_tensor_tensor", Bs[:, bl, 0:256], ai3[:, j, :], tw[:, 1, b:b + 1],
                                     tmp[0][:, j, :], ALU.mult, ALU.add)
                            r2 = p.V([t3, tk], "scalar_tensor_tensor", Bs[:, bl, 256:512], ar3[:, j, :], tw[:, 2, b:b + 1],
                                     tmp[1][:, j, :], ALU.mult, ALU.add)
                            last = [r1, r2]
                        p.act.wait(*last)
                        prevr[bp % 2] = last[1]
                        tw_tok = last
                    ts_all.append(p.st(Bd.ap()[:, bh * MB:(bh + 1) * MB, :], Bs[:], "fnb", deps=tw_tok))
                    p.barrier()
                if True:
                    ts_ = ts_all[-1]
                    yst = [None, None]
                    evp = [None, None]
                    bt_read = None
                    for cp in range(128 // CP):
                        tl = p.ld(Bt[:], Bd.ap()[cp * CP:(cp + 1) * CP, :, :].rearrange("c b f -> b c f"), "fnbt",
                                  deps=[ts_, bt_read])
                        for c2 in range(CP // 2):
                            pk = c2 % 2
                            ps3 = p.ps[pk][0:M, :].rearrange("p (c f) -> p c f", c=2)
                            p.mm([tl, evp[pk]], ps3, cm[0:M, 0, :], Bt[:, 2 * c2:2 * c2 + 2, 0:256], True, False)
                            tk = p.mm([], ps3, cm[0:M, 1, :], Bt[:, 2 * c2:2 * c2 + 2, 256:512], False, True, mark=True)
                            if c2 % 2 == 0:
                                ev = p.A([tk, yst[pk]], out=Y[pk][:], in_=ps3, func=AF.Copy)
                            else:
                                ev = p.V([tk, yst[pk]], "tensor_copy", Y[pk][:], ps3)
                            c_abs = cp * CP + 2 * c2
                            dst = ya.ap().rearrange("(d c) f -> d c f", c=128)[:, c_abs:c_abs + 2, g * 256:(g + 1) * 256]
                            yst[pk] = p.st(dst, Y[pk][:], f"fny{pk}", deps=[ev])
                            evp[pk] = ev
                            bt_read = tk
                    p.barrier()
        p.barrier()

    def hgrn(qrT, ffT, fbT, vtok, gates, L, ymix, lbt, hm, ghg_sb):
        SEG = min(L, 2048)
        NCH = SEG // 64
        nseg = L // SEG
        NT = L // 64
        dS = p.dram(f"hg_dS{p.ndram}", [2, NT, 128, 128], F32)
        Sb = p.dram(f"hg_Sb{p.ndram}", [2, NT, 128, 128], BF16)
        qd = p.dram(f"hg_qd{p.ndram}", [2, 128, L], BF16)
        scd = p.dram(f"hg_sc{p.ndram}", [64, NT, 64], BF16)
        eld = p.dram(f"hg_el{p.ndram}", [2, 128, NT], F32)
        p.ndram += 1
        for h in range(8):
            with contextlib.ExitStack() as es:
                qr = p.sb(es, "h_qr", [128, SEG], F32)
                fr = p.sb(es, "h_fr", [128, SEG], F32)
                f_ = p.sb(es, "h_f", [128, SEG], F32)
                g_ = p.sb(es, "h_g", [128, SEG], F32)
                k_ = p.sb(es, "h_k", [128, SEG], F32)
                cum = p.sb(es, "h_cum", [128, SEG], F32)
                cb = p.sb(es, "h_cb", [128, SEG], F32)
                ex = p.sb(es, "h_ex", [128, SEG], F32)
                ex2 = p.sb(es, "h_ex2", [128, SEG], F32)
                ex3 = p.sb(es, "h_ex3", [128, SEG], F32)
                rmask = p.sb(es, "h_rm", [128, SEG], F32)
                qdec = [p.sb(es, f"h_qd{i}", [128, SEG], BF16) for i in range(2)]
                kdec = p.sb(es, "h_kd", [128, SEG], BF16)
                kend = p.sb(es, "h_ke", [128, SEG], BF16)
                kendT = p.sb(es, "h_keT", [64, NCH, 128], BF16)
                vt = p.sb(es, "h_v", [64, NCH, 128], BF16)
                sct = p.sb(es, "h_sc", [64, NCH, 64], BF16)
                sc1 = p.sb(es, "h_sc1", [64, NCH, 64], F32)
                el = p.sb(es, "h_el", [128, 2, NCH], F32)
                dSs = [[p.sb(es, f"h_dS{i}{j}", [128, 4, 128], F32) for j in range(2)] for i in range(2)]
                hz = {}
                m1 = p.G([], "memset", rmask[:], 1.0)
                m2 = p.G([], "memset", rmask[:].rearrange("p (n s) -> p n s", s=64)[:, :, 0:1], 0.0)
                p.barrier()
                for sg in range(nseg):
                    t0 = sg * SEG
                    tq = p.ld(qr[:], qrT.ap()[h * 128:(h + 1) * 128, t0:t0 + SEG], "hq")
                    tv = p.ld(vt[:], vtok.ap()[t0:t0 + SEG, h * 128:(h + 1) * 128].rearrange("(n s) v -> s n v", s=64),
                              "hv")
                    aq = p.A([tq], out=qr[:], in_=qr[:], func=AF.Silu)
                    for d in range(2):
                        src = ffT if d == 0 else fbT
                        tf = p.ld(fr[:], src.ap()[h * 128:(h + 1) * 128, t0:t0 + SEG], "hf")
                        a1 = p.A([tf], out=f_[:], in_=fr[:], func=AF.Sigmoid)
                        v1 = p.V([a1], "tensor_scalar", f_[:], f_[:], lbt[:, 2 + d, h:h + 1], lbt[:, d, h:h + 1],
                                 ALU.mult, ALU.add)
                        a2 = p.A([v1], out=g_[:], in_=f_[:], func=AF.Ln)
                        g1 = p.G([v1], "tensor_scalar", k_[:], f_[:], -1.0, 1.0, ALU.mult, ALU.add)
                        v2 = p.V([a2], "tensor_tensor_scan", cum[:], rmask[:], g_[:], 0.0, ALU.mult, ALU.add)
                        cum3 = cum[:].rearrange("p (n s) -> p n s", s=64)
                        lastb = bass.AP(cum, 63, [[SEG, 128], [64, NCH], [0, 64]])
                        if d == 0:
                            cc = cum
                            v3 = p.V([v2], "tensor_tensor", cb[:].rearrange("p (n s) -> p n s", s=64), lastb, cum3,
                                     ALU.subtract)
                            dl = cb
                        else:
                            v3a = p.V([v2], "tensor_tensor", cb[:].rearrange("p (n s) -> p n s", s=64), lastb, cum3,
                                      ALU.subtract)
                            v3b = p.V([v3a], "tensor_tensor", cb[:], cb[:], g_[:], ALU.add)
                            cc = cb
                            v3 = p.V([v3b], "tensor_tensor", g_[:], cum[:], g_[:], ALU.subtract)
                            dl = g_
                        a3 = p.A([v3], out=ex[:], in_=cc[:], func=AF.Exp)
                        a4 = p.A([v3], out=ex2[:], in_=cc[:], func=AF.Exp, scale=-1.0)
                        a5 = p.A([v3], out=ex3[:], in_=dl[:], func=AF.Exp)
                        g2 = p.V([a3, aq], "tensor_tensor", qdec[d][:], qr[:], ex[:], ALU.mult)
                        g3 = p.G([a4, g1], "tensor_tensor", kdec[:], k_[:], ex2[:], ALU.mult)
                        g4 = p.V([a5, g1], "tensor_tensor", kend[:], k_[:], ex3[:], ALU.mult)
                        a6 = p.A([v2], out=el[:, d, :], in_=cum3[:, :, 63], func=AF.Exp)
                        sq = p.st(qd.ap()[d, :, t0:t0 + SEG], qdec[d][:], "hsq", deps=[g2])
                        se = p.st(eld.ap()[d, :, sg * NCH:(sg + 1) * NCH], el[:, d, :], "hse", deps=[a6])
                        nb = min(8, NCH)
                        NGR = NCH // nb
                        mk_ap = bass.AP(hm, d * 64, [[128, 64], [0, nb], [1, 64]])
                        for gq in range(NGR + 1):
                            if gq < NGR:
                                n0 = gq * nb
                                par = gq % 2
                                key = ("sc", par)
                                for j in range(nb):
                                    sl = slice((n0 + j) * 64, (n0 + j + 1) * 64)
                                    tk = p.mm([g2, g3, hz.get(key)] if j == 0 else [], p.ps[par][0:64, j * 64:(j + 1) * 64],
                                              kdec[:, sl], qdec[d][:, sl], True, True, mark=(j == nb - 1))
                                psv = p.ps[par][0:64, 0:nb * 64].rearrange("p (n s) -> p n s", s=64)
                                if d == 0:
                                    ev = p.V([tk], "tensor_tensor", sc1[:, n0:n0 + nb, :], psv, mk_ap, ALU.mult)
                                else:
                                    ev0 = p.V([tk], "tensor_tensor", sct[:, n0:n0 + nb, :], psv, mk_ap, ALU.mult)
                                    ev = p.V([ev0], "tensor_tensor", sct[:, n0:n0 + nb, :], sct[:, n0:n0 + nb, :],
                                             sc1[:, n0:n0 + nb, :], ALU.add)
                                hz[key] = ev
                                key = ("tr", par)
                                for j in range(nb):
                                    sl = slice((n0 + j) * 64, (n0 + j + 1) * 64)
                                    tk2 = p.tr([g4, hz.get(key)] if j == 0 else [], p.pb[par][0:64, j * 128:(j + 1) * 128],
                                               kend[:, sl], ident[:], mark=(j == nb - 1))
                                ev2 = p.A([tk2, hz.get(("ds_mm", par))], out=kendT[:, n0:n0 + nb, :],
                                          in_=p.pb[par][0:64, 0:nb * 128].rearrange("p (n k) -> p n k", k=128), func=AF.Copy)
                                hz[key] = ev2
                                hz[("kT", par)] = ev2
                            if gq >= 1:
                                gprev = gq - 1
                                n0 = gprev * nb
                                par = gprev % 2
                                nbk = (nb + 3) // 4
                                for j in range(nb):
                                    bank = p.ps[2 + 2 * par + j // 4]
                                    tk3 = p.mm([hz[("kT", par)], tv, hz.get(("dsb", par, j // 4))] if j % 4 == 0 else [],
                                               bank[:, (j % 4) * 128:(j % 4 + 1) * 128], kendT[:, n0 + j, :], vt[:, n0 + j, :],
                                               True, True, mark=(j % 4 == 3 or j == nb - 1))
                                    if j % 4 == 3 or j == nb - 1:
                                        i4 = j // 4
                                        w4 = j % 4 + 1
                                        dst = dSs[par][i4][:, 0:w4, :]
                                        src = bank[:, 0:w4 * 128].rearrange("p (n v) -> p n v", v=128)
                                        if i4 == 0:
                                            ev3 = p.A([tk3, hz.get(("dss", par, i4))], out=dst, in_=src, func=AF.Copy)
                                        else:
                                            ev3 = p.V([tk3, hz.get(("dss", par, i4))], "tensor_copy", dst, src)
                                        hz[("dsb", par, i4)] = ev3
                                        c0 = sg * NCH + n0 + i4 * 4
                                        hz[("dss", par, i4)] = p.st(
                                            dS.ap()[d, c0:c0 + w4, :, :].rearrange("n p v -> p n v"), dst, f"hds{par}{i4}",
                                            deps=[ev3])
                                hz[("ds_mm", par)] = tk3
                        if d == 1:
                            p.st(scd.ap()[:, sg * NCH:(sg + 1) * NCH, :], sct[:], "hsc", deps=[ev])
                        p.barrier()
                        hz.clear()
            with contextlib.ExitStack() as es:
                G = min(NT, 32)
                NG = NT // G
                dsl = [p.sb(es, f"h2_ds{i}", [128, G, 128], F32) for i in range(2)]
                sall = [p.sb(es, f"h2_sa{i}", [128, G + 1, 128], F32) for i in range(2)]
                sbo = [p.sb(es, f"h2_sb{i}", [128, G, 128], BF16) for i in range(2)]
                ela = p.sb(es, "h2_el", [128, 2, NT], F32)
                te = p.ld(ela[:], eld.ap().rearrange("d p n -> p d n"), "h2e")
                z0 = p.V([], "memset", sall[0][:, 0, :], 0.0)
                z1 = p.V([], "memset", sall[1][:, G, :], 0.0)
                last = [z0, z1]
                for gi in range(NG):
                    gf, gb = gi, NG - 1 - gi
                    tl = [p.ld(dsl[0][:], dS.ap()[0, gf * G:(gf + 1) * G, :, :].rearrange("n p v -> p n v"), "h2l0"),
                          p.ld(dsl[1][:], dS.ap()[1, gb * G:(gb + 1) * G, :, :].rearrange("n p v -> p n v"), "h2l1")]
                    for i in range(G):
                        nf = gf * G + i
                        last[0] = p.V([tl[0], te, last[0]], "scalar_tensor_tensor", sall[0][:, i + 1, :], sall[0][:, i, :],
                                      ela[:, 0, nf:nf + 1], dsl[0][:, i, :], ALU.mult, ALU.add)
                        j = G - 1 - i
                        nbk = gb * G + j
                        last[1] = p.V([tl[1], te, last[1]], "scalar_tensor_tensor", sall[1][:, j, :], sall[1][:, j + 1, :],
                                      ela[:, 1, nbk:nbk + 1], dsl[1][:, j, :], ALU.mult, ALU.add)
                    c0 = p.G(last, "tensor_copy", sbo[0][:], sall[0][:, 0:G, :])
                    c1 = p.A(last, out=sbo[1][:], in_=sall[1][:, 1:G + 1, :], func=AF.Copy)
                    p.st(Sb.ap()[0, gf * G:(gf + 1) * G, :, :].rearrange("n p v -> p n v"), sbo[0][:], "h2s0", deps=[c0])
                    p.st(Sb.ap()[1, gb * G:(gb + 1) * G, :, :].rearrange("n p v -> p n v"), sbo[1][:], "h2s1", deps=[c1])
                    last[0] = p.V([c0, c1] + last, "tensor_copy", sall[0][:, 0, :], sall[0][:, G, :])
                    last[1] = p.V([last[0]], "tensor_copy", sall[1][:, G, :], sall[1][:, 0, :])
                    p.barrier()
            with contextlib.ExitStack() as es:
                G = min(NT, 32)
                qd3 = p.sb(es, "h3_qd", [128, 2, G * 64], BF16)
                sc3 = p.sb(es, "h3_sc", [64, G, 64], BF16)
                v3_ = p.sb(es, "h3_v", [64, G, 128], BF16)
                gt3 = p.sb(es, "h3_g", [64, G, 128], F32)
                s3 = p.sb(es, "h3_s", [128, 2, G, 128], BF16)
                o3 = p.sb(es, "h3_o", [64, G, 128], F32)
                ob3 = p.sb(es, "h3_ob", [64, G, 128], BF16)
                sq3 = p.sb(es, "h3_sq", [64, G, 128], F32)
                ss = p.sb(es, "h3_ss", [64, G], F32)
                for gi in range(NT // G):
                    t0 = gi * G * 64
                    tl = [p.ld(qd3[:], qd.ap()[:, :, t0:t0 + G * 64].rearrange("d p t -> p d t"), "h3a"),
                          p.ld(sc3[:], scd.ap()[:, gi * G:(gi + 1) * G, :], "h3a"),
                          p.ld(v3_[:], vtok.ap()[t0:t0 + G * 64, h * 128:(h + 1) * 128].rearrange("(n s) v -> s n v", s=64), "h3a"),
                          p.ld(gt3[:], gates.ap()[t0:t0 + G * 64, 1024 + h * 128:1024 + (h + 1) * 128].rearrange("(n s) v -> s n v", s=64), "h3a"),
                          p.ld(s3[:, 0], Sb.ap()[0, gi * G:(gi + 1) * G, :, :].rearrange("n p v -> p n v"), "h3a"),
                          p.ld(s3[:, 1], Sb.ap()[1, gi * G:(gi + 1) * G, :, :].rearrange("n p v -> p n v"), "h3a")]
                    ag = p.A([tl[3]], out=gt3[:], in_=gt3[:], func=AF.Silu)
                    prev = [None, None]
                    nb3 = min(4, G)
                    for j0 in range(0, G, nb3):
                        pk = (j0 // nb3) % 2
                        for jj in range(nb3):
                            j = j0 + jj
                            ps = p.ps[pk][0:64, jj * 128:(jj + 1) * 128]
                            sl = slice(j * 64, (j + 1) * 64)
                            p.mm(tl + [prev[pk]] if jj == 0 else [], ps, sc3[:, j, :], v3_[:, j, :], True, False)
                            p.mm([], ps, qd3[:, 0, sl], s3[:, 0, j, :], False, False)
                            tk = p.mm([], ps, qd3[:, 1, sl], s3[:, 1, j, :], False, True, mark=(jj == nb3 - 1))
                        ev = p.V([tk], "tensor_copy", o3[:, j0:j0 + nb3, :],
                                 p.ps[pk][0:64, 0:nb3 * 128].rearrange("p (n v) -> p n v", v=128))
                        prev[pk] = ev
                    e2 = p.A([ev], out=sq3[:], in_=o3[:], func=AF.Square)
                    e3 = p.V([e2], "tensor_reduce", ss[:], sq3[:], mybir.AxisListType.X, ALU.add)
                    r2 = p.rsqrt([e3], ss[:], ss[:], 1.0 / 128, EPS)
                    ssb = bass.AP(ss, 0, [[G, 64], [1, G], [0, 128]])
                    r3 = p.V([r2], "tensor_tensor", o3[:], o3[:], ssb, ALU.mult)
                    ghb = bass.AP(ghg_sb, 0, [[128, 64], [0, G], [1, 128]])
                    r4 = p.V([r3], "tensor_tensor", o3[:], o3[:], ghb, ALU.mult)
                    r5 = p.V([r4, ag], "tensor_tensor", ob3[:], o3[:], gt3[:], ALU.mult)
                    p.st(ymix.ap()[t0:t0 + G * 64, 1024 + h * 128:1024 + (h + 1) * 128].rearrange("(n s) v -> s n v", s=64),
                         ob3[:], "h3s", deps=[r5])
                    p.barrier()
        p.barrier()

    def outproj(L, ymix, ya, gates, wbo, gpost, xres, xout, hT_next):
        NTL = L // 128
        with contextlib.ExitStack() as es:
            wsb = p.sb(es, "op_w", [128, KC, D], BF16)
            tw = p.ld(wsb[:], wbo.ap().rearrange("(k p) n -> p k n", p=128), "opw")
            ym = [p.sb(es, f"op_ym{i}", [128, D], BF16) for i in range(2)]
            yaf = [p.sb(es, f"op_ya{i}", [128, 1024], F32) for i in range(2)] if ya is not None else None
            gaf = [p.sb(es, f"op_ga{i}", [128, 1024], F32) for i in range(2)] if ya is not None else None
            ymT = [p.sb(es, f"op_ymT{i}", [128, KC, 128], BF16) for i in range(2)]
            xr = [p.sb(es, f"op_x{i}", [128, D], F32) for i in range(2)]
            yo = [p.sb(es, f"op_y{i}", [128, D], F32) for i in range(2)]
            junk = p.sb(es, "op_j", [128, D], BF16)
            st_ = [p.sb(es, f"op_st{i}", [128, 4], F32) for i in range(2)]
            hb = [p.sb(es, f"op_hb{i}", [128, D], BF16) for i in range(2)]
            ho = [p.sb(es, f"op_ho{i}", [128, KC, 128], BF16) for i in range(2)]
            T = {}
            g = lambda k, t: T.get((k, t))
            pbv = [p.pb[i][:].rearrange("p (k t) -> p k t", k=8) for i in range(2)]
            for ti in range(NTL + 1):
                if ti < NTL:
                    b = ti % 2
                    rows = slice(ti * 128, (ti + 1) * 128)
                    T[("tx", ti)] = p.ld(xr[b][:], xres[rows, :], f"opx{b}", deps=[g("v5", ti - 2)])
                    if ya is not None:
                        t1 = p.ld(ym[b][:, 1024:2048], ymix.ap()[rows, 1024:2048], f"opm{b}", deps=[g("tr", ti - 2)])
                        t2 = p.ld(yaf[b][:], ya.ap()[rows, :], f"opa{b}", deps=[g("v1", ti - 2)])
                        t3 = p.ld(gaf[b][:], gates.ap()[rows, 0:1024], f"opa{b}", deps=[g("v1", ti - 2)])
                        a1 = p.A([t3], out=gaf[b][:], in_=gaf[b][:], func=AF.Silu)
                        v1 = p.V([a1, t2, g("tr", ti - 2)], "tensor_tensor", ym[b][:, 0:1024], yaf[b][:], gaf[b][:], ALU.mult)
                        T[("v1", ti)] = v1
                        rdy = [t1, v1]
                    else:
                        rdy = [p.ld(ym[b][:], ymix.ap()[rows, :], f"opm{b}", deps=[g("tr", ti - 2)])]
                    for kc in range(KC):
                        tk = p.tr(rdy + [g("c1h", ti - 2), g("c2h", ti - 2)] if kc == 0 else [],
                                  p.pb[kc // 8][:, (kc % 8) * 128:(kc % 8 + 1) * 128],
                                  ym[b][:, kc * 128:(kc + 1) * 128], ident[:], mark=(kc == KC - 1))
                    T[("tr", ti)] = tk
                    c1 = p.A([tk, g("mm", ti - 2)], out=ymT[b][:, 0:8, :], in_=pbv[0], func=AF.Copy)
                    c2 = p.V([tk, g("mm", ti - 2)], "tensor_copy", ymT[b][:, 8:16, :], pbv[1])
                    T[("c1", ti)], T[("c2", ti)] = c1, c2
                    for cb in range(4):
                        for kc in range(KC):
                            tk = p.mm([c1, c2, tw] + [T.get(("ev", ti - 1, i)) for i in range(4)] if (kc == 0 and cb == 0) else [],
                                      p.ps[cb][:], ymT[b][:, kc, :],
                                      wsb[:, kc, cb * 512:(cb + 1) * 512], kc == 0, kc == KC - 1, mark=(kc == KC - 1))
                    T[("mm", ti)] = tk
                if ti >= 1 and hT_next is not None:
                    tj = ti - 1
                    bj = tj % 2
                    rows_j = slice(tj * 128, (tj + 1) * 128)
                    for kc in range(KC):
                        tk2 = p.tr([g("v8", tj), g("c1", ti), g("c2", ti)] if kc == 0 else [],
                                   p.pb[kc // 8][:, (kc % 8) * 128:(kc % 8 + 1) * 128],
                                   hb[bj][:, kc * 128:(kc + 1) * 128], ident[:], mark=(kc == KC - 1))
                    T[("trh", tj)] = tk2
                    c1h = p.A([tk2, g("sh", tj - 2)], out=ho[bj][:, 0:8, :], in_=pbv[0], func=AF.Copy)
                    c2h = p.V([tk2, g("sh", tj - 2)], "tensor_copy", ho[bj][:, 8:16, :], pbv[1])
                    T[("c1h", tj)], T[("c2h", tj)] = c1h, c2h
                    T[("sh", tj)] = p.st(hT_next.ap().rearrange("(k p) t -> p k t", p=128)[:, :, rows_j], ho[bj][:],
                                         f"oph{bj}", deps=[c1h, c2h])
                if ti < NTL:
                    tk = T[("mm", ti)]
                    evs = []
                    for cb in range(4):
                        dep = [tk, g("so", ti - 2), g("v8", ti - 2)]
                        if cb % 2 == 0:
                            ev = p.A(dep, out=yo[b][:, cb * 512:(cb + 1) * 512], in_=p.ps[cb][:], func=AF.Copy)
                        else:
                            ev = p.V(dep, "tensor_copy", yo[b][:, cb * 512:(cb + 1) * 512], p.ps[cb][:])
                        T[("ev", ti, cb)] = ev
                        evs.append(ev)
                    a2 = p.A(evs, out=junk[:], in_=yo[b][:], func=AF.Square, accum_out=st_[b][:, 0:1])
                    v3 = p.rsqrt([a2], st_[b][:, 1:2], st_[b][:, 0:1], 1.0 / D, EPS)
                    v4 = p.V([v3] + evs, "scalar_tensor_tensor", yo[b][:], yo[b][:], st_[b][:, 1:2], gpost[:], ALU.mult, ALU.mult)
                    v5 = p.V([v4, T[("tx", ti)]], "tensor_tensor", yo[b][:], yo[b][:], xr[b][:], ALU.add)
                    T[("v5", ti)] = v5
                    T[("so", ti)] = p.st(xout[rows, :], yo[b][:], f"opo{b}", deps=[v5])
                    if hT_next is not None:
                        a3 = p.A([v5], out=junk[:], in_=yo[b][:], func=AF.Square, accum_out=st_[b][:, 2:3])
                        v7 = p.rsqrt([a3], st_[b][:, 3:4], st_[b][:, 2:3], 1.0 / D, EPS)
                        T[("v8", ti)] = p.V([v7, g("trh", ti - 2)], "tensor_scalar", hb[b][:], yo[b][:], st_[b][:, 3:4], None, ALU.mult)
        p.barrier()

    def attention(qT, kT, vtok, gates, Lq, Lk, og, dtab, lam_sb, gsub_sb, delta):
        NKT, NQB = Lk // 128, Lq // 256
        SK = 2
        with contextlib.ExitStack() as es:
            qs = p.sb(es, "at_q", [128, 2, Lq], BF16)
            ks = p.sb(es, "at_k", [128, 2, Lk], BF16)
            vs = p.sb(es, "at_v", [128, NKT, 257], BF16)
            absd = [p.sb(es, f"at_ad{i}", [128, 256], F32) for i in range(3)]
            sb_ = [p.sb(es, f"at_s{i}", [128, 256], F32) for i in range(4)]
            pt = [p.sb(es, f"at_p{i}", [128, 256], BF16) for i in range(4)]
            gt = [p.sb(es, f"at_g{i}", [128, 2, 256], F32) for i in range(2)]
            o0 = [p.sb(es, f"at_o0{i}", [128, 2, 256], F32) for i in range(2)]
            o1 = [p.sb(es, f"at_o1{i}", [128, 2, 256], F32) for i in range(2)]
            rs = [p.sb(es, f"at_rs{i}", [128, 8], F32) for i in range(2)]
            ob = [p.sb(es, f"at_ob{i}", [128, 2, 256], BF16) for i in range(2)]
            jk = p.sb(es, "at_j", [128, 256], F32)
            Sps = [p.ps[i][:, 0:256] for i in range(2)]
            acc = [[p.ps[2 + 2 * c + j] for j in range(2)] for c in range(2)]
            qbi = 0
            gt_free = [[], []]
            ob_free = [None, None]
            for h in range(8):
                slope = SLOPES[h]
                t_in = [p.ld(qs[:], qT.ap()[h * 256:(h + 1) * 256, :].rearrange("(c p) t -> p c t", p=128), "atq"),
                        p.ld(ks[:], kT.ap()[h * 256:(h + 1) * 256, :].rearrange("(c p) t -> p c t", p=128), "atq"),
                        p.ld(vs[:, :, 0:256], vtok.ap()[:, h * 256:(h + 1) * 256].rearrange("(n p) v -> p n v", p=128),
                             "atq")]
                t_in.append(p.G([], "memset", vs[:, :, 256:257], 1.0))
                acc_free = []
                for qb in range(NQB):
                    par = qbi % 2
                    qbi += 1
                    q0 = qb * 256
                    tg = p.ld(gt[par][:],
                              gates.ap()[q0:q0 + 256, h * 256:(h + 1) * 256].rearrange("(j p) v -> p j v", p=128),
                              f"atg{par}", deps=gt_free[par])
                    units = [(kt, c) for kt in range(NKT) for c in range(2)]
                    NU = len(units)
                    rd_ad = [None] * 3
                    rd_s = [None] * 4
                    rd_p = [None] * 4
                    rd_ps = [None] * 2
                    absT = {}
                    expT = {}
                    state = {"lastpv": None}

                    def do_abs(kt):
                        ab = kt % 3
                        idx = kt * NQB + qb
                        absT[kt] = p.A([rd_ad[ab]], out=absd[ab][:], in_=delta[:], func=AF.Abs, bias=dtab[:, idx:idx + 1])

                    def front(u):
                        kt, c = units[u]
                        b = u % 4
                        if c == 0:
                            if kt == 0:
                                do_abs(0)
                            if kt + 1 < NKT:
                                do_abs(kt + 1)
                        b2 = u % 2
                        tk = p.mm(t_in + [rd_ps[b2]], Sps[b2], ks[:, c, kt * 128:(kt + 1) * 128], qs[:, c, q0:q0 + 256],
                                  True, True, mark=True)
                        v1 = p.V([tk, absT[kt], rd_s[b]], "scalar_tensor_tensor", sb_[b][:], absd[kt % 3][:], -slope,
                                 Sps[b2], ALU.mult, ALU.add)
                        rd_ps[b2] = v1
                        rd_ad[kt % 3] = v1
                        a1 = p.A([v1, rd_p[b]], out=pt[b][:], in_=sb_[b][:], func=AF.Exp)
                        rd_s[b] = a1
                        expT[u] = a1

                    def back(u):
                        kt, c = units[u]
                        b = u % 4
                        for j in range(2):
                            deps = [expT[u]] if j == 0 else []
                            if kt == 0:
                                deps = deps + acc_free
                            state["lastpv"] = p.mm(deps, acc[c][j][:, 0:257], pt[b][:, j * 128:(j + 1) * 128],
                                                   vs[:, kt, :], kt == 0, kt == NKT - 1, mark=(j == 1))
                        rd_p[b] = state["lastpv"]

                    for u in range(NU + SK):
                        if u < NU:
                            front(u)
                        if u >= SK:
                            back(u - SK)
                    lastpv = state["lastpv"]
                    evs = []
                    for c in range(2):
                        for j in range(2):
                            dst = (o0[par] if c == 0 else o1[par])
                            e1 = p.V([lastpv], "reciprocal", rs[par][:, 2 * c + j:2 * c + j + 1], acc[c][j][:, 256:257])
                            if c == 0:
                                evs.append(p.V([e1], "tensor_scalar", dst[:, j, :], acc[c][j][:, 0:256],
                                               rs[par][:, 2 * c + j:2 * c + j + 1], None, ALU.mult))
                            else:
                                evs.append(p.V([e1], "tensor_scalar", dst[:, j, :], acc[c][j][:, 0:256],
                                               rs[par][:, 2 * c + j:2 * c + j + 1], lam_sb[:, 1:2], ALU.mult, ALU.mult))
                    acc_free = [evs[-1]]
                    f1 = p.V(evs, "tensor_tensor", o0[par][:], o0[par][:], o1[par][:], ALU.add)
                    for j in range(2):
                        sqt = p.A([f1], out=jk[:], in_=o0[par][:, j, :], func=AF.Square, accum_out=rs[par][:, 4 + j:5 + j])
                    f2 = p.rsqrt([sqt], rs[par][:, 4:6], rs[par][:, 4:6], 1.0 / 256, EPS)
                    f3 = p.V([f2], "tensor_scalar", rs[par][:, 4:6], rs[par][:, 4:6], (1.0 - LAMBDA_INIT), None, ALU.mult)
                    ag = p.A([tg], out=gt[par][:], in_=gt[par][:], func=AF.Silu)
                    for j in range(2):
                        f4 = p.V([f3], "scalar_tensor_tensor", o0[par][:, j, :], o0[par][:, j, :], rs[par][:, 4 + j:5 + j],
                                 gsub_sb[:], ALU.mult, ALU.mult)
                    f5 = p.V([f4, ag, ob_free[par]], "tensor_tensor", ob[par][:], o0[par][:], gt[par][:], ALU.mult)
                    ob_free[par] = p.st(og.ap()[q0:q0 + 256, h * 256:(h + 1) * 256].rearrange("(j p) v -> p j v", p=128),
                                        ob[par][:], f"ato{par}", deps=[f5])
                    gt_free[par] = [f5, ag]
                p.barrier()
                gt_free = [[], []]
                ob_free = [None, None]
        p.barrier()

    c128 = p.sb(ges, "c128_sb", [128, 3, 128], BF16)
    p.ld(c128[:], c128_d.ap().rearrange("k p c -> p k c"), "c0")
    cs256 = p.sb(ges, "cs256_sb", [128, 2, 512], BF16)
    p.ld(cs256[:], cs256_d.ap().rearrange("(c p) n -> p c n", p=128), "c0")
    tw_sb, cm_sb = {}, {}
    for L in tw_d:
        M = L // 128
        tw_sb[L] = p.sb(ges, f"tw{L}_sb", [128, 3, M], F32)
        p.ld(tw_sb[L][:], tw_d[L].ap().rearrange("k p m -> p k m"), "c0")
        cm_sb[L] = p.sb(ges, f"cm{L}_sb", [M, 2, M], BF16)
        p.ld(cm_sb[L][:], cm_d[L].ap().rearrange("k b d -> b k d"), "c0")
    hm = p.sb(ges, "hm_sb", [64, 2, 64], F32)
    p.ld(hm[:], hmask_d.ap().rearrange("k s t -> s k t"), "c0")
    delta = p.sb(ges, "delta_sb", [128, 256], F32)
    p.ld(delta[:], delta_d.ap(), "c0")
    dtabs = p.sb(ges, "dtabs_sb", [128, (LS // 128) * (LS // 256)], F32)
    p.ld(dtabs[:], dtabs_d.ap(), "c0")
    dtabp = p.sb(ges, "dtabp_sb", [128, (LP // 128) * (OWN // 256)], F32)
    p.ld(dtabp[:], dtabp_d.ap(), "c0")
    ownidx = p.sb(ges, "ownidx_sb", [128, OWN // 128], I32)
    p.ld(ownidx[:], ownidx_d.ap(), "c0")
    ghg_sb = p.sb(ges, "ghg_sb", [64, 128], F32)
    p.ld(ghg_sb[:], bcast_rows(ghg.ap(), 128, 64), "c0")
    gsub_sb = p.sb(ges, "gsub_sb", [128, 256], F32)
    p.ld(gsub_sb[:], bcast_rows(gsub.ap(), 256), "c0")
    lraw = p.sb(ges, "lraw_sb", [128, 2, 3, 8], F32)
    p.ld(lraw[:], lbl.ap().rearrange("d s (h k) -> k d s h", k=128), "c0", slow=True)
    lbt = p.sb(ges, "lbt_sb", [128, 4, 8], F32)
    lsum = p.sb(ges, "lsum_sb", [128, 2, 8], F32)
    lamv = p.sb(ges, "lamv_sb", [128, 4, 128], F32)
    p.ld(lamv[:], bass.AP(lam4.ap().tensor, 0, [[0, 128], [128, 4], [1, 128]]), "c0")
    lam_sb = p.sb(ges, "lam_sb", [128, 4], F32)
    ljunk = p.sb(ges, "ljunk_sb", [128, 128], F32)
    p.barrier()
    a = p.A([], out=lraw[:], in_=lraw[:], func=AF.Exp)
    v = p.V([a], "tensor_tensor", lsum[:], lraw[:, :, 0, :], lraw[:, :, 1, :], ALU.add)
    v = p.V([v], "tensor_tensor", lsum[:], lsum[:], lraw[:, :, 2, :], ALU.add)
    v = p.V([v], "reciprocal", lsum[:], lsum[:])
    v = p.V([v], "tensor_tensor", lbt[:, 0:2, :], lraw[:, :, 0, :], lsum[:], ALU.mult)
    v = p.V([v], "tensor_scalar", lbt[:, 2:4, :], lbt[:, 0:2, :], -1.0, 1.0, ALU.mult, ALU.add)
    v = p.V([v], "tensor_tensor", ljunk[:], lamv[:, 0, :], lamv[:, 1, :], ALU.mult)
    v = p.V([v], "tensor_reduce", lam_sb[:, 2:3], ljunk[:], mybir.AxisListType.X, ALU.add)
    v = p.V([v], "tensor_tensor", ljunk[:], lamv[:, 2, :], lamv[:, 3, :], ALU.mult)
    v = p.V([v], "tensor_reduce", lam_sb[:, 3:4], ljunk[:], mybir.AxisListType.X, ALU.add)
    a = p.A([v], out=lam_sb[:, 2:4], in_=lam_sb[:, 2:4], func=AF.Exp)
    v = p.V([a], "tensor_tensor", lam_sb[:, 0:1], lam_sb[:, 2:3], lam_sb[:, 3:4], ALU.subtract)
    v = p.V([v], "tensor_scalar", lam_sb[:, 1:2], lam_sb[:, 0:1], LAMBDA_INIT, -1.0, ALU.add, ALU.mult)
    p.barrier()

    seqs = [("s%d" % i, LS, xs.ap()[i * LS:(i + 1) * LS, :], ys.ap()[i * LS:(i + 1) * LS, :]) for i in range(NS)]
    seqs.append(("p", LP, xp.ap(), None))
    scale_q = 128 ** -0.5
    for (nm, L, xin, yout) in seqs:
        isP = yout is None
        hT = p.dram(f"hT_{nm}", [D, L], BF16)
        norm_T(xin, L, hT)
        done("norm")
        uT = p.dram(f"uT_{nm}", [1024, L], BF16)
        gat0 = p.dram(f"g0_{nm}", [L, 2048], F32)
        qrT = p.dram(f"qr_{nm}", [1024, L], F32)
        ffT = p.dram(f"ff_{nm}", [1024, L], F32)
        fbT = p.dram(f"fb_{nm}", [1024, L], F32)
        vtk = p.dram(f"vt_{nm}", [L, 1024], BF16)
        proj(hT, L, wb0i, 7168, [
            (0, 1024, "F", uT, 0, 1.0, BF16),
            (1024, 1024, "T", gat0, 0, 1.0, F32),
            (2048, 1024, "F", qrT, 0, 1.0, F32),
            (3072, 1024, "T", vtk, 0, 1.0, BF16),
            (4096, 1024, "F", ffT, 0, 1.0, F32),
            (5120, 1024, "F", fbT, 0, 1.0, F32),
            (6144, 1024, "T", gat0, 1024, 1.0, F32),
        ])
        done("proj0")
        ya = p.dram(f"ya_{nm}", [L, 1024], F32)
        fnet(uT, L, ya, (c128, cs256, tw_sb[L], cm_sb[L]))
        done("fnet")
        ymix = p.dram(f"ym_{nm}", [L, 2048], BF16)
        hgrn(qrT, ffT, fbT, vtk, gat0, L, ymix, lbt, hm, ghg_sb)
        done("hgrn")
        x1 = p.dram(f"x1_{nm}", [L, D], F32)
        h1T = p.dram(f"h1T_{nm}", [D, L], BF16)
        outproj(L, ymix, ya, gat0, wb0o, gpost_sb[0], xin, x1.ap(), h1T)
        done("out0")
        kT = p.dram(f"kT_{nm}", [2048, L], BF16)
        v1t = p.dram(f"v1_{nm}", [L, 2048], BF16)
        if not isP:
            Lq = L
            qT = p.dram(f"qT_{nm}", [2048, Lq], BF16)
            gat1 = p.dram(f"g1_{nm}", [Lq, 2048], F32)
            proj(h1T, L, wb1i, 8192, [
                (0, 2048, "F", qT, 0, scale_q, BF16),
                (2048, 2048, "F", kT, 0, 1.0, BF16),
                (4096, 2048, "T", v1t, 0, 1.0, BF16),
                (6144, 2048, "T", gat1, 0, 1.0, F32),
            ])
            xres1 = x1.ap()
            dtab = dtabs
        else:
            Lq = OWN
            proj(h1T, L, wb1i, 8192, [
                (2048, 2048, "F", kT, 0, 1.0, BF16),
                (4096, 2048, "T", v1t, 0, 1.0, BF16),
            ])
            x1own = p.dram("x1own", [OWN, D], F32)
            with contextlib.ExitStack() as es:
                gx = p.sb(es, "gx", [128, D], F32)
                prev = None
                for ti in range(OWN // 128):
                    p.pool.wait(prev)
                    ds = p.dsem("gath")
                    p.nc.gpsimd.indirect_dma_start(
                        out=gx[:], out_offset=None, in_=x1.ap(),
                        in_offset=bass.IndirectOffsetOnAxis(ap=ownidx[:, ti:ti + 1], axis=0),
                    ).then_inc(ds.sem, 16)
                    ds.n += 16
                    tok = (ds, ds.n)
                    p.pending.append(tok)
                    prev = p.st(x1own.ap()[ti * 128:(ti + 1) * 128, :], gx[:], "gaths", deps=[tok])
            p.barrier()
            h1To = p.dram("h1To", [D, OWN], BF16)
            norm_T(x1own.ap(), OWN, h1To)
            qT = p.dram(f"qT_{nm}", [2048, Lq], BF16)
            gat1 = p.dram(f"g1_{nm}", [Lq, 2048], F32)
            proj(h1To, OWN, wb1i, 8192, [
                (0, 2048, "F", qT, 0, scale_q, BF16),
                (6144, 2048, "T", gat1, 0, 1.0, F32),
            ])
            xres1 = x1own.ap()
            yout = yp.ap()
            dtab = dtabp
        done("proj1")
        og = p.dram(f"og_{nm}", [Lq, 2048], BF16)
        attention(qT, kT, v1t, gat1, Lq, L, og, dtab, lam_sb, gsub_sb, delta)
        done("attn")
        outproj(Lq, og, None, None, wb1o, gpost_sb[1], xres1, yout, None)
        done("seq0")


def dft_cs(n):
    j = np.arange(n)
    ang = 2 * np.pi * np.outer(j, j) / n
    return np.cos(ang), np.sin(ang)


def host_tables(cfg, core):
    NS, LS, LP, OWN = cfg["NS"], cfg["LS"], cfg["LP"], cfg["OWN"]
    t = {}
    t["ident"] = np.eye(128, dtype=np.float32).astype(NPBF)
    c, s = dft_cs(128)
    t["c128"] = np.stack([c, s, -s]).astype(np.float32).astype(NPBF)
    c, s = dft_cs(256)
    t["cs256"] = np.concatenate([c, -s], axis=1).astype(np.float32).astype(NPBF)
    for L in sorted({LS, LP}):
        M = L // 128
        ang = 2 * np.pi * np.outer(np.arange(128), np.arange(M)) / L
        t[f"tw{L}"] = np.stack([np.cos(ang), np.sin(ang), -np.sin(ang)]).astype(np.float32)
        cm, sm = dft_cs(M)
        sc = 1.0 / math.sqrt(L * 256)
        t[f"cm{L}"] = np.stack([cm * sc, sm * sc]).astype(np.float32).astype(NPBF)
    s_, t_ = np.meshgrid(np.arange(64), np.arange(64), indexing="ij")
    t["hmask"] = np.stack([(s_ <= t_), (s_ >= t_)]).astype(np.float32)
    t["delta"] = (np.arange(128)[:, None] - np.arange(256)[None, :]).astype(np.float32)
    kt, qb = np.meshgrid(np.arange(LS // 128), np.arange(LS // 256), indexing="ij")
    t["dtabs"] = np.broadcast_to((kt * 128 - qb * 256).reshape(1, -1), (128, kt.size)).astype(np.float32).copy()
    kt, qb = np.meshgrid(np.arange(LP // 128), np.arange(OWN // 256), indexing="ij")
    t["dtabp"] = np.broadcast_to((kt * 128 - (core * OWN + qb * 256)).reshape(1, -1), (128, kt.size)).astype(np.float32).copy()
    t["ownidx"] = (core * OWN + np.arange(OWN)).reshape(OWN // 128, 128).T.astype(np.int32).copy()
    return t


def run(inputs, cfg, ncores):
    NS, LS, LP, OWN = cfg["NS"], cfg["LS"], cfg["LP"], cfg["OWN"]
    f = lambda a: np.ascontiguousarray(np.asarray(a, dtype=np.float32))
    xsamp = f(inputs["x_sample"])
    shared = {
        "xp": f(inputs["x_prompt"])[0],
        "w0i": f(inputs["ev_w_in"])[0], "w0o": f(inputs["ev_w_out"])[0],
        "w1i": f(inputs["od_w_in"])[0], "w1o": f(inputs["od_w_out"])[0],
        "g0pre": f(inputs["ev_norm_pre"])[0], "g0post": f(inputs["ev_norm_post"])[0],
        "g1pre": f(inputs["od_norm_pre"])[0], "g1post": f(inputs["od_norm_post"])[0],
        "lbl": f(inputs["hgrn_lb_logits"]), "ghg": f(inputs["hgrn_norm"])[0],
        "lam4": np.stack([f(inputs["lambda_q1"])[0], f(inputs["lambda_k1"])[0],
                          f(inputs["lambda_q2"])[0], f(inputs["lambda_k2"])[0]]),
        "gsub": f(inputs["subln"])[0],
    }
    nc = build(cfg)
    in_maps = []
    for c in range(ncores):
        m = dict(shared)
        m["xs"] = xsamp[c * NS:(c + 1) * NS].reshape(NS * LS, D)
        m.update(host_tables(cfg, c))
        in_maps.append(m)
    res = run_bass_kernel_spmd(nc, in_maps, core_ids=list(range(ncores)))
    global LAST_RES
    LAST_RES = res.results
    y_s = np.concatenate([r["ys"].reshape(NS, LS, D) for r in res.results], axis=0)
    y_p = np.concatenate([r["yp"] for r in res.results], axis=0)[None]
    return y_p.astype(np.float32), y_s.astype(np.float32)


def kernel(**inputs):
    import os
    cfg = {"NS": 2, "LS": 2048, "LP": 8192, "OWN": 1024}
    if os.environ.get("K_STOP"):
        cfg["stop"] = os.environ["K_STOP"]
    return run(inputs, cfg, 8)
```

```python
import contextlib, math
import numpy as np
import ml_dtypes
import concourse.bass as bass
import concourse.mybir as mybir
from concourse.bass_utils import run_bass_kernel_spmd

F32, BF16, I32 = mybir.dt.float32, mybir.dt.bfloat16, mybir.dt.int32
AF = mybir.ActivationFunctionType
ALU = mybir.AluOpType
D = 2048
KC = 16
EPS = 1e-6
LAMBDA_INIT = 0.8 - 0.6 * math.exp(-0.3 * 1)
SLOPES = [2.0 ** (-8.0 * (h + 1) / 8) for h in range(8)]
NPBF = ml_dtypes.bfloat16


class StopBuild(Exception):
    pass


class Eng:
    def __init__(self, e, sem):
        self.e, self.sem, self.n, self.seen = e, sem, 0, {}

    def mark(self, ins):
        ins.then_inc(self.sem, 1)
        self.n += 1
        return (self, self.n)

    def wait(self, *toks):
        for tok in toks:
            if tok is None:
                continue
            src, n = tok
            if self.seen.get(src, 0) >= n:
                continue
            self.e.wait_ge(src.sem, n)
            self.seen[src] = n


class DSem:
    def __init__(self, sem):
        self.sem, self.n = sem, 0


class P:
    def __init__(self, cfg):
        self.cfg = cfg
        self.es = contextlib.ExitStack()
        nc = self.nc = bass.Bass("TRN2", target_bir_lowering=False)
        mk = lambda nm: self.es.enter_context(nc.semaphore(nm))
        self.pe = Eng(nc.tensor, mk("s_pe"))
        self.act = Eng(nc.scalar, mk("s_act"))
        self.dve = Eng(nc.vector, mk("s_dve"))
        self.pool = Eng(nc.gpsimd, mk("s_pool"))
        self.sp = Eng(nc.sync, mk("s_sp"))
        self.engs = [self.pe, self.act, self.dve, self.pool, self.sp]
        self.dsems = {}
        self.pending = []
        self.ndram = 0
        self.ps = [self.es.enter_context(nc.psum_tensor(f"ps{i}", [128, 512], F32)) for i in range(6)]
        self.pb = [self.es.enter_context(nc.psum_tensor(f"pb{i}", [128, 1024], BF16)) for i in range(2)]
        self.dummy = self.es.enter_context(nc.sbuf_tensor("dummy_sb", [128, 2], F32))
        self.pool.mark(nc.gpsimd.memset(self.dummy[:], 0.0))

    def sb(self, es, name, shape, dt):
        self.nsb = getattr(self, "nsb", 0) + 1
        return es.enter_context(self.nc.sbuf_tensor(f"{name}_{self.nsb}", shape, dt))

    def dram(self, name, shape, dt, kind="Internal"):
        t = self.nc.dram_tensor(name, list(shape), dt, kind=kind)
        if not hasattr(self, "named"):
            self.named = {}
        self.named[name] = (t, list(shape), dt)
        return t

    def dump(self):
        self.barrier()
        for name in self.cfg.get("dump", []):
            if name not in self.named:
                continue
            t, shape, dt = self.named[name]
            o = self.nc.dram_tensor("dbg_" + name, shape, dt, kind="ExternalOutput")
            self.ld(o.ap(), t.ap(), "dump")
        self.barrier()

    def dsem(self, key):
        if key not in self.dsems:
            self.dsems[key] = DSem(self.es.enter_context(self.nc.semaphore("d_" + key)))
        return self.dsems[key]

    def dma(self, q, out, in_, key, deps=(), slow=False):
        q.wait(*deps)
        ds = self.dsem(key)
        kw = {"allow_slow_non_contiguous": True} if slow else {}
        q.e.dma_start(out=out, in_=in_, **kw).then_inc(ds.sem, 16)
        ds.n += 16
        tok = (ds, ds.n)
        self.pending.append(tok)
        return tok

    def ld(self, out, in_, key, deps=(), slow=False):
        return self.dma(self.sp, out, in_, key, deps, slow)

    def st(self, out, in_, key, deps=()):
        return self.dma(self.pool, out, in_, key, deps)

    def barrier(self):
        toks = [(e, e.n) for e in self.engs if e.n > 0] + self.pending
        for e in self.engs:
            e.wait(*toks)
        self.pending = []

    def A(self, deps, *a, **k):
        self.act.wait(*deps)
        tok = self.act.mark(self.act.e.activation(*a, **k))
        if k.get("accum_out") is not None:
            tok = self.act.mark(self.act.e.activation(out=self.dummy[:, 1:2], in_=self.dummy[:, 0:1], func=AF.Copy))
        return tok

    def V(self, deps, fn, *a, **k):
        self.dve.wait(*deps)
        return self.dve.mark(getattr(self.dve.e, fn)(*a, **k))

    def G(self, deps, fn, *a, **k):
        self.pool.wait(*deps)
        return self.pool.mark(getattr(self.pool.e, fn)(*a, **k))

    def X(self, eng, deps, fn, *a, **k):
        eng.wait(*deps)
        return eng.mark(getattr(eng.e, fn)(*a, **k))

    def rsqrt(self, deps, out, in_, mul, add):
        a = self.A(deps, out=out, in_=in_, func=AF.Ln, scale=float(mul), bias=float(add))
        return self.A([a], out=out, in_=out, func=AF.Exp, scale=-0.5)

    def mm(self, deps, out, lhsT, rhs, start, stop, mark=False):
        self.pe.wait(*deps)
        ins = self.pe.e.matmul(out, lhsT, rhs, start=start, stop=stop)
        return self.pe.mark(ins) if mark else None

    def tr(self, deps, out, in_, ident, mark=False):
        self.pe.wait(*deps)
        ins = self.pe.e.transpose(out, in_, ident)
        return self.pe.mark(ins) if mark else None


def bcast_rows(ap_dram_1d, n, parts=128):
    return bass.AP(ap_dram_1d.tensor, ap_dram_1d.offset, [[0, parts], [1, n]])


def build(cfg):
    p = P(cfg)
    try:
        _build(cfg, p)
    except StopBuild:
        p.dump()
        return p.nc
    p.dump()
    p.es.close()
    return p.nc


def _build(cfg, p):
    NS, LS, LP, OWN = cfg["NS"], cfg["LS"], cfg["LP"], cfg["OWN"]

    def done(tag):
        if cfg.get("stop") == tag:
            raise StopBuild()
    nc = p.nc
    inp = lambda name, shape, dt=F32: nc.dram_tensor(name, list(shape), dt, kind="ExternalInput")
    xs = inp("xs", [NS * LS, D])
    xp = inp("xp", [LP, D])
    w0i, w0o = inp("w0i", [D, 7168]), inp("w0o", [D, D])
    w1i, w1o = inp("w1i", [D, 8192]), inp("w1o", [D, D])
    g0pre, g0post = inp("g0pre", [D]), inp("g0post", [D])
    g1pre, g1post = inp("g1pre", [D]), inp("g1post", [D])
    lbl = inp("lbl", [2, 3, 1024])
    ghg = inp("ghg", [128])
    lam4 = inp("lam4", [4, 128])
    gsub = inp("gsub", [256])
    ident_d = inp("ident", [128, 128], BF16)
    c128_d = inp("c128", [3, 128, 128], BF16)
    cs256_d = inp("cs256", [256, 512], BF16)
    tw_d = {L: inp(f"tw{L}", [3, 128, L // 128]) for L in sorted({LS, LP})}
    cm_d = {L: inp(f"cm{L}", [2, L // 128, L // 128], BF16) for L in sorted({LS, LP})}
    hmask_d = inp("hmask", [2, 64, 64])
    delta_d = inp("delta", [128, 256])
    dtabs_d = inp("dtabs", [128, (LS // 128) * (LS // 256)])
    dtabp_d = inp("dtabp", [128, (LP // 128) * (OWN // 256)])
    ownidx_d = inp("ownidx", [128, OWN // 128], I32)
    ys = nc.dram_tensor("ys", [NS * LS, D], F32, kind="ExternalOutput")
    yp = nc.dram_tensor("yp", [OWN, D], F32, kind="ExternalOutput")

    ges = p.es
    ident = p.sb(ges, "ident_sb", [128, 128], BF16)
    toks = [p.ld(ident[:], ident_d.ap(), "c0")]
    gpost_sb = [p.sb(ges, f"gpost{i}", [128, D], F32) for i in range(2)]
    toks.append(p.ld(gpost_sb[0][:], bcast_rows(g0post.ap(), D), "c0"))
    toks.append(p.ld(gpost_sb[1][:], bcast_rows(g1post.ap(), D), "c0"))
    gpre_sb = [p.sb(ges, f"gpre{i}", [128, KC], F32) for i in range(2)]
    toks.append(p.ld(gpre_sb[0][:], g0pre.ap().rearrange("(c p) -> p c", p=128), "c0", slow=True))
    toks.append(p.ld(gpre_sb[1][:], g1pre.ap().rearrange("(c p) -> p c", p=128), "c0", slow=True))
    p.barrier()

    def prep_w(w, ncols, gcol, name):
        wb = p.dram(name, [D, ncols], BF16)
        with contextlib.ExitStack() as es:
            CW = 1024
            wf = [p.sb(es, f"wf{i}", [128, CW], F32) for i in range(2)]
            wo = [p.sb(es, f"wo{i}", [128, CW], BF16) for i in range(2)]
            cons = [None, None]
            sts = [None, None]
            i = 0
            for kc in range(KC):
                for c0 in range(0, ncols, CW):
                    b = i % 2
                    t = p.ld(wf[b][:], w.ap()[kc * 128:(kc + 1) * 128, c0:c0 + CW], f"wl{b}", deps=[cons[b]])
                    eng = p.dve if b == 0 else p.pool
                    if gcol is None:
                        cons[b] = p.X(eng, [t, sts[b]], "tensor_copy", wo[b][:], wf[b][:])
                    else:
                        cons[b] = p.X(eng, [t, sts[b]], "tensor_scalar", wo[b][:], wf[b][:],
                                      gcol[:, kc:kc + 1], None, ALU.mult)
                    sts[b] = p.dma(p.act, wb.ap()[kc * 128:(kc + 1) * 128, c0:c0 + CW], wo[b][:], f"ws{b}",
                                   deps=[cons[b]])
                    i += 1
        p.barrier()
        return wb

    wb0i = prep_w(w0i, 7168, gpre_sb[0], "wb0i")
    wb0o = prep_w(w0o, D, None, "wb0o")
    wb1i = prep_w(w1i, 8192, gpre_sb[1], "wb1i")
    wb1o = prep_w(w1o, D, None, "wb1o")
    done("prep")

    def norm_T(x_rows, L, hT):
        with contextlib.ExitStack() as es:
            xt = [p.sb(es, f"nx{i}", [128, D], F32) for i in range(2)]
            junk = p.sb(es, "njunk", [128, D], BF16)
            hb = [p.sb(es, f"nhb{i}", [128, D], BF16) for i in range(2)]
            ho = [p.sb(es, f"nho{i}", [128, KC, 128], BF16) for i in range(2)]
            st_ = [p.sb(es, f"nst{i}", [128, 2], F32) for i in range(2)]
            rd = [None, None]
            hbr = [None, None]
            hor = [None, None]
            for ti in range(L // 128):
                b = ti % 2
                t = p.ld(xt[b][:], x_rows[ti * 128:(ti + 1) * 128, :], f"nl{b}", deps=[rd[b]])
                a1 = p.A([t], out=junk[:], in_=xt[b][:], func=AF.Square, accum_out=st_[b][:, 0:1])
                v2 = p.rsqrt([a1], st_[b][:, 1:2], st_[b][:, 0:1], 1.0 / D, EPS)
                v3 = p.V([v2, t, hbr[b]], "tensor_scalar", hb[b][:], xt[b][:], st_[b][:, 1:2], None, ALU.mult)
                rd[b] = v3
                for kc in range(KC):
                    tk = p.tr([v3, hor[b]] if kc == 0 else [], p.pb[kc // 8][:, (kc % 8) * 128:(kc % 8 + 1) * 128],
                              hb[b][:, kc * 128:(kc + 1) * 128], ident[:], mark=(kc == KC - 1))
                hbr[b] = tk
                c1 = p.A([tk, hor[b]], out=ho[b][:, 0:8, :], in_=p.pb[0][:].rearrange("p (k t) -> p k t", k=8),
                         func=AF.Copy)
                c2 = p.V([tk, hor[b]], "tensor_copy", ho[b][:, 8:16, :],
                         p.pb[1][:].rearrange("p (k t) -> p k t", k=8))
                p.pe.wait(c1, c2)
                hor[b] = p.st(hT.ap().rearrange("(k p) t -> p k t", p=128)[:, :, ti * 128:(ti + 1) * 128],
                              ho[b][:], f"ns{b}", deps=[c1, c2])
        p.barrier()

    def proj(hT, L, wb, ncols_total, jobs):
        TB = min(L, 1024)
        with contextlib.ExitStack() as es:
            hblk = p.sb(es, "pj_h", [128, KC, TB], BF16)
            wblk = [p.sb(es, f"pj_w{i}", [128, KC, 512], BF16) for i in range(2)]
            osb = {F32: [p.sb(es, f"pj_of{i}", [128, 512], F32) for i in range(2)],
                   BF16: [p.sb(es, f"pj_ob{i}", [128, 512], BF16) for i in range(2)]}
            wread = [None, None]
            ost = {F32: [None, None], BF16: [None, None]}
            psr = [None] * 4
            wi = 0
            oi = 0
            pi = 0
            hread = None
            for tb in range(L // TB):
                th = p.ld(hblk[:], hT.ap().rearrange("(k p) t -> p k t", p=128)[:, :, tb * TB:(tb + 1) * TB],
                          "pjh", deps=[hread])
                cbs = [(j, c) for j in jobs for c in range(0, j[1], 512)]
                for (job, c) in cbs:
                    col0, ncols, mode, od, o0, scale, odt = job
                    b = wi % 2
                    wi += 1
                    tw = p.ld(wblk[b][:], wb.ap().rearrange("(k p) n -> p k n", p=128)[:, :, col0 + c:col0 + c + 512],
                              f"pjw{b}", deps=[wread[b]])
                    last = None
                    TW = min(512, TB)
                    if mode == "F":
                        subs = [(s4, t5) for s4 in range(4) for t5 in range(TB // TW)]
                    else:
                        subs = [(s4, 0) for s4 in range(TB // 128)]
                    for (s4, t5) in subs:
                        pk = pi % 4
                        pi += 1
                        W_ = TW if mode == "F" else 512
                        ps = p.ps[pk][:, 0:W_]
                        for kc in range(KC):
                            if mode == "F":
                                lhsT, rhs = wblk[b][:, kc, s4 * 128:(s4 + 1) * 128], hblk[:, kc, t5 * TW:(t5 + 1) * TW]
                            else:
                                lhsT, rhs = hblk[:, kc, s4 * 128:(s4 + 1) * 128], wblk[b][:, kc, :]
                            tk = p.mm([th, tw, psr[pk]] if kc == 0 else [], ps, lhsT, rhs, kc == 0, kc == KC - 1,
                                      mark=(kc == KC - 1))
                        last = tk
                        ob = oi % 2
                        oi += 1
                        o = osb[odt][ob][:, 0:W_]
                        if oi % 2 == 0:
                            ev = p.A([tk, ost[odt][ob]], out=o, in_=ps, func=AF.Copy, scale=float(scale))
                        else:
                            ev = p.V([tk, ost[odt][ob]], "tensor_scalar", o, ps, float(scale), None, ALU.mult)
                        psr[pk] = ev
                        if mode == "F":
                            dst = od.ap()[o0 + c + s4 * 128:o0 + c + (s4 + 1) * 128,
                                          tb * TB + t5 * TW:tb * TB + (t5 + 1) * TW]
                        else:
                            dst = od.ap()[tb * TB + s4 * 128:tb * TB + (s4 + 1) * 128, o0 + c:o0 + c + 512]
                        ost[odt][ob] = p.st(dst, o, f"pjs{ob}{'f' if odt == F32 else 'b'}", deps=[ev])
                    wread[b] = last
                    hread = last
        p.barrier()

    def fnet(uT, L, ya, tabs):
        M = L // 128
        c128, cs256, tw, cm = tabs
        Bd = p.dram(f"fn_B{p.ndram}", [128, M, 512], BF16)
        p.ndram += 1
        CP = 32
        with contextlib.ExitStack() as es:
            ug = p.sb(es, "fn_u", [128, 2, L], BF16)
            MB = min(M, 32)
            V = p.sb(es, "fn_V", [128, MB, 512], BF16)
            Bs = p.sb(es, "fn_Bs", [128, MB, 512], BF16)
            tmp = [p.sb(es, f"fn_t{i}", [128, 2, 256], F32) for i in range(2)]
            Bt = p.sb(es, "fn_Bt", [M, CP, 512], BF16)
            Y = [p.sb(es, f"fn_Y{i}", [M, 2, 256], F32) for i in range(2)]
            for g in range(4):
                tu = p.ld(ug[:], uT.ap()[g * 256:(g + 1) * 256, :].rearrange("(c p) t -> p c t", p=128), "fnu")
                ts_all = []
                for bh in range(M // MB):
                    evs = []
                    prev = [None, None]
                    for bl in range(MB):
                        b = bh * MB + bl
                        pk = b % 2
                        for ch in range(2):
                            lhsT = bass.AP(ug, ch * L + b, [[2 * L, 128], [M, 128]])
                            tk = p.mm([tu, prev[pk]] if ch == 0 else [], p.ps[pk][:], lhsT, cs256[:, ch, :],
                                      ch == 0, ch == 1, mark=(ch == 1))
                        if b % 2 == 0:
                            ev = p.A([tk], out=V[:, bl, :], in_=p.ps[pk][:], func=AF.Copy)
                        else:
                            ev = p.V([tk], "tensor_copy", V[:, bl, :], p.ps[pk][:])
                        prev[pk] = ev
                        evs.append(ev)
                    prevr = [None, None]
                    tw_tok = []
                    for bp in range(MB // 2):
                        pk = 2 + (bp % 2) * 2
                        Ar, Ai = p.ps[pk], p.ps[pk + 1]
                        b0 = 2 * bp
                        dep = [evs[b0], evs[b0 + 1], prevr[bp % 2]]
                        ar3 = Ar[:].rearrange("p (b f) -> p b f", b=2)
                        ai3 = Ai[:].rearrange("p (b f) -> p b f", b=2)
                        p.mm(dep, ar3, c128[:, 0, :], V[:, b0:b0 + 2, 0:256], True, False)
                        p.mm([], ar3, c128[:, 1, :], V[:, b0:b0 + 2, 256:512], False, True)
                        p.mm([], ai3, c128[:, 0, :], V[:, b0:b0 + 2, 256:512], True, False)
                        tk = p.mm([], ai3, c128[:, 2, :], V[:, b0:b0 + 2, 0:256], False, True, mark=True)
                        last = []
                        for j in range(2):
                            bl = b0 + j
                            b = bh * MB + bl
                            t1 = p.V([tk], "tensor_scalar", tmp[0][:, j, :], ar3[:, j, :], tw[:, 0, b:b + 1], None, ALU.mult)
                            t3 = p.V([tk], "tensor_scalar", tmp[1][:, j, :], ai3[:, j, :], tw[:, 0, b:b + 1], None, ALU.mult)
                            r1 = p.V([t1, tk], "scalar_tensor_tensor", Bs[:, bl, 0:256], ai3[:, j, :], tw[:, 1, b:b + 1],
                                     tmp[0][:, j, :], ALU.mult, ALU.add)
                            r2 = p.V([t3, tk], "scalar_tensor_tensor", Bs[:, bl, 256:512], ar3[:, j, :], tw[:, 2, b:b + 1],
                                     tmp[1][:, j, :], ALU.mult, ALU.add)
                            last = [r1, r2]
                        p.act.wait(*last)
                        prevr[bp % 2] = last[1]
                        tw_tok = last
                    ts_all.append(p.st(Bd.ap()[:, bh * MB:(bh + 1) * MB, :], Bs[:], "fnb", deps=tw_tok))
                    p.barrier()
                if True:
                    ts_ = ts_all[-1]
                    yst = [None, None]
                    evp = [None, None]
                    bt_read = None
                    for cp in range(128 // CP):
                        tl = p.ld(Bt[:], Bd.ap()[cp * CP:(cp + 1) * CP, :, :].rearrange("c b f -> b c f"), "fnbt",
                                  deps=[ts_, bt_read])
                        for c2 in range(CP // 2):
                            pk = c2 % 2
                            ps3 = p.ps[pk][0:M, :].rearrange("p (c f) -> p c f", c=2)
                            p.mm([tl, evp[pk]], ps3, cm[0:M, 0, :], Bt[:, 2 * c2:2 * c2 + 2, 0:256], True, False)
                            tk = p.mm([], ps3, cm[0:M, 1, :], Bt[:, 2 * c2:2 * c2 + 2, 256:512], False, True, mark=True)
                            if c2 % 2 == 0:
                                ev = p.A([tk, yst[pk]], out=Y[pk][:], in_=ps3, func=AF.Copy)
                            else:
                                ev = p.V([tk, yst[pk]], "tensor_copy", Y[pk][:], ps3)
                            c_abs = cp * CP + 2 * c2
                            dst = ya.ap().rearrange("(d c) f -> d c f", c=128)[:, c_abs:c_abs + 2, g * 256:(g + 1) * 256]
                            yst[pk] = p.st(dst, Y[pk][:], f"fny{pk}", deps=[ev])
                            evp[pk] = ev
                            bt_read = tk
                    p.barrier()
        p.barrier()

    def hgrn(qrT, ffT, fbT, vtok, gates, L, ymix, lbt, hm, ghg_sb):
        SEG = min(L, 2048)
        NCH = SEG // 64
        nseg = L // SEG
        NT = L // 64
        dS = p.dram(f"hg_dS{p.ndram}", [2, NT, 128, 128], F32)
        Sb = p.dram(f"hg_Sb{p.ndram}", [2, NT, 128, 128], BF16)
        qd = p.dram(f"hg_qd{p.ndram}", [2, 128, L], BF16)
        scd = p.dram(f"hg_sc{p.ndram}", [64, NT, 64], BF16)
        eld = p.dram(f"hg_el{p.ndram}", [2, 128, NT], F32)
        p.ndram += 1
        for h in range(8):
            SEG1 = min(L, 1024)
            NCH1 = SEG1 // 64
            nseg1 = L // SEG1
            with contextlib.ExitStack() as es:
                qr = p.sb(es, "h_qr", [128, SEG1], F32)
                rmask = p.sb(es, "h_rm", [128, SEG1], F32)
                qdec = [p.sb(es, f"h_qd{i}", [128, SEG1], BF16) for i in range(2)]
                vt = p.sb(es, "h_v", [64, NCH1, 128], BF16)
                sct = p.sb(es, "h_sc", [64, NCH1, 64], BF16)
                sc1 = p.sb(es, "h_sc1", [64, NCH1, 64], F32)
                el = p.sb(es, "h_el", [128, 2, NCH1], F32)
                B = []
                for d in range(2):
                    bd = {}
                    for nm_ in ("fr", "f", "g", "k", "cum", "cb", "ex", "ex2", "ex3"):
                        bd[nm_] = p.sb(es, f"h_{nm_}{d}", [128, SEG1], F32)
                    bd["kdec"] = p.sb(es, f"h_kd{d}", [128, SEG1], BF16)
                    bd["kend"] = p.sb(es, f"h_ke{d}", [128, SEG1], BF16)
                    bd["kendT"] = p.sb(es, f"h_keT{d}", [64, NCH1, 128], BF16)
                    bd["dSs"] = [p.sb(es, f"h_dS{d}{j}", [128, 4, 128], F32) for j in range(2)]
                    B.append(bd)
                m1 = p.G([], "memset", rmask[:], 1.0)
                m2 = p.G([], "memset", rmask[:].rearrange("p (n s) -> p n s", s=64)[:, :, 0:1], 0.0)
                p.barrier()
                for sg in range(nseg1):
                    t0 = sg * SEG1
                    tq = p.ld(qr[:], qrT.ap()[h * 128:(h + 1) * 128, t0:t0 + SEG1], "hq")
                    tv = p.ld(vt[:], vtok.ap()[t0:t0 + SEG1, h * 128:(h + 1) * 128].rearrange("(n s) v -> s n v", s=64),
                              "hv")
                    aq = p.A([tq], out=qr[:], in_=qr[:], func=AF.Silu)
                    shared = {}

                    def dir_gen(d):
                        bd = B[d]
                        fr, f_, g_, k_, cum, cb = bd["fr"], bd["f"], bd["g"], bd["k"], bd["cum"], bd["cb"]
                        ex, ex2, ex3, kdec, kend, kendT, dSs = (bd["ex"], bd["ex2"], bd["ex3"], bd["kdec"], bd["kend"],
                                                                bd["kendT"], bd["dSs"])
                        src = ffT if d == 0 else fbT
                        tf = p.ld(fr[:], src.ap()[h * 128:(h + 1) * 128, t0:t0 + SEG1], f"hf{d}")
                        a1 = p.A([tf], out=f_[:], in_=fr[:], func=AF.Sigmoid)
                        yield
                        v1 = p.V([a1], "tensor_scalar", f_[:], f_[:], lbt[:, 2 + d, h:h + 1], lbt[:, d, h:h + 1],
                                 ALU.mult, ALU.add)
                        yield
                        a2 = p.A([v1], out=g_[:], in_=f_[:], func=AF.Ln)
                        g1 = p.G([v1], "tensor_scalar", k_[:], f_[:], -1.0, 1.0, ALU.mult, ALU.add)
                        yield
                        v2 = p.V([a2], "tensor_tensor_scan", cum[:], rmask[:], g_[:], 0.0, ALU.mult, ALU.add)
                        yield
                        cum3 = cum[:].rearrange("p (n s) -> p n s", s=64)
                        lastb = bass.AP(cum, 63, [[SEG1, 128], [64, NCH1], [0, 64]])
                        a6 = p.A([v2], out=el[:, d, :], in_=cum3[:, :, 63], func=AF.Exp)
                        if d == 0:
                            cc = cum
                            v3 = p.V([v2], "tensor_tensor", cb[:].rearrange("p (n s) -> p n s", s=64), lastb, cum3,
                                     ALU.subtract)
                            dl = cb
                            yield
                        else:
                            v3a = p.V([v2], "tensor_tensor", cb[:].rearrange("p (n s) -> p n s", s=64), lastb, cum3,
                                      ALU.subtract)
                            yield
                            v3b = p.V([v3a], "tensor_tensor", cb[:], cb[:], g_[:], ALU.add)
                            yield
                            cc = cb
                            v3 = p.V([v3b], "tensor_tensor", g_[:], cum[:], g_[:], ALU.subtract)
                            dl = g_
                            yield
                        a3 = p.A([v3], out=ex[:], in_=cc[:], func=AF.Exp)
                        yield
                        a4 = p.A([v3], out=ex2[:], in_=cc[:], func=AF.Exp, scale=-1.0)
                        g2 = p.V([a3, aq], "tensor_tensor", qdec[d][:], qr[:], ex[:], ALU.mult)
                        yield
                        a5 = p.A([v3], out=ex3[:], in_=dl[:], func=AF.Exp)
                        g3 = p.G([a4, g1], "tensor_tensor", kdec[:], k_[:], ex2[:], ALU.mult)
                        yield
                        g4 = p.V([a5, g1], "tensor_tensor", kend[:], k_[:], ex3[:], ALU.mult)
                        p.st(qd.ap()[d, :, t0:t0 + SEG1], qdec[d][:], f"hsq{d}", deps=[g2])
                        p.st(eld.ap()[d, :, sg * NCH1:(sg + 1) * NCH1], el[:, d, :], f"hse{d}", deps=[a6])
                        yield
                        nb = min(8, NCH1)
                        NGR = NCH1 // nb
                        mk_ap = bass.AP(hm, d * 64, [[128, 64], [0, nb], [1, 64]])
                        hz = {}
                        psS, pbT = p.ps[d], p.pb[d]
                        ev = None
                        for gq in range(NGR + 1):
                            if gq < NGR:
                                n0 = gq * nb
                                for j in range(nb):
                                    sl = slice((n0 + j) * 64, (n0 + j + 1) * 64)
                                    tk = p.mm([g2, g3, hz.get("sc")] if j == 0 else [], psS[0:64, j * 64:(j + 1) * 64],
                                              kdec[:, sl], qdec[d][:, sl], True, True, mark=(j == nb - 1))
                                psv = psS[0:64, 0:nb * 64].rearrange("p (n s) -> p n s", s=64)
                                if d == 0:
                                    ev = p.V([tk], "tensor_tensor", sc1[:, n0:n0 + nb, :], psv, mk_ap, ALU.mult)
                                    shared[("sc1", gq)] = ev
                                else:
                                    ev0 = p.V([tk], "tensor_tensor", sct[:, n0:n0 + nb, :], psv, mk_ap, ALU.mult)
                                    ev = p.V([ev0, shared[("sc1", gq)]], "tensor_tensor", sct[:, n0:n0 + nb, :],
                                             sct[:, n0:n0 + nb, :], sc1[:, n0:n0 + nb, :], ALU.add)
                                hz["sc"] = ev
                                yield
                                for j in range(nb):
                                    sl = slice((n0 + j) * 64, (n0 + j + 1) * 64)
                                    tk2 = p.tr([g4, hz.get("tr")] if j == 0 else [], pbT[0:64, j * 128:(j + 1) * 128],
                                               kend[:, sl], ident[:], mark=(j == nb - 1))
                                ev2 = p.A([tk2], out=kendT[:, n0:n0 + nb, :],
                                          in_=pbT[0:64, 0:nb * 128].rearrange("p (n k) -> p n k", k=128), func=AF.Copy)
                                hz["tr"] = ev2
                                hz[("kT", gq)] = ev2
                                yield
                            if gq >= 1:
                                gprev = gq - 1
                                n0 = gprev * nb
                                for j in range(nb):
                                    i4 = j // 4
                                    bank = p.ps[2 + 2 * d + i4]
                                    tk3 = p.mm([hz[("kT", gprev)], tv, hz.get(("dsb", i4))] if j % 4 == 0 else [],
                                               bank[:, (j % 4) * 128:(j % 4 + 1) * 128], kendT[:, n0 + j, :], vt[:, n0 + j, :],
                                               True, True, mark=(j % 4 == 3 or j == nb - 1))
                                    if j % 4 == 3 or j == nb - 1:
                                        w4 = j % 4 + 1
                                        dst = dSs[i4][:, 0:w4, :]
                                        srcp = bank[:, 0:w4 * 128].rearrange("p (n v) -> p n v", v=128)
                                        if i4 == 0:
                                            ev3 = p.A([tk3, hz.get(("dss", i4))], out=dst, in_=srcp, func=AF.Copy)
                                        else:
                                            ev3 = p.V([tk3, hz.get(("dss", i4))], "tensor_copy", dst, srcp)
                                        hz[("dsb", i4)] = ev3
                                        c0 = sg * NCH1 + n0 + i4 * 4
                                        hz[("dss", i4)] = p.st(
                                            dS.ap()[d, c0:c0 + w4, :, :].rearrange("n p v -> p n v"), dst, f"hds{d}{i4}",
                                            deps=[ev3])
                                yield
                        if d == 1:
                            p.st(scd.ap()[:, sg * NCH1:(sg + 1) * NCH1, :], sct[:], "hsc", deps=[ev])

                    gens = [dir_gen(0), dir_gen(1)]
                    while gens:
                        for gg in list(gens):
                            try:
                                next(gg)
                            except StopIteration:
                                gens.remove(gg)
                    p.barrier()
            with contextlib.ExitStack() as es:
                G = min(NT, 32)
                NG = NT // G
                dsl = [p.sb(es, f"h2_ds{i}", [128, G, 128], F32) for i in range(2)]
                sall = [p.sb(es, f"h2_sa{i}", [128, G + 1, 128], F32) for i in range(2)]
                sbo = [p.sb(es, f"h2_sb{i}", [128, G, 128], BF16) for i in range(2)]
                ela = p.sb(es, "h2_el", [128, 2, NT], F32)
                te = p.ld(ela[:], eld.ap().rearrange("d p n -> p d n"), "h2e")
                z0 = p.V([], "memset", sall[0][:, 0, :], 0.0)
                z1 = p.V([], "memset", sall[1][:, G, :], 0.0)
                last = [z0, z1]
                for gi in range(NG):
                    gf, gb = gi, NG - 1 - gi
                    tl = [p.ld(dsl[0][:], dS.ap()[0, gf * G:(gf + 1) * G, :, :].rearrange("n p v -> p n v"), "h2l0"),
                          p.ld(dsl[1][:], dS.ap()[1, gb * G:(gb + 1) * G, :, :].rearrange("n p v -> p n v"), "h2l1")]
                    for i in range(G):
                        nf = gf * G + i
                        last[0] = p.V([tl[0], te, last[0]], "scalar_tensor_tensor", sall[0][:, i + 1, :], sall[0][:, i, :],
                                      ela[:, 0, nf:nf + 1], dsl[0][:, i, :], ALU.mult, ALU.add)
                        j = G - 1 - i
                        nbk = gb * G + j
                        last[1] = p.V([tl[1], te, last[1]], "scalar_tensor_tensor", sall[1][:, j, :], sall[1][:, j + 1, :],
                                      ela[:, 1, nbk:nbk + 1], dsl[1][:, j, :], ALU.mult, ALU.add)
                    c0 = p.G(last, "tensor_copy", sbo[0][:], sall[0][:, 0:G, :])
                    c1 = p.A(last, out=sbo[1][:], in_=sall[1][:, 1:G + 1, :], func=AF.Copy)
                    p.st(Sb.ap()[0, gf * G:(gf + 1) * G, :, :].rearrange("n p v -> p n v"), sbo[0][:], "h2s0", deps=[c0])
                    p.st(Sb.ap()[1, gb * G:(gb + 1) * G, :, :].rearrange("n p v -> p n v"), sbo[1][:], "h2s1", deps=[c1])
                    last[0] = p.V([c0, c1] + last, "tensor_copy", sall[0][:, 0, :], sall[0][:, G, :])
                    last[1] = p.V([last[0]], "tensor_copy", sall[1][:, G, :], sall[1][:, 0, :])
                    p.barrier()
            with contextlib.ExitStack() as es:
                G = min(NT, 32)
                qd3 = p.sb(es, "h3_qd", [128, 2, G * 64], BF16)
                sc3 = p.sb(es, "h3_sc", [64, G, 64], BF16)
                v3_ = p.sb(es, "h3_v", [64, G, 128], BF16)
                gt3 = p.sb(es, "h3_g", [64, G, 128], F32)
                s3 = p.sb(es, "h3_s", [128, 2, G, 128], BF16)
                o3 = p.sb(es, "h3_o", [64, G, 128], F32)
                ob3 = p.sb(es, "h3_ob", [64, G, 128], BF16)
                sq3 = p.sb(es, "h3_sq", [64, G, 128], F32)
                ss = p.sb(es, "h3_ss", [64, G], F32)
                for gi in range(NT // G):
                    t0 = gi * G * 64
                    tl = [p.ld(qd3[:], qd.ap()[:, :, t0:t0 + G * 64].rearrange("d p t -> p d t"), "h3a"),
                          p.ld(sc3[:], scd.ap()[:, gi * G:(gi + 1) * G, :], "h3a"),
                          p.ld(v3_[:], vtok.ap()[t0:t0 + G * 64, h * 128:(h + 1) * 128].rearrange("(n s) v -> s n v", s=64), "h3a"),
                          p.ld(gt3[:], gates.ap()[t0:t0 + G * 64, 1024 + h * 128:1024 + (h + 1) * 128].rearrange("(n s) v -> s n v", s=64), "h3a"),
                          p.ld(s3[:, 0], Sb.ap()[0, gi * G:(gi + 1) * G, :, :].rearrange("n p v -> p n v"), "h3a"),
                          p.ld(s3[:, 1], Sb.ap()[1, gi * G:(gi + 1) * G, :, :].rearrange("n p v -> p n v"), "h3a")]
                    ag = p.A([tl[3]], out=gt3[:], in_=gt3[:], func=AF.Silu)
                    prev = [None, None]
                    nb3 = min(4, G)
                    for j0 in range(0, G, nb3):
                        pk = (j0 // nb3) % 2
                        for jj in range(nb3):
                            j = j0 + jj
                            ps = p.ps[pk][0:64, jj * 128:(jj + 1) * 128]
                            sl = slice(j * 64, (j + 1) * 64)
                            p.mm(tl + [prev[pk]] if jj == 0 else [], ps, sc3[:, j, :], v3_[:, j, :], True, False)
                            p.mm([], ps, qd3[:, 0, sl], s3[:, 0, j, :], False, False)
                            tk = p.mm([], ps, qd3[:, 1, sl], s3[:, 1, j, :], False, True, mark=(jj == nb3 - 1))
                        ev = p.V([tk], "tensor_copy", o3[:, j0:j0 + nb3, :],
                                 p.ps[pk][0:64, 0:nb3 * 128].rearrange("p (n v) -> p n v", v=128))
                        prev[pk] = ev
                    e2 = p.A([ev], out=sq3[:], in_=o3[:], func=AF.Square)
                    e3 = p.V([e2], "tensor_reduce", ss[:], sq3[:], mybir.AxisListType.X, ALU.add)
                    r2 = p.rsqrt([e3], ss[:], ss[:], 1.0 / 128, EPS)
                    ssb = bass.AP(ss, 0, [[G, 64], [1, G], [0, 128]])
                    r3 = p.V([r2], "tensor_tensor", o3[:], o3[:], ssb, ALU.mult)
                    ghb = bass.AP(ghg_sb, 0, [[128, 64], [0, G], [1, 128]])
                    r4 = p.V([r3], "tensor_tensor", o3[:], o3[:], ghb, ALU.mult)
                    r5 = p.V([r4, ag], "tensor_tensor", ob3[:], o3[:], gt3[:], ALU.mult)
                    p.st(ymix.ap()[t0:t0 + G * 64, 1024 + h * 128:1024 + (h + 1) * 128].rearrange("(n s) v -> s n v", s=64),
                         ob3[:], "h3s", deps=[r5])
                    p.barrier()
        p.barrier()

    def outproj(L, ymix, ya, gates, wbo, gpost, xres, xout, hT_next):
        NTL = L // 128
        with contextlib.ExitStack() as es:
            wsb = p.sb(es, "op_w", [128, KC, D], BF16)
            tw = p.ld(wsb[:], wbo.ap().rearrange("(k p) n -> p k n", p=128), "opw")
            ym = [p.sb(es, f"op_ym{i}", [128, D], BF16) for i in range(2)]
            yaf = [p.sb(es, f"op_ya{i}", [128, 1024], F32) for i in range(2)] if ya is not None else None
            gaf = [p.sb(es, f"op_ga{i}", [128, 1024], F32) for i in range(2)] if ya is not None else None
            ymT = [p.sb(es, f"op_ymT{i}", [128, KC, 128], BF16) for i in range(2)]
            xr = [p.sb(es, f"op_x{i}", [128, D], F32) for i in range(2)]
            yo = [p.sb(es, f"op_y{i}", [128, D], F32) for i in range(2)]
            junk = p.sb(es, "op_j", [128, D], BF16)
            st_ = [p.sb(es, f"op_st{i}", [128, 4], F32) for i in range(2)]
            hb = [p.sb(es, f"op_hb{i}", [128, D], BF16) for i in range(2)]
            ho = [p.sb(es, f"op_ho{i}", [128, KC, 128], BF16) for i in range(2)]
            T = {}
            g = lambda k, t: T.get((k, t))
            pbv = [p.pb[i][:].rearrange("p (k t) -> p k t", k=8) for i in range(2)]
            for ti in range(NTL + 1):
                if ti < NTL:
                    b = ti % 2
                    rows = slice(ti * 128, (ti + 1) * 128)
                    T[("tx", ti)] = p.ld(xr[b][:], xres[rows, :], f"opx{b}", deps=[g("v5", ti - 2)])
                    if ya is not None:
                        t1 = p.ld(ym[b][:, 1024:2048], ymix.ap()[rows, 1024:2048], f"opm{b}", deps=[g("tr", ti - 2)])
                        t2 = p.ld(yaf[b][:], ya.ap()[rows, :], f"opa{b}", deps=[g("v1", ti - 2)])
                        t3 = p.ld(gaf[b][:], gates.ap()[rows, 0:1024], f"opa{b}", deps=[g("v1", ti - 2)])
                        a1 = p.A([t3], out=gaf[b][:], in_=gaf[b][:], func=AF.Silu)
                        v1 = p.V([a1, t2, g("tr", ti - 2)], "tensor_tensor", ym[b][:, 0:1024], yaf[b][:], gaf[b][:], ALU.mult)
                        T[("v1", ti)] = v1
                        rdy = [t1, v1]
                    else:
                        rdy = [p.ld(ym[b][:], ymix.ap()[rows, :], f"opm{b}", deps=[g("tr", ti - 2)])]
                    for kc in range(KC):
                        tk = p.tr(rdy + [g("c1h", ti - 2), g("c2h", ti - 2)] if kc == 0 else [],
                                  p.pb[kc // 8][:, (kc % 8) * 128:(kc % 8 + 1) * 128],
                                  ym[b][:, kc * 128:(kc + 1) * 128], ident[:], mark=(kc == KC - 1))
                    T[("tr", ti)] = tk
                    c1 = p.A([tk, g("mm", ti - 2)], out=ymT[b][:, 0:8, :], in_=pbv[0], func=AF.Copy)
                    c2 = p.V([tk, g("mm", ti - 2)], "tensor_copy", ymT[b][:, 8:16, :], pbv[1])
                    T[("c1", ti)], T[("c2", ti)] = c1, c2
                    for cb in range(4):
                        for kc in range(KC):
                            tk = p.mm([c1, c2, tw] + [T.get(("ev", ti - 1, i)) for i in range(4)] if (kc == 0 and cb == 0) else [],
                                      p.ps[cb][:], ymT[b][:, kc, :],
                                      wsb[:, kc, cb * 512:(cb + 1) * 512], kc == 0, kc == KC - 1, mark=(kc == KC - 1))
                    T[("mm", ti)] = tk
                if ti >= 1 and hT_next is not None:
                    tj = ti - 1
                    bj = tj % 2
                    rows_j = slice(tj * 128, (tj + 1) * 128)
                    for kc in range(KC):
                        tk2 = p.tr([g("v8", tj), g("c1", ti), g("c2", ti)] if kc == 0 else [],
                                   p.pb[kc // 8][:, (kc % 8) * 128:(kc % 8 + 1) * 128],
                                   hb[bj][:, kc * 128:(kc + 1) * 128], ident[:], mark=(kc == KC - 1))
                    T[("trh", tj)] = tk2
                    c1h = p.A([tk2, g("sh", tj - 2)], out=ho[bj][:, 0:8, :], in_=pbv[0], func=AF.Copy)
                    c2h = p.V([tk2, g("sh", tj - 2)], "tensor_copy", ho[bj][:, 8:16, :], pbv[1])
                    T[("c1h", tj)], T[("c2h", tj)] = c1h, c2h
                    T[("sh", tj)] = p.st(hT_next.ap().rearrange("(k p) t -> p k t", p=128)[:, :, rows_j], ho[bj][:],
                                         f"oph{bj}", deps=[c1h, c2h])
                if ti < NTL:
                    tk = T[("mm", ti)]
                    evs = []
                    for cb in range(4):
                        dep = [tk, g("so", ti - 2), g("v8", ti - 2)]
                        if cb % 2 == 0:
                            ev = p.A(dep, out=yo[b][:, cb * 512:(cb + 1) * 512], in_=p.ps[cb][:], func=AF.Copy)
                        else:
                            ev = p.V(dep, "tensor_copy", yo[b][:, cb * 512:(cb + 1) * 512], p.ps[cb][:])
                        T[("ev", ti, cb)] = ev
                        evs.append(ev)
                    a2 = p.A(evs, out=junk[:], in_=yo[b][:], func=AF.Square, accum_out=st_[b][:, 0:1])
                    v3 = p.rsqrt([a2], st_[b][:, 1:2], st_[b][:, 0:1], 1.0 / D, EPS)
                    v4 = p.V([v3] + evs, "scalar_tensor_tensor", yo[b][:], yo[b][:], st_[b][:, 1:2], gpost[:], ALU.mult, ALU.mult)
                    v5 = p.V([v4, T[("tx", ti)]], "tensor_tensor", yo[b][:], yo[b][:], xr[b][:], ALU.add)
                    T[("v5", ti)] = v5
                    T[("so", ti)] = p.st(xout[rows, :], yo[b][:], f"opo{b}", deps=[v5])
                    if hT_next is not None:
                        a3 = p.A([v5], out=junk[:], in_=yo[b][:], func=AF.Square, accum_out=st_[b][:, 2:3])
                        v7 = p.rsqrt([a3], st_[b][:, 3:4], st_[b][:, 2:3], 1.0 / D, EPS)
                        T[("v8", ti)] = p.V([v7, g("trh", ti - 2)], "tensor_scalar", hb[b][:], yo[b][:], st_[b][:, 3:4], None, ALU.mult)
        p.barrier()

    def attention(qT, kT, vtok, gates, Lq, Lk, og, dtab, lam_sb, gsub_sb, delta):
        NKT, NQB = Lk // 128, Lq // 256
        SK = 2
        with contextlib.ExitStack() as es:
            qs = p.sb(es, "at_q", [128, 2, Lq], BF16)
            ks = p.sb(es, "at_k", [128, 2, Lk], BF16)
            vs = p.sb(es, "at_v", [128, NKT, 257], BF16)
            absd = [p.sb(es, f"at_ad{i}", [128, 256], F32) for i in range(3)]
            sb_ = [p.sb(es, f"at_s{i}", [128, 256], F32) for i in range(4)]
            pt = [p.sb(es, f"at_p{i}", [128, 256], BF16) for i in range(4)]
            gt = [p.sb(es, f"at_g{i}", [128, 2, 256], F32) for i in range(2)]
            o0 = [p.sb(es, f"at_o0{i}", [128, 2, 256], F32) for i in range(2)]
            o1 = [p.sb(es, f"at_o1{i}", [128, 2, 256], F32) for i in range(2)]
            rs = [p.sb(es, f"at_rs{i}", [128, 8], F32) for i in range(2)]
            ob = [p.sb(es, f"at_ob{i}", [128, 2, 256], BF16) for i in range(2)]
            jk = p.sb(es, "at_j", [128, 256], F32)
            Sps = [p.ps[i][:, 0:256] for i in range(2)]
            acc = [[p.ps[2 + 2 * c + j] for j in range(2)] for c in range(2)]
            qbi = 0
            gt_free = [[], []]
            ob_free = [None, None]
            for h in range(8):
                slope = SLOPES[h]
                t_in = [p.ld(qs[:], qT.ap()[h * 256:(h + 1) * 256, :].rearrange("(c p) t -> p c t", p=128), "atq"),
                        p.ld(ks[:], kT.ap()[h * 256:(h + 1) * 256, :].rearrange("(c p) t -> p c t", p=128), "atq"),
                        p.ld(vs[:, :, 0:256], vtok.ap()[:, h * 256:(h + 1) * 256].rearrange("(n p) v -> p n v", p=128),
                             "atq")]
                t_in.append(p.G([], "memset", vs[:, :, 256:257], 1.0))
                acc_free = []
                for qb in range(NQB):
                    par = qbi % 2
                    qbi += 1
                    q0 = qb * 256
                    tg = p.ld(gt[par][:],
                              gates.ap()[q0:q0 + 256, h * 256:(h + 1) * 256].rearrange("(j p) v -> p j v", p=128),
                              f"atg{par}", deps=gt_free[par])
                    units = [(kt, c) for kt in range(NKT) for c in range(2)]
                    NU = len(units)
                    rd_ad = [None] * 3
                    rd_s = [None] * 4
                    rd_p = [None] * 4
                    rd_ps = [None] * 2
                    absT = {}
                    expT = {}
                    state = {"lastpv": None}

                    def do_abs(kt):
                        ab = kt % 3
                        idx = kt * NQB + qb
                        absT[kt] = p.A([rd_ad[ab]], out=absd[ab][:], in_=delta[:], func=AF.Abs, bias=dtab[:, idx:idx + 1])

                    def front(u):
                        kt, c = units[u]
                        b = u % 4
                        if c == 0:
                            if kt == 0:
                                do_abs(0)
                            if kt + 1 < NKT:
                                do_abs(kt + 1)
                        b2 = u % 2
                        tk = p.mm(t_in + [rd_ps[b2]], Sps[b2], ks[:, c, kt * 128:(kt + 1) * 128], qs[:, c, q0:q0 + 256],
                                  True, True, mark=True)
                        v1 = p.V([tk, absT[kt], rd_s[b]], "scalar_tensor_tensor", sb_[b][:], absd[kt % 3][:], -slope,
                                 Sps[b2], ALU.mult, ALU.add)
                        rd_ps[b2] = v1
                        rd_ad[kt % 3] = v1
                        a1 = p.A([v1, rd_p[b]], out=pt[b][:], in_=sb_[b][:], func=AF.Exp)
                        rd_s[b] = a1
                        expT[u] = a1

                    def back(u):
                        kt, c = units[u]
                        b = u % 4
                        for j in range(2):
                            deps = [expT[u]] if j == 0 else []
                            if kt == 0:
                                deps = deps + acc_free
                            state["lastpv"] = p.mm(deps, acc[c][j][:, 0:257], pt[b][:, j * 128:(j + 1) * 128],
                                                   vs[:, kt, :], kt == 0, kt == NKT - 1, mark=(j == 1))
                        rd_p[b] = state["lastpv"]

                    for u in range(NU + SK):
                        if u < NU:
                            front(u)
                        if u >= SK:
                            back(u - SK)
                    lastpv = state["lastpv"]
                    evs = []
                    for c in range(2):
                        for j in range(2):
                            dst = (o0[par] if c == 0 else o1[par])
                            e1 = p.V([lastpv], "reciprocal", rs[par][:, 2 * c + j:2 * c + j + 1], acc[c][j][:, 256:257])
                            if c == 0:
                                evs.append(p.V([e1], "tensor_scalar", dst[:, j, :], acc[c][j][:, 0:256],
                                               rs[par][:, 2 * c + j:2 * c + j + 1], None, ALU.mult))
                            else:
                                evs.append(p.V([e1], "tensor_scalar", dst[:, j, :], acc[c][j][:, 0:256],
                                               rs[par][:, 2 * c + j:2 * c + j + 1], lam_sb[:, 1:2], ALU.mult, ALU.mult))
                    acc_free = [evs[-1]]
                    f1 = p.V(evs, "tensor_tensor", o0[par][:], o0[par][:], o1[par][:], ALU.add)
                    for j in range(2):
                        sqt = p.A([f1], out=jk[:], in_=o0[par][:, j, :], func=AF.Square, accum_out=rs[par][:, 4 + j:5 + j])
                    f2 = p.rsqrt([sqt], rs[par][:, 4:6], rs[par][:, 4:6], 1.0 / 256, EPS)
                    f3 = p.V([f2], "tensor_scalar", rs[par][:, 4:6], rs[par][:, 4:6], (1.0 - LAMBDA_INIT), None, ALU.mult)
                    ag = p.A([tg], out=gt[par][:], in_=gt[par][:], func=AF.Silu)
                    for j in range(2):
                        f4 = p.V([f3], "scalar_tensor_tensor", o0[par][:, j, :], o0[par][:, j, :], rs[par][:, 4 + j:5 + j],
                                 gsub_sb[:], ALU.mult, ALU.mult)
                    f5 = p.V([f4, ag, ob_free[par]], "tensor_tensor", ob[par][:], o0[par][:], gt[par][:], ALU.mult)
                    ob_free[par] = p.st(og.ap()[q0:q0 + 256, h * 256:(h + 1) * 256].rearrange("(j p) v -> p j v", p=128),
                                        ob[par][:], f"ato{par}", deps=[f5])
                    gt_free[par] = [f5, ag]
                p.barrier()
                gt_free = [[], []]
                ob_free = [None, None]
        p.barrier()

    c128 = p.sb(ges, "c128_sb", [128, 3, 128], BF16)
    p.ld(c128[:], c128_d.ap().rearrange("k p c -> p k c"), "c0")
    cs256 = p.sb(ges, "cs256_sb", [128, 2, 512], BF16)
    p.ld(cs256[:], cs256_d.ap().rearrange("(c p) n -> p c n", p=128), "c0")
    tw_sb, cm_sb = {}, {}
    for L in tw_d:
        M = L // 128
        tw_sb[L] = p.sb(ges, f"tw{L}_sb", [128, 3, M], F32)
        p.ld(tw_sb[L][:], tw_d[L].ap().rearrange("k p m -> p k m"), "c0")
        cm_sb[L] = p.sb(ges, f"cm{L}_sb", [M, 2, M], BF16)
        p.ld(cm_sb[L][:], cm_d[L].ap().rearrange("k b d -> b k d"), "c0")
    hm = p.sb(ges, "hm_sb", [64, 2, 64], F32)
    p.ld(hm[:], hmask_d.ap().rearrange("k s t -> s k t"), "c0")
    delta = p.sb(ges, "delta_sb", [128, 256], F32)
    p.ld(delta[:], delta_d.ap(), "c0")
    dtabs = p.sb(ges, "dtabs_sb", [128, (LS // 128) * (LS // 256)], F32)
    p.ld(dtabs[:], dtabs_d.ap(), "c0")
    dtabp = p.sb(ges, "dtabp_sb", [128, (LP // 128) * (OWN // 256)], F32)
    p.ld(dtabp[:], dtabp_d.ap(), "c0")
    ownidx = p.sb(ges, "ownidx_sb", [128, OWN // 128], I32)
    p.ld(ownidx[:], ownidx_d.ap(), "c0")
    ghg_sb = p.sb(ges, "ghg_sb", [64, 128], F32)
    p.ld(ghg_sb[:], bcast_rows(ghg.ap(), 128, 64), "c0")
    gsub_sb = p.sb(ges, "gsub_sb", [128, 256], F32)
    p.ld(gsub_sb[:], bcast_rows(gsub.ap(), 256), "c0")
    lraw = p.sb(ges, "lraw_sb", [128, 2, 3, 8], F32)
    p.ld(lraw[:], lbl.ap().rearrange("d s (h k) -> k d s h", k=128), "c0", slow=True)
    lbt = p.sb(ges, "lbt_sb", [128, 4, 8], F32)
    lsum = p.sb(ges, "lsum_sb", [128, 2, 8], F32)
    lamv = p.sb(ges, "lamv_sb", [128, 4, 128], F32)
    p.ld(lamv[:], bass.AP(lam4.ap().tensor, 0, [[0, 128], [128, 4], [1, 128]]), "c0")
    lam_sb = p.sb(ges, "lam_sb", [128, 4], F32)
    ljunk = p.sb(ges, "ljunk_sb", [128, 128], F32)
    p.barrier()
    a = p.A([], out=lraw[:], in_=lraw[:], func=AF.Exp)
    v = p.V([a], "tensor_tensor", lsum[:], lraw[:, :, 0, :], lraw[:, :, 1, :], ALU.add)
    v = p.V([v], "tensor_tensor", lsum[:], lsum[:], lraw[:, :, 2, :], ALU.add)
    v = p.V([v], "reciprocal", lsum[:], lsum[:])
    v = p.V([v], "tensor_tensor", lbt[:, 0:2, :], lraw[:, :, 0, :], lsum[:], ALU.mult)
    v = p.V([v], "tensor_scalar", lbt[:, 2:4, :], lbt[:, 0:2, :], -1.0, 1.0, ALU.mult, ALU.add)
    v = p.V([v], "tensor_tensor", ljunk[:], lamv[:, 0, :], lamv[:, 1, :], ALU.mult)
    v = p.V([v], "tensor_reduce", lam_sb[:, 2:3], ljunk[:], mybir.AxisListType.X, ALU.add)
    v = p.V([v], "tensor_tensor", ljunk[:], lamv[:, 2, :], lamv[:, 3, :], ALU.mult)
    v = p.V([v], "tensor_reduce", lam_sb[:, 3:4], ljunk[:], mybir.AxisListType.X, ALU.add)
    a = p.A([v], out=lam_sb[:, 2:4], in_=lam_sb[:, 2:4], func=AF.Exp)
    v = p.V([a], "tensor_tensor", lam_sb[:, 0:1], lam_sb[:, 2:3], lam_sb[:, 3:4], ALU.subtract)
    v = p.V([v], "tensor_scalar", lam_sb[:, 1:2], lam_sb[:, 0:1], LAMBDA_INIT, -1.0, ALU.add, ALU.mult)
    p.barrier()

    seqs = [("s%d" % i, LS, xs.ap()[i * LS:(i + 1) * LS, :], ys.ap()[i * LS:(i + 1) * LS, :]) for i in range(NS)]
    seqs.append(("p", LP, xp.ap(), None))
    scale_q = 128 ** -0.5
    for (nm, L, xin, yout) in seqs:
        isP = yout is None
        hT = p.dram(f"hT_{nm}", [D, L], BF16)
        norm_T(xin, L, hT)
        done("norm")
        uT = p.dram(f"uT_{nm}", [1024, L], BF16)
        gat0 = p.dram(f"g0_{nm}", [L, 2048], F32)
        qrT = p.dram(f"qr_{nm}", [1024, L], F32)
        ffT = p.dram(f"ff_{nm}", [1024, L], F32)
        fbT = p.dram(f"fb_{nm}", [1024, L], F32)
        vtk = p.dram(f"vt_{nm}", [L, 1024], BF16)
        proj(hT, L, wb0i, 7168, [
            (0, 1024, "F", uT, 0, 1.0, BF16),
            (1024, 1024, "T", gat0, 0, 1.0, F32),
            (2048, 1024, "F", qrT, 0, 1.0, F32),
            (3072, 1024, "T", vtk, 0, 1.0, BF16),
            (4096, 1024, "F", ffT, 0, 1.0, F32),
            (5120, 1024, "F", fbT, 0, 1.0, F32),
            (6144, 1024, "T", gat0, 1024, 1.0, F32),
        ])
        done("proj0")
        ya = p.dram(f"ya_{nm}", [L, 1024], F32)
        fnet(uT, L, ya, (c128, cs256, tw_sb[L], cm_sb[L]))
        done("fnet")
        ymix = p.dram(f"ym_{nm}", [L, 2048], BF16)
        hgrn(qrT, ffT, fbT, vtk, gat0, L, ymix, lbt, hm, ghg_sb)
        done("hgrn")
        x1 = p.dram(f"x1_{nm}", [L, D], F32)
        h1T = p.dram(f"h1T_{nm}", [D, L], BF16)
        outproj(L, ymix, ya, gat0, wb0o, gpost_sb[0], xin, x1.ap(), h1T)
        done("out0")
        kT = p.dram(f"kT_{nm}", [2048, L], BF16)
        v1t = p.dram(f"v1_{nm}", [L, 2048], BF16)
        if not isP:
            Lq = L
            qT = p.dram(f"qT_{nm}", [2048, Lq], BF16)
            gat1 = p.dram(f"g1_{nm}", [Lq, 2048], F32)
            proj(h1T, L, wb1i, 8192, [
                (0, 2048, "F", qT, 0, scale_q, BF16),
                (2048, 2048, "F", kT, 0, 1.0, BF16),
                (4096, 2048, "T", v1t, 0, 1.0, BF16),
                (6144, 2048, "T", gat1, 0, 1.0, F32),
            ])
            xres1 = x1.ap()
            dtab = dtabs
        else:
            Lq = OWN
            proj(h1T, L, wb1i, 8192, [
                (2048, 2048, "F", kT, 0, 1.0, BF16),
                (4096, 2048, "T", v1t, 0, 1.0, BF16),
            ])
            x1own = p.dram("x1own", [OWN, D], F32)
            with contextlib.ExitStack() as es:
                gx = p.sb(es, "gx", [128, D], F32)
                prev = None
                for ti in range(OWN // 128):
                    p.pool.wait(prev)
                    ds = p.dsem("gath")
                    p.nc.gpsimd.indirect_dma_start(
                        out=gx[:], out_offset=None, in_=x1.ap(),
                        in_offset=bass.IndirectOffsetOnAxis(ap=ownidx[:, ti:ti + 1], axis=0),
                    ).then_inc(ds.sem, 16)
                    ds.n += 16
                    tok = (ds, ds.n)
                    p.pending.append(tok)
                    prev = p.st(x1own.ap()[ti * 128:(ti + 1) * 128, :], gx[:], "gaths", deps=[tok])
            p.barrier()
            h1To = p.dram("h1To", [D, OWN], BF16)
            norm_T(x1own.ap(), OWN, h1To)
            qT = p.dram(f"qT_{nm}", [2048, Lq], BF16)
            gat1 = p.dram(f"g1_{nm}", [Lq, 2048], F32)
            proj(h1To, OWN, wb1i, 8192, [
                (0, 2048, "F", qT, 0, scale_q, BF16),
                (6144, 2048, "T", gat1, 0, 1.0, F32),
            ])
            xres1 = x1own.ap()
            yout = yp.ap()
            dtab = dtabp
        done("proj1")
        og = p.dram(f"og_{nm}", [Lq, 2048], BF16)
        attention(qT, kT, v1t, gat1, Lq, L, og, dtab, lam_sb, gsub_sb, delta)
        done("attn")
        outproj(Lq, og, None, None, wb1o, gpost_sb[1], xres1, yout, None)
        done("seq0")


def dft_cs(n):
    j = np.arange(n)
    ang = 2 * np.pi * np.outer(j, j) / n
    return np.cos(ang), np.sin(ang)


def host_tables(cfg, core):
    NS, LS, LP, OWN = cfg["NS"], cfg["LS"], cfg["LP"], cfg["OWN"]
    t = {}
    t["ident"] = np.eye(128, dtype=np.float32).astype(NPBF)
    c, s = dft_cs(128)
    t["c128"] = np.stack([c, s, -s]).astype(np.float32).astype(NPBF)
    c, s = dft_cs(256)
    t["cs256"] = np.concatenate([c, -s], axis=1).astype(np.float32).astype(NPBF)
    for L in sorted({LS, LP}):
        M = L // 128
        ang = 2 * np.pi * np.outer(np.arange(128), np.arange(M)) / L
        t[f"tw{L}"] = np.stack([np.cos(ang), np.sin(ang), -np.sin(ang)]).astype(np.float32)
        cm, sm = dft_cs(M)
        sc = 1.0 / math.sqrt(L * 256)
        t[f"cm{L}"] = np.stack([cm * sc, sm * sc]).astype(np.float32).astype(NPBF)
    s_, t_ = np.meshgrid(np.arange(64), np.arange(64), indexing="ij")
    t["hmask"] = np.stack([(s_ <= t_), (s_ >= t_)]).astype(np.float32)
    t["delta"] = (np.arange(128)[:, None] - np.arange(256)[None, :]).astype(np.float32)
    kt, qb = np.meshgrid(np.arange(LS // 128), np.arange(LS // 256), indexing="ij")
    t["dtabs"] = np.broadcast_to((kt * 128 - qb * 256).reshape(1, -1), (128, kt.size)).astype(np.float32).copy()
    kt, qb = np.meshgrid(np.arange(LP // 128), np.arange(OWN // 256), indexing="ij")
    t["dtabp"] = np.broadcast_to((kt * 128 - (core * OWN + qb * 256)).reshape(1, -1), (128, kt.size)).astype(np.float32).copy()
    t["ownidx"] = (core * OWN + np.arange(OWN)).reshape(OWN // 128, 128).T.astype(np.int32).copy()
    return t


def run(inputs, cfg, ncores):
    NS, LS, LP, OWN = cfg["NS"], cfg["LS"], cfg["LP"], cfg["OWN"]
    f = lambda a: np.ascontiguousarray(np.asarray(a, dtype=np.float32))
    xsamp = f(inputs["x_sample"])
    shared = {
        "xp": f(inputs["x_prompt"])[0],
        "w0i": f(inputs["ev_w_in"])[0], "w0o": f(inputs["ev_w_out"])[0],
        "w1i": f(inputs["od_w_in"])[0], "w1o": f(inputs["od_w_out"])[0],
        "g0pre": f(inputs["ev_norm_pre"])[0], "g0post": f(inputs["ev_norm_post"])[0],
        "g1pre": f(inputs["od_norm_pre"])[0], "g1post": f(inputs["od_norm_post"])[0],
        "lbl": f(inputs["hgrn_lb_logits"]), "ghg": f(inputs["hgrn_norm"])[0],
        "lam4": np.stack([f(inputs["lambda_q1"])[0], f(inputs["lambda_k1"])[0],
                          f(inputs["lambda_q2"])[0], f(inputs["lambda_k2"])[0]]),
        "gsub": f(inputs["subln"])[0],
    }
    nc = build(cfg)
    in_maps = []
    for c in range(ncores):
        m = dict(shared)
        m["xs"] = xsamp[c * NS:(c + 1) * NS].reshape(NS * LS, D)
        m.update(host_tables(cfg, c))
        in_maps.append(m)
    res = run_bass_kernel_spmd(nc, in_maps, core_ids=list(range(ncores)))
    global LAST_RES
    LAST_RES = res.results
    y_s = np.concatenate([r["ys"].reshape(NS, LS, D) for r in res.results], axis=0)
    y_p = np.concatenate([r["yp"] for r in res.results], axis=0)[None]
    return y_p.astype(np.float32), y_s.astype(np.float32)


def kernel(**inputs):
    import os
    cfg = {"NS": 2, "LS": 2048, "LP": 8192, "OWN": 1024}
    if os.environ.get("K_STOP"):
        cfg["stop"] = os.environ["K_STOP"]
    return run(inputs, cfg, 8)
```

```python
import contextlib, math
import numpy as np
import ml_dtypes
import concourse.bass as bass
import concourse.mybir as mybir
from concourse.bass_utils import run_bass_kernel_spmd

F32, BF16, I32 = mybir.dt.float32, mybir.dt.bfloat16, mybir.dt.int32
AF = mybir.ActivationFunctionType
ALU = mybir.AluOpType
D = 2048
KC = 16
EPS = 1e-6
LAMBDA_INIT = 0.8 - 0.6 * math.exp(-0.3 * 1)
SLOPES = [2.0 ** (-8.0 * (h + 1) / 8) for h in range(8)]
NPBF = ml_dtypes.bfloat16


class StopBuild(Exception):
    pass


class Eng:
    def __init__(self, e, sem):
        self.e, self.sem, self.n, self.seen = e, sem, 0, {}

    def mark(self, ins):
        ins.then_inc(self.sem, 1)
        self.n += 1
        return (self, self.n)

    def wait(self, *toks):
        for tok in toks:
            if tok is None:
                continue
            src, n = tok
            if self.seen.get(src, 0) >= n:
                continue
            self.e.wait_ge(src.sem, n)
            self.seen[src] = n


class DSem:
    def __init__(self, sem):
        self.sem, self.n = sem, 0


class P:
    def __init__(self, cfg):
        self.cfg = cfg
        self.es = contextlib.ExitStack()
        nc = self.nc = bass.Bass("TRN2", target_bir_lowering=False)
        mk = lambda nm: self.es.enter_context(nc.semaphore(nm))
        self.pe = Eng(nc.tensor, mk("s_pe"))
        self.act = Eng(nc.scalar, mk("s_act"))
        self.dve = Eng(nc.vector, mk("s_dve"))
        self.pool = Eng(nc.gpsimd, mk("s_pool"))
        self.sp = Eng(nc.sync, mk("s_sp"))
        self.engs = [self.pe, self.act, self.dve, self.pool, self.sp]
        self.dsems = {}
        self.pending = []
        self.ndram = 0
        self.ps = [self.es.enter_context(nc.psum_tensor(f"ps{i}", [128, 512], F32)) for i in range(6)]
        self.pb = [self.es.enter_context(nc.psum_tensor(f"pb{i}", [128, 1024], BF16)) for i in range(2)]
        self.dummy = self.es.enter_context(nc.sbuf_tensor("dummy_sb", [128, 2], F32))
        self.pool.mark(nc.gpsimd.memset(self.dummy[:], 0.0))

    def sb(self, es, name, shape, dt):
        self.nsb = getattr(self, "nsb", 0) + 1
        return es.enter_context(self.nc.sbuf_tensor(f"{name}_{self.nsb}", shape, dt))

    def dram(self, name, shape, dt, kind="Internal"):
        t = self.nc.dram_tensor(name, list(shape), dt, kind=kind)
        if not hasattr(self, "named"):
            self.named = {}
        self.named[name] = (t, list(shape), dt)
        return t

    def dump(self):
        self.barrier()
        for name in self.cfg.get("dump", []):
            if name not in self.named:
                continue
            t, shape, dt = self.named[name]
            o = self.nc.dram_tensor("dbg_" + name, shape, dt, kind="ExternalOutput")
            self.ld(o.ap(), t.ap(), "dump")
        self.barrier()

    def dsem(self, key):
        if key not in self.dsems:
            self.dsems[key] = DSem(self.es.enter_context(self.nc.semaphore("d_" + key)))
        return self.dsems[key]

    def dma(self, q, out, in_, key, deps=(), slow=False):
        q.wait(*deps)
        ds = self.dsem(key)
        kw = {"allow_slow_non_contiguous": True} if slow else {}
        q.e.dma_start(out=out, in_=in_, **kw).then_inc(ds.sem, 16)
        ds.n += 16
        tok = (ds, ds.n)
        self.pending.append(tok)
        return tok

    def ld(self, out, in_, key, deps=(), slow=False):
        return self.dma(self.sp, out, in_, key, deps, slow)

    def st(self, out, in_, key, deps=()):
        return self.dma(self.pool, out, in_, key, deps)

    def barrier(self):
        toks = [(e, e.n) for e in self.engs if e.n > 0] + self.pending
        for e in self.engs:
            e.wait(*toks)
        self.pending = []

    def A(self, deps, *a, **k):
        self.act.wait(*deps)
        tok = self.act.mark(self.act.e.activation(*a, **k))
        if k.get("accum_out") is not None:
            tok = self.act.mark(self.act.e.activation(out=self.dummy[:, 1:2], in_=self.dummy[:, 0:1], func=AF.Copy))
        return tok

    def V(self, deps, fn, *a, **k):
        self.dve.wait(*deps)
        return self.dve.mark(getattr(self.dve.e, fn)(*a, **k))

    def G(self, deps, fn, *a, **k):
        self.pool.wait(*deps)
        return self.pool.mark(getattr(self.pool.e, fn)(*a, **k))

    def X(self, eng, deps, fn, *a, **k):
        eng.wait(*deps)
        return eng.mark(getattr(eng.e, fn)(*a, **k))

    def rsqrt(self, deps, out, in_, mul, add):
        a = self.A(deps, out=out, in_=in_, func=AF.Ln, scale=float(mul), bias=float(add))
        return self.A([a], out=out, in_=out, func=AF.Exp, scale=-0.5)

    def mm(self, deps, out, lhsT, rhs, start, stop, mark=False):
        self.pe.wait(*deps)
        ins = self.pe.e.matmul(out, lhsT, rhs, start=start, stop=stop)
        return self.pe.mark(ins) if mark else None

    def tr(self, deps, out, in_, ident, mark=False):
        self.pe.wait(*deps)
        ins = self.pe.e.transpose(out, in_, ident)
        return self.pe.mark(ins) if mark else None


def bcast_rows(ap_dram_1d, n, parts=128):
    return bass.AP(ap_dram_1d.tensor, ap_dram_1d.offset, [[0, parts], [1, n]])


def build(cfg):
    p = P(cfg)
    try:
        _build(cfg, p)
    except StopBuild:
        p.dump()
        return p.nc
    p.dump()
    p.es.close()
    return p.nc


def _build(cfg, p):
    NS, LS, LP, OWN = cfg["NS"], cfg["LS"], cfg["LP"], cfg["OWN"]

    def done(tag):
        if cfg.get("stop") == tag:
            raise StopBuild()
    nc = p.nc
    inp = lambda name, shape, dt=F32: nc.dram_tensor(name, list(shape), dt, kind="ExternalInput")
    xs = inp("xs", [NS * LS, D])
    xp = inp("xp", [LP, D])
    w0i, w0o = inp("w0i", [D, 7168]), inp("w0o", [D, D])
    w1i, w1o = inp("w1i", [D, 8192]), inp("w1o", [D, D])
    g0pre, g0post = inp("g0pre", [D]), inp("g0post", [D])
    g1pre, g1post = inp("g1pre", [D]), inp("g1post", [D])
    lbl = inp("lbl", [2, 3, 1024])
    ghg = inp("ghg", [128])
    lam4 = inp("lam4", [4, 128])
    gsub = inp("gsub", [256])
    ident_d = inp("ident", [128, 128], BF16)
    c128_d = inp("c128", [3, 128, 128], BF16)
    cs256_d = inp("cs256", [256, 512], BF16)
    tw_d = {L: inp(f"tw{L}", [3, 128, L // 128]) for L in sorted({LS, LP})}
    cm_d = {L: inp(f"cm{L}", [2, L // 128, L // 128], BF16) for L in sorted({LS, LP})}
    hmask_d = inp("hmask", [2, 64, 64])
    delta_d = inp("delta", [128, 256])
    dtabs_d = inp("dtabs", [128, (LS // 128) * (LS // 256)])
    dtabp_d = inp("dtabp", [128, (LP // 128) * (OWN // 256)])
    ownidx_d = inp("ownidx", [128, OWN // 128], I32)
    ys = nc.dram_tensor("ys", [NS * LS, D], F32, kind="ExternalOutput")
    yp = nc.dram_tensor("yp", [OWN, D], F32, kind="ExternalOutput")

    ges = p.es
    ident = p.sb(ges, "ident_sb", [128, 128], BF16)
    toks = [p.ld(ident[:], ident_d.ap(), "c0")]
    gpost_sb = [p.sb(ges, f"gpost{i}", [128, D], F32) for i in range(2)]
    toks.append(p.ld(gpost_sb[0][:], bcast_rows(g0post.ap(), D), "c0"))
    toks.append(p.ld(gpost_sb[1][:], bcast_rows(g1post.ap(), D), "c0"))
    gpre_sb = [p.sb(ges, f"gpre{i}", [128, KC], F32) for i in range(2)]
    toks.append(p.ld(gpre_sb[0][:], g0pre.ap().rearrange("(c p) -> p c", p=128), "c0", slow=True))
    toks.append(p.ld(gpre_sb[1][:], g1pre.ap().rearrange("(c p) -> p c", p=128), "c0", slow=True))
    p.barrier()

    def prep_w(w, ncols, gcol, name):
        wb = p.dram(name, [D, ncols], BF16)
        with contextlib.ExitStack() as es:
            CW = 1024
            wf = [p.sb(es, f"wf{i}", [128, CW], F32) for i in range(2)]
            wo = [p.sb(es, f"wo{i}", [128, CW], BF16) for i in range(2)]
            cons = [None, None]
            sts = [None, None]
            i = 0
            for kc in range(KC):
                for c0 in range(0, ncols, CW):
                    b = i % 2
                    t = p.ld(wf[b][:], w.ap()[kc * 128:(kc + 1) * 128, c0:c0 + CW], f"wl{b}", deps=[cons[b]])
                    eng = p.dve if b == 0 else p.pool
                    if gcol is None:
                        cons[b] = p.X(eng, [t, sts[b]], "tensor_copy", wo[b][:], wf[b][:])
                    else:
                        cons[b] = p.X(eng, [t, sts[b]], "tensor_scalar", wo[b][:], wf[b][:],
                                      gcol[:, kc:kc + 1], None, ALU.mult)
                    sts[b] = p.dma(p.act, wb.ap()[kc * 128:(kc + 1) * 128, c0:c0 + CW], wo[b][:], f"ws{b}",
                                   deps=[cons[b]])
                    i += 1
        p.barrier()
        return wb

    wb0i = prep_w(w0i, 7168, gpre_sb[0], "wb0i")
    wb0o = prep_w(w0o, D, None, "wb0o")
    wb1i = prep_w(w1i, 8192, gpre_sb[1], "wb1i")
    wb1o = prep_w(w1o, D, None, "wb1o")
    done("prep")

    def norm_T(x_rows, L, hT):
        with contextlib.ExitStack() as es:
            xt = [p.sb(es, f"nx{i}", [128, D], F32) for i in range(2)]
            junk = p.sb(es, "njunk", [128, D], BF16)
            hb = [p.sb(es, f"nhb{i}", [128, D], BF16) for i in range(2)]
            ho = [p.sb(es, f"nho{i}", [128, KC, 128], BF16) for i in range(2)]
            st_ = [p.sb(es, f"nst{i}", [128, 2], F32) for i in range(2)]
            rd = [None, None]
            hbr = [None, None]
            hor = [None, None]
            for ti in range(L // 128):
                b = ti % 2
                t = p.ld(xt[b][:], x_rows[ti * 128:(ti + 1) * 128, :], f"nl{b}", deps=[rd[b]])
                a1 = p.A([t], out=junk[:], in_=xt[b][:], func=AF.Square, accum_out=st_[b][:, 0:1])
                v2 = p.rsqrt([a1], st_[b][:, 1:2], st_[b][:, 0:1], 1.0 / D, EPS)
                v3 = p.V([v2, t, hbr[b]], "tensor_scalar", hb[b][:], xt[b][:], st_[b][:, 1:2], None, ALU.mult)
                rd[b] = v3
                for kc in range(KC):
                    tk = p.tr([v3, hor[b]] if kc == 0 else [], p.pb[kc // 8][:, (kc % 8) * 128:(kc % 8 + 1) * 128],
                              hb[b][:, kc * 128:(kc + 1) * 128], ident[:], mark=(kc == KC - 1))
                hbr[b] = tk
                c1 = p.A([tk, hor[b]], out=ho[b][:, 0:8, :], in_=p.pb[0][:].rearrange("p (k t) -> p k t", k=8),
                         func=AF.Copy)
                c2 = p.V([tk, hor[b]], "tensor_copy", ho[b][:, 8:16, :],
                         p.pb[1][:].rearrange("p (k t) -> p k t", k=8))
                p.pe.wait(c1, c2)
                hor[b] = p.st(hT.ap().rearrange("(k p) t -> p k t", p=128)[:, :, ti * 128:(ti + 1) * 128],
                              ho[b][:], f"ns{b}", deps=[c1, c2])
        p.barrier()

    def proj(hT, L, wb, ncols_total, jobs):
        TB = min(L, 1024)
        with contextlib.ExitStack() as es:
            hblk = p.sb(es, "pj_h", [128, KC, TB], BF16)
            wblk = [p.sb(es, f"pj_w{i}", [128, KC, 512], BF16) for i in range(2)]
            osb = {F32: [p.sb(es, f"pj_of{i}", [128, 512], F32) for i in range(2)],
                   BF16: [p.sb(es, f"pj_ob{i}", [128, 512], BF16) for i in range(2)]}
            wread = [None, None]
            ost = {F32: [None, None], BF16: [None, None]}
            psr = [None] * 4
            wi = 0
            oi = 0
            pi = 0
            hread = None
            for tb in range(L // TB):
                th = p.ld(hblk[:], hT.ap().rearrange("(k p) t -> p k t", p=128)[:, :, tb * TB:(tb + 1) * TB],
                          "pjh", deps=[hread])
                cbs = [(j, c) for j in jobs for c in range(0, j[1], 512)]
                for (job, c) in cbs:
                    col0, ncols, mode, od, o0, scale, odt = job
                    b = wi % 2
                    wi += 1
                    tw = p.ld(wblk[b][:], wb.ap().rearrange("(k p) n -> p k n", p=128)[:, :, col0 + c:col0 + c + 512],
                              f"pjw{b}", deps=[wread[b]])
                    last = None
                    TW = min(512, TB)
                    if mode == "F":
                        subs = [(s4, t5) for s4 in range(4) for t5 in range(TB // TW)]
                    else:
                        subs = [(s4, 0) for s4 in range(TB // 128)]
                    for (s4, t5) in subs:
                        pk = pi % 4
                        pi += 1
                        W_ = TW if mode == "F" else 512
                        ps = p.ps[pk][:, 0:W_]
                        for kc in range(KC):
                            if mode == "F":
                                lhsT, rhs = wblk[b][:, kc, s4 * 128:(s4 + 1) * 128], hblk[:, kc, t5 * TW:(t5 + 1) * TW]
                            else:
                                lhsT, rhs = hblk[:, kc, s4 * 128:(s4 + 1) * 128], wblk[b][:, kc, :]
                            tk = p.mm([th, tw, psr[pk]] if kc == 0 else [], ps, lhsT, rhs, kc == 0, kc == KC - 1,
                                      mark=(kc == KC - 1))
                        last = tk
                        ob = oi % 2
                        oi += 1
                        o = osb[odt][ob][:, 0:W_]
                        if oi % 2 == 0:
                            ev = p.A([tk, ost[odt][ob]], out=o, in_=ps, func=AF.Copy, scale=float(scale))
                        else:
                            ev = p.V([tk, ost[odt][ob]], "tensor_scalar", o, ps, float(scale), None, ALU.mult)
                        psr[pk] = ev
                        if mode == "F":
                            dst = od.ap()[o0 + c + s4 * 128:o0 + c + (s4 + 1) * 128,
                                          tb * TB + t5 * TW:tb * TB + (t5 + 1) * TW]
                        else:
                            dst = od.ap()[tb * TB + s4 * 128:tb * TB + (s4 + 1) * 128, o0 + c:o0 + c + 512]
                        ost[odt][ob] = p.st(dst, o, f"pjs{ob}{'f' if odt == F32 else 'b'}", deps=[ev])
                    wread[b] = last
                    hread = last
        p.barrier()

    def fnet(uT, L, ya, tabs):
        M = L // 128
        c128, cs256, tw, cm = tabs
        Bd = p.dram(f"fn_B{p.ndram}", [128, M, 512], BF16)
        p.ndram += 1
        CP = 32
        with contextlib.ExitStack() as es:
            ug = p.sb(es, "fn_u", [128, 2, L], BF16)
            MB = min(M, 32)
            V = p.sb(es, "fn_V", [128, MB, 512], BF16)
            Bs = p.sb(es, "fn_Bs", [128, MB, 512], BF16)
            tmp = [p.sb(es, f"fn_t{i}", [128, 2, 256], F32) for i in range(2)]
            Bt = p.sb(es, "fn_Bt", [M, CP, 512], BF16)
            Y = [p.sb(es, f"fn_Y{i}", [M, 2, 256], F32) for i in range(2)]
            for g in range(4):
                tu = p.ld(ug[:], uT.ap()[g * 256:(g + 1) * 256, :].rearrange("(c p) t -> p c t", p=128), "fnu")
                ts_all = []
                for bh in range(M // MB):
                    evs = []
                    prev = [None, None]
                    for bl in range(MB):
                        b = bh * MB + bl
                        pk = b % 2
                        for ch in range(2):
                            lhsT = bass.AP(ug, ch * L + b, [[2 * L, 128], [M, 128]])
                            tk = p.mm([tu, prev[pk]] if ch == 0 else [], p.ps[pk][:], lhsT, cs256[:, ch, :],
                                      ch == 0, ch == 1, mark=(ch == 1))
                        if b % 2 == 0:
                            ev = p.A([tk], out=V[:, bl, :], in_=p.ps[pk][:], func=AF.Copy)
                        else:
                            ev = p.V([tk], "tensor_copy", V[:, bl, :], p.ps[pk][:])
                        prev[pk] = ev
                        evs.append(ev)
                    prevr = [None, None]
                    tw_tok = []
                    for bp in range(MB // 2):
                        pk = 2 + (bp % 2) * 2
                        Ar, Ai = p.ps[pk], p.ps[pk + 1]
                        b0 = 2 * bp
                        dep = [evs[b0], evs[b0 + 1], prevr[bp % 2]]
                        ar3 = Ar[:].rearrange("p (b f) -> p b f", b=2)
                        ai3 = Ai[:].rearrange("p (b f) -> p b f", b=2)
                        p.mm(dep, ar3, c128[:, 0, :], V[:, b0:b0 + 2, 0:256], True, False)
                        p.mm([], ar3, c128[:, 1, :], V[:, b0:b0 + 2, 256:512], False, True)
                        p.mm([], ai3, c128[:, 0, :], V[:, b0:b0 + 2, 256:512], True, False)
                        tk = p.mm([], ai3, c128[:, 2, :], V[:, b0:b0 + 2, 0:256], False, True, mark=True)
                        last = []
                        for j in range(2):
                            bl = b0 + j
                            b = bh * MB + bl
                            t1 = p.V([tk], "tensor_scalar", tmp[0][:, j, :], ar3[:, j, :], tw[:, 0, b:b + 1], None, ALU.mult)
                            t3 = p.V([tk], "tensor_scalar", tmp[1][:, j, :], ai3[:, j, :], tw[:, 0, b:b + 1], None, ALU.mult)
                            r1 = p.V([t1, tk], "scalar_tensor_tensor", Bs[:, bl, 0:256], ai3[:, j, :], tw[:, 1, b:b + 1],
                                     tmp[0][:, j, :], ALU.mult, ALU.add)
                            r2 = p.V([t3, tk], "scalar_tensor_tensor", Bs[:, bl, 256:512], ar3[:, j, :], tw[:, 2, b:b + 1],
                                     tmp[1][:, j, :], ALU.mult, ALU.add)
                            last = [r1, r2]
                        p.act.wait(*last)
                        prevr[bp % 2] = last[1]
                        tw_tok = last
                    ts_all.append(p.st(Bd.ap()[:, bh * MB:(bh + 1) * MB, :], Bs[:], "fnb", deps=tw_tok))
                    p.barrier()
                if True:
                    ts_ = ts_all[-1]
                    yst = [None, None]
                    evp = [None, None]
                    bt_read = None
                    for cp in range(128 // CP):
                        tl = p.ld(Bt[:], Bd.ap()[cp * CP:(cp + 1) * CP, :, :].rearrange("c b f -> b c f"), "fnbt",
                                  deps=[ts_, bt_read])
                        for c2 in range(CP // 2):
                            pk = c2 % 2
                            ps3 = p.ps[pk][0:M, :].rearrange("p (c f) -> p c f", c=2)
                            p.mm([tl, evp[pk]], ps3, cm[0:M, 0, :], Bt[:, 2 * c2:2 * c2 + 2, 0:256], True, False)
                            tk = p.mm([], ps3, cm[0:M, 1, :], Bt[:, 2 * c2:2 * c2 + 2, 256:512], False, True, mark=True)
                            if c2 % 2 == 0:
                                ev = p.A([tk, yst[pk]], out=Y[pk][:], in_=ps3, func=AF.Copy)
                            else:
                                ev = p.V([tk, yst[pk]], "tensor_copy", Y[pk][:], ps3)
                            c_abs = cp * CP + 2 * c2
                            dst = ya.ap().rearrange("(d c) f -> d c f", c=128)[:, c_abs:c_abs + 2, g * 256:(g + 1) * 256]
                            yst[pk] = p.st(dst, Y[pk][:], f"fny{pk}", deps=[ev])
                            evp[pk] = ev
                            bt_read = tk
                    p.barrier()
        p.barrier()

    def hgrn(qrT, ffT, fbT, vtok, gates, L, ymix, lbt, hm, ghg_sb):
        SEG = min(L, 2048)
        NCH = SEG // 64
        nseg = L // SEG
        NT = L // 64
        dS = p.dram(f"hg_dS{p.ndram}", [2, NT, 128, 128], F32)
        Sb = p.dram(f"hg_Sb{p.ndram}", [2, NT, 128, 128], BF16)
        qd = p.dram(f"hg_qd{p.ndram}", [2, 128, L], BF16)
        scd = p.dram(f"hg_sc{p.ndram}", [64, NT, 64], BF16)
        eld = p.dram(f"hg_el{p.ndram}", [2, 128, NT], F32)
        p.ndram += 1
        for h in range(8):
            SEG1 = min(L, 1024)
            NCH1 = SEG1 // 64
            nseg1 = L // SEG1
            with contextlib.ExitStack() as es:
                qr = p.sb(es, "h_qr", [128, SEG1], F32)
                rmask = p.sb(es, "h_rm", [128, SEG1], F32)
                qdec = [p.sb(es, f"h_qd{i}", [128, SEG1], BF16) for i in range(2)]
                vt = p.sb(es, "h_v", [64, NCH1, 128], BF16)
                sct = p.sb(es, "h_sc", [64, NCH1, 64], BF16)
                sc1 = p.sb(es, "h_sc1", [64, NCH1, 64], F32)
                el = p.sb(es, "h_el", [128, 2, NCH1], F32)
                B = []
                for d in range(2):
                    bd = {}
                    for nm_ in ("fr", "f", "g", "k", "cum", "cb", "ex", "ex2", "ex3"):
                        bd[nm_] = p.sb(es, f"h_{nm_}{d}", [128, SEG1], F32)
                    bd["kdec"] = p.sb(es, f"h_kd{d}", [128, SEG1], BF16)
                    bd["kend"] = p.sb(es, f"h_ke{d}", [128, SEG1], BF16)
                    bd["kendT"] = p.sb(es, f"h_keT{d}", [64, NCH1, 128], BF16)
                    bd["dSs"] = [p.sb(es, f"h_dS{d}{j}", [128, 4, 128], F32) for j in range(2)]
                    B.append(bd)
                m1 = p.G([], "memset", rmask[:], 1.0)
                m2 = p.G([], "memset", rmask[:].rearrange("p (n s) -> p n s", s=64)[:, :, 0:1], 0.0)
                p.barrier()
                for sg in range(nseg1):
                    t0 = sg * SEG1
                    tq = p.ld(qr[:], qrT.ap()[h * 128:(h + 1) * 128, t0:t0 + SEG1], "hq")
                    tv = p.ld(vt[:], vtok.ap()[t0:t0 + SEG1, h * 128:(h + 1) * 128].rearrange("(n s) v -> s n v", s=64),
                              "hv")
                    aq = p.A([tq], out=qr[:], in_=qr[:], func=AF.Silu)
                    shared = {}

                    def dir_gen(d):
                        bd = B[d]
                        fr, f_, g_, k_, cum, cb = bd["fr"], bd["f"], bd["g"], bd["k"], bd["cum"], bd["cb"]
                        ex, ex2, ex3, kdec, kend, kendT, dSs = (bd["ex"], bd["ex2"], bd["ex3"], bd["kdec"], bd["kend"],
                                                                bd["kendT"], bd["dSs"])
                        src = ffT if d == 0 else fbT
                        tf = p.ld(fr[:], src.ap()[h * 128:(h + 1) * 128, t0:t0 + SEG1], f"hf{d}")
                        a1 = p.A([tf], out=f_[:], in_=fr[:], func=AF.Sigmoid)
                        yield
                        v1 = p.V([a1], "tensor_scalar", f_[:], f_[:], lbt[:, 2 + d, h:h + 1], lbt[:, d, h:h + 1],
                                 ALU.mult, ALU.add)
                        yield
                        a2 = p.A([v1], out=g_[:], in_=f_[:], func=AF.Ln)
                        g1 = p.G([v1], "tensor_scalar", k_[:], f_[:], -1.0, 1.0, ALU.mult, ALU.add)
                        yield
                        v2 = p.V([a2], "tensor_tensor_scan", cum[:], rmask[:], g_[:], 0.0, ALU.mult, ALU.add)
                        yield
                        cum3 = cum[:].rearrange("p (n s) -> p n s", s=64)
                        lastb = bass.AP(cum, 63, [[SEG1, 128], [64, NCH1], [0, 64]])
                        a6 = p.A([v2], out=el[:, d, :], in_=cum3[:, :, 63], func=AF.Exp)
                        if d == 0:
                            cc = cum
                            v3 = p.V([v2], "tensor_tensor", cb[:].rearrange("p (n s) -> p n s", s=64), lastb, cum3,
                                     ALU.subtract)
                            dl = cb
                            yield
                        else:
                            v3a = p.V([v2], "tensor_tensor", cb[:].rearrange("p (n s) -> p n s", s=64), lastb, cum3,
                                      ALU.subtract)
                            yield
                            v3b = p.V([v3a], "tensor_tensor", cb[:], cb[:], g_[:], ALU.add)
                            yield
                            cc = cb
                            v3 = p.V([v3b], "tensor_tensor", g_[:], cum[:], g_[:], ALU.subtract)
                            dl = g_
                            yield
                        a3 = p.A([v3], out=ex[:], in_=cc[:], func=AF.Exp)
                        yield
                        a4 = p.A([v3], out=ex2[:], in_=cc[:], func=AF.Exp, scale=-1.0)
                        g2 = p.V([a3, aq], "tensor_tensor", qdec[d][:], qr[:], ex[:], ALU.mult)
                        yield
                        a5 = p.A([v3], out=ex3[:], in_=dl[:], func=AF.Exp)
                        g3 = p.G([a4, g1], "tensor_tensor", kdec[:], k_[:], ex2[:], ALU.mult)
                        yield
                        g4 = p.V([a5, g1], "tensor_tensor", kend[:], k_[:], ex3[:], ALU.mult)
                        p.st(qd.ap()[d, :, t0:t0 + SEG1], qdec[d][:], f"hsq{d}", deps=[g2])
                        p.st(eld.ap()[d, :, sg * NCH1:(sg + 1) * NCH1], el[:, d, :], f"hse{d}", deps=[a6])
                        yield
                        nb = min(8, NCH1)
                        NGR = NCH1 // nb
                        mk_ap = bass.AP(hm, d * 64, [[128, 64], [0, nb], [1, 64]])
                        hz = {}
                        psS, pbT = p.ps[d], p.pb[d]
                        ev = None
                        for gq in range(NGR + 1):
                            if gq < NGR:
                                n0 = gq * nb
                                for j in range(nb):
                                    sl = slice((n0 + j) * 64, (n0 + j + 1) * 64)
                                    tk = p.mm([g2, g3, hz.get("sc")] if j == 0 else [], psS[0:64, j * 64:(j + 1) * 64],
                                              kdec[:, sl], qdec[d][:, sl], True, True, mark=(j == nb - 1))
                                psv = psS[0:64, 0:nb * 64].rearrange("p (n s) -> p n s", s=64)
                                if d == 0:
                                    ev = p.V([tk], "tensor_tensor", sc1[:, n0:n0 + nb, :], psv, mk_ap, ALU.mult)
                                    shared[("sc1", gq)] = ev
                                else:
                                    ev0 = p.V([tk], "tensor_tensor", sct[:, n0:n0 + nb, :], psv, mk_ap, ALU.mult)
                                    ev = p.V([ev0, shared[("sc1", gq)]], "tensor_tensor", sct[:, n0:n0 + nb, :],
                                             sct[:, n0:n0 + nb, :], sc1[:, n0:n0 + nb, :], ALU.add)
                                hz["sc"] = ev
                                yield
                                for j in range(nb):
                                    sl = slice((n0 + j) * 64, (n0 + j + 1) * 64)
                                    tk2 = p.tr([g4, hz.get("tr")] if j == 0 else [], pbT[0:64, j * 128:(j + 1) * 128],
                                               kend[:, sl], ident[:], mark=(j == nb - 1))
                                ev2 = p.A([tk2], out=kendT[:, n0:n0 + nb, :],
                                          in_=pbT[0:64, 0:nb * 128].rearrange("p (n k) -> p n k", k=128), func=AF.Copy)
                                hz["tr"] = ev2
                                hz[("kT", gq)] = ev2
                                yield
                            if gq >= 1:
                                gprev = gq - 1
                                n0 = gprev * nb
                                for j in range(nb):
                                    i4 = j // 4
                                    bank = p.ps[2 + 2 * d + i4]
                                    tk3 = p.mm([hz[("kT", gprev)], tv, hz.get(("dsb", i4))] if j % 4 == 0 else [],
                                               bank[:, (j % 4) * 128:(j % 4 + 1) * 128], kendT[:, n0 + j, :], vt[:, n0 + j, :],
                                               True, True, mark=(j % 4 == 3 or j == nb - 1))
                                    if j % 4 == 3 or j == nb - 1:
                                        w4 = j % 4 + 1
                                        dst = dSs[i4][:, 0:w4, :]
                                        srcp = bank[:, 0:w4 * 128].rearrange("p (n v) -> p n v", v=128)
                                        if i4 == 0:
                                            ev3 = p.A([tk3, hz.get(("dss", i4))], out=dst, in_=srcp, func=AF.Copy)
                                        else:
                                            ev3 = p.V([tk3, hz.get(("dss", i4))], "tensor_copy", dst, srcp)
                                        hz[("dsb", i4)] = ev3
                                        c0 = sg * NCH1 + n0 + i4 * 4
                                        hz[("dss", i4)] = p.st(
                                            dS.ap()[d, c0:c0 + w4, :, :].rearrange("n p v -> p n v"), dst, f"hds{d}{i4}",
                                            deps=[ev3])
                                yield
                        if d == 1:
                            p.st(scd.ap()[:, sg * NCH1:(sg + 1) * NCH1, :], sct[:], "hsc", deps=[ev])

                    gens = [dir_gen(0), dir_gen(1)]
                    while gens:
                        for gg in list(gens):
                            try:
                                next(gg)
                            except StopIteration:
                                gens.remove(gg)
                    p.barrier()
            with contextlib.ExitStack() as es:
                G = min(NT, 32)
                NG = NT // G
                dsl = [p.sb(es, f"h2_ds{i}", [128, G, 128], F32) for i in range(2)]
                sall = [p.sb(es, f"h2_sa{i}", [128, G + 1, 128], F32) for i in range(2)]
                sbo = [p.sb(es, f"h2_sb{i}", [128, G, 128], BF16) for i in range(2)]
                ela = p.sb(es, "h2_el", [128, 2, NT], F32)
                te = p.ld(ela[:], eld.ap().rearrange("d p n -> p d n"), "h2e")
                z0 = p.V([], "memset", sall[0][:, 0, :], 0.0)
                z1 = p.V([], "memset", sall[1][:, G, :], 0.0)
                last = [z0, z1]
                for gi in range(NG):
                    gf, gb = gi, NG - 1 - gi
                    tl = [p.ld(dsl[0][:], dS.ap()[0, gf * G:(gf + 1) * G, :, :].rearrange("n p v -> p n v"), "h2l0"),
                          p.ld(dsl[1][:], dS.ap()[1, gb * G:(gb + 1) * G, :, :].rearrange("n p v -> p n v"), "h2l1")]
                    for i in range(G):
                        nf = gf * G + i
                        last[0] = p.V([tl[0], te, last[0]], "scalar_tensor_tensor", sall[0][:, i + 1, :], sall[0][:, i, :],
                                      ela[:, 0, nf:nf + 1], dsl[0][:, i, :], ALU.mult, ALU.add)
                        j = G - 1 - i
                        nbk = gb * G + j
                        last[1] = p.V([tl[1], te, last[1]], "scalar_tensor_tensor", sall[1][:, j, :], sall[1][:, j + 1, :],
                                      ela[:, 1, nbk:nbk + 1], dsl[1][:, j, :], ALU.mult, ALU.add)
                    c0 = p.G(last, "tensor_copy", sbo[0][:], sall[0][:, 0:G, :])
                    c1 = p.A(last, out=sbo[1][:], in_=sall[1][:, 1:G + 1, :], func=AF.Copy)
                    p.st(Sb.ap()[0, gf * G:(gf + 1) * G, :, :].rearrange("n p v -> p n v"), sbo[0][:], "h2s0", deps=[c0])
                    p.st(Sb.ap()[1, gb * G:(gb + 1) * G, :, :].rearrange("n p v -> p n v"), sbo[1][:], "h2s1", deps=[c1])
                    last[0] = p.V([c0, c1] + last, "tensor_copy", sall[0][:, 0, :], sall[0][:, G, :])
                    last[1] = p.V([last[0]], "tensor_copy", sall[1][:, G, :], sall[1][:, 0, :])
                    p.barrier()
            with contextlib.ExitStack() as es:
                G = min(NT, 16)
                NG3 = NT // G
                mk2 = lambda nm_, shp, dt: [p.sb(es, f"{nm_}{i}", shp, dt) for i in range(2)]
                qd3 = mk2("h3_qd", [128, 2, G * 64], BF16)
                sc3 = mk2("h3_sc", [64, G, 64], BF16)
                v3_ = mk2("h3_v", [64, G, 128], BF16)
                gt3 = [p.sb(es, f"h3_g{i}", [64, G, 128], F32) for i in range(3)]
                s3 = mk2("h3_s", [128, 2, G, 128], BF16)
                o3 = mk2("h3_o", [64, G, 128], F32)
                ob3 = mk2("h3_ob", [64, G, 128], BF16)
                ss = mk2("h3_ss", [64, G], F32)
                sq3 = p.sb(es, "h3_sq", [64, G, 128], F32)
                T3 = {}
                g3_ = lambda k, t: T3.get((k, t))
                nb3 = min(4, G)
                prevb = [None, None]

                def L3(gi):
                    bq = gi % 2
                    t0 = gi * G * 64
                    rdC = [g3_("pe", gi - 2)]
                    T3[("ld", gi)] = [
                        p.ld(qd3[bq][:], qd.ap()[:, :, t0:t0 + G * 64].rearrange("d p t -> p d t"), f"h3a{bq}", deps=rdC),
                        p.ld(sc3[bq][:], scd.ap()[:, gi * G:(gi + 1) * G, :], f"h3a{bq}"),
                        p.ld(v3_[bq][:], vtok.ap()[t0:t0 + G * 64, h * 128:(h + 1) * 128].rearrange("(n s) v -> s n v", s=64), f"h3a{bq}"),
                        p.ld(s3[bq][:, 0], Sb.ap()[0, gi * G:(gi + 1) * G, :, :].rearrange("n p v -> p n v"), f"h3a{bq}"),
                        p.ld(s3[bq][:, 1], Sb.ap()[1, gi * G:(gi + 1) * G, :, :].rearrange("n p v -> p n v"), f"h3a{bq}")]
                    T3[("ldg", gi)] = p.ld(
                        gt3[gi % 3][:], gates.ap()[t0:t0 + G * 64, 1024 + h * 128:1024 + (h + 1) * 128].rearrange("(n s) v -> s n v", s=64),
                        f"h3g{gi % 3}", deps=[g3_("r5", gi - 3)])

                def C3(gi):
                    bq = gi % 2
                    T3[("ag", gi)] = p.A([T3[("ldg", gi)]], out=gt3[gi % 3][:], in_=gt3[gi % 3][:], func=AF.Silu)
                    tk = None
                    ev = None
                    for j0 in range(0, G, nb3):
                        pk = (j0 // nb3) % 2
                        for jj in range(nb3):
                            j = j0 + jj
                            ps = p.ps[pk][0:64, jj * 128:(jj + 1) * 128]
                            sl = slice(j * 64, (j + 1) * 64)
                            p.mm(T3[("ld", gi)] + [prevb[pk]] if jj == 0 else [], ps, sc3[bq][:, j, :], v3_[bq][:, j, :], True, False)
                            p.mm([], ps, qd3[bq][:, 0, sl], s3[bq][:, 0, j, :], False, False)
                            tk = p.mm([], ps, qd3[bq][:, 1, sl], s3[bq][:, 1, j, :], False, True, mark=(jj == nb3 - 1))
                        ev = p.A([tk, g3_("r5", gi - 2)], out=o3[bq][:, j0:j0 + nb3, :],
                                 in_=p.ps[pk][0:64, 0:nb3 * 128].rearrange("p (n v) -> p n v", v=128), func=AF.Copy)
                        prevb[pk] = ev
                    T3[("pe", gi)] = tk
                    T3[("ev", gi)] = ev

                def E3(gi):
                    bq = gi % 2
                    t0 = gi * G * 64
                    e2 = p.A([T3[("ev", gi)], g3_("e3", gi - 1)], out=sq3[:], in_=o3[bq][:], func=AF.Square)
                    e3 = p.V([e2], "tensor_reduce", ss[bq][:], sq3[:], mybir.AxisListType.X, ALU.add)
                    T3[("e3", gi)] = e3
                    r2 = p.rsqrt([e3], ss[bq][:], ss[bq][:], 1.0 / 128, EPS)
                    ssb = bass.AP(ss[bq], 0, [[G, 64], [1, G], [0, 128]])
                    r3 = p.V([r2], "tensor_tensor", o3[bq][:], o3[bq][:], ssb, ALU.mult)
                    ghb = bass.AP(ghg_sb, 0, [[128, 64], [0, G], [1, 128]])
                    r4 = p.V([r3], "tensor_tensor", o3[bq][:], o3[bq][:], ghb, ALU.mult)
                    r5 = p.V([r4, T3[("ag", gi)], g3_("st", gi - 2)], "tensor_tensor", ob3[bq][:], o3[bq][:], gt3[gi % 3][:], ALU.mult)
                    T3[("r5", gi)] = r5
                    T3[("st", gi)] = p.st(
                        ymix.ap()[t0:t0 + G * 64, 1024 + h * 128:1024 + (h + 1) * 128].rearrange("(n s) v -> s n v", s=64),
                        ob3[bq][:], f"h3s{bq}", deps=[r5])

                L3(0)
                for gi in range(NG3 + 1):
                    if gi < NG3:
                        if gi + 1 < NG3:
                            L3(gi + 1)
                        C3(gi)
                    if gi >= 1:
                        E3(gi - 1)
                p.barrier()
        p.barrier()

    def outproj(L, ymix, ya, gates, wbo, gpost, xres, xout, hT_next):
        NTL = L // 128
        with contextlib.ExitStack() as es:
            wsb = p.sb(es, "op_w", [128, KC, D], BF16)
            tw = p.ld(wsb[:], wbo.ap().rearrange("(k p) n -> p k n", p=128), "opw")
            ym = [p.sb(es, f"op_ym{i}", [128, D], BF16) for i in range(2)]
            yaf = [p.sb(es, f"op_ya{i}", [128, 1024], F32) for i in range(2)] if ya is not None else None
            gaf = [p.sb(es, f"op_ga{i}", [128, 1024], F32) for i in range(2)] if ya is not None else None
            ymT = [p.sb(es, f"op_ymT{i}", [128, KC, 128], BF16) for i in range(2)]
            xr = [p.sb(es, f"op_x{i}", [128, D], F32) for i in range(2)]
            yo = [p.sb(es, f"op_y{i}", [128, D], F32) for i in range(2)]
            junk = p.sb(es, "op_j", [128, D], BF16)
            st_ = [p.sb(es, f"op_st{i}", [128, 4], F32) for i in range(2)]
            hb = [p.sb(es, f"op_hb{i}", [128, D], BF16) for i in range(2)]
            ho = [p.sb(es, f"op_ho{i}", [128, KC, 128], BF16) for i in range(2)]
            T = {}
            g = lambda k, t: T.get((k, t))
            pbv = [p.pb[i][:].rearrange("p (k t) -> p k t", k=8) for i in range(2)]
            for ti in range(NTL + 1):
                if ti < NTL:
                    b = ti % 2
                    rows = slice(ti * 128, (ti + 1) * 128)
                    T[("tx", ti)] = p.ld(xr[b][:], xres[rows, :], f"opx{b}", deps=[g("v5", ti - 2)])
                    if ya is not None:
                        t1 = p.ld(ym[b][:, 1024:2048], ymix.ap()[rows, 1024:2048], f"opm{b}", deps=[g("tr", ti - 2)])
                        t2 = p.ld(yaf[b][:], ya.ap()[rows, :], f"opa{b}", deps=[g("v1", ti - 2)])
                        t3 = p.ld(gaf[b][:], gates.ap()[rows, 0:1024], f"opa{b}", deps=[g("v1", ti - 2)])
                        a1 = p.A([t3], out=gaf[b][:], in_=gaf[b][:], func=AF.Silu)
                        v1 = p.V([a1, t2, g("tr", ti - 2)], "tensor_tensor", ym[b][:, 0:1024], yaf[b][:], gaf[b][:], ALU.mult)
                        T[("v1", ti)] = v1
                        rdy = [t1, v1]
                    else:
                        rdy = [p.ld(ym[b][:], ymix.ap()[rows, :], f"opm{b}", deps=[g("tr", ti - 2)])]
                    for kc in range(KC):
                        tk = p.tr(rdy + [g("c1h", ti - 2), g("c2h", ti - 2)] if kc == 0 else [],
                                  p.pb[kc // 8][:, (kc % 8) * 128:(kc % 8 + 1) * 128],
                                  ym[b][:, kc * 128:(kc + 1) * 128], ident[:], mark=(kc == KC - 1))
                    T[("tr", ti)] = tk
                    c1 = p.A([tk, g("mm", ti - 2)], out=ymT[b][:, 0:8, :], in_=pbv[0], func=AF.Copy)
                    c2 = p.V([tk, g("mm", ti - 2)], "tensor_copy", ymT[b][:, 8:16, :], pbv[1])
                    T[("c1", ti)], T[("c2", ti)] = c1, c2
                    for cb in range(4):
                        for kc in range(KC):
                            tk = p.mm([c1, c2, tw] + [T.get(("ev", ti - 1, i)) for i in range(4)] if (kc == 0 and cb == 0) else [],
                                      p.ps[cb][:], ymT[b][:, kc, :],
                                      wsb[:, kc, cb * 512:(cb + 1) * 512], kc == 0, kc == KC - 1, mark=(kc == KC - 1))
                    T[("mm", ti)] = tk
                if ti >= 1 and hT_next is not None:
                    tj = ti - 1
                    bj = tj % 2
                    rows_j = slice(tj * 128, (tj + 1) * 128)
                    for kc in range(KC):
                        tk2 = p.tr([g("v8", tj), g("c1", ti), g("c2", ti)] if kc == 0 else [],
                                   p.pb[kc // 8][:, (kc % 8) * 128:(kc % 8 + 1) * 128],
                                   hb[bj][:, kc * 128:(kc + 1) * 128], ident[:], mark=(kc == KC - 1))
                    T[("trh", tj)] = tk2
                    c1h = p.A([tk2, g("sh", tj - 2)], out=ho[bj][:, 0:8, :], in_=pbv[0], func=AF.Copy)
                    c2h = p.V([tk2, g("sh", tj - 2)], "tensor_copy", ho[bj][:, 8:16, :], pbv[1])
                    T[("c1h", tj)], T[("c2h", tj)] = c1h, c2h
                    T[("sh", tj)] = p.st(hT_next.ap().rearrange("(k p) t -> p k t", p=128)[:, :, rows_j], ho[bj][:],
                                         f"oph{bj}", deps=[c1h, c2h])
                if ti < NTL:
                    tk = T[("mm", ti)]
                    evs = []
                    for cb in range(4):
                        dep = [tk, g("so", ti - 2), g("v8", ti - 2)]
                        if cb % 2 == 0:
                            ev = p.A(dep, out=yo[b][:, cb * 512:(cb + 1) * 512], in_=p.ps[cb][:], func=AF.Copy)
                        else:
                            ev = p.V(dep, "tensor_copy", yo[b][:, cb * 512:(cb + 1) * 512], p.ps[cb][:])
                        T[("ev", ti, cb)] = ev
                        evs.append(ev)
                    a2 = p.A(evs, out=junk[:], in_=yo[b][:], func=AF.Square, accum_out=st_[b][:, 0:1])
                    v3 = p.rsqrt([a2], st_[b][:, 1:2], st_[b][:, 0:1], 1.0 / D, EPS)
                    v4 = p.V([v3] + evs, "scalar_tensor_tensor", yo[b][:], yo[b][:], st_[b][:, 1:2], gpost[:], ALU.mult, ALU.mult)
                    v5 = p.V([v4, T[("tx", ti)]], "tensor_tensor", yo[b][:], yo[b][:], xr[b][:], ALU.add)
                    T[("v5", ti)] = v5
                    T[("so", ti)] = p.st(xout[rows, :], yo[b][:], f"opo{b}", deps=[v5])
                    if hT_next is not None:
                        a3 = p.A([v5], out=junk[:], in_=yo[b][:], func=AF.Square, accum_out=st_[b][:, 2:3])
                        v7 = p.rsqrt([a3], st_[b][:, 3:4], st_[b][:, 2:3], 1.0 / D, EPS)
                        T[("v8", ti)] = p.V([v7, g("trh", ti - 2)], "tensor_scalar", hb[b][:], yo[b][:], st_[b][:, 3:4], None, ALU.mult)
        p.barrier()

    def attention(qT, kT, vtok, gates, Lq, Lk, og, dtab, lam_sb, gsub_sb, delta):
        NKT, NQB = Lk // 128, Lq // 256
        SK = 2
        with contextlib.ExitStack() as es:
            qs = p.sb(es, "at_q", [128, 2, Lq], BF16)
            ks = p.sb(es, "at_k", [128, 2, Lk], BF16)
            vs = p.sb(es, "at_v", [128, NKT, 257], BF16)
            absd = [p.sb(es, f"at_ad{i}", [128, 256], F32) for i in range(3)]
            sb_ = [p.sb(es, f"at_s{i}", [128, 256], F32) for i in range(4)]
            pt = [p.sb(es, f"at_p{i}", [128, 256], BF16) for i in range(4)]
            gt = [p.sb(es, f"at_g{i}", [128, 2, 256], F32) for i in range(2)]
            o0 = [p.sb(es, f"at_o0{i}", [128, 2, 256], F32) for i in range(2)]
            o1 = [p.sb(es, f"at_o1{i}", [128, 2, 256], F32) for i in range(2)]
            rs = [p.sb(es, f"at_rs{i}", [128, 8], F32) for i in range(2)]
            ob = [p.sb(es, f"at_ob{i}", [128, 2, 256], BF16) for i in range(2)]
            jk = p.sb(es, "at_j", [128, 256], F32)
            Sps = [p.ps[i][:, 0:256] for i in range(2)]
            acc = [[p.ps[2 + 2 * c + j] for j in range(2)] for c in range(2)]
            qbi = 0
            gt_free = [[], []]
            ob_free = [None, None]
            for h in range(8):
                slope = SLOPES[h]
                t_in = [p.ld(qs[:], qT.ap()[h * 256:(h + 1) * 256, :].rearrange("(c p) t -> p c t", p=128), "atq"),
                        p.ld(ks[:], kT.ap()[h * 256:(h + 1) * 256, :].rearrange("(c p) t -> p c t", p=128), "atq"),
                        p.ld(vs[:, :, 0:256], vtok.ap()[:, h * 256:(h + 1) * 256].rearrange("(n p) v -> p n v", p=128),
                             "atq")]
                t_in.append(p.G([], "memset", vs[:, :, 256:257], 1.0))
                acc_free = []
                for qb in range(NQB):
                    par = qbi % 2
                    qbi += 1
                    q0 = qb * 256
                    tg = p.ld(gt[par][:],
                              gates.ap()[q0:q0 + 256, h * 256:(h + 1) * 256].rearrange("(j p) v -> p j v", p=128),
                              f"atg{par}", deps=gt_free[par])
                    units = [(kt, c) for kt in range(NKT) for c in range(2)]
                    NU = len(units)
                    rd_ad = [None] * 3
                    rd_s = [None] * 4
                    rd_p = [None] * 4
                    rd_ps = [None] * 2
                    absT = {}
                    expT = {}
                    state = {"lastpv": None}

                    def do_abs(kt):
                        ab = kt % 3
                        idx = kt * NQB + qb
                        absT[kt] = p.A([rd_ad[ab]], out=absd[ab][:], in_=delta[:], func=AF.Abs, bias=dtab[:, idx:idx + 1])

                    def front(u):
                        kt, c = units[u]
                        b = u % 4
                        if c == 0:
                            if kt == 0:
                                do_abs(0)
                            if kt + 1 < NKT:
                                do_abs(kt + 1)
                        b2 = u % 2
                        tk = p.mm(t_in + [rd_ps[b2]], Sps[b2], ks[:, c, kt * 128:(kt + 1) * 128], qs[:, c, q0:q0 + 256],
                                  True, True, mark=True)
                        v1 = p.V([tk, absT[kt], rd_s[b]], "scalar_tensor_tensor", sb_[b][:], absd[kt % 3][:], -slope,
                                 Sps[b2], ALU.mult, ALU.add)
                        rd_ps[b2] = v1
                        rd_ad[kt % 3] = v1
                        a1 = p.A([v1, rd_p[b]], out=pt[b][:], in_=sb_[b][:], func=AF.Exp)
                        rd_s[b] = a1
                        expT[u] = a1

                    def back(u):
                        kt, c = units[u]
                        b = u % 4
                        for j in range(2):
                            deps = [expT[u]] if j == 0 else []
                            if kt == 0:
                                deps = deps + acc_free
                            state["lastpv"] = p.mm(deps, acc[c][j][:, 0:257], pt[b][:, j * 128:(j + 1) * 128],
                                                   vs[:, kt, :], kt == 0, kt == NKT - 1, mark=(j == 1))
                        rd_p[b] = state["lastpv"]

                    for u in range(NU + SK):
                        if u < NU:
                            front(u)
                        if u >= SK:
                            back(u - SK)
                    lastpv = state["lastpv"]
                    evs = []
                    for c in range(2):
                        for j in range(2):
                            dst = (o0[par] if c == 0 else o1[par])
                            e1 = p.V([lastpv], "reciprocal", rs[par][:, 2 * c + j:2 * c + j + 1], acc[c][j][:, 256:257])
                            if c == 0:
                                evs.append(p.V([e1], "tensor_scalar", dst[:, j, :], acc[c][j][:, 0:256],
                                               rs[par][:, 2 * c + j:2 * c + j + 1], None, ALU.mult))
                            else:
                                evs.append(p.V([e1], "tensor_scalar", dst[:, j, :], acc[c][j][:, 0:256],
                                               rs[par][:, 2 * c + j:2 * c + j + 1], lam_sb[:, 1:2], ALU.mult, ALU.mult))
                    acc_free = [evs[-1]]
                    f1 = p.V(evs, "tensor_tensor", o0[par][:], o0[par][:], o1[par][:], ALU.add)
                    for j in range(2):
                        sqt = p.A([f1], out=jk[:], in_=o0[par][:, j, :], func=AF.Square, accum_out=rs[par][:, 4 + j:5 + j])
                    f2 = p.rsqrt([sqt], rs[par][:, 4:6], rs[par][:, 4:6], 1.0 / 256, EPS)
                    f3 = p.V([f2], "tensor_scalar", rs[par][:, 4:6], rs[par][:, 4:6], (1.0 - LAMBDA_INIT), None, ALU.mult)
                    ag = p.A([tg], out=gt[par][:], in_=gt[par][:], func=AF.Silu)
                    for j in range(2):
                        f4 = p.V([f3], "scalar_tensor_tensor", o0[par][:, j, :], o0[par][:, j, :], rs[par][:, 4 + j:5 + j],
                                 gsub_sb[:], ALU.mult, ALU.mult)
                    f5 = p.V([f4, ag, ob_free[par]], "tensor_tensor", ob[par][:], o0[par][:], gt[par][:], ALU.mult)
                    ob_free[par] = p.st(og.ap()[q0:q0 + 256, h * 256:(h + 1) * 256].rearrange("(j p) v -> p j v", p=128),
                                        ob[par][:], f"ato{par}", deps=[f5])
                    gt_free[par] = [f5, ag]
                p.barrier()
                gt_free = [[], []]
                ob_free = [None, None]
        p.barrier()

    c128 = p.sb(ges, "c128_sb", [128, 3, 128], BF16)
    p.ld(c128[:], c128_d.ap().rearrange("k p c -> p k c"), "c0")
    cs256 = p.sb(ges, "cs256_sb", [128, 2, 512], BF16)
    p.ld(cs256[:], cs256_d.ap().rearrange("(c p) n -> p c n", p=128), "c0")
    tw_sb, cm_sb = {}, {}
    for L in tw_d:
        M = L // 128
        tw_sb[L] = p.sb(ges, f"tw{L}_sb", [128, 3, M], F32)
        p.ld(tw_sb[L][:], tw_d[L].ap().rearrange("k p m -> p k m"), "c0")
        cm_sb[L] = p.sb(ges, f"cm{L}_sb", [M, 2, M], BF16)
        p.ld(cm_sb[L][:], cm_d[L].ap().rearrange("k b d -> b k d"), "c0")
    hm = p.sb(ges, "hm_sb", [64, 2, 64], F32)
    p.ld(hm[:], hmask_d.ap().rearrange("k s t -> s k t"), "c0")
    delta = p.sb(ges, "delta_sb", [128, 256], F32)
    p.ld(delta[:], delta_d.ap(), "c0")
    dtabs = p.sb(ges, "dtabs_sb", [128, (LS // 128) * (LS // 256)], F32)
    p.ld(dtabs[:], dtabs_d.ap(), "c0")
    dtabp = p.sb(ges, "dtabp_sb", [128, (LP // 128) * (OWN // 256)], F32)
    p.ld(dtabp[:], dtabp_d.ap(), "c0")
    ownidx = p.sb(ges, "ownidx_sb", [128, OWN // 128], I32)
    p.ld(ownidx[:], ownidx_d.ap(), "c0")
    ghg_sb = p.sb(ges, "ghg_sb", [64, 128], F32)
    p.ld(ghg_sb[:], bcast_rows(ghg.ap(), 128, 64), "c0")
    gsub_sb = p.sb(ges, "gsub_sb", [128, 256], F32)
    p.ld(gsub_sb[:], bcast_rows(gsub.ap(), 256), "c0")
    lraw = p.sb(ges, "lraw_sb", [128, 2, 3, 8], F32)
    p.ld(lraw[:], lbl.ap().rearrange("d s (h k) -> k d s h", k=128), "c0", slow=True)
    lbt = p.sb(ges, "lbt_sb", [128, 4, 8], F32)
    lsum = p.sb(ges, "lsum_sb", [128, 2, 8], F32)
    lamv = p.sb(ges, "lamv_sb", [128, 4, 128], F32)
    p.ld(lamv[:], bass.AP(lam4.ap().tensor, 0, [[0, 128], [128, 4], [1, 128]]), "c0")
    lam_sb = p.sb(ges, "lam_sb", [128, 4], F32)
    ljunk = p.sb(ges, "ljunk_sb", [128, 128], F32)
    p.barrier()
    a = p.A([], out=lraw[:], in_=lraw[:], func=AF.Exp)
    v = p.V([a], "tensor_tensor", lsum[:], lraw[:, :, 0, :], lraw[:, :, 1, :], ALU.add)
    v = p.V([v], "tensor_tensor", lsum[:], lsum[:], lraw[:, :, 2, :], ALU.add)
    v = p.V([v], "reciprocal", lsum[:], lsum[:])
    v = p.V([v], "tensor_tensor", lbt[:, 0:2, :], lraw[:, :, 0, :], lsum[:], ALU.mult)
    v = p.V([v], "tensor_scalar", lbt[:, 2:4, :], lbt[:, 0:2, :], -1.0, 1.0, ALU.mult, ALU.add)
    v = p.V([v], "tensor_tensor", ljunk[:], lamv[:, 0, :], lamv[:, 1, :], ALU.mult)
    v = p.V([v], "tensor_reduce", lam_sb[:, 2:3], ljunk[:], mybir.AxisListType.X, ALU.add)
    v = p.V([v], "tensor_tensor", ljunk[:], lamv[:, 2, :], lamv[:, 3, :], ALU.mult)
    v = p.V([v], "tensor_reduce", lam_sb[:, 3:4], ljunk[:], mybir.AxisListType.X, ALU.add)
    a = p.A([v], out=lam_sb[:, 2:4], in_=lam_sb[:, 2:4], func=AF.Exp)
    v = p.V([a], "tensor_tensor", lam_sb[:, 0:1], lam_sb[:, 2:3], lam_sb[:, 3:4], ALU.subtract)
    v = p.V([v], "tensor_scalar", lam_sb[:, 1:2], lam_sb[:, 0:1], LAMBDA_INIT, -1.0, ALU.add, ALU.mult)
    p.barrier()

    seqs = [("s%d" % i, LS, xs.ap()[i * LS:(i + 1) * LS, :], ys.ap()[i * LS:(i + 1) * LS, :]) for i in range(NS)]
    seqs.append(("p", LP, xp.ap(), None))
    scale_q = 128 ** -0.5
    for (nm, L, xin, yout) in seqs:
        isP = yout is None
        hT = p.dram(f"hT_{nm}", [D, L], BF16)
        norm_T(xin, L, hT)
        done("norm")
        uT = p.dram(f"uT_{nm}", [1024, L], BF16)
        gat0 = p.dram(f"g0_{nm}", [L, 2048], F32)
        qrT = p.dram(f"qr_{nm}", [1024, L], F32)
        ffT = p.dram(f"ff_{nm}", [1024, L], F32)
        fbT = p.dram(f"fb_{nm}", [1024, L], F32)
        vtk = p.dram(f"vt_{nm}", [L, 1024], BF16)
        proj(hT, L, wb0i, 7168, [
            (0, 1024, "F", uT, 0, 1.0, BF16),
            (1024, 1024, "T", gat0, 0, 1.0, F32),
            (2048, 1024, "F", qrT, 0, 1.0, F32),
            (3072, 1024, "T", vtk, 0, 1.0, BF16),
            (4096, 1024, "F", ffT, 0, 1.0, F32),
            (5120, 1024, "F", fbT, 0, 1.0, F32),
            (6144, 1024, "T", gat0, 1024, 1.0, F32),
        ])
        done("proj0")
        ya = p.dram(f"ya_{nm}", [L, 1024], F32)
        fnet(uT, L, ya, (c128, cs256, tw_sb[L], cm_sb[L]))
        done("fnet")
        ymix = p.dram(f"ym_{nm}", [L, 2048], BF16)
        hgrn(qrT, ffT, fbT, vtk, gat0, L, ymix, lbt, hm, ghg_sb)
        done("hgrn")
        x1 = p.dram(f"x1_{nm}", [L, D], F32)
        h1T = p.dram(f"h1T_{nm}", [D, L], BF16)
        outproj(L, ymix, ya, gat0, wb0o, gpost_sb[0], xin, x1.ap(), h1T)
        done("out0")
        kT = p.dram(f"kT_{nm}", [2048, L], BF16)
        v1t = p.dram(f"v1_{nm}", [L, 2048], BF16)
        if not isP:
            Lq = L
            qT = p.dram(f"qT_{nm}", [2048, Lq], BF16)
            gat1 = p.dram(f"g1_{nm}", [Lq, 2048], F32)
            proj(h1T, L, wb1i, 8192, [
                (0, 2048, "F", qT, 0, scale_q, BF16),
                (2048, 2048, "F", kT, 0, 1.0, BF16),
                (4096, 2048, "T", v1t, 0, 1.0, BF16),
                (6144, 2048, "T", gat1, 0, 1.0, F32),
            ])
            xres1 = x1.ap()
            dtab = dtabs
        else:
            Lq = OWN
            proj(h1T, L, wb1i, 8192, [
                (2048, 2048, "F", kT, 0, 1.0, BF16),
                (4096, 2048, "T", v1t, 0, 1.0, BF16),
            ])
            x1own = p.dram("x1own", [OWN, D], F32)
            with contextlib.ExitStack() as es:
                gx = p.sb(es, "gx", [128, D], F32)
                prev = None
                for ti in range(OWN // 128):
                    p.pool.wait(prev)
                    ds = p.dsem("gath")
                    p.nc.gpsimd.indirect_dma_start(
                        out=gx[:], out_offset=None, in_=x1.ap(),
                        in_offset=bass.IndirectOffsetOnAxis(ap=ownidx[:, ti:ti + 1], axis=0),
                    ).then_inc(ds.sem, 16)
                    ds.n += 16
                    tok = (ds, ds.n)
                    p.pending.append(tok)
                    prev = p.st(x1own.ap()[ti * 128:(ti + 1) * 128, :], gx[:], "gaths", deps=[tok])
            p.barrier()
            h1To = p.dram("h1To", [D, OWN], BF16)
            norm_T(x1own.ap(), OWN, h1To)
            qT = p.dram(f"qT_{nm}", [2048, Lq], BF16)
            gat1 = p.dram(f"g1_{nm}", [Lq, 2048], F32)
            proj(h1To, OWN, wb1i, 8192, [
                (0, 2048, "F", qT, 0, scale_q, BF16),
                (6144, 2048, "T", gat1, 0, 1.0, F32),
            ])
            xres1 = x1own.ap()
            yout = yp.ap()
            dtab = dtabp
        done("proj1")
        og = p.dram(f"og_{nm}", [Lq, 2048], BF16)
        attention(qT, kT, v1t, gat1, Lq, L, og, dtab, lam_sb, gsub_sb, delta)
        done("attn")
        outproj(Lq, og, None, None, wb1o, gpost_sb[1], xres1, yout, None)
        done("seq0")


def dft_cs(n):
    j = np.arange(n)
    ang = 2 * np.pi * np.outer(j, j) / n
    return np.cos(ang), np.sin(ang)


def host_tables(cfg, core):
    NS, LS, LP, OWN = cfg["NS"], cfg["LS"], cfg["LP"], cfg["OWN"]
    t = {}
    t["ident"] = np.eye(128, dtype=np.float32).astype(NPBF)
    c, s = dft_cs(128)
    t["c128"] = np.stack([c, s, -s]).astype(np.float32).astype(NPBF)
    c, s = dft_cs(256)
    t["cs256"] = np.concatenate([c, -s], axis=1).astype(np.float32).astype(NPBF)
    for L in sorted({LS, LP}):
        M = L // 128
        ang = 2 * np.pi * np.outer(np.arange(128), np.arange(M)) / L
        t[f"tw{L}"] = np.stack([np.cos(ang), np.sin(ang), -np.sin(ang)]).astype(np.float32)
        cm, sm = dft_cs(M)
        sc = 1.0 / math.sqrt(L * 256)
        t[f"cm{L}"] = np.stack([cm * sc, sm * sc]).astype(np.float32).astype(NPBF)
    s_, t_ = np.meshgrid(np.arange(64), np.arange(64), indexing="ij")
    t["hmask"] = np.stack([(s_ <= t_), (s_ >= t_)]).astype(np.float32)
    t["delta"] = (np.arange(128)[:, None] - np.arange(256)[None, :]).astype(np.float32)
    kt, qb = np.meshgrid(np.arange(LS // 128), np.arange(LS // 256), indexing="ij")
    t["dtabs"] = np.broadcast_to((kt * 128 - qb * 256).reshape(1, -1), (128, kt.size)).astype(np.float32).copy()
    kt, qb = np.meshgrid(np.arange(LP // 128), np.arange(OWN // 256), indexing="ij")
    t["dtabp"] = np.broadcast_to((kt * 128 - (core * OWN + qb * 256)).reshape(1, -1), (128, kt.size)).astype(np.float32).copy()
    t["ownidx"] = (core * OWN + np.arange(OWN)).reshape(OWN // 128, 128).T.astype(np.int32).copy()
    return t


def run(inputs, cfg, ncores):
    NS, LS, LP, OWN = cfg["NS"], cfg["LS"], cfg["LP"], cfg["OWN"]
    f = lambda a: np.ascontiguousarray(np.asarray(a, dtype=np.float32))
    xsamp = f(inputs["x_sample"])
    shared = {
        "xp": f(inputs["x_prompt"])[0],
        "w0i": f(inputs["ev_w_in"])[0], "w0o": f(inputs["ev_w_out"])[0],
        "w1i": f(inputs["od_w_in"])[0], "w1o": f(inputs["od_w_out"])[0],
        "g0pre": f(inputs["ev_norm_pre"])[0], "g0post": f(inputs["ev_norm_post"])[0],
        "g1pre": f(inputs["od_norm_pre"])[0], "g1post": f(inputs["od_norm_post"])[0],
        "lbl": f(inputs["hgrn_lb_logits"]), "ghg": f(inputs["hgrn_norm"])[0],
        "lam4": np.stack([f(inputs["lambda_q1"])[0], f(inputs["lambda_k1"])[0],
                          f(inputs["lambda_q2"])[0], f(inputs["lambda_k2"])[0]]),
        "gsub": f(inputs["subln"])[0],
    }
    nc = build(cfg)
    in_maps = []
    for c in range(ncores):
        m = dict(shared)
        m["xs"] = xsamp[c * NS:(c + 1) * NS].reshape(NS * LS, D)
        m.update(host_tables(cfg, c))
        in_maps.append(m)
    res = run_bass_kernel_spmd(nc, in_maps, core_ids=list(range(ncores)))
    global LAST_RES
    LAST_RES = res.results
    y_s = np.concatenate([r["ys"].reshape(NS, LS, D) for r in res.results], axis=0)
    y_p = np.concatenate([r["yp"] for r in res.results], axis=0)[None]
    return y_p.astype(np.float32), y_s.astype(np.float32)


def kernel(**inputs):
    import os
    cfg = {"NS": 2, "LS": 2048, "LP": 8192, "OWN": 1024}
    if os.environ.get("K_STOP"):
        cfg["stop"] = os.environ["K_STOP"]
    return run(inputs, cfg, 8)
```

```python
import contextlib, math
import numpy as np
import ml_dtypes
import concourse.bass as bass
import concourse.mybir as mybir
from concourse.bass_utils import run_bass_kernel_spmd

F32, BF16, I32 = mybir.dt.float32, mybir.dt.bfloat16, mybir.dt.int32
AF = mybir.ActivationFunctionType
ALU = mybir.AluOpType
D = 2048
KC = 16
EPS = 1e-6
LAMBDA_INIT = 0.8 - 0.6 * math.exp(-0.3 * 1)
SLOPES = [2.0 ** (-8.0 * (h + 1) / 8) for h in range(8)]
NPBF = ml_dtypes.bfloat16


class StopBuild(Exception):
    pass


class Eng:
    def __init__(self, e, sem):
        self.e, self.sem, self.n, self.seen = e, sem, 0, {}

    def mark(self, ins):
        ins.then_inc(self.sem, 1)
        self.n += 1
        return (self, self.n)

    def wait(self, *toks):
        for tok in toks:
            if tok is None:
                continue
            src, n = tok
            if self.seen.get(src, 0) >= n:
                continue
            self.e.wait_ge(src.sem, n)
            self.seen[src] = n


class DSem:
    def __init__(self, sem):
        self.sem, self.n = sem, 0


class P:
    def __init__(self, cfg):
        self.cfg = cfg
        self.es = contextlib.ExitStack()
        nc = self.nc = bass.Bass("TRN2", target_bir_lowering=False)
        mk = lambda nm: self.es.enter_context(nc.semaphore(nm))
        self.pe = Eng(nc.tensor, mk("s_pe"))
        self.act = Eng(nc.scalar, mk("s_act"))
        self.dve = Eng(nc.vector, mk("s_dve"))
        self.pool = Eng(nc.gpsimd, mk("s_pool"))
        self.sp = Eng(nc.sync, mk("s_sp"))
        self.engs = [self.pe, self.act, self.dve, self.pool, self.sp]
        self.dsems = {}
        self.pending = []
        self.ndram = 0
        self.ps = [self.es.enter_context(nc.psum_tensor(f"ps{i}", [128, 512], F32)) for i in range(6)]
        self.pb = [self.es.enter_context(nc.psum_tensor(f"pb{i}", [128, 1024], BF16)) for i in range(2)]
        self.dummy = self.es.enter_context(nc.sbuf_tensor("dummy_sb", [128, 2], F32))
        self.pool.mark(nc.gpsimd.memset(self.dummy[:], 0.0))

    def sb(self, es, name, shape, dt):
        self.nsb = getattr(self, "nsb", 0) + 1
        return es.enter_context(self.nc.sbuf_tensor(f"{name}_{self.nsb}", shape, dt))

    def dram(self, name, shape, dt, kind="Internal"):
        t = self.nc.dram_tensor(name, list(shape), dt, kind=kind)
        if not hasattr(self, "named"):
            self.named = {}
        self.named[name] = (t, list(shape), dt)
        return t

    def dump(self):
        self.barrier()
        for name in self.cfg.get("dump", []):
            if name not in self.named:
                continue
            t, shape, dt = self.named[name]
            o = self.nc.dram_tensor("dbg_" + name, shape, dt, kind="ExternalOutput")
            self.ld(o.ap(), t.ap(), "dump")
        self.barrier()

    def dsem(self, key):
        if key not in self.dsems:
            self.dsems[key] = DSem(self.es.enter_context(self.nc.semaphore("d_" + key)))
        return self.dsems[key]

    def dma(self, q, out, in_, key, deps=(), slow=False):
        q.wait(*deps)
        ds = self.dsem(key)
        kw = {"allow_slow_non_contiguous": True} if slow else {}
        q.e.dma_start(out=out, in_=in_, **kw).then_inc(ds.sem, 16)
        ds.n += 16
        tok = (ds, ds.n)
        self.pending.append(tok)
        return tok

    def ld(self, out, in_, key, deps=(), slow=False):
        return self.dma(self.sp, out, in_, key, deps, slow)

    def st(self, out, in_, key, deps=()):
        return self.dma(self.pool, out, in_, key, deps)

    def barrier(self):
        toks = [(e, e.n) for e in self.engs if e.n > 0] + self.pending
        for e in self.engs:
            e.wait(*toks)
        self.pending = []

    def A(self, deps, *a, **k):
        self.act.wait(*deps)
        tok = self.act.mark(self.act.e.activation(*a, **k))
        if k.get("accum_out") is not None:
            tok = self.act.mark(self.act.e.activation(out=self.dummy[:, 1:2], in_=self.dummy[:, 0:1], func=AF.Copy))
        return tok

    def V(self, deps, fn, *a, **k):
        self.dve.wait(*deps)
        return self.dve.mark(getattr(self.dve.e, fn)(*a, **k))

    def G(self, deps, fn, *a, **k):
        self.pool.wait(*deps)
        return self.pool.mark(getattr(self.pool.e, fn)(*a, **k))

    def X(self, eng, deps, fn, *a, **k):
        eng.wait(*deps)
        return eng.mark(getattr(eng.e, fn)(*a, **k))

    def rsqrt(self, deps, out, in_, mul, add):
        a = self.A(deps, out=out, in_=in_, func=AF.Ln, scale=float(mul), bias=float(add))
        return self.A([a], out=out, in_=out, func=AF.Exp, scale=-0.5)

    def mm(self, deps, out, lhsT, rhs, start, stop, mark=False):
        self.pe.wait(*deps)
        ins = self.pe.e.matmul(out, lhsT, rhs, start=start, stop=stop)
        return self.pe.mark(ins) if mark else None

    def tr(self, deps, out, in_, ident, mark=False):
        self.pe.wait(*deps)
        ins = self.pe.e.transpose(out, in_, ident)
        return self.pe.mark(ins) if mark else None


def bcast_rows(ap_dram_1d, n, parts=128):
    return bass.AP(ap_dram_1d.tensor, ap_dram_1d.offset, [[0, parts], [1, n]])


def build(cfg):
    p = P(cfg)
    try:
        _build(cfg, p)
    except StopBuild:
        p.dump()
        return p.nc
    p.dump()
    p.es.close()
    return p.nc


def _build(cfg, p):
    NS, LS, LP, OWN = cfg["NS"], cfg["LS"], cfg["LP"], cfg["OWN"]

    def done(tag):
        if cfg.get("stop") == tag:
            raise StopBuild()
    nc = p.nc
    inp = lambda name, shape, dt=F32: nc.dram_tensor(name, list(shape), dt, kind="ExternalInput")
    xs = inp("xs", [NS * LS, D])
    xp = inp("xp", [LP, D])
    w0i, w0o = inp("w0i", [D, 7168]), inp("w0o", [D, D])
    w1i, w1o = inp("w1i", [D, 8192]), inp("w1o", [D, D])
    g0pre, g0post = inp("g0pre", [D]), inp("g0post", [D])
    g1pre, g1post = inp("g1pre", [D]), inp("g1post", [D])
    lbl = inp("lbl", [2, 3, 1024])
    ghg = inp("ghg", [128])
    lam4 = inp("lam4", [4, 128])
    gsub = inp("gsub", [256])
    ident_d = inp("ident", [128, 128], BF16)
    c128_d = inp("c128", [3, 128, 128], BF16)
    cs256_d = inp("cs256", [256, 512], BF16)
    tw_d = {L: inp(f"tw{L}", [3, 128, L // 128]) for L in sorted({LS, LP})}
    cm_d = {L: inp(f"cm{L}", [2, L // 128, L // 128], BF16) for L in sorted({LS, LP})}
    hmask_d = inp("hmask", [2, 64, 64])
    delta_d = inp("delta", [128, 256])
    dtabs_d = inp("dtabs", [128, (LS // 128) * (LS // 256)])
    dtabp_d = inp("dtabp", [128, (LP // 128) * (OWN // 256)])
    ownidx_d = inp("ownidx", [128, OWN // 128], I32)
    ys = nc.dram_tensor("ys", [NS * LS, D], F32, kind="ExternalOutput")
    yp = nc.dram_tensor("yp", [OWN, D], F32, kind="ExternalOutput")

    ges = p.es
    ident = p.sb(ges, "ident_sb", [128, 128], BF16)
    toks = [p.ld(ident[:], ident_d.ap(), "c0")]
    gpost_sb = [p.sb(ges, f"gpost{i}", [128, D], F32) for i in range(2)]
    toks.append(p.ld(gpost_sb[0][:], bcast_rows(g0post.ap(), D), "c0"))
    toks.append(p.ld(gpost_sb[1][:], bcast_rows(g1post.ap(), D), "c0"))
    gpre_sb = [p.sb(ges, f"gpre{i}", [128, KC], F32) for i in range(2)]
    toks.append(p.ld(gpre_sb[0][:], g0pre.ap().rearrange("(c p) -> p c", p=128), "c0", slow=True))
    toks.append(p.ld(gpre_sb[1][:], g1pre.ap().rearrange("(c p) -> p c", p=128), "c0", slow=True))
    p.barrier()

    def prep_w(w, ncols, gcol, name):
        wb = p.dram(name, [D, ncols], BF16)
        with contextlib.ExitStack() as es:
            CW = 1024
            wf = [p.sb(es, f"wf{i}", [128, CW], F32) for i in range(2)]
            wo = [p.sb(es, f"wo{i}", [128, CW], BF16) for i in range(2)]
            cons = [None, None]
            sts = [None, None]
            i = 0
            for kc in range(KC):
                for c0 in range(0, ncols, CW):
                    b = i % 2
                    t = p.ld(wf[b][:], w.ap()[kc * 128:(kc + 1) * 128, c0:c0 + CW], f"wl{b}", deps=[cons[b]])
                    eng = p.dve if b == 0 else p.pool
                    if gcol is None:
                        cons[b] = p.X(eng, [t, sts[b]], "tensor_copy", wo[b][:], wf[b][:])
                    else:
                        cons[b] = p.X(eng, [t, sts[b]], "tensor_scalar", wo[b][:], wf[b][:],
                                      gcol[:, kc:kc + 1], None, ALU.mult)
                    sts[b] = p.dma(p.act, wb.ap()[kc * 128:(kc + 1) * 128, c0:c0 + CW], wo[b][:], f"ws{b}",
                                   deps=[cons[b]])
                    i += 1
        p.barrier()
        return wb

    wb0i = prep_w(w0i, 7168, gpre_sb[0], "wb0i")
    wb0o = prep_w(w0o, D, None, "wb0o")
    wb1i = prep_w(w1i, 8192, gpre_sb[1], "wb1i")
    wb1o = prep_w(w1o, D, None, "wb1o")
    done("prep")

    def norm_T(x_rows, L, hT):
        with contextlib.ExitStack() as es:
            xt = [p.sb(es, f"nx{i}", [128, D], F32) for i in range(2)]
            junk = p.sb(es, "njunk", [128, D], BF16)
            hb = [p.sb(es, f"nhb{i}", [128, D], BF16) for i in range(2)]
            ho = [p.sb(es, f"nho{i}", [128, KC, 128], BF16) for i in range(2)]
            st_ = [p.sb(es, f"nst{i}", [128, 2], F32) for i in range(2)]
            rd = [None, None]
            hbr = [None, None]
            hor = [None, None]
            for ti in range(L // 128):
                b = ti % 2
                t = p.ld(xt[b][:], x_rows[ti * 128:(ti + 1) * 128, :], f"nl{b}", deps=[rd[b]])
                a1 = p.A([t], out=junk[:], in_=xt[b][:], func=AF.Square, accum_out=st_[b][:, 0:1])
                v2 = p.rsqrt([a1], st_[b][:, 1:2], st_[b][:, 0:1], 1.0 / D, EPS)
                v3 = p.V([v2, t, hbr[b]], "tensor_scalar", hb[b][:], xt[b][:], st_[b][:, 1:2], None, ALU.mult)
                rd[b] = v3
                for kc in range(KC):
                    tk = p.tr([v3, hor[b]] if kc == 0 else [], p.pb[kc // 8][:, (kc % 8) * 128:(kc % 8 + 1) * 128],
                              hb[b][:, kc * 128:(kc + 1) * 128], ident[:], mark=(kc == KC - 1))
                hbr[b] = tk
                c1 = p.A([tk, hor[b]], out=ho[b][:, 0:8, :], in_=p.pb[0][:].rearrange("p (k t) -> p k t", k=8),
                         func=AF.Copy)
                c2 = p.V([tk, hor[b]], "tensor_copy", ho[b][:, 8:16, :],
                         p.pb[1][:].rearrange("p (k t) -> p k t", k=8))
                p.pe.wait(c1, c2)
                hor[b] = p.st(hT.ap().rearrange("(k p) t -> p k t", p=128)[:, :, ti * 128:(ti + 1) * 128],
                              ho[b][:], f"ns{b}", deps=[c1, c2])
        p.barrier()

    def proj(hT, L, wb, ncols_total, jobs):
        TB = min(L, 1024)
        with contextlib.ExitStack() as es:
            hblk = p.sb(es, "pj_h", [128, KC, TB], BF16)
            wblk = [p.sb(es, f"pj_w{i}", [128, KC, 512], BF16) for i in range(2)]
            osb = {F32: [p.sb(es, f"pj_of{i}", [128, 512], F32) for i in range(2)],
                   BF16: [p.sb(es, f"pj_ob{i}", [128, 512], BF16) for i in range(2)]}
            wread = [None, None]
            ost = {F32: [None, None], BF16: [None, None]}
            psr = [None] * 4
            wi = 0
            oi = 0
            pi = 0
            hread = None
            for tb in range(L // TB):
                th = p.ld(hblk[:], hT.ap().rearrange("(k p) t -> p k t", p=128)[:, :, tb * TB:(tb + 1) * TB],
                          "pjh", deps=[hread])
                cbs = [(j, c) for j in jobs for c in range(0, j[1], 512)]
                for (job, c) in cbs:
                    col0, ncols, mode, od, o0, scale, odt = job
                    b = wi % 2
                    wi += 1
                    tw = p.ld(wblk[b][:], wb.ap().rearrange("(k p) n -> p k n", p=128)[:, :, col0 + c:col0 + c + 512],
                              f"pjw{b}", deps=[wread[b]])
                    last = None
                    TW = min(512, TB)
                    if mode == "F":
                        subs = [(s4, t5) for s4 in range(4) for t5 in range(TB // TW)]
                    else:
                        subs = [(s4, 0) for s4 in range(TB // 128)]
                    for (s4, t5) in subs:
                        pk = pi % 4
                        pi += 1
                        W_ = TW if mode == "F" else 512
                        ps = p.ps[pk][:, 0:W_]
                        for kc in range(KC):
                            if mode == "F":
                                lhsT, rhs = wblk[b][:, kc, s4 * 128:(s4 + 1) * 128], hblk[:, kc, t5 * TW:(t5 + 1) * TW]
                            else:
                                lhsT, rhs = hblk[:, kc, s4 * 128:(s4 + 1) * 128], wblk[b][:, kc, :]
                            tk = p.mm([th, tw, psr[pk]] if kc == 0 else [], ps, lhsT, rhs, kc == 0, kc == KC - 1,
                                      mark=(kc == KC - 1))
                        last = tk
                        ob = oi % 2
                        oi += 1
                        o = osb[odt][ob][:, 0:W_]
                        if oi % 2 == 0:
                            ev = p.A([tk, ost[odt][ob]], out=o, in_=ps, func=AF.Copy, scale=float(scale))
                        else:
                            ev = p.V([tk, ost[odt][ob]], "tensor_scalar", o, ps, float(scale), None, ALU.mult)
                        psr[pk] = ev
                        if mode == "F":
                            dst = od.ap()[o0 + c + s4 * 128:o0 + c + (s4 + 1) * 128,
                                          tb * TB + t5 * TW:tb * TB + (t5 + 1) * TW]
                        else:
                            dst = od.ap()[tb * TB + s4 * 128:tb * TB + (s4 + 1) * 128, o0 + c:o0 + c + 512]
                        ost[odt][ob] = p.st(dst, o, f"pjs{ob}{'f' if odt == F32 else 'b'}", deps=[ev])
                    wread[b] = last
                    hread = last
        p.barrier()

    def fnet(uT, L, ya, tabs):
        M = L // 128
        c128, cs256, tw, cm = tabs
        Bd = p.dram(f"fn_B{p.ndram}", [128, M, 512], BF16)
        p.ndram += 1
        CP = 32
        with contextlib.ExitStack() as es:
            ug = p.sb(es, "fn_u", [128, 2, L], BF16)
            MB = min(M, 32)
            V = p.sb(es, "fn_V", [128, MB, 512], BF16)
            Bs = p.sb(es, "fn_Bs", [128, MB, 512], BF16)
            tmp = [p.sb(es, f"fn_t{i}", [128, 2, 256], F32) for i in range(2)]
            Bt = p.sb(es, "fn_Bt", [M, CP, 512], BF16)
            Y = [p.sb(es, f"fn_Y{i}", [M, 2, 256], F32) for i in range(2)]
            for g in range(4):
                tu = p.ld(ug[:], uT.ap()[g * 256:(g + 1) * 256, :].rearrange("(c p) t -> p c t", p=128), "fnu")
                ts_all = []
                for bh in range(M // MB):
                    evs = []
                    prev = [None, None]
                    for bl in range(MB):
                        b = bh * MB + bl
                        pk = b % 2
                        for ch in range(2):
                            lhsT = bass.AP(ug, ch * L + b, [[2 * L, 128], [M, 128]])
                            tk = p.mm([tu, prev[pk]] if ch == 0 else [], p.ps[pk][:], lhsT, cs256[:, ch, :],
                                      ch == 0, ch == 1, mark=(ch == 1))
                        if b % 2 == 0:
                            ev = p.A([tk], out=V[:, bl, :], in_=p.ps[pk][:], func=AF.Copy)
                        else:
                            ev = p.V([tk], "tensor_copy", V[:, bl, :], p.ps[pk][:])
                        prev[pk] = ev
                        evs.append(ev)
                    prevr = [None, None]
                    tw_tok = []
                    for bp in range(MB // 2):
                        pk = 2 + (bp % 2) * 2
                        Ar, Ai = p.ps[pk], p.ps[pk + 1]
                        b0 = 2 * bp
                        dep = [evs[b0], evs[b0 + 1], prevr[bp % 2]]
                        ar3 = Ar[:].rearrange("p (b f) -> p b f", b=2)
                        ai3 = Ai[:].rearrange("p (b f) -> p b f", b=2)
                        p.mm(dep, ar3, c128[:, 0, :], V[:, b0:b0 + 2, 0:256], True, False)
                        p.mm([], ar3, c128[:, 1, :], V[:, b0:b0 + 2, 256:512], False, True)
                        p.mm([], ai3, c128[:, 0, :], V[:, b0:b0 + 2, 256:512], True, False)
                        tk = p.mm([], ai3, c128[:, 2, :], V[:, b0:b0 + 2, 0:256], False, True, mark=True)
                        last = []
                        for j in range(2):
                            bl = b0 + j
                            b = bh * MB + bl
                            t1 = p.V([tk], "tensor_scalar", tmp[0][:, j, :], ar3[:, j, :], tw[:, 0, b:b + 1], None, ALU.mult)
                            t3 = p.V([tk], "tensor_scalar", tmp[1][:, j, :], ai3[:, j, :], tw[:, 0, b:b + 1], None, ALU.mult)
                            r1 = p.V([t1, tk], "scalar_tensor_tensor", Bs[:, bl, 0:256], ai3[:, j, :], tw[:, 1, b:b + 1],
                                     tmp[0][:, j, :], ALU.mult, ALU.add)
                            r2 = p.V([t3, tk], "scalar_tensor_tensor", Bs[:, bl, 256:512], ar3[:, j, :], tw[:, 2, b:b + 1],
                                     tmp[1][:, j, :], ALU.mult, ALU.add)
                            last = [r1, r2]
                        p.act.wait(*last)
                        prevr[bp % 2] = last[1]
                        tw_tok = last
                    ts_all.append(p.st(Bd.ap()[:, bh * MB:(bh + 1) * MB, :], Bs[:], "fnb", deps=tw_tok))
                    p.barrier()
                if True:
                    ts_ = ts_all[-1]
                    yst = [None, None]
                    evp = [None, None]
                    bt_read = None
                    for cp in range(128 // CP):
                        tl = p.ld(Bt[:], Bd.ap()[cp * CP:(cp + 1) * CP, :, :].rearrange("c b f -> b c f"), "fnbt",
                                  deps=[ts_, bt_read])
                        for c2 in range(CP // 2):
                            pk = c2 % 2
                            ps3 = p.ps[pk][0:M, :].rearrange("p (c f) -> p c f", c=2)
                            p.mm([tl, evp[pk]], ps3, cm[0:M, 0, :], Bt[:, 2 * c2:2 * c2 + 2, 0:256], True, False)
                            tk = p.mm([], ps3, cm[0:M, 1, :], Bt[:, 2 * c2:2 * c2 + 2, 256:512], False, True, mark=True)
                            if c2 % 2 == 0:
                                ev = p.A([tk, yst[pk]], out=Y[pk][:], in_=ps3, func=AF.Copy)
                            else:
                                ev = p.V([tk, yst[pk]], "tensor_copy", Y[pk][:], ps3)
                            c_abs = cp * CP + 2 * c2
                            dst = ya.ap().rearrange("(d c) f -> d c f", c=128)[:, c_abs:c_abs + 2, g * 256:(g + 1) * 256]
                            yst[pk] = p.st(dst, Y[pk][:], f"fny{pk}", deps=[ev])
                            evp[pk] = ev
                            bt_read = tk
                    p.barrier()
        p.barrier()

    def hgrn(qrT, ffT, fbT, vtok, gates, L, ymix, lbt, hm, ghg_sb):
        SEG = min(L, 2048)
        NCH = SEG // 64
        nseg = L // SEG
        NT = L // 64
        dS = p.dram(f"hg_dS{p.ndram}", [2, NT, 128, 128], F32)
        Sb = p.dram(f"hg_Sb{p.ndram}", [2, NT, 128, 128], BF16)
        qd = p.dram(f"hg_qd{p.ndram}", [2, 128, L], BF16)
        scd = p.dram(f"hg_sc{p.ndram}", [64, NT, 64], BF16)
        eld = p.dram(f"hg_el{p.ndram}", [2, 128, NT], F32)
        p.ndram += 1
        for h in range(8):
            SEG1 = min(L, 1024)
            NCH1 = SEG1 // 64
            nseg1 = L // SEG1
            with contextlib.ExitStack() as es:
                qr = p.sb(es, "h_qr", [128, SEG1], F32)
                rmask = p.sb(es, "h_rm", [128, SEG1], F32)
                qdec = [p.sb(es, f"h_qd{i}", [128, SEG1], BF16) for i in range(2)]
                vt = p.sb(es, "h_v", [64, NCH1, 128], BF16)
                sct = p.sb(es, "h_sc", [64, NCH1, 64], BF16)
                sc1 = p.sb(es, "h_sc1", [64, NCH1, 64], F32)
                el = p.sb(es, "h_el", [128, 2, NCH1], F32)
                B = []
                for d in range(2):
                    bd = {}
                    for nm_ in ("fr", "f", "g", "k", "cum", "cb", "ex", "ex2", "ex3"):
                        bd[nm_] = p.sb(es, f"h_{nm_}{d}", [128, SEG1], F32)
                    bd["kdec"] = p.sb(es, f"h_kd{d}", [128, SEG1], BF16)
                    bd["kend"] = p.sb(es, f"h_ke{d}", [128, SEG1], BF16)
                    bd["kendT"] = p.sb(es, f"h_keT{d}", [64, NCH1, 128], BF16)
                    bd["dSs"] = [p.sb(es, f"h_dS{d}{j}", [128, 4, 128], F32) for j in range(2)]
                    B.append(bd)
                m1 = p.G([], "memset", rmask[:], 1.0)
                m2 = p.G([], "memset", rmask[:].rearrange("p (n s) -> p n s", s=64)[:, :, 0:1], 0.0)
                p.barrier()
                for sg in range(nseg1):
                    t0 = sg * SEG1
                    tq = p.ld(qr[:], qrT.ap()[h * 128:(h + 1) * 128, t0:t0 + SEG1], "hq")
                    tv = p.ld(vt[:], vtok.ap()[t0:t0 + SEG1, h * 128:(h + 1) * 128].rearrange("(n s) v -> s n v", s=64),
                              "hv")
                    aq = p.A([tq], out=qr[:], in_=qr[:], func=AF.Silu)
                    shared = {}

                    def dir_gen(d):
                        bd = B[d]
                        fr, f_, g_, k_, cum, cb = bd["fr"], bd["f"], bd["g"], bd["k"], bd["cum"], bd["cb"]
                        ex, ex2, ex3, kdec, kend, kendT, dSs = (bd["ex"], bd["ex2"], bd["ex3"], bd["kdec"], bd["kend"],
                                                                bd["kendT"], bd["dSs"])
                        src = ffT if d == 0 else fbT
                        tf = p.ld(fr[:], src.ap()[h * 128:(h + 1) * 128, t0:t0 + SEG1], f"hf{d}")
                        a1 = p.A([tf], out=f_[:], in_=fr[:], func=AF.Sigmoid)
                        yield
                        v1 = p.V([a1], "tensor_scalar", f_[:], f_[:], lbt[:, 2 + d, h:h + 1], lbt[:, d, h:h + 1],
                                 ALU.mult, ALU.add)
                        yield
                        a2 = p.A([v1], out=g_[:], in_=f_[:], func=AF.Ln)
                        g1 = p.G([v1], "tensor_scalar", k_[:], f_[:], -1.0, 1.0, ALU.mult, ALU.add)
                        yield
                        v2 = p.V([a2], "tensor_tensor_scan", cum[:], rmask[:], g_[:], 0.0, ALU.mult, ALU.add)
                        yield
                        cum3 = cum[:].rearrange("p (n s) -> p n s", s=64)
                        lastb = bass.AP(cum, 63, [[SEG1, 128], [64, NCH1], [0, 64]])
                        a6 = p.A([v2], out=el[:, d, :], in_=cum3[:, :, 63], func=AF.Exp)
                        if d == 0:
                            cc = cum
                            v3 = p.V([v2], "tensor_tensor", cb[:].rearrange("p (n s) -> p n s", s=64), lastb, cum3,
                                     ALU.subtract)
                            dl = cb
                            yield
                        else:
                            v3a = p.V([v2], "tensor_tensor", cb[:].rearrange("p (n s) -> p n s", s=64), lastb, cum3,
                                      ALU.subtract)
                            yield
                            v3b = p.V([v3a], "tensor_tensor", cb[:], cb[:], g_[:], ALU.add)
                            yield
                            cc = cb
                            v3 = p.V([v3b], "tensor_tensor", g_[:], cum[:], g_[:], ALU.subtract)
                            dl = g_
                            yield
                        a3 = p.A([v3], out=ex[:], in_=cc[:], func=AF.Exp)
                        yield
                        a4 = p.A([v3], out=ex2[:], in_=cc[:], func=AF.Exp, scale=-1.0)
                        g2 = p.V([a3, aq], "tensor_tensor", qdec[d][:], qr[:], ex[:], ALU.mult)
                        yield
                        a5 = p.A([v3], out=ex3[:], in_=dl[:], func=AF.Exp)
                        g3 = p.G([a4, g1], "tensor_tensor", kdec[:], k_[:], ex2[:], ALU.mult)
                        yield
                        g4 = p.V([a5, g1], "tensor_tensor", kend[:], k_[:], ex3[:], ALU.mult)
                        p.st(qd.ap()[d, :, t0:t0 + SEG1], qdec[d][:], f"hsq{d}", deps=[g2])
                        p.st(eld.ap()[d, :, sg * NCH1:(sg + 1) * NCH1], el[:, d, :], f"hse{d}", deps=[a6])
                        yield
                        nb = min(8, NCH1)
                        NGR = NCH1 // nb
                        mk_ap = bass.AP(hm, d * 64, [[128, 64], [0, nb], [1, 64]])
                        hz = {}
                        psS, pbT = p.ps[d], p.pb[d]
                        ev = None
                        for gq in range(NGR + 1):
                            if gq < NGR:
                                n0 = gq * nb
                                for j in range(nb):
                                    sl = slice((n0 + j) * 64, (n0 + j + 1) * 64)
                                    tk = p.mm([g2, g3, hz.get("sc")] if j == 0 else [], psS[0:64, j * 64:(j + 1) * 64],
                                              kdec[:, sl], qdec[d][:, sl], True, True, mark=(j == nb - 1))
                                psv = psS[0:64, 0:nb * 64].rearrange("p (n s) -> p n s", s=64)
                                if d == 0:
                                    ev = p.V([tk], "tensor_tensor", sc1[:, n0:n0 + nb, :], psv, mk_ap, ALU.mult)
                                    shared[("sc1", gq)] = ev
                                else:
                                    ev0 = p.V([tk], "tensor_tensor", sct[:, n0:n0 + nb, :], psv, mk_ap, ALU.mult)
                                    ev = p.V([ev0, shared[("sc1", gq)]], "tensor_tensor", sct[:, n0:n0 + nb, :],
                                             sct[:, n0:n0 + nb, :], sc1[:, n0:n0 + nb, :], ALU.add)
                                hz["sc"] = ev
                                yield
                                for j in range(nb):
                                    sl = slice((n0 + j) * 64, (n0 + j + 1) * 64)
                                    tk2 = p.tr([g4, hz.get("tr")] if j == 0 else [], pbT[0:64, j * 128:(j + 1) * 128],
                                               kend[:, sl], ident[:], mark=(j == nb - 1))
                                ev2 = p.A([tk2], out=kendT[:, n0:n0 + nb, :],
                                          in_=pbT[0:64, 0:nb * 128].rearrange("p (n k) -> p n k", k=128), func=AF.Copy)
                                hz["tr"] = ev2
                                hz[("kT", gq)] = ev2
                                yield
                            if gq >= 1:
                                gprev = gq - 1
                                n0 = gprev * nb
                                for j in range(nb):
                                    i4 = j // 4
                                    bank = p.ps[2 + 2 * d + i4]
                                    tk3 = p.mm([hz[("kT", gprev)], tv, hz.get(("dsb", i4))] if j % 4 == 0 else [],
                                               bank[:, (j % 4) * 128:(j % 4 + 1) * 128], kendT[:, n0 + j, :], vt[:, n0 + j, :],
                                               True, True, mark=(j % 4 == 3 or j == nb - 1))
                                    if j % 4 == 3 or j == nb - 1:
                                        w4 = j % 4 + 1
                                        dst = dSs[i4][:, 0:w4, :]
                                        srcp = bank[:, 0:w4 * 128].rearrange("p (n v) -> p n v", v=128)
                                        if i4 == 0:
                                            ev3 = p.A([tk3, hz.get(("dss", i4))], out=dst, in_=srcp, func=AF.Copy)
                                        else:
                                            ev3 = p.V([tk3, hz.get(("dss", i4))], "tensor_copy", dst, srcp)
                                        hz[("dsb", i4)] = ev3
                                        c0 = sg * NCH1 + n0 + i4 * 4
                                        hz[("dss", i4)] = p.st(
                                            dS.ap()[d, c0:c0 + w4, :, :].rearrange("n p v -> p n v"), dst, f"hds{d}{i4}",
                                            deps=[ev3])
                                yield
                        if d == 1:
                            p.st(scd.ap()[:, sg * NCH1:(sg + 1) * NCH1, :], sct[:], "hsc", deps=[ev])

                    gens = [dir_gen(0), dir_gen(1)]
                    while gens:
                        for gg in list(gens):
                            try:
                                next(gg)
                            except StopIteration:
                                gens.remove(gg)
                    p.barrier()
            with contextlib.ExitStack() as es:
                G = min(NT, 32)
                NG = NT // G
                dsl = [p.sb(es, f"h2_ds{i}", [128, G, 128], F32) for i in range(2)]
                sall = [p.sb(es, f"h2_sa{i}", [128, G + 1, 128], F32) for i in range(2)]
                sbo = [p.sb(es, f"h2_sb{i}", [128, G, 128], BF16) for i in range(2)]
                ela = p.sb(es, "h2_el", [128, 2, NT], F32)
                te = p.ld(ela[:], eld.ap().rearrange("d p n -> p d n"), "h2e")
                z0 = p.V([], "memset", sall[0][:, 0, :], 0.0)
                z1 = p.V([], "memset", sall[1][:, G, :], 0.0)
                last = [z0, z1]
                for gi in range(NG):
                    gf, gb = gi, NG - 1 - gi
                    tl = [p.ld(dsl[0][:], dS.ap()[0, gf * G:(gf + 1) * G, :, :].rearrange("n p v -> p n v"), "h2l0"),
                          p.ld(dsl[1][:], dS.ap()[1, gb * G:(gb + 1) * G, :, :].rearrange("n p v -> p n v"), "h2l1")]
                    for i in range(G):
                        nf = gf * G + i
                        last[0] = p.V([tl[0], te, last[0]], "scalar_tensor_tensor", sall[0][:, i + 1, :], sall[0][:, i, :],
                                      ela[:, 0, nf:nf + 1], dsl[0][:, i, :], ALU.mult, ALU.add)
                        j = G - 1 - i
                        nbk = gb * G + j
                        last[1] = p.V([tl[1], te, last[1]], "scalar_tensor_tensor", sall[1][:, j, :], sall[1][:, j + 1, :],
                                      ela[:, 1, nbk:nbk + 1], dsl[1][:, j, :], ALU.mult, ALU.add)
                    c0 = p.G(last, "tensor_copy", sbo[0][:], sall[0][:, 0:G, :])
                    c1 = p.A(last, out=sbo[1][:], in_=sall[1][:, 1:G + 1, :], func=AF.Copy)
                    p.st(Sb.ap()[0, gf * G:(gf + 1) * G, :, :].rearrange("n p v -> p n v"), sbo[0][:], "h2s0", deps=[c0])
                    p.st(Sb.ap()[1, gb * G:(gb + 1) * G, :, :].rearrange("n p v -> p n v"), sbo[1][:], "h2s1", deps=[c1])
                    last[0] = p.V([c0, c1] + last, "tensor_copy", sall[0][:, 0, :], sall[0][:, G, :])
                    last[1] = p.V([last[0]], "tensor_copy", sall[1][:, G, :], sall[1][:, 0, :])
                    p.barrier()
            with contextlib.ExitStack() as es:
                G = min(NT, 16)
                NG3 = NT // G
                mk2 = lambda nm_, shp, dt: [p.sb(es, f"{nm_}{i}", shp, dt) for i in range(2)]
                qd3 = mk2("h3_qd", [128, 2, G * 64], BF16)
                sc3 = mk2("h3_sc", [64, G, 64], BF16)
                v3_ = mk2("h3_v", [64, G, 128], BF16)
                gt3 = [p.sb(es, f"h3_g{i}", [64, G, 128], F32) for i in range(3)]
                s3 = mk2("h3_s", [128, 2, G, 128], BF16)
                o3 = mk2("h3_o", [64, G, 128], F32)
                ob3 = mk2("h3_ob", [64, G, 128], BF16)
                ss = mk2("h3_ss", [64, G], F32)
                sq3 = p.sb(es, "h3_sq", [64, G, 128], F32)
                T3 = {}
                g3_ = lambda k, t: T3.get((k, t))
                nb3 = min(4, G)
                prevb = [None, None]

                def L3(gi):
                    bq = gi % 2
                    t0 = gi * G * 64
                    rdC = [g3_("pe", gi - 2)]
                    T3[("ld", gi)] = [
                        p.ld(qd3[bq][:], qd.ap()[:, :, t0:t0 + G * 64].rearrange("d p t -> p d t"), f"h3a{bq}", deps=rdC),
                        p.ld(sc3[bq][:], scd.ap()[:, gi * G:(gi + 1) * G, :], f"h3a{bq}"),
                        p.ld(v3_[bq][:], vtok.ap()[t0:t0 + G * 64, h * 128:(h + 1) * 128].rearrange("(n s) v -> s n v", s=64), f"h3a{bq}"),
                        p.ld(s3[bq][:, 0], Sb.ap()[0, gi * G:(gi + 1) * G, :, :].rearrange("n p v -> p n v"), f"h3a{bq}"),
                        p.ld(s3[bq][:, 1], Sb.ap()[1, gi * G:(gi + 1) * G, :, :].rearrange("n p v -> p n v"), f"h3a{bq}")]
                    T3[("ldg", gi)] = p.ld(
                        gt3[gi % 3][:], gates.ap()[t0:t0 + G * 64, 1024 + h * 128:1024 + (h + 1) * 128].rearrange("(n s) v -> s n v", s=64),
                        f"h3g{gi % 3}", deps=[g3_("r5", gi - 3)])

                def C3(gi):
                    bq = gi % 2
                    T3[("ag", gi)] = p.A([T3[("ldg", gi)]], out=gt3[gi % 3][:], in_=gt3[gi % 3][:], func=AF.Silu)
                    tk = None
                    ev = None
                    for j0 in range(0, G, nb3):
                        pk = (j0 // nb3) % 2
                        for jj in range(nb3):
                            j = j0 + jj
                            ps = p.ps[pk][0:64, jj * 128:(jj + 1) * 128]
                            sl = slice(j * 64, (j + 1) * 64)
                            p.mm(T3[("ld", gi)] + [prevb[pk]] if jj == 0 else [], ps, sc3[bq][:, j, :], v3_[bq][:, j, :], True, False)
                            p.mm([], ps, qd3[bq][:, 0, sl], s3[bq][:, 0, j, :], False, False)
                            tk = p.mm([], ps, qd3[bq][:, 1, sl], s3[bq][:, 1, j, :], False, True, mark=(jj == nb3 - 1))
                        ev = p.A([tk, g3_("r5", gi - 2)], out=o3[bq][:, j0:j0 + nb3, :],
                                 in_=p.ps[pk][0:64, 0:nb3 * 128].rearrange("p (n v) -> p n v", v=128), func=AF.Copy)
                        prevb[pk] = ev
                    T3[("pe", gi)] = tk
                    T3[("ev", gi)] = ev

                def E3(gi):
                    bq = gi % 2
                    t0 = gi * G * 64
                    e2 = p.A([T3[("ev", gi)], g3_("e3", gi - 1)], out=sq3[:], in_=o3[bq][:], func=AF.Square)
                    e3 = p.V([e2], "tensor_reduce", ss[bq][:], sq3[:], mybir.AxisListType.X, ALU.add)
                    T3[("e3", gi)] = e3
                    r2 = p.rsqrt([e3], ss[bq][:], ss[bq][:], 1.0 / 128, EPS)
                    ssb = bass.AP(ss[bq], 0, [[G, 64], [1, G], [0, 128]])
                    r3 = p.V([r2], "tensor_tensor", o3[bq][:], o3[bq][:], ssb, ALU.mult)
                    ghb = bass.AP(ghg_sb, 0, [[128, 64], [0, G], [1, 128]])
                    r4 = p.V([r3], "tensor_tensor", o3[bq][:], o3[bq][:], ghb, ALU.mult)
                    r5 = p.V([r4, T3[("ag", gi)], g3_("st", gi - 2)], "tensor_tensor", ob3[bq][:], o3[bq][:], gt3[gi % 3][:], ALU.mult)
                    T3[("r5", gi)] = r5
                    T3[("st", gi)] = p.st(
                        ymix.ap()[t0:t0 + G * 64, 1024 + h * 128:1024 + (h + 1) * 128].rearrange("(n s) v -> s n v", s=64),
                        ob3[bq][:], f"h3s{bq}", deps=[r5])

                L3(0)
                for gi in range(NG3 + 1):
                    if gi < NG3:
                        if gi + 1 < NG3:
                            L3(gi + 1)
                        C3(gi)
                    if gi >= 1:
                        E3(gi - 1)
                p.barrier()
        p.barrier()

    def outproj(L, ymix, ya, gates, wbo, gpost, xres, xout, hT_next):
        NTL = L // 128
        with contextlib.ExitStack() as es:
            wsb = p.sb(es, "op_w", [128, KC, D], BF16)
            tw = p.ld(wsb[:], wbo.ap().rearrange("(k p) n -> p k n", p=128), "opw")
            ym = [p.sb(es, f"op_ym{i}", [128, D], BF16) for i in range(2)]
            yaf = [p.sb(es, f"op_ya{i}", [128, 1024], F32) for i in range(2)] if ya is not None else None
            gaf = [p.sb(es, f"op_ga{i}", [128, 1024], F32) for i in range(2)] if ya is not None else None
            ymT = [p.sb(es, f"op_ymT{i}", [128, KC, 128], BF16) for i in range(2)]
            xr = [p.sb(es, f"op_x{i}", [128, D], F32) for i in range(2)]
            yo = [p.sb(es, f"op_y{i}", [128, D], F32) for i in range(2)]
            junk = p.sb(es, "op_j", [128, D], BF16)
            st_ = [p.sb(es, f"op_st{i}", [128, 4], F32) for i in range(2)]
            hb = [p.sb(es, f"op_hb{i}", [128, D], BF16) for i in range(2)]
            ho = [p.sb(es, f"op_ho{i}", [128, KC, 128], BF16) for i in range(2)]
            T = {}
            g = lambda k, t: T.get((k, t))
            pbv = [p.pb[i][:].rearrange("p (k t) -> p k t", k=8) for i in range(2)]
            def A1a(ti):
                b = ti % 2
                rows = slice(ti * 128, (ti + 1) * 128)
                T[("tx", ti)] = p.ld(xr[b][:], xres[rows, :], f"opx{b}", deps=[g("v5", ti - 2)])
                if ya is not None:
                    t1 = p.ld(ym[b][:, 1024:2048], ymix.ap()[rows, 1024:2048], f"opm{b}", deps=[g("tr", ti - 2)])
                    t2 = p.ld(yaf[b][:], ya.ap()[rows, :], f"opa{b}", deps=[g("v1", ti - 2)])
                    t3 = p.ld(gaf[b][:], gates.ap()[rows, 0:1024], f"opa{b}", deps=[g("v1", ti - 2)])
                    a1 = p.A([t3], out=gaf[b][:], in_=gaf[b][:], func=AF.Silu)
                    v1 = p.V([a1, t2, g("tr", ti - 2)], "tensor_tensor", ym[b][:, 0:1024], yaf[b][:], gaf[b][:], ALU.mult)
                    T[("v1", ti)] = v1
                    rdy = [t1, v1]
                else:
                    rdy = [p.ld(ym[b][:], ymix.ap()[rows, :], f"opm{b}", deps=[g("tr", ti - 2)])]
                for kc in range(KC):
                    tk = p.tr(rdy + [g("c1h", ti - 2), g("c2h", ti - 2)] if kc == 0 else [],
                              p.pb[kc // 8][:, (kc % 8) * 128:(kc % 8 + 1) * 128],
                              ym[b][:, kc * 128:(kc + 1) * 128], ident[:], mark=(kc == KC - 1))
                T[("tr", ti)] = tk

            def A1b(ti):
                b = ti % 2
                tk = T[("tr", ti)]
                c1 = p.A([tk, g("mm", ti - 2)], out=ymT[b][:, 0:8, :], in_=pbv[0], func=AF.Copy)
                c2 = p.V([tk, g("mm", ti - 2)], "tensor_copy", ymT[b][:, 8:16, :], pbv[1])
                T[("c1", ti)], T[("c2", ti)] = c1, c2
                for cb in range(4):
                    for kc in range(KC):
                        tk = p.mm([c1, c2, tw] + [T.get(("ev", ti - 1, i)) for i in range(4)] if (kc == 0 and cb == 0) else [],
                                  p.ps[cb][:], ymT[b][:, kc, :],
                                  wsb[:, kc, cb * 512:(cb + 1) * 512], kc == 0, kc == KC - 1, mark=(kc == KC - 1))
                T[("mm", ti)] = tk

            def A2a(ti):
                b = ti % 2
                tk = T[("mm", ti)]
                for cb in range(4):
                    dep = [tk, g("so", ti - 2), g("v8", ti - 2)]
                    if cb % 2 == 0:
                        ev = p.A(dep, out=yo[b][:, cb * 512:(cb + 1) * 512], in_=p.ps[cb][:], func=AF.Copy)
                    else:
                        ev = p.V(dep, "tensor_copy", yo[b][:, cb * 512:(cb + 1) * 512], p.ps[cb][:])
                    T[("ev", ti, cb)] = ev

            def A2b(ti):
                b = ti % 2
                rows = slice(ti * 128, (ti + 1) * 128)
                evs = [T[("ev", ti, cb)] for cb in range(4)]
                a2 = p.A(evs, out=junk[:], in_=yo[b][:], func=AF.Square, accum_out=st_[b][:, 0:1])
                v3 = p.rsqrt([a2], st_[b][:, 1:2], st_[b][:, 0:1], 1.0 / D, EPS)
                v4 = p.V([v3] + evs, "scalar_tensor_tensor", yo[b][:], yo[b][:], st_[b][:, 1:2], gpost[:], ALU.mult, ALU.mult)
                v5 = p.V([v4, T[("tx", ti)]], "tensor_tensor", yo[b][:], yo[b][:], xr[b][:], ALU.add)
                T[("v5", ti)] = v5
                T[("so", ti)] = p.st(xout[rows, :], yo[b][:], f"opo{b}", deps=[v5])
                if hT_next is not None:
                    a3 = p.A([v5], out=junk[:], in_=yo[b][:], func=AF.Square, accum_out=st_[b][:, 2:3])
                    v7 = p.rsqrt([a3], st_[b][:, 3:4], st_[b][:, 2:3], 1.0 / D, EPS)
                    T[("v8", ti)] = p.V([v7, g("trh", ti - 2)], "tensor_scalar", hb[b][:], yo[b][:], st_[b][:, 3:4], None, ALU.mult)

            def Bst(tj):
                bj = tj % 2
                rows_j = slice(tj * 128, (tj + 1) * 128)
                for kc in range(KC):
                    tk2 = p.tr([g("v8", tj), g("c1", tj + 1), g("c2", tj + 1)] if kc == 0 else [],
                               p.pb[kc // 8][:, (kc % 8) * 128:(kc % 8 + 1) * 128],
                               hb[bj][:, kc * 128:(kc + 1) * 128], ident[:], mark=(kc == KC - 1))
                T[("trh", tj)] = tk2
                c1h = p.A([tk2, g("sh", tj - 2)], out=ho[bj][:, 0:8, :], in_=pbv[0], func=AF.Copy)
                c2h = p.V([tk2, g("sh", tj - 2)], "tensor_copy", ho[bj][:, 8:16, :], pbv[1])
                T[("c1h", tj)], T[("c2h", tj)] = c1h, c2h
                T[("sh", tj)] = p.st(hT_next.ap().rearrange("(k p) t -> p k t", p=128)[:, :, rows_j], ho[bj][:],
                                     f"oph{bj}", deps=[c1h, c2h])

            for ti in range(NTL + 1):
                if ti < NTL:
                    A1a(ti)
                if ti >= 1:
                    A2a(ti - 1)
                if ti < NTL:
                    A1b(ti)
                if ti >= 1:
                    A2b(ti - 1)
                    if hT_next is not None:
                        Bst(ti - 1)
        p.barrier()

    def attention(qT, kT, vtok, gates, Lq, Lk, og, dtab, lam_sb, gsub_sb, delta):
        NKT, NQB = Lk // 128, Lq // 256
        SK = 2
        with contextlib.ExitStack() as es:
            qs = p.sb(es, "at_q", [128, 2, Lq], BF16)
            ks = p.sb(es, "at_k", [128, 2, Lk], BF16)
            vs = p.sb(es, "at_v", [128, NKT, 257], BF16)
            absd = [p.sb(es, f"at_ad{i}", [128, 256], F32) for i in range(3)]
            sb_ = [p.sb(es, f"at_s{i}", [128, 256], F32) for i in range(4)]
            pt = [p.sb(es, f"at_p{i}", [128, 256], BF16) for i in range(4)]
            gt = [p.sb(es, f"at_g{i}", [128, 2, 256], F32) for i in range(2)]
            o0 = [p.sb(es, f"at_o0{i}", [128, 2, 256], F32) for i in range(2)]
            o1 = [p.sb(es, f"at_o1{i}", [128, 2, 256], F32) for i in range(2)]
            rs = [p.sb(es, f"at_rs{i}", [128, 8], F32) for i in range(2)]
            ob = [p.sb(es, f"at_ob{i}", [128, 2, 256], BF16) for i in range(2)]
            jk = p.sb(es, "at_j", [128, 256], F32)
            Sps = [p.ps[i][:, 0:256] for i in range(2)]
            acc = [[p.ps[2 + 2 * c + j] for j in range(2)] for c in range(2)]
            qbi = 0
            gt_free = [[], []]
            ob_free = [None, None]
            for h in range(8):
                slope = SLOPES[h]
                t_in = [p.ld(qs[:], qT.ap()[h * 256:(h + 1) * 256, :].rearrange("(c p) t -> p c t", p=128), "atq"),
                        p.ld(ks[:], kT.ap()[h * 256:(h + 1) * 256, :].rearrange("(c p) t -> p c t", p=128), "atq"),
                        p.ld(vs[:, :, 0:256], vtok.ap()[:, h * 256:(h + 1) * 256].rearrange("(n p) v -> p n v", p=128),
                             "atq")]
                t_in.append(p.G([], "memset", vs[:, :, 256:257], 1.0))
                acc_free = []
                for qb in range(NQB):
                    par = qbi % 2
                    qbi += 1
                    q0 = qb * 256
                    tg = p.ld(gt[par][:],
                              gates.ap()[q0:q0 + 256, h * 256:(h + 1) * 256].rearrange("(j p) v -> p j v", p=128),
                              f"atg{par}", deps=gt_free[par])
                    units = [(kt, c) for kt in range(NKT) for c in range(2)]
                    NU = len(units)
                    rd_ad = [None] * 3
                    rd_s = [None] * 4
                    rd_p = [None] * 4
                    rd_ps = [None] * 2
                    absT = {}
                    expT = {}
                    state = {"lastpv": None}

                    def do_abs(kt):
                        ab = kt % 3
                        idx = kt * NQB + qb
                        absT[kt] = p.A([rd_ad[ab]], out=absd[ab][:], in_=delta[:], func=AF.Abs, bias=dtab[:, idx:idx + 1])

                    def front(u):
                        kt, c = units[u]
                        b = u % 4
                        if c == 0:
                            if kt == 0:
                                do_abs(0)
                            if kt + 1 < NKT:
                                do_abs(kt + 1)
                        b2 = u % 2
                        tk = p.mm(t_in + [rd_ps[b2]], Sps[b2], ks[:, c, kt * 128:(kt + 1) * 128], qs[:, c, q0:q0 + 256],
                                  True, True, mark=True)
                        v1 = p.V([tk, absT[kt], rd_s[b]], "scalar_tensor_tensor", sb_[b][:], absd[kt % 3][:], -slope,
                                 Sps[b2], ALU.mult, ALU.add)
                        rd_ps[b2] = v1
                        rd_ad[kt % 3] = v1
                        a1 = p.A([v1, rd_p[b]], out=pt[b][:], in_=sb_[b][:], func=AF.Exp)
                        rd_s[b] = a1
                        expT[u] = a1

                    def back(u):
                        kt, c = units[u]
                        b = u % 4
                        for j in range(2):
                            deps = [expT[u]] if j == 0 else []
                            if kt == 0:
                                deps = deps + acc_free
                            state["lastpv"] = p.mm(deps, acc[c][j][:, 0:257], pt[b][:, j * 128:(j + 1) * 128],
                                                   vs[:, kt, :], kt == 0, kt == NKT - 1, mark=(j == 1))
                        rd_p[b] = state["lastpv"]

                    for u in range(NU + SK):
                        if u < NU:
                            front(u)
                        if u >= SK:
                            back(u - SK)
                    lastpv = state["lastpv"]
                    evs = []
                    for c in range(2):
                        for j in range(2):
                            dst = (o0[par] if c == 0 else o1[par])
                            e1 = p.V([lastpv], "reciprocal", rs[par][:, 2 * c + j:2 * c + j + 1], acc[c][j][:, 256:257])
                            if c == 0:
                                evs.append(p.V([e1], "tensor_scalar", dst[:, j, :], acc[c][j][:, 0:256],
                                               rs[par][:, 2 * c + j:2 * c + j + 1], None, ALU.mult))
                            else:
                                evs.append(p.V([e1], "tensor_scalar", dst[:, j, :], acc[c][j][:, 0:256],
                                               rs[par][:, 2 * c + j:2 * c + j + 1], lam_sb[:, 1:2], ALU.mult, ALU.mult))
                    acc_free = [evs[-1]]
                    f1 = p.V(evs, "tensor_tensor", o0[par][:], o0[par][:], o1[par][:], ALU.add)
                    for j in range(2):
                        sqt = p.A([f1], out=jk[:], in_=o0[par][:, j, :], func=AF.Square, accum_out=rs[par][:, 4 + j:5 + j])
                    f2 = p.rsqrt([sqt], rs[par][:, 4:6], rs[par][:, 4:6], 1.0 / 256, EPS)
                    f3 = p.V([f2], "tensor_scalar", rs[par][:, 4:6], rs[par][:, 4:6], (1.0 - LAMBDA_INIT), None, ALU.mult)
                    ag = p.A([tg], out=gt[par][:], in_=gt[par][:], func=AF.Silu)
                    for j in range(2):
                        f4 = p.V([f3], "scalar_tensor_tensor", o0[par][:, j, :], o0[par][:, j, :], rs[par][:, 4 + j:5 + j],
                                 gsub_sb[:], ALU.mult, ALU.mult)
                    f5 = p.V([f4, ag, ob_free[par]], "tensor_tensor", ob[par][:], o0[par][:], gt[par][:], ALU.mult)
                    ob_free[par] = p.st(og.ap()[q0:q0 + 256, h * 256:(h + 1) * 256].rearrange("(j p) v -> p j v", p=128),
                                        ob[par][:], f"ato{par}", deps=[f5])
                    gt_free[par] = [f5, ag]
                p.barrier()
                gt_free = [[], []]
                ob_free = [None, None]
        p.barrier()

    c128 = p.sb(ges, "c128_sb", [128, 3, 128], BF16)
    p.ld(c128[:], c128_d.ap().rearrange("k p c -> p k c"), "c0")
    cs256 = p.sb(ges, "cs256_sb", [128, 2, 512], BF16)
    p.ld(cs256[:], cs256_d.ap().rearrange("(c p) n -> p c n", p=128), "c0")
    tw_sb, cm_sb = {}, {}
    for L in tw_d:
        M = L // 128
        tw_sb[L] = p.sb(ges, f"tw{L}_sb", [128, 3, M], F32)
        p.ld(tw_sb[L][:], tw_d[L].ap().rearrange("k p m -> p k m"), "c0")
        cm_sb[L] = p.sb(ges, f"cm{L}_sb", [M, 2, M], BF16)
        p.ld(cm_sb[L][:], cm_d[L].ap().rearrange("k b d -> b k d"), "c0")
    hm = p.sb(ges, "hm_sb", [64, 2, 64], F32)
    p.ld(hm[:], hmask_d.ap().rearrange("k s t -> s k t"), "c0")
    delta = p.sb(ges, "delta_sb", [128, 256], F32)
    p.ld(delta[:], delta_d.ap(), "c0")
    dtabs = p.sb(ges, "dtabs_sb", [128, (LS // 128) * (LS // 256)], F32)
    p.ld(dtabs[:], dtabs_d.ap(), "c0")
    dtabp = p.sb(ges, "dtabp_sb", [128, (LP // 128) * (OWN // 256)], F32)
    p.ld(dtabp[:], dtabp_d.ap(), "c0")
    ownidx = p.sb(ges, "ownidx_sb", [128, OWN // 128], I32)
    p.ld(ownidx[:], ownidx_d.ap(), "c0")
    ghg_sb = p.sb(ges, "ghg_sb", [64, 128], F32)
    p.ld(ghg_sb[:], bcast_rows(ghg.ap(), 128, 64), "c0")
    gsub_sb = p.sb(ges, "gsub_sb", [128, 256], F32)
    p.ld(gsub_sb[:], bcast_rows(gsub.ap(), 256), "c0")
    lraw = p.sb(ges, "lraw_sb", [128, 2, 3, 8], F32)
    p.ld(lraw[:], lbl.ap().rearrange("d s (h k) -> k d s h", k=128), "c0", slow=True)
    lbt = p.sb(ges, "lbt_sb", [128, 4, 8], F32)
    lsum = p.sb(ges, "lsum_sb", [128, 2, 8], F32)
    lamv = p.sb(ges, "lamv_sb", [128, 4, 128], F32)
    p.ld(lamv[:], bass.AP(lam4.ap().tensor, 0, [[0, 128], [128, 4], [1, 128]]), "c0")
    lam_sb = p.sb(ges, "lam_sb", [128, 4], F32)
    ljunk = p.sb(ges, "ljunk_sb", [128, 128], F32)
    p.barrier()
    a = p.A([], out=lraw[:], in_=lraw[:], func=AF.Exp)
    v = p.V([a], "tensor_tensor", lsum[:], lraw[:, :, 0, :], lraw[:, :, 1, :], ALU.add)
    v = p.V([v], "tensor_tensor", lsum[:], lsum[:], lraw[:, :, 2, :], ALU.add)
    v = p.V([v], "reciprocal", lsum[:], lsum[:])
    v = p.V([v], "tensor_tensor", lbt[:, 0:2, :], lraw[:, :, 0, :], lsum[:], ALU.mult)
    v = p.V([v], "tensor_scalar", lbt[:, 2:4, :], lbt[:, 0:2, :], -1.0, 1.0, ALU.mult, ALU.add)
    v = p.V([v], "tensor_tensor", ljunk[:], lamv[:, 0, :], lamv[:, 1, :], ALU.mult)
    v = p.V([v], "tensor_reduce", lam_sb[:, 2:3], ljunk[:], mybir.AxisListType.X, ALU.add)
    v = p.V([v], "tensor_tensor", ljunk[:], lamv[:, 2, :], lamv[:, 3, :], ALU.mult)
    v = p.V([v], "tensor_reduce", lam_sb[:, 3:4], ljunk[:], mybir.AxisListType.X, ALU.add)
    a = p.A([v], out=lam_sb[:, 2:4], in_=lam_sb[:, 2:4], func=AF.Exp)
    v = p.V([a], "tensor_tensor", lam_sb[:, 0:1], lam_sb[:, 2:3], lam_sb[:, 3:4], ALU.subtract)
    v = p.V([v], "tensor_scalar", lam_sb[:, 1:2], lam_sb[:, 0:1], LAMBDA_INIT, -1.0, ALU.add, ALU.mult)
    p.barrier()

    seqs = [("s%d" % i, LS, xs.ap()[i * LS:(i + 1) * LS, :], ys.ap()[i * LS:(i + 1) * LS, :]) for i in range(NS)]
    seqs.append(("p", LP, xp.ap(), None))
    scale_q = 128 ** -0.5
    for (nm, L, xin, yout) in seqs:
        isP = yout is None
        hT = p.dram(f"hT_{nm}", [D, L], BF16)
        norm_T(xin, L, hT)
        done("norm")
        uT = p.dram(f"uT_{nm}", [1024, L], BF16)
        gat0 = p.dram(f"g0_{nm}", [L, 2048], F32)
        qrT = p.dram(f"qr_{nm}", [1024, L], F32)
        ffT = p.dram(f"ff_{nm}", [1024, L], F32)
        fbT = p.dram(f"fb_{nm}", [1024, L], F32)
        vtk = p.dram(f"vt_{nm}", [L, 1024], BF16)
        proj(hT, L, wb0i, 7168, [
            (0, 1024, "F", uT, 0, 1.0, BF16),
            (1024, 1024, "T", gat0, 0, 1.0, F32),
            (2048, 1024, "F", qrT, 0, 1.0, F32),
            (3072, 1024, "T", vtk, 0, 1.0, BF16),
            (4096, 1024, "F", ffT, 0, 1.0, F32),
            (5120, 1024, "F", fbT, 0, 1.0, F32),
            (6144, 1024, "T", gat0, 1024, 1.0, F32),
        ])
        done("proj0")
        ya = p.dram(f"ya_{nm}", [L, 1024], F32)
        fnet(uT, L, ya, (c128, cs256, tw_sb[L], cm_sb[L]))
        done("fnet")
        ymix = p.dram(f"ym_{nm}", [L, 2048], BF16)
        hgrn(qrT, ffT, fbT, vtk, gat0, L, ymix, lbt, hm, ghg_sb)
        done("hgrn")
        x1 = p.dram(f"x1_{nm}", [L, D], F32)
        h1T = p.dram(f"h1T_{nm}", [D, L], BF16)
        outproj(L, ymix, ya, gat0, wb0o, gpost_sb[0], xin, x1.ap(), h1T)
        done("out0")
        kT = p.dram(f"kT_{nm}", [2048, L], BF16)
        v1t = p.dram(f"v1_{nm}", [L, 2048], BF16)
        if not isP:
            Lq = L
            qT = p.dram(f"qT_{nm}", [2048, Lq], BF16)
            gat1 = p.dram(f"g1_{nm}", [Lq, 2048], F32)
            proj(h1T, L, wb1i, 8192, [
                (0, 2048, "F", qT, 0, scale_q, BF16),
                (2048, 2048, "F", kT, 0, 1.0, BF16),
                (4096, 2048, "T", v1t, 0, 1.0, BF16),
                (6144, 2048, "T", gat1, 0, 1.0, F32),
            ])
            xres1 = x1.ap()
            dtab = dtabs
        else:
            Lq = OWN
            proj(h1T, L, wb1i, 8192, [
                (2048, 2048, "F", kT, 0, 1.0, BF16),
                (4096, 2048, "T", v1t, 0, 1.0, BF16),
            ])
            x1own = p.dram("x1own", [OWN, D], F32)
            with contextlib.ExitStack() as es:
                gx = p.sb(es, "gx", [128, D], F32)
                prev = None
                for ti in range(OWN // 128):
                    p.pool.wait(prev)
                    ds = p.dsem("gath")
                    p.nc.gpsimd.indirect_dma_start(
                        out=gx[:], out_offset=None, in_=x1.ap(),
                        in_offset=bass.IndirectOffsetOnAxis(ap=ownidx[:, ti:ti + 1], axis=0),
                    ).then_inc(ds.sem, 16)
                    ds.n += 16
                    tok = (ds, ds.n)
                    p.pending.append(tok)
                    prev = p.st(x1own.ap()[ti * 128:(ti + 1) * 128, :], gx[:], "gaths", deps=[tok])
            p.barrier()
            h1To = p.dram("h1To", [D, OWN], BF16)
            norm_T(x1own.ap(), OWN, h1To)
            qT = p.dram(f"qT_{nm}", [2048, Lq], BF16)
            gat1 = p.dram(f"g1_{nm}", [Lq, 2048], F32)
            proj(h1To, OWN, wb1i, 8192, [
                (0, 2048, "F", qT, 0, scale_q, BF16),
                (6144, 2048, "T", gat1, 0, 1.0, F32),
            ])
            xres1 = x1own.ap()
            yout = yp.ap()
            dtab = dtabp
        done("proj1")
        og = p.dram(f"og_{nm}", [Lq, 2048], BF16)
        attention(qT, kT, v1t, gat1, Lq, L, og, dtab, lam_sb, gsub_sb, delta)
        done("attn")
        outproj(Lq, og, None, None, wb1o, gpost_sb[1], xres1, yout, None)
        done("seq0")


def dft_cs(n):
    j = np.arange(n)
    ang = 2 * np.pi * np.outer(j, j) / n
    return np.cos(ang), np.sin(ang)


def host_tables(cfg, core):
    NS, LS, LP, OWN = cfg["NS"], cfg["LS"], cfg["LP"], cfg["OWN"]
    t = {}
    t["ident"] = np.eye(128, dtype=np.float32).astype(NPBF)
    c, s = dft_cs(128)
    t["c128"] = np.stack([c, s, -s]).astype(np.float32).astype(NPBF)
    c, s = dft_cs(256)
    t["cs256"] = np.concatenate([c, -s], axis=1).astype(np.float32).astype(NPBF)
    for L in sorted({LS, LP}):
        M = L // 128
        ang = 2 * np.pi * np.outer(np.arange(128), np.arange(M)) / L
        t[f"tw{L}"] = np.stack([np.cos(ang), np.sin(ang), -np.sin(ang)]).astype(np.float32)
        cm, sm = dft_cs(M)
        sc = 1.0 / math.sqrt(L * 256)
        t[f"cm{L}"] = np.stack([cm * sc, sm * sc]).astype(np.float32).astype(NPBF)
    s_, t_ = np.meshgrid(np.arange(64), np.arange(64), indexing="ij")
    t["hmask"] = np.stack([(s_ <= t_), (s_ >= t_)]).astype(np.float32)
    t["delta"] = (np.arange(128)[:, None] - np.arange(256)[None, :]).astype(np.float32)
    kt, qb = np.meshgrid(np.arange(LS // 128), np.arange(LS // 256), indexing="ij")
    t["dtabs"] = np.broadcast_to((kt * 128 - qb * 256).reshape(1, -1), (128, kt.size)).astype(np.float32).copy()
    kt, qb = np.meshgrid(np.arange(LP // 128), np.arange(OWN // 256), indexing="ij")
    t["dtabp"] = np.broadcast_to((kt * 128 - (core * OWN + qb * 256)).reshape(1, -1), (128, kt.size)).astype(np.float32).copy()
    t["ownidx"] = (core * OWN + np.arange(OWN)).reshape(OWN // 128, 128).T.astype(np.int32).copy()
    return t


def run(inputs, cfg, ncores):
    NS, LS, LP, OWN = cfg["NS"], cfg["LS"], cfg["LP"], cfg["OWN"]
    f = lambda a: np.ascontiguousarray(np.asarray(a, dtype=np.float32))
    xsamp = f(inputs["x_sample"])
    shared = {
        "xp": f(inputs["x_prompt"])[0],
        "w0i": f(inputs["ev_w_in"])[0], "w0o": f(inputs["ev_w_out"])[0],
        "w1i": f(inputs["od_w_in"])[0], "w1o": f(inputs["od_w_out"])[0],
        "g0pre": f(inputs["ev_norm_pre"])[0], "g0post": f(inputs["ev_norm_post"])[0],
        "g1pre": f(inputs["od_norm_pre"])[0], "g1post": f(inputs["od_norm_post"])[0],
        "lbl": f(inputs["hgrn_lb_logits"]), "ghg": f(inputs["hgrn_norm"])[0],
        "lam4": np.stack([f(inputs["lambda_q1"])[0], f(inputs["lambda_k1"])[0],
                          f(inputs["lambda_q2"])[0], f(inputs["lambda_k2"])[0]]),
        "gsub": f(inputs["subln"])[0],
    }
    nc = build(cfg)
    in_maps = []
    for c in range(ncores):
        m = dict(shared)
        m["xs"] = xsamp[c * NS:(c + 1) * NS].reshape(NS * LS, D)
        m.update(host_tables(cfg, c))
        in_maps.append(m)
    res = run_bass_kernel_spmd(nc, in_maps, core_ids=list(range(ncores)))
    global LAST_RES
    LAST_RES = res.results
    y_s = np.concatenate([r["ys"].reshape(NS, LS, D) for r in res.results], axis=0)
    y_p = np.concatenate([r["yp"] for r in res.results], axis=0)[None]
    return y_p.astype(np.float32), y_s.astype(np.float32)


def kernel(**inputs):
    import os
    cfg = {"NS": 2, "LS": 2048, "LP": 8192, "OWN": 1024}
    if os.environ.get("K_STOP"):
        cfg["stop"] = os.environ["K_STOP"]
    return run(inputs, cfg, 8)
```

```python
import contextlib, math
import numpy as np
import ml_dtypes
import concourse.bass as bass
import concourse.mybir as mybir
from concourse.bass_utils import run_bass_kernel_spmd

F32, BF16, I32 = mybir.dt.float32, mybir.dt.bfloat16, mybir.dt.int32
AF = mybir.ActivationFunctionType
ALU = mybir.AluOpType
D = 2048
KC = 16
EPS = 1e-6
LAMBDA_INIT = 0.8 - 0.6 * math.exp(-0.3 * 1)
SLOPES = [2.0 ** (-8.0 * (h + 1) / 8) for h in range(8)]
NPBF = ml_dtypes.bfloat16


class StopBuild(Exception):
    pass


class Eng:
    def __init__(self, e, sem):
        self.e, self.sem, self.n, self.seen = e, sem, 0, {}

    def mark(self, ins):
        ins.then_inc(self.sem, 1)
        self.n += 1
        return (self, self.n)

    def wait(self, *toks):
        for tok in toks:
            if tok is None:
                continue
            src, n = tok
            if self.seen.get(src, 0) >= n:
                continue
            self.e.wait_ge(src.sem, n)
            self.seen[src] = n


class DSem:
    def __init__(self, sem):
        self.sem, self.n = sem, 0


class P:
    def __init__(self, cfg):
        self.cfg = cfg
        self.es = contextlib.ExitStack()
        nc = self.nc = bass.Bass("TRN2", target_bir_lowering=False)
        mk = lambda nm: self.es.enter_context(nc.semaphore(nm))
        self.pe = Eng(nc.tensor, mk("s_pe"))
        self.act = Eng(nc.scalar, mk("s_act"))
        self.dve = Eng(nc.vector, mk("s_dve"))
        self.pool = Eng(nc.gpsimd, mk("s_pool"))
        self.sp = Eng(nc.sync, mk("s_sp"))
        self.engs = [self.pe, self.act, self.dve, self.pool, self.sp]
        self.dsems = {}
        self.pending = []
        self.ndram = 0
        self.ps = [self.es.enter_context(nc.psum_tensor(f"ps{i}", [128, 512], F32)) for i in range(6)]
        self.pb = [self.es.enter_context(nc.psum_tensor(f"pb{i}", [128, 1024], BF16)) for i in range(2)]
        self.dummy = self.es.enter_context(nc.sbuf_tensor("dummy_sb", [128, 2], F32))
        self.pool.mark(nc.gpsimd.memset(self.dummy[:], 0.0))

    def sb(self, es, name, shape, dt):
        self.nsb = getattr(self, "nsb", 0) + 1
        return es.enter_context(self.nc.sbuf_tensor(f"{name}_{self.nsb}", shape, dt))

    def dram(self, name, shape, dt, kind="Internal"):
        t = self.nc.dram_tensor(name, list(shape), dt, kind=kind)
        if not hasattr(self, "named"):
            self.named = {}
        self.named[name] = (t, list(shape), dt)
        return t

    def dump(self):
        self.barrier()
        for name in self.cfg.get("dump", []):
            if name not in self.named:
                continue
            t, shape, dt = self.named[name]
            o = self.nc.dram_tensor("dbg_" + name, shape, dt, kind="ExternalOutput")
            self.ld(o.ap(), t.ap(), "dump")
        self.barrier()

    def dsem(self, key):
        if key not in self.dsems:
            self.dsems[key] = DSem(self.es.enter_context(self.nc.semaphore("d_" + key)))
        return self.dsems[key]

    def dma(self, q, out, in_, key, deps=(), slow=False):
        q.wait(*deps)
        ds = self.dsem(key)
        kw = {"allow_slow_non_contiguous": True} if slow else {}
        q.e.dma_start(out=out, in_=in_, **kw).then_inc(ds.sem, 16)
        ds.n += 16
        tok = (ds, ds.n)
        self.pending.append(tok)
        return tok

    def ld(self, out, in_, key, deps=(), slow=False):
        return self.dma(self.sp, out, in_, key, deps, slow)

    def st(self, out, in_, key, deps=()):
        return self.dma(self.pool, out, in_, key, deps)

    def barrier(self):
        toks = [(e, e.n) for e in self.engs if e.n > 0] + self.pending
        for e in self.engs:
            e.wait(*toks)
        self.pending = []

    def A(self, deps, *a, **k):
        self.act.wait(*deps)
        tok = self.act.mark(self.act.e.activation(*a, **k))
        if k.get("accum_out") is not None:
            tok = self.act.mark(self.act.e.activation(out=self.dummy[:, 1:2], in_=self.dummy[:, 0:1], func=AF.Copy))
        return tok

    def V(self, deps, fn, *a, **k):
        self.dve.wait(*deps)
        return self.dve.mark(getattr(self.dve.e, fn)(*a, **k))

    def G(self, deps, fn, *a, **k):
        self.pool.wait(*deps)
        return self.pool.mark(getattr(self.pool.e, fn)(*a, **k))

    def X(self, eng, deps, fn, *a, **k):
        eng.wait(*deps)
        return eng.mark(getattr(eng.e, fn)(*a, **k))

    def rsqrt(self, deps, out, in_, mul, add):
        a = self.A(deps, out=out, in_=in_, func=AF.Ln, scale=float(mul), bias=float(add))
        return self.A([a], out=out, in_=out, func=AF.Exp, scale=-0.5)

    def mm(self, deps, out, lhsT, rhs, start, stop, mark=False):
        self.pe.wait(*deps)
        ins = self.pe.e.matmul(out, lhsT, rhs, start=start, stop=stop)
        return self.pe.mark(ins) if mark else None

    def tr(self, deps, out, in_, ident, mark=False):
        self.pe.wait(*deps)
        ins = self.pe.e.transpose(out, in_, ident)
        return self.pe.mark(ins) if mark else None


def bcast_rows(ap_dram_1d, n, parts=128):
    return bass.AP(ap_dram_1d.tensor, ap_dram_1d.offset, [[0, parts], [1, n]])


def build(cfg):
    p = P(cfg)
    try:
        _build(cfg, p)
    except StopBuild:
        p.dump()
        return p.nc
    p.dump()
    p.es.close()
    return p.nc


def _build(cfg, p):
    NS, LS, LP, OWN = cfg["NS"], cfg["LS"], cfg["LP"], cfg["OWN"]

    def done(tag):
        if cfg.get("stop") == tag:
            raise StopBuild()
    nc = p.nc
    inp = lambda name, shape, dt=F32: nc.dram_tensor(name, list(shape), dt, kind="ExternalInput")
    xs = inp("xs", [NS * LS, D])
    xp = inp("xp", [LP, D])
    w0i, w0o = inp("w0i", [D, 7168]), inp("w0o", [D, D])
    w1i, w1o = inp("w1i", [D, 8192]), inp("w1o", [D, D])
    g0pre, g0post = inp("g0pre", [D]), inp("g0post", [D])
    g1pre, g1post = inp("g1pre", [D]), inp("g1post", [D])
    lbl = inp("lbl", [2, 3, 1024])
    ghg = inp("ghg", [128])
    lam4 = inp("lam4", [4, 128])
    gsub = inp("gsub", [256])
    ident_d = inp("ident", [128, 128], BF16)
    c128_d = inp("c128", [3, 128, 128], BF16)
    cs256_d = inp("cs256", [256, 512], BF16)
    tw_d = {L: inp(f"tw{L}", [3, 128, L // 128]) for L in sorted({LS, LP})}
    cm_d = {L: inp(f"cm{L}", [2, L // 128, L // 128], BF16) for L in sorted({LS, LP})}
    hmask_d = inp("hmask", [2, 64, 64])
    delta_d = inp("delta", [128, 256])
    dtabs_d = inp("dtabs", [128, (LS // 128) * (LS // 256)])
    dtabp_d = inp("dtabp", [128, (LP // 128) * (OWN // 256)])
    ownidx_d = inp("ownidx", [128, OWN // 128], I32)
    ys = nc.dram_tensor("ys", [NS * LS, D], F32, kind="ExternalOutput")
    yp = nc.dram_tensor("yp", [OWN, D], F32, kind="ExternalOutput")

    ges = p.es
    ident = p.sb(ges, "ident_sb", [128, 128], BF16)
    toks = [p.ld(ident[:], ident_d.ap(), "c0")]
    gpost_sb = [p.sb(ges, f"gpost{i}", [128, D], F32) for i in range(2)]
    toks.append(p.ld(gpost_sb[0][:], bcast_rows(g0post.ap(), D), "c0"))
    toks.append(p.ld(gpost_sb[1][:], bcast_rows(g1post.ap(), D), "c0"))
    gpre_sb = [p.sb(ges, f"gpre{i}", [128, KC], F32) for i in range(2)]
    toks.append(p.ld(gpre_sb[0][:], g0pre.ap().rearrange("(c p) -> p c", p=128), "c0", slow=True))
    toks.append(p.ld(gpre_sb[1][:], g1pre.ap().rearrange("(c p) -> p c", p=128), "c0", slow=True))
    p.barrier()

    def prep_w(w, ncols, gcol, name):
        wb = p.dram(name, [D, ncols], BF16)
        with contextlib.ExitStack() as es:
            CW = 1024
            NBUF = 6
            wf = [p.sb(es, f"wf{i}", [128, CW], F32) for i in range(NBUF)]
            wo = [p.sb(es, f"wo{i}", [128, CW], BF16) for i in range(NBUF)]
            cons = [None] * NBUF
            sts = [None] * NBUF
            i = 0
            for kc in range(KC):
                for c0 in range(0, ncols, CW):
                    b = i % NBUF
                    t = p.ld(wf[b][:], w.ap()[kc * 128:(kc + 1) * 128, c0:c0 + CW], f"wl{b}", deps=[cons[b]])
                    if gcol is None:
                        eng = (p.dve, p.pool, p.act)[b % 3]
                    else:
                        eng = (p.dve, p.pool, p.dve)[b % 3]
                    if gcol is None:
                        if eng is p.act:
                            cons[b] = p.A([t, sts[b]], out=wo[b][:], in_=wf[b][:], func=AF.Copy)
                        else:
                            cons[b] = p.X(eng, [t, sts[b]], "tensor_copy", wo[b][:], wf[b][:])
                    else:
                        cons[b] = p.X(eng, [t, sts[b]], "tensor_scalar", wo[b][:], wf[b][:],
                                      gcol[:, kc:kc + 1], None, ALU.mult)
                    sts[b] = p.dma(p.act, wb.ap()[kc * 128:(kc + 1) * 128, c0:c0 + CW], wo[b][:], f"ws{b}",
                                   deps=[cons[b]])
                    i += 1
        p.barrier()
        return wb

    wb0i = prep_w(w0i, 7168, gpre_sb[0], "wb0i")
    wb0o = prep_w(w0o, D, None, "wb0o")
    wb1i = prep_w(w1i, 8192, gpre_sb[1], "wb1i")
    wb1o = prep_w(w1o, D, None, "wb1o")
    done("prep")

    def norm_T(x_rows, L, hT):
        with contextlib.ExitStack() as es:
            xt = [p.sb(es, f"nx{i}", [128, D], F32) for i in range(2)]
            junk = p.sb(es, "njunk", [128, D], BF16)
            hb = [p.sb(es, f"nhb{i}", [128, D], BF16) for i in range(2)]
            ho = [p.sb(es, f"nho{i}", [128, KC, 128], BF16) for i in range(2)]
            st_ = [p.sb(es, f"nst{i}", [128, 2], F32) for i in range(2)]
            rd = [None, None]
            hbr = [None, None]
            hor = [None, None]
            for ti in range(L // 128):
                b = ti % 2
                t = p.ld(xt[b][:], x_rows[ti * 128:(ti + 1) * 128, :], f"nl{b}", deps=[rd[b]])
                a1 = p.A([t], out=junk[:], in_=xt[b][:], func=AF.Square, accum_out=st_[b][:, 0:1])
                v2 = p.rsqrt([a1], st_[b][:, 1:2], st_[b][:, 0:1], 1.0 / D, EPS)
                v3 = p.V([v2, t, hbr[b]], "tensor_scalar", hb[b][:], xt[b][:], st_[b][:, 1:2], None, ALU.mult)
                rd[b] = v3
                for kc in range(KC):
                    tk = p.tr([v3, hor[b]] if kc == 0 else [], p.pb[kc // 8][:, (kc % 8) * 128:(kc % 8 + 1) * 128],
                              hb[b][:, kc * 128:(kc + 1) * 128], ident[:], mark=(kc == KC - 1))
                hbr[b] = tk
                c1 = p.A([tk, hor[b]], out=ho[b][:, 0:8, :], in_=p.pb[0][:].rearrange("p (k t) -> p k t", k=8),
                         func=AF.Copy)
                c2 = p.V([tk, hor[b]], "tensor_copy", ho[b][:, 8:16, :],
                         p.pb[1][:].rearrange("p (k t) -> p k t", k=8))
                p.pe.wait(c1, c2)
                hor[b] = p.st(hT.ap().rearrange("(k p) t -> p k t", p=128)[:, :, ti * 128:(ti + 1) * 128],
                              ho[b][:], f"ns{b}", deps=[c1, c2])
        p.barrier()

    def proj(hT, L, wb, ncols_total, jobs):
        TB = min(L, 1024)
        with contextlib.ExitStack() as es:
            hblk = p.sb(es, "pj_h", [128, KC, TB], BF16)
            wblk = [p.sb(es, f"pj_w{i}", [128, KC, 512], BF16) for i in range(2)]
            osb = {F32: [p.sb(es, f"pj_of{i}", [128, 512], F32) for i in range(2)],
                   BF16: [p.sb(es, f"pj_ob{i}", [128, 512], BF16) for i in range(2)]}
            wread = [None, None]
            ost = {F32: [None, None], BF16: [None, None]}
            psr = [None] * 4
            wi = 0
            oi = 0
            pi = 0
            hread = None
            for tb in range(L // TB):
                th = p.ld(hblk[:], hT.ap().rearrange("(k p) t -> p k t", p=128)[:, :, tb * TB:(tb + 1) * TB],
                          "pjh", deps=[hread])
                cbs = [(j, c) for j in jobs for c in range(0, j[1], 512)]
                for (job, c) in cbs:
                    col0, ncols, mode, od, o0, scale, odt = job
                    b = wi % 2
                    wi += 1
                    tw = p.ld(wblk[b][:], wb.ap().rearrange("(k p) n -> p k n", p=128)[:, :, col0 + c:col0 + c + 512],
                              f"pjw{b}", deps=[wread[b]])
                    last = None
                    TW = min(512, TB)
                    if mode == "F":
                        subs = [(s4, t5) for s4 in range(4) for t5 in range(TB // TW)]
                    else:
                        subs = [(s4, 0) for s4 in range(TB // 128)]
                    for (s4, t5) in subs:
                        pk = pi % 4
                        pi += 1
                        W_ = TW if mode == "F" else 512
                        ps = p.ps[pk][:, 0:W_]
                        for kc in range(KC):
                            if mode == "F":
                                lhsT, rhs = wblk[b][:, kc, s4 * 128:(s4 + 1) * 128], hblk[:, kc, t5 * TW:(t5 + 1) * TW]
                            else:
                                lhsT, rhs = hblk[:, kc, s4 * 128:(s4 + 1) * 128], wblk[b][:, kc, :]
                            tk = p.mm([th, tw, psr[pk]] if kc == 0 else [], ps, lhsT, rhs, kc == 0, kc == KC - 1,
                                      mark=(kc == KC - 1))
                        last = tk
                        ob = oi % 2
                        oi += 1
                        o = osb[odt][ob][:, 0:W_]
                        if oi % 2 == 0:
                            ev = p.A([tk, ost[odt][ob]], out=o, in_=ps, func=AF.Copy, scale=float(scale))
                        else:
                            ev = p.V([tk, ost[odt][ob]], "tensor_scalar", o, ps, float(scale), None, ALU.mult)
                        psr[pk] = ev
                        if mode == "F":
                            dst = od.ap()[o0 + c + s4 * 128:o0 + c + (s4 + 1) * 128,
                                          tb * TB + t5 * TW:tb * TB + (t5 + 1) * TW]
                        else:
                            dst = od.ap()[tb * TB + s4 * 128:tb * TB + (s4 + 1) * 128, o0 + c:o0 + c + 512]
                        ost[odt][ob] = p.st(dst, o, f"pjs{ob}{'f' if odt == F32 else 'b'}", deps=[ev])
                    wread[b] = last
                    hread = last
        p.barrier()

    def fnet(uT, L, ya, tabs):
        M = L // 128
        c128, cs256, tw, cm = tabs
        Bd = p.dram(f"fn_B{p.ndram}", [128, M, 512], BF16)
        p.ndram += 1
        CP = 32
        with contextlib.ExitStack() as es:
            ug = p.sb(es, "fn_u", [128, 2, L], BF16)
            MB = min(M, 32)
            V = p.sb(es, "fn_V", [128, MB, 512], BF16)
            Bs = p.sb(es, "fn_Bs", [128, MB, 512], BF16)
            tmp = [p.sb(es, f"fn_t{i}", [128, 2, 256], F32) for i in range(2)]
            Bt = p.sb(es, "fn_Bt", [M, CP, 512], BF16)
            Y = [p.sb(es, f"fn_Y{i}", [M, 2, 256], F32) for i in range(2)]
            for g in range(4):
                tu = p.ld(ug[:], uT.ap()[g * 256:(g + 1) * 256, :].rearrange("(c p) t -> p c t", p=128), "fnu")
                ts_all = []
                for bh in range(M // MB):
                    evs = []
                    prev = [None, None]
                    for bl in range(MB):
                        b = bh * MB + bl
                        pk = b % 2
                        for ch in range(2):
                            lhsT = bass.AP(ug, ch * L + b, [[2 * L, 128], [M, 128]])
                            tk = p.mm([tu, prev[pk]] if ch == 0 else [], p.ps[pk][:], lhsT, cs256[:, ch, :],
                                      ch == 0, ch == 1, mark=(ch == 1))
                        if b % 2 == 0:
                            ev = p.A([tk], out=V[:, bl, :], in_=p.ps[pk][:], func=AF.Copy)
                        else:
                            ev = p.V([tk], "tensor_copy", V[:, bl, :], p.ps[pk][:])
                        prev[pk] = ev
                        evs.append(ev)
                    prevr = [None, None]
                    tw_tok = []
                    for bp in range(MB // 2):
                        pk = 2 + (bp % 2) * 2
                        Ar, Ai = p.ps[pk], p.ps[pk + 1]
                        b0 = 2 * bp
                        dep = [evs[b0], evs[b0 + 1], prevr[bp % 2]]
                        ar3 = Ar[:].rearrange("p (b f) -> p b f", b=2)
                        ai3 = Ai[:].rearrange("p (b f) -> p b f", b=2)
                        p.mm(dep, ar3, c128[:, 0, :], V[:, b0:b0 + 2, 0:256], True, False)
                        p.mm([], ar3, c128[:, 1, :], V[:, b0:b0 + 2, 256:512], False, True)
                        p.mm([], ai3, c128[:, 0, :], V[:, b0:b0 + 2, 256:512], True, False)
                        tk = p.mm([], ai3, c128[:, 2, :], V[:, b0:b0 + 2, 0:256], False, True, mark=True)
                        last = []
                        for j in range(2):
                            bl = b0 + j
                            b = bh * MB + bl
                            t1 = p.V([tk], "tensor_scalar", tmp[0][:, j, :], ar3[:, j, :], tw[:, 0, b:b + 1], None, ALU.mult)
                            t3 = p.V([tk], "tensor_scalar", tmp[1][:, j, :], ai3[:, j, :], tw[:, 0, b:b + 1], None, ALU.mult)
                            r1 = p.V([t1, tk], "scalar_tensor_tensor", Bs[:, bl, 0:256], ai3[:, j, :], tw[:, 1, b:b + 1],
                                     tmp[0][:, j, :], ALU.mult, ALU.add)
                            r2 = p.V([t3, tk], "scalar_tensor_tensor", Bs[:, bl, 256:512], ar3[:, j, :], tw[:, 2, b:b + 1],
                                     tmp[1][:, j, :], ALU.mult, ALU.add)
                            last = [r1, r2]
                        p.act.wait(*last)
                        prevr[bp % 2] = last[1]
                        tw_tok = last
                    ts_all.append(p.st(Bd.ap()[:, bh * MB:(bh + 1) * MB, :], Bs[:], "fnb", deps=tw_tok))
                    p.barrier()
                if True:
                    ts_ = ts_all[-1]
                    yst = [None, None]
                    evp = [None, None]
                    bt_read = None
                    for cp in range(128 // CP):
                        tl = p.ld(Bt[:], Bd.ap()[cp * CP:(cp + 1) * CP, :, :].rearrange("c b f -> b c f"), "fnbt",
                                  deps=[ts_, bt_read])
                        for c2 in range(CP // 2):
                            pk = c2 % 2
                            ps3 = p.ps[pk][0:M, :].rearrange("p (c f) -> p c f", c=2)
                            p.mm([tl, evp[pk]], ps3, cm[0:M, 0, :], Bt[:, 2 * c2:2 * c2 + 2, 0:256], True, False)
                            tk = p.mm([], ps3, cm[0:M, 1, :], Bt[:, 2 * c2:2 * c2 + 2, 256:512], False, True, mark=True)
                            if c2 % 2 == 0:
                                ev = p.A([tk, yst[pk]], out=Y[pk][:], in_=ps3, func=AF.Copy)
                            else:
                                ev = p.V([tk, yst[pk]], "tensor_copy", Y[pk][:], ps3)
                            c_abs = cp * CP + 2 * c2
                            dst = ya.ap().rearrange("(d c) f -> d c f", c=128)[:, c_abs:c_abs + 2, g * 256:(g + 1) * 256]
                            yst[pk] = p.st(dst, Y[pk][:], f"fny{pk}", deps=[ev])
                            evp[pk] = ev
                            bt_read = tk
                    p.barrier()
        p.barrier()

    def hgrn(qrT, ffT, fbT, vtok, gates, L, ymix, lbt, hm, ghg_sb):
        SEG = min(L, 2048)
        NCH = SEG // 64
        nseg = L // SEG
        NT = L // 64
        dS = p.dram(f"hg_dS{p.ndram}", [2, NT, 128, 128], F32)
        Sb = p.dram(f"hg_Sb{p.ndram}", [2, NT, 128, 128], BF16)
        qd = p.dram(f"hg_qd{p.ndram}", [2, 128, L], BF16)
        scd = p.dram(f"hg_sc{p.ndram}", [64, NT, 64], BF16)
        eld = p.dram(f"hg_el{p.ndram}", [2, 128, NT], F32)
        p.ndram += 1
        for h in range(8):
            SEG1 = min(L, 1024)
            NCH1 = SEG1 // 64
            nseg1 = L // SEG1
            with contextlib.ExitStack() as es:
                qr = p.sb(es, "h_qr", [128, SEG1], F32)
                rmask = p.sb(es, "h_rm", [128, SEG1], F32)
                qdec = [p.sb(es, f"h_qd{i}", [128, SEG1], BF16) for i in range(2)]
                vt = p.sb(es, "h_v", [64, NCH1, 128], BF16)
                sct = p.sb(es, "h_sc", [64, NCH1, 64], BF16)
                sc1 = p.sb(es, "h_sc1", [64, NCH1, 64], F32)
                el = p.sb(es, "h_el", [128, 2, NCH1], F32)
                B = []
                for d in range(2):
                    bd = {}
                    for nm_ in ("fr", "f", "g", "k", "cum", "cb", "ex", "ex2", "ex3"):
                        bd[nm_] = p.sb(es, f"h_{nm_}{d}", [128, SEG1], F32)
                    bd["kdec"] = p.sb(es, f"h_kd{d}", [128, SEG1], BF16)
                    bd["kend"] = p.sb(es, f"h_ke{d}", [128, SEG1], BF16)
                    bd["kendT"] = p.sb(es, f"h_keT{d}", [64, NCH1, 128], BF16)
                    bd["dSs"] = [p.sb(es, f"h_dS{d}{j}", [128, 4, 128], F32) for j in range(2)]
                    B.append(bd)
                m1 = p.G([], "memset", rmask[:], 1.0)
                m2 = p.G([], "memset", rmask[:].rearrange("p (n s) -> p n s", s=64)[:, :, 0:1], 0.0)
                p.barrier()
                for sg in range(nseg1):
                    t0 = sg * SEG1
                    tq = p.ld(qr[:], qrT.ap()[h * 128:(h + 1) * 128, t0:t0 + SEG1], "hq")
                    tv = p.ld(vt[:], vtok.ap()[t0:t0 + SEG1, h * 128:(h + 1) * 128].rearrange("(n s) v -> s n v", s=64),
                              "hv")
                    aq = p.A([tq], out=qr[:], in_=qr[:], func=AF.Silu)
                    shared = {}

                    def dir_gen(d):
                        bd = B[d]
                        fr, f_, g_, k_, cum, cb = bd["fr"], bd["f"], bd["g"], bd["k"], bd["cum"], bd["cb"]
                        ex, ex2, ex3, kdec, kend, kendT, dSs = (bd["ex"], bd["ex2"], bd["ex3"], bd["kdec"], bd["kend"],
                                                                bd["kendT"], bd["dSs"])
                        src = ffT if d == 0 else fbT
                        tf = p.ld(fr[:], src.ap()[h * 128:(h + 1) * 128, t0:t0 + SEG1], f"hf{d}")
                        a1 = p.A([tf], out=f_[:], in_=fr[:], func=AF.Sigmoid)
                        yield
                        v1 = p.V([a1], "tensor_scalar", f_[:], f_[:], lbt[:, 2 + d, h:h + 1], lbt[:, d, h:h + 1],
                                 ALU.mult, ALU.add)
                        yield
                        a2 = p.A([v1], out=g_[:], in_=f_[:], func=AF.Ln)
                        g1 = p.G([v1], "tensor_scalar", k_[:], f_[:], -1.0, 1.0, ALU.mult, ALU.add)
                        yield
                        v2 = p.V([a2], "tensor_tensor_scan", cum[:], rmask[:], g_[:], 0.0, ALU.mult, ALU.add)
                        yield
                        cum3 = cum[:].rearrange("p (n s) -> p n s", s=64)
                        lastb = bass.AP(cum, 63, [[SEG1, 128], [64, NCH1], [0, 64]])
                        a6 = p.A([v2], out=el[:, d, :], in_=cum3[:, :, 63], func=AF.Exp)
                        if d == 0:
                            cc = cum
                            v3 = p.V([v2], "tensor_tensor", cb[:].rearrange("p (n s) -> p n s", s=64), lastb, cum3,
                                     ALU.subtract)
                            dl = cb
                            yield
                        else:
                            v3a = p.V([v2], "tensor_tensor", cb[:].rearrange("p (n s) -> p n s", s=64), lastb, cum3,
                                      ALU.subtract)
                            yield
                            v3b = p.V([v3a], "tensor_tensor", cb[:], cb[:], g_[:], ALU.add)
                            yield
                            cc = cb
                            v3 = p.V([v3b], "tensor_tensor", g_[:], cum[:], g_[:], ALU.subtract)
                            dl = g_
                            yield
                        a3 = p.A([v3], out=ex[:], in_=cc[:], func=AF.Exp)
                        yield
                        a4 = p.A([v3], out=ex2[:], in_=cc[:], func=AF.Exp, scale=-1.0)
                        g2 = p.V([a3, aq], "tensor_tensor", qdec[d][:], qr[:], ex[:], ALU.mult)
                        yield
                        a5 = p.A([v3], out=ex3[:], in_=dl[:], func=AF.Exp)
                        g3 = p.G([a4, g1], "tensor_tensor", kdec[:], k_[:], ex2[:], ALU.mult)
                        yield
                        g4 = p.V([a5, g1], "tensor_tensor", kend[:], k_[:], ex3[:], ALU.mult)
                        p.st(qd.ap()[d, :, t0:t0 + SEG1], qdec[d][:], f"hsq{d}", deps=[g2])
                        p.st(eld.ap()[d, :, sg * NCH1:(sg + 1) * NCH1], el[:, d, :], f"hse{d}", deps=[a6])
                        yield
                        nb = min(8, NCH1)
                        NGR = NCH1 // nb
                        mk_ap = bass.AP(hm, d * 64, [[128, 64], [0, nb], [1, 64]])
                        hz = {}
                        psS, pbT = p.ps[d], p.pb[d]
                        ev = None
                        for gq in range(NGR + 1):
                            if gq < NGR:
                                n0 = gq * nb
                                for j in range(nb):
                                    sl = slice((n0 + j) * 64, (n0 + j + 1) * 64)
                                    tk = p.mm([g2, g3, hz.get("sc")] if j == 0 else [], psS[0:64, j * 64:(j + 1) * 64],
                                              kdec[:, sl], qdec[d][:, sl], True, True, mark=(j == nb - 1))
                                psv = psS[0:64, 0:nb * 64].rearrange("p (n s) -> p n s", s=64)
                                if d == 0:
                                    ev = p.V([tk], "tensor_tensor", sc1[:, n0:n0 + nb, :], psv, mk_ap, ALU.mult)
                                    shared[("sc1", gq)] = ev
                                else:
                                    ev0 = p.V([tk], "tensor_tensor", sct[:, n0:n0 + nb, :], psv, mk_ap, ALU.mult)
                                    ev = p.V([ev0, shared[("sc1", gq)]], "tensor_tensor", sct[:, n0:n0 + nb, :],
                                             sct[:, n0:n0 + nb, :], sc1[:, n0:n0 + nb, :], ALU.add)
                                hz["sc"] = ev
                                yield
                                for j in range(nb):
                                    sl = slice((n0 + j) * 64, (n0 + j + 1) * 64)
                                    tk2 = p.tr([g4, hz.get("tr")] if j == 0 else [], pbT[0:64, j * 128:(j + 1) * 128],
                                               kend[:, sl], ident[:], mark=(j == nb - 1))
                                ev2 = p.A([tk2], out=kendT[:, n0:n0 + nb, :],
                                          in_=pbT[0:64, 0:nb * 128].rearrange("p (n k) -> p n k", k=128), func=AF.Copy)
                                hz["tr"] = ev2
                                hz[("kT", gq)] = ev2
                                yield
                            if gq >= 1:
                                gprev = gq - 1
                                n0 = gprev * nb
                                for j in range(nb):
                                    i4 = j // 4
                                    bank = p.ps[2 + 2 * d + i4]
                                    tk3 = p.mm([hz[("kT", gprev)], tv, hz.get(("dsb", i4))] if j % 4 == 0 else [],
                                               bank[:, (j % 4) * 128:(j % 4 + 1) * 128], kendT[:, n0 + j, :], vt[:, n0 + j, :],
                                               True, True, mark=(j % 4 == 3 or j == nb - 1))
                                    if j % 4 == 3 or j == nb - 1:
                                        w4 = j % 4 + 1
                                        dst = dSs[i4][:, 0:w4, :]
                                        srcp = bank[:, 0:w4 * 128].rearrange("p (n v) -> p n v", v=128)
                                        if i4 == 0:
                                            ev3 = p.A([tk3, hz.get(("dss", i4))], out=dst, in_=srcp, func=AF.Copy)
                                        else:
                                            ev3 = p.V([tk3, hz.get(("dss", i4))], "tensor_copy", dst, srcp)
                                        hz[("dsb", i4)] = ev3
                                        c0 = sg * NCH1 + n0 + i4 * 4
                                        hz[("dss", i4)] = p.st(
                                            dS.ap()[d, c0:c0 + w4, :, :].rearrange("n p v -> p n v"), dst, f"hds{d}{i4}",
                                            deps=[ev3])
                                yield
                        if d == 1:
                            p.st(scd.ap()[:, sg * NCH1:(sg + 1) * NCH1, :], sct[:], "hsc", deps=[ev])

                    gens = [dir_gen(0), dir_gen(1)]
                    while gens:
                        for gg in list(gens):
                            try:
                                next(gg)
                            except StopIteration:
                                gens.remove(gg)
                    p.barrier()
            with contextlib.ExitStack() as es:
                G = min(NT, 32)
                NG = NT // G
                dsl = [p.sb(es, f"h2_ds{i}", [128, G, 128], F32) for i in range(2)]
                sall = [p.sb(es, f"h2_sa{i}", [128, G + 1, 128], F32) for i in range(2)]
                sbo = [p.sb(es, f"h2_sb{i}", [128, G, 128], BF16) for i in range(2)]
                ela = p.sb(es, "h2_el", [128, 2, NT], F32)
                te = p.ld(ela[:], eld.ap().rearrange("d p n -> p d n"), "h2e")
                z0 = p.V([], "memset", sall[0][:, 0, :], 0.0)
                z1 = p.V([], "memset", sall[1][:, G, :], 0.0)
                last = [z0, z1]
                for gi in range(NG):
                    gf, gb = gi, NG - 1 - gi
                    tl = [p.ld(dsl[0][:], dS.ap()[0, gf * G:(gf + 1) * G, :, :].rearrange("n p v -> p n v"), "h2l0"),
                          p.ld(dsl[1][:], dS.ap()[1, gb * G:(gb + 1) * G, :, :].rearrange("n p v -> p n v"), "h2l1")]
                    for i in range(G):
                        nf = gf * G + i
                        last[0] = p.V([tl[0], te, last[0]], "scalar_tensor_tensor", sall[0][:, i + 1, :], sall[0][:, i, :],
                                      ela[:, 0, nf:nf + 1], dsl[0][:, i, :], ALU.mult, ALU.add)
                        j = G - 1 - i
                        nbk = gb * G + j
                        last[1] = p.V([tl[1], te, last[1]], "scalar_tensor_tensor", sall[1][:, j, :], sall[1][:, j + 1, :],
                                      ela[:, 1, nbk:nbk + 1], dsl[1][:, j, :], ALU.mult, ALU.add)
                    c0 = p.G(last, "tensor_copy", sbo[0][:], sall[0][:, 0:G, :])
                    c1 = p.A(last, out=sbo[1][:], in_=sall[1][:, 1:G + 1, :], func=AF.Copy)
                    p.st(Sb.ap()[0, gf * G:(gf + 1) * G, :, :].rearrange("n p v -> p n v"), sbo[0][:], "h2s0", deps=[c0])
                    p.st(Sb.ap()[1, gb * G:(gb + 1) * G, :, :].rearrange("n p v -> p n v"), sbo[1][:], "h2s1", deps=[c1])
                    last[0] = p.V([c0, c1] + last, "tensor_copy", sall[0][:, 0, :], sall[0][:, G, :])
                    last[1] = p.V([last[0]], "tensor_copy", sall[1][:, G, :], sall[1][:, 0, :])
                    p.barrier()
            with contextlib.ExitStack() as es:
                G = min(NT, 16)
                NG3 = NT // G
                mk2 = lambda nm_, shp, dt: [p.sb(es, f"{nm_}{i}", shp, dt) for i in range(2)]
                qd3 = mk2("h3_qd", [128, 2, G * 64], BF16)
                sc3 = mk2("h3_sc", [64, G, 64], BF16)
                v3_ = mk2("h3_v", [64, G, 128], BF16)
                gt3 = [p.sb(es, f"h3_g{i}", [64, G, 128], F32) for i in range(3)]
                s3 = mk2("h3_s", [128, 2, G, 128], BF16)
                o3 = mk2("h3_o", [64, G, 128], F32)
                ob3 = mk2("h3_ob", [64, G, 128], BF16)
                ss = mk2("h3_ss", [64, G], F32)
                sq3 = p.sb(es, "h3_sq", [64, G, 128], F32)
                T3 = {}
                g3_ = lambda k, t: T3.get((k, t))
                nb3 = min(4, G)
                prevb = [None, None]

                def L3(gi):
                    bq = gi % 2
                    t0 = gi * G * 64
                    rdC = [g3_("pe", gi - 2)]
                    T3[("ld", gi)] = [
                        p.ld(qd3[bq][:], qd.ap()[:, :, t0:t0 + G * 64].rearrange("d p t -> p d t"), f"h3a{bq}", deps=rdC),
                        p.ld(sc3[bq][:], scd.ap()[:, gi * G:(gi + 1) * G, :], f"h3a{bq}"),
                        p.ld(v3_[bq][:], vtok.ap()[t0:t0 + G * 64, h * 128:(h + 1) * 128].rearrange("(n s) v -> s n v", s=64), f"h3a{bq}"),
                        p.ld(s3[bq][:, 0], Sb.ap()[0, gi * G:(gi + 1) * G, :, :].rearrange("n p v -> p n v"), f"h3a{bq}"),
                        p.ld(s3[bq][:, 1], Sb.ap()[1, gi * G:(gi + 1) * G, :, :].rearrange("n p v -> p n v"), f"h3a{bq}")]
                    T3[("ldg", gi)] = p.ld(
                        gt3[gi % 3][:], gates.ap()[t0:t0 + G * 64, 1024 + h * 128:1024 + (h + 1) * 128].rearrange("(n s) v -> s n v", s=64),
                        f"h3g{gi % 3}", deps=[g3_("r5", gi - 3)])

                def C3(gi):
                    bq = gi % 2
                    T3[("ag", gi)] = p.A([T3[("ldg", gi)]], out=gt3[gi % 3][:], in_=gt3[gi % 3][:], func=AF.Silu)
                    tk = None
                    ev = None
                    for j0 in range(0, G, nb3):
                        pk = (j0 // nb3) % 2
                        for jj in range(nb3):
                            j = j0 + jj
                            ps = p.ps[pk][0:64, jj * 128:(jj + 1) * 128]
                            sl = slice(j * 64, (j + 1) * 64)
                            p.mm(T3[("ld", gi)] + [prevb[pk]] if jj == 0 else [], ps, sc3[bq][:, j, :], v3_[bq][:, j, :], True, False)
                            p.mm([], ps, qd3[bq][:, 0, sl], s3[bq][:, 0, j, :], False, False)
                            tk = p.mm([], ps, qd3[bq][:, 1, sl], s3[bq][:, 1, j, :], False, True, mark=(jj == nb3 - 1))
                        ev = p.A([tk, g3_("r5", gi - 2)], out=o3[bq][:, j0:j0 + nb3, :],
                                 in_=p.ps[pk][0:64, 0:nb3 * 128].rearrange("p (n v) -> p n v", v=128), func=AF.Copy)
                        prevb[pk] = ev
                    T3[("pe", gi)] = tk
                    T3[("ev", gi)] = ev

                def E3(gi):
                    bq = gi % 2
                    t0 = gi * G * 64
                    e2 = p.A([T3[("ev", gi)], g3_("e3", gi - 1)], out=sq3[:], in_=o3[bq][:], func=AF.Square)
                    e3 = p.V([e2], "tensor_reduce", ss[bq][:], sq3[:], mybir.AxisListType.X, ALU.add)
                    T3[("e3", gi)] = e3
                    r2 = p.rsqrt([e3], ss[bq][:], ss[bq][:], 1.0 / 128, EPS)
                    ssb = bass.AP(ss[bq], 0, [[G, 64], [1, G], [0, 128]])
                    r3 = p.V([r2], "tensor_tensor", o3[bq][:], o3[bq][:], ssb, ALU.mult)
                    ghb = bass.AP(ghg_sb, 0, [[128, 64], [0, G], [1, 128]])
                    r4 = p.V([r3], "tensor_tensor", o3[bq][:], o3[bq][:], ghb, ALU.mult)
                    r5 = p.V([r4, T3[("ag", gi)], g3_("st", gi - 2)], "tensor_tensor", ob3[bq][:], o3[bq][:], gt3[gi % 3][:], ALU.mult)
                    T3[("r5", gi)] = r5
                    T3[("st", gi)] = p.st(
                        ymix.ap()[t0:t0 + G * 64, 1024 + h * 128:1024 + (h + 1) * 128].rearrange("(n s) v -> s n v", s=64),
                        ob3[bq][:], f"h3s{bq}", deps=[r5])

                L3(0)
                for gi in range(NG3 + 1):
                    if gi < NG3:
                        if gi + 1 < NG3:
                            L3(gi + 1)
                        C3(gi)
                    if gi >= 1:
                        E3(gi - 1)
                p.barrier()
        p.barrier()

    def outproj(L, ymix, ya, gates, wbo, gpost, xres, xout, hT_next):
        NTL = L // 128
        with contextlib.ExitStack() as es:
            wsb = p.sb(es, "op_w", [128, KC, D], BF16)
            tw = p.ld(wsb[:], wbo.ap().rearrange("(k p) n -> p k n", p=128), "opw")
            ym = [p.sb(es, f"op_ym{i}", [128, D], BF16) for i in range(2)]
            yaf = [p.sb(es, f"op_ya{i}", [128, 1024], F32) for i in range(2)] if ya is not None else None
            gaf = [p.sb(es, f"op_ga{i}", [128, 1024], F32) for i in range(2)] if ya is not None else None
            ymT = [p.sb(es, f"op_ymT{i}", [128, KC, 128], BF16) for i in range(2)]
            xr = [p.sb(es, f"op_x{i}", [128, D], F32) for i in range(2)]
            yo = [p.sb(es, f"op_y{i}", [128, D], F32) for i in range(2)]
            junk = p.sb(es, "op_j", [128, D], BF16)
            st_ = [p.sb(es, f"op_st{i}", [128, 4], F32) for i in range(2)]
            hb = [p.sb(es, f"op_hb{i}", [128, D], BF16) for i in range(2)]
            ho = [p.sb(es, f"op_ho{i}", [128, KC, 128], BF16) for i in range(2)]
            T = {}
            g = lambda k, t: T.get((k, t))
            pbv = [p.pb[i][:].rearrange("p (k t) -> p k t", k=8) for i in range(2)]
            def A1a(ti):
                b = ti % 2
                rows = slice(ti * 128, (ti + 1) * 128)
                T[("tx", ti)] = p.ld(xr[b][:], xres[rows, :], f"opx{b}", deps=[g("v5", ti - 2)])
                if ya is not None:
                    t1 = p.ld(ym[b][:, 1024:2048], ymix.ap()[rows, 1024:2048], f"opm{b}", deps=[g("tr", ti - 2)])
                    t2 = p.ld(yaf[b][:], ya.ap()[rows, :], f"opa{b}", deps=[g("v1", ti - 2)])
                    t3 = p.ld(gaf[b][:], gates.ap()[rows, 0:1024], f"opa{b}", deps=[g("v1", ti - 2)])
                    a1 = p.A([t3], out=gaf[b][:], in_=gaf[b][:], func=AF.Silu)
                    v1 = p.V([a1, t2, g("tr", ti - 2)], "tensor_tensor", ym[b][:, 0:1024], yaf[b][:], gaf[b][:], ALU.mult)
                    T[("v1", ti)] = v1
                    rdy = [t1, v1]
                else:
                    rdy = [p.ld(ym[b][:], ymix.ap()[rows, :], f"opm{b}", deps=[g("tr", ti - 2)])]
                for kc in range(KC):
                    tk = p.tr(rdy + [g("c1h", ti - 2), g("c2h", ti - 2)] if kc == 0 else [],
                              p.pb[kc // 8][:, (kc % 8) * 128:(kc % 8 + 1) * 128],
                              ym[b][:, kc * 128:(kc + 1) * 128], ident[:], mark=(kc == KC - 1))
                T[("tr", ti)] = tk

            def A1b(ti):
                b = ti % 2
                tk = T[("tr", ti)]
                c1 = p.A([tk, g("mm", ti - 2)], out=ymT[b][:, 0:8, :], in_=pbv[0], func=AF.Copy)
                c2 = p.V([tk, g("mm", ti - 2)], "tensor_copy", ymT[b][:, 8:16, :], pbv[1])
                T[("c1", ti)], T[("c2", ti)] = c1, c2
                for cb in range(4):
                    for kc in range(KC):
                        tk = p.mm([c1, c2, tw] + [T.get(("ev", ti - 1, i)) for i in range(4)] if (kc == 0 and cb == 0) else [],
                                  p.ps[cb][:], ymT[b][:, kc, :],
                                  wsb[:, kc, cb * 512:(cb + 1) * 512], kc == 0, kc == KC - 1, mark=(kc == KC - 1))
                T[("mm", ti)] = tk

            def A2a(ti):
                b = ti % 2
                tk = T[("mm", ti)]
                for cb in range(4):
                    dep = [tk, g("so", ti - 2), g("v8", ti - 2)]
                    if cb % 2 == 0:
                        ev = p.A(dep, out=yo[b][:, cb * 512:(cb + 1) * 512], in_=p.ps[cb][:], func=AF.Copy)
                    else:
                        ev = p.V(dep, "tensor_copy", yo[b][:, cb * 512:(cb + 1) * 512], p.ps[cb][:])
                    T[("ev", ti, cb)] = ev

            def A2b(ti):
                b = ti % 2
                rows = slice(ti * 128, (ti + 1) * 128)
                evs = [T[("ev", ti, cb)] for cb in range(4)]
                a2 = p.A(evs, out=junk[:], in_=yo[b][:], func=AF.Square, accum_out=st_[b][:, 0:1])
                v3 = p.rsqrt([a2], st_[b][:, 1:2], st_[b][:, 0:1], 1.0 / D, EPS)
                v4 = p.V([v3] + evs, "scalar_tensor_tensor", yo[b][:], yo[b][:], st_[b][:, 1:2], gpost[:], ALU.mult, ALU.mult)
                v5 = p.V([v4, T[("tx", ti)]], "tensor_tensor", yo[b][:], yo[b][:], xr[b][:], ALU.add)
                T[("v5", ti)] = v5
                T[("so", ti)] = p.st(xout[rows, :], yo[b][:], f"opo{b}", deps=[v5])
                if hT_next is not None:
                    a3 = p.A([v5], out=junk[:], in_=yo[b][:], func=AF.Square, accum_out=st_[b][:, 2:3])
                    v7 = p.rsqrt([a3], st_[b][:, 3:4], st_[b][:, 2:3], 1.0 / D, EPS)
                    T[("v8", ti)] = p.V([v7, g("trh", ti - 2)], "tensor_scalar", hb[b][:], yo[b][:], st_[b][:, 3:4], None, ALU.mult)

            def Bst(tj):
                bj = tj % 2
                rows_j = slice(tj * 128, (tj + 1) * 128)
                for kc in range(KC):
                    tk2 = p.tr([g("v8", tj), g("c1", tj + 1), g("c2", tj + 1)] if kc == 0 else [],
                               p.pb[kc // 8][:, (kc % 8) * 128:(kc % 8 + 1) * 128],
                               hb[bj][:, kc * 128:(kc + 1) * 128], ident[:], mark=(kc == KC - 1))
                T[("trh", tj)] = tk2
                c1h = p.A([tk2, g("sh", tj - 2)], out=ho[bj][:, 0:8, :], in_=pbv[0], func=AF.Copy)
                c2h = p.V([tk2, g("sh", tj - 2)], "tensor_copy", ho[bj][:, 8:16, :], pbv[1])
                T[("c1h", tj)], T[("c2h", tj)] = c1h, c2h
                T[("sh", tj)] = p.st(hT_next.ap().rearrange("(k p) t -> p k t", p=128)[:, :, rows_j], ho[bj][:],
                                     f"oph{bj}", deps=[c1h, c2h])

            for ti in range(NTL + 1):
                if ti < NTL:
                    A1a(ti)
                if ti >= 1:
                    A2a(ti - 1)
                if ti < NTL:
                    A1b(ti)
                if ti >= 1:
                    A2b(ti - 1)
                    if hT_next is not None:
                        Bst(ti - 1)
        p.barrier()

    def attention(qT, kT, vtok, gates, Lq, Lk, og, dtab, lam_sb, gsub_sb, delta):
        NKT, NQB = Lk // 128, Lq // 256
        SK = 2
        with contextlib.ExitStack() as es:
            qs = p.sb(es, "at_q", [128, 2, Lq], BF16)
            ks = p.sb(es, "at_k", [128, 2, Lk], BF16)
            vs = p.sb(es, "at_v", [128, NKT, 257], BF16)
            absd = [p.sb(es, f"at_ad{i}", [128, 256], F32) for i in range(3)]
            sb_ = [p.sb(es, f"at_s{i}", [128, 256], F32) for i in range(4)]
            pt = [p.sb(es, f"at_p{i}", [128, 256], BF16) for i in range(4)]
            gt = [p.sb(es, f"at_g{i}", [128, 2, 256], F32) for i in range(2)]
            o0 = [p.sb(es, f"at_o0{i}", [128, 2, 256], F32) for i in range(2)]
            o1 = [p.sb(es, f"at_o1{i}", [128, 2, 256], F32) for i in range(2)]
            rs = [p.sb(es, f"at_rs{i}", [128, 8], F32) for i in range(2)]
            ob = [p.sb(es, f"at_ob{i}", [128, 2, 256], BF16) for i in range(2)]
            jk = p.sb(es, "at_j", [128, 256], F32)
            Sps = [p.ps[i][:, 0:256] for i in range(2)]
            acc = [[p.ps[2 + 2 * c + j] for j in range(2)] for c in range(2)]
            qbi = 0
            gt_free = [[], []]
            ob_free = [None, None]
            for h in range(8):
                slope = SLOPES[h]
                t_in = [p.ld(qs[:], qT.ap()[h * 256:(h + 1) * 256, :].rearrange("(c p) t -> p c t", p=128), "atq"),
                        p.ld(ks[:], kT.ap()[h * 256:(h + 1) * 256, :].rearrange("(c p) t -> p c t", p=128), "atq"),
                        p.ld(vs[:, :, 0:256], vtok.ap()[:, h * 256:(h + 1) * 256].rearrange("(n p) v -> p n v", p=128),
                             "atq")]
                t_in.append(p.G([], "memset", vs[:, :, 256:257], 1.0))
                acc_free = []
                for qb in range(NQB):
                    par = qbi % 2
                    qbi += 1
                    q0 = qb * 256
                    tg = p.ld(gt[par][:],
                              gates.ap()[q0:q0 + 256, h * 256:(h + 1) * 256].rearrange("(j p) v -> p j v", p=128),
                              f"atg{par}", deps=gt_free[par])
                    units = [(kt, c) for kt in range(NKT) for c in range(2)]
                    NU = len(units)
                    rd_ad = [None] * 3
                    rd_s = [None] * 4
                    rd_p = [None] * 4
                    rd_ps = [None] * 2
                    absT = {}
                    expT = {}
                    state = {"lastpv": None}

                    def do_abs(kt):
                        ab = kt % 3
                        idx = kt * NQB + qb
                        absT[kt] = p.A([rd_ad[ab]], out=absd[ab][:], in_=delta[:], func=AF.Abs, bias=dtab[:, idx:idx + 1])

                    def front(u):
                        kt, c = units[u]
                        b = u % 4
                        if c == 0:
                            if kt == 0:
                                do_abs(0)
                            if kt + 1 < NKT:
                                do_abs(kt + 1)
                        b2 = u % 2
                        tk = p.mm(t_in + [rd_ps[b2]], Sps[b2], ks[:, c, kt * 128:(kt + 1) * 128], qs[:, c, q0:q0 + 256],
                                  True, True, mark=True)
                        v1 = p.V([tk, absT[kt], rd_s[b]], "scalar_tensor_tensor", sb_[b][:], absd[kt % 3][:], -slope,
                                 Sps[b2], ALU.mult, ALU.add)
                        rd_ps[b2] = v1
                        rd_ad[kt % 3] = v1
                        a1 = p.A([v1, rd_p[b]], out=pt[b][:], in_=sb_[b][:], func=AF.Exp)
                        rd_s[b] = a1
                        expT[u] = a1

                    def back(u):
                        kt, c = units[u]
                        b = u % 4
                        for j in range(2):
                            deps = [expT[u]] if j == 0 else []
                            if kt == 0:
                                deps = deps + acc_free
                            state["lastpv"] = p.mm(deps, acc[c][j][:, 0:257], pt[b][:, j * 128:(j + 1) * 128],
                                                   vs[:, kt, :], kt == 0, kt == NKT - 1, mark=(j == 1))
                        rd_p[b] = state["lastpv"]

                    for u in range(NU + SK):
                        if u < NU:
                            front(u)
                        if u >= SK:
                            back(u - SK)
                    lastpv = state["lastpv"]
                    evs = []
                    for c in range(2):
                        for j in range(2):
                            dst = (o0[par] if c == 0 else o1[par])
                            e1 = p.V([lastpv], "reciprocal", rs[par][:, 2 * c + j:2 * c + j + 1], acc[c][j][:, 256:257])
                            if c == 0:
                                evs.append(p.V([e1], "tensor_scalar", dst[:, j, :], acc[c][j][:, 0:256],
                                               rs[par][:, 2 * c + j:2 * c + j + 1], None, ALU.mult))
                            else:
                                evs.append(p.V([e1], "tensor_scalar", dst[:, j, :], acc[c][j][:, 0:256],
                                               rs[par][:, 2 * c + j:2 * c + j + 1], lam_sb[:, 1:2], ALU.mult, ALU.mult))
                    acc_free = [evs[-1]]
                    f1 = p.V(evs, "tensor_tensor", o0[par][:], o0[par][:], o1[par][:], ALU.add)
                    for j in range(2):
                        sqt = p.A([f1], out=jk[:], in_=o0[par][:, j, :], func=AF.Square, accum_out=rs[par][:, 4 + j:5 + j])
                    f2 = p.rsqrt([sqt], rs[par][:, 4:6], rs[par][:, 4:6], 1.0 / 256, EPS)
                    f3 = p.V([f2], "tensor_scalar", rs[par][:, 4:6], rs[par][:, 4:6], (1.0 - LAMBDA_INIT), None, ALU.mult)
                    ag = p.A([tg], out=gt[par][:], in_=gt[par][:], func=AF.Silu)
                    for j in range(2):
                        f4 = p.V([f3], "scalar_tensor_tensor", o0[par][:, j, :], o0[par][:, j, :], rs[par][:, 4 + j:5 + j],
                                 gsub_sb[:], ALU.mult, ALU.mult)
                    f5 = p.V([f4, ag, ob_free[par]], "tensor_tensor", ob[par][:], o0[par][:], gt[par][:], ALU.mult)
                    ob_free[par] = p.st(og.ap()[q0:q0 + 256, h * 256:(h + 1) * 256].rearrange("(j p) v -> p j v", p=128),
                                        ob[par][:], f"ato{par}", deps=[f5])
                    gt_free[par] = [f5, ag]
                p.barrier()
                gt_free = [[], []]
                ob_free = [None, None]
        p.barrier()

    c128 = p.sb(ges, "c128_sb", [128, 3, 128], BF16)
    p.ld(c128[:], c128_d.ap().rearrange("k p c -> p k c"), "c0")
    cs256 = p.sb(ges, "cs256_sb", [128, 2, 512], BF16)
    p.ld(cs256[:], cs256_d.ap().rearrange("(c p) n -> p c n", p=128), "c0")
    tw_sb, cm_sb = {}, {}
    for L in tw_d:
        M = L // 128
        tw_sb[L] = p.sb(ges, f"tw{L}_sb", [128, 3, M], F32)
        p.ld(tw_sb[L][:], tw_d[L].ap().rearrange("k p m -> p k m"), "c0")
        cm_sb[L] = p.sb(ges, f"cm{L}_sb", [M, 2, M], BF16)
        p.ld(cm_sb[L][:], cm_d[L].ap().rearrange("k b d -> b k d"), "c0")
    hm = p.sb(ges, "hm_sb", [64, 2, 64], F32)
    p.ld(hm[:], hmask_d.ap().rearrange("k s t -> s k t"), "c0")
    delta = p.sb(ges, "delta_sb", [128, 256], F32)
    p.ld(delta[:], delta_d.ap(), "c0")
    dtabs = p.sb(ges, "dtabs_sb", [128, (LS // 128) * (LS // 256)], F32)
    p.ld(dtabs[:], dtabs_d.ap(), "c0")
    dtabp = p.sb(ges, "dtabp_sb", [128, (LP // 128) * (OWN // 256)], F32)
    p.ld(dtabp[:], dtabp_d.ap(), "c0")
    ownidx = p.sb(ges, "ownidx_sb", [128, OWN // 128], I32)
    p.ld(ownidx[:], ownidx_d.ap(), "c0")
    ghg_sb = p.sb(ges, "ghg_sb", [64, 128], F32)
    p.ld(ghg_sb[:], bcast_rows(ghg.ap(), 128, 64), "c0")
    gsub_sb = p.sb(ges, "gsub_sb", [128, 256], F32)
    p.ld(gsub_sb[:], bcast_rows(gsub.ap(), 256), "c0")
    lraw = p.sb(ges, "lraw_sb", [128, 2, 3, 8], F32)
    p.ld(lraw[:], lbl.ap().rearrange("d s (h k) -> k d s h", k=128), "c0", slow=True)
    lbt = p.sb(ges, "lbt_sb", [128, 4, 8], F32)
    lsum = p.sb(ges, "lsum_sb", [128, 2, 8], F32)
    lamv = p.sb(ges, "lamv_sb", [128, 4, 128], F32)
    p.ld(lamv[:], bass.AP(lam4.ap().tensor, 0, [[0, 128], [128, 4], [1, 128]]), "c0")
    lam_sb = p.sb(ges, "lam_sb", [128, 4], F32)
    ljunk = p.sb(ges, "ljunk_sb", [128, 128], F32)
    p.barrier()
    a = p.A([], out=lraw[:], in_=lraw[:], func=AF.Exp)
    v = p.V([a], "tensor_tensor", lsum[:], lraw[:, :, 0, :], lraw[:, :, 1, :], ALU.add)
    v = p.V([v], "tensor_tensor", lsum[:], lsum[:], lraw[:, :, 2, :], ALU.add)
    v = p.V([v], "reciprocal", lsum[:], lsum[:])
    v = p.V([v], "tensor_tensor", lbt[:, 0:2, :], lraw[:, :, 0, :], lsum[:], ALU.mult)
    v = p.V([v], "tensor_scalar", lbt[:, 2:4, :], lbt[:, 0:2, :], -1.0, 1.0, ALU.mult, ALU.add)
    v = p.V([v], "tensor_tensor", ljunk[:], lamv[:, 0, :], lamv[:, 1, :], ALU.mult)
    v = p.V([v], "tensor_reduce", lam_sb[:, 2:3], ljunk[:], mybir.AxisListType.X, ALU.add)
    v = p.V([v], "tensor_tensor", ljunk[:], lamv[:, 2, :], lamv[:, 3, :], ALU.mult)
    v = p.V([v], "tensor_reduce", lam_sb[:, 3:4], ljunk[:], mybir.AxisListType.X, ALU.add)
    a = p.A([v], out=lam_sb[:, 2:4], in_=lam_sb[:, 2:4], func=AF.Exp)
    v = p.V([a], "tensor_tensor", lam_sb[:, 0:1], lam_sb[:, 2:3], lam_sb[:, 3:4], ALU.subtract)
    v = p.V([v], "tensor_scalar", lam_sb[:, 1:2], lam_sb[:, 0:1], LAMBDA_INIT, -1.0, ALU.add, ALU.mult)
    p.barrier()

    seqs = [("s%d" % i, LS, xs.ap()[i * LS:(i + 1) * LS, :], ys.ap()[i * LS:(i + 1) * LS, :]) for i in range(NS)]
    seqs.append(("p", LP, xp.ap(), None))
    scale_q = 128 ** -0.5
    for (nm, L, xin, yout) in seqs:
        isP = yout is None
        hT = p.dram(f"hT_{nm}", [D, L], BF16)
        norm_T(xin, L, hT)
        done("norm")
        uT = p.dram(f"uT_{nm}", [1024, L], BF16)
        gat0 = p.dram(f"g0_{nm}", [L, 2048], F32)
        qrT = p.dram(f"qr_{nm}", [1024, L], F32)
        ffT = p.dram(f"ff_{nm}", [1024, L], F32)
        fbT = p.dram(f"fb_{nm}", [1024, L], F32)
        vtk = p.dram(f"vt_{nm}", [L, 1024], BF16)
        proj(hT, L, wb0i, 7168, [
            (0, 1024, "F", uT, 0, 1.0, BF16),
            (1024, 1024, "T", gat0, 0, 1.0, F32),
            (2048, 1024, "F", qrT, 0, 1.0, F32),
            (3072, 1024, "T", vtk, 0, 1.0, BF16),
            (4096, 1024, "F", ffT, 0, 1.0, F32),
            (5120, 1024, "F", fbT, 0, 1.0, F32),
            (6144, 1024, "T", gat0, 1024, 1.0, F32),
        ])
        done("proj0")
        ya = p.dram(f"ya_{nm}", [L, 1024], F32)
        fnet(uT, L, ya, (c128, cs256, tw_sb[L], cm_sb[L]))
        done("fnet")
        ymix = p.dram(f"ym_{nm}", [L, 2048], BF16)
        hgrn(qrT, ffT, fbT, vtk, gat0, L, ymix, lbt, hm, ghg_sb)
        done("hgrn")
        x1 = p.dram(f"x1_{nm}", [L, D], F32)
        h1T = p.dram(f"h1T_{nm}", [D, L], BF16)
        outproj(L, ymix, ya, gat0, wb0o, gpost_sb[0], xin, x1.ap(), h1T)
        done("out0")
        kT = p.dram(f"kT_{nm}", [2048, L], BF16)
        v1t = p.dram(f"v1_{nm}", [L, 2048], BF16)
        if not isP:
            Lq = L
            qT = p.dram(f"qT_{nm}", [2048, Lq], BF16)
            gat1 = p.dram(f"g1_{nm}", [Lq, 2048], F32)
            proj(h1T, L, wb1i, 8192, [
                (0, 2048, "F", qT, 0, scale_q, BF16),
                (2048, 2048, "F", kT, 0, 1.0, BF16),
                (4096, 2048, "T", v1t, 0, 1.0, BF16),
                (6144, 2048, "T", gat1, 0, 1.0, F32),
            ])
            xres1 = x1.ap()
            dtab = dtabs
        else:
            Lq = OWN
            proj(h1T, L, wb1i, 8192, [
                (2048, 2048, "F", kT, 0, 1.0, BF16),
                (4096, 2048, "T", v1t, 0, 1.0, BF16),
            ])
            x1own = p.dram("x1own", [OWN, D], F32)
            with contextlib.ExitStack() as es:
                gx = p.sb(es, "gx", [128, D], F32)
                prev = None
                for ti in range(OWN // 128):
                    p.pool.wait(prev)
                    ds = p.dsem("gath")
                    p.nc.gpsimd.indirect_dma_start(
                        out=gx[:], out_offset=None, in_=x1.ap(),
                        in_offset=bass.IndirectOffsetOnAxis(ap=ownidx[:, ti:ti + 1], axis=0),
                    ).then_inc(ds.sem, 16)
                    ds.n += 16
                    tok = (ds, ds.n)
                    p.pending.append(tok)
                    prev = p.st(x1own.ap()[ti * 128:(ti + 1) * 128, :], gx[:], "gaths", deps=[tok])
            p.barrier()
            h1To = p.dram("h1To", [D, OWN], BF16)
            norm_T(x1own.ap(), OWN, h1To)
            qT = p.dram(f"qT_{nm}", [2048, Lq], BF16)
            gat1 = p.dram(f"g1_{nm}", [Lq, 2048], F32)
            proj(h1To, OWN, wb1i, 8192, [
                (0, 2048, "F", qT, 0, scale_q, BF16),
                (6144, 2048, "T", gat1, 0, 1.0, F32),
            ])
            xres1 = x1own.ap()
            yout = yp.ap()
            dtab = dtabp
        done("proj1")
        og = p.dram(f"og_{nm}", [Lq, 2048], BF16)
        attention(qT, kT, v1t, gat1, Lq, L, og, dtab, lam_sb, gsub_sb, delta)
        done("attn")
        outproj(Lq, og, None, None, wb1o, gpost_sb[1], xres1, yout, None)
        done("seq0")


def dft_cs(n):
    j = np.arange(n)
    ang = 2 * np.pi * np.outer(j, j) / n
    return np.cos(ang), np.sin(ang)


def host_tables(cfg, core):
    NS, LS, LP, OWN = cfg["NS"], cfg["LS"], cfg["LP"], cfg["OWN"]
    t = {}
    t["ident"] = np.eye(128, dtype=np.float32).astype(NPBF)
    c, s = dft_cs(128)
    t["c128"] = np.stack([c, s, -s]).astype(np.float32).astype(NPBF)
    c, s = dft_cs(256)
    t["cs256"] = np.concatenate([c, -s], axis=1).astype(np.float32).astype(NPBF)
    for L in sorted({LS, LP}):
        M = L // 128
        ang = 2 * np.pi * np.outer(np.arange(128), np.arange(M)) / L
        t[f"tw{L}"] = np.stack([np.cos(ang), np.sin(ang), -np.sin(ang)]).astype(np.float32)
        cm, sm = dft_cs(M)
        sc = 1.0 / math.sqrt(L * 256)
        t[f"cm{L}"] = np.stack([cm * sc, sm * sc]).astype(np.float32).astype(NPBF)
    s_, t_ = np.meshgrid(np.arange(64), np.arange(64), indexing="ij")
    t["hmask"] = np.stack([(s_ <= t_), (s_ >= t_)]).astype(np.float32)
    t["delta"] = (np.arange(128)[:, None] - np.arange(256)[None, :]).astype(np.float32)
    kt, qb = np.meshgrid(np.arange(LS // 128), np.arange(LS // 256), indexing="ij")
    t["dtabs"] = np.broadcast_to((kt * 128 - qb * 256).reshape(1, -1), (128, kt.size)).astype(np.float32).copy()
    kt, qb = np.meshgrid(np.arange(LP // 128), np.arange(OWN // 256), indexing="ij")
    t["dtabp"] = np.broadcast_to((kt * 128 - (core * OWN + qb * 256)).reshape(1, -1), (128, kt.size)).astype(np.float32).copy()
    t["ownidx"] = (core * OWN + np.arange(OWN)).reshape(OWN // 128, 128).T.astype(np.int32).copy()
    return t


def run(inputs, cfg, ncores):
    NS, LS, LP, OWN = cfg["NS"], cfg["LS"], cfg["LP"], cfg["OWN"]
    f = lambda a: np.ascontiguousarray(np.asarray(a, dtype=np.float32))
    xsamp = f(inputs["x_sample"])
    shared = {
        "xp": f(inputs["x_prompt"])[0],
        "w0i": f(inputs["ev_w_in"])[0], "w0o": f(inputs["ev_w_out"])[0],
        "w1i": f(inputs["od_w_in"])[0], "w1o": f(inputs["od_w_out"])[0],
        "g0pre": f(inputs["ev_norm_pre"])[0], "g0post": f(inputs["ev_norm_post"])[0],
        "g1pre": f(inputs["od_norm_pre"])[0], "g1post": f(inputs["od_norm_post"])[0],
        "lbl": f(inputs["hgrn_lb_logits"]), "ghg": f(inputs["hgrn_norm"])[0],
        "lam4": np.stack([f(inputs["lambda_q1"])[0], f(inputs["lambda_k1"])[0],
                          f(inputs["lambda_q2"])[0], f(inputs["lambda_k2"])[0]]),
        "gsub": f(inputs["subln"])[0],
    }
    nc = build(cfg)
    in_maps = []
    for c in range(ncores):
        m = dict(shared)
        m["xs"] = xsamp[c * NS:(c + 1) * NS].reshape(NS * LS, D)
        m.update(host_tables(cfg, c))
        in_maps.append(m)
    res = run_bass_kernel_spmd(nc, in_maps, core_ids=list(range(ncores)))
    global LAST_RES
    LAST_RES = res.results
    y_s = np.concatenate([r["ys"].reshape(NS, LS, D) for r in res.results], axis=0)
    y_p = np.concatenate([r["yp"] for r in res.results], axis=0)[None]
    return y_p.astype(np.float32), y_s.astype(np.float32)


def kernel(**inputs):
    import os
    cfg = {"NS": 2, "LS": 2048, "LP": 8192, "OWN": 1024}
    if os.environ.get("K_STOP"):
        cfg["stop"] = os.environ["K_STOP"]
    return run(inputs, cfg, 8)
```

```python
import contextlib, math
import numpy as np
import ml_dtypes
import concourse.bass as bass
import concourse.mybir as mybir
from concourse.bass_utils import run_bass_kernel_spmd

F32, BF16, I32 = mybir.dt.float32, mybir.dt.bfloat16, mybir.dt.int32
AF = mybir.ActivationFunctionType
ALU = mybir.AluOpType
D = 2048
KC = 16
EPS = 1e-6
LAMBDA_INIT = 0.8 - 0.6 * math.exp(-0.3 * 1)
SLOPES = [2.0 ** (-8.0 * (h + 1) / 8) for h in range(8)]
NPBF = ml_dtypes.bfloat16


class StopBuild(Exception):
    pass


class Eng:
    def __init__(self, e, sem):
        self.e, self.sem, self.n, self.seen = e, sem, 0, {}

    def mark(self, ins):
        ins.then_inc(self.sem, 1)
        self.n += 1
        return (self, self.n)

    def wait(self, *toks):
        for tok in toks:
            if tok is None:
                continue
            src, n = tok
            if self.seen.get(src, 0) >= n:
                continue
            self.e.wait_ge(src.sem, n)
            self.seen[src] = n


class DSem:
    def __init__(self, sem):
        self.sem, self.n = sem, 0


class P:
    def __init__(self, cfg):
        self.cfg = cfg
        self.es = contextlib.ExitStack()
        nc = self.nc = bass.Bass("TRN2", target_bir_lowering=False)
        mk = lambda nm: self.es.enter_context(nc.semaphore(nm))
        self.pe = Eng(nc.tensor, mk("s_pe"))
        self.act = Eng(nc.scalar, mk("s_act"))
        self.dve = Eng(nc.vector, mk("s_dve"))
        self.pool = Eng(nc.gpsimd, mk("s_pool"))
        self.sp = Eng(nc.sync, mk("s_sp"))
        self.engs = [self.pe, self.act, self.dve, self.pool, self.sp]
        self.dsems = {}
        self.pending = []
        self.ndram = 0
        self.ps = [self.es.enter_context(nc.psum_tensor(f"ps{i}", [128, 512], F32)) for i in range(6)]
        self.pb = [self.es.enter_context(nc.psum_tensor(f"pb{i}", [128, 1024], BF16)) for i in range(2)]
        self.dummy = self.es.enter_context(nc.sbuf_tensor("dummy_sb", [128, 2], F32))
        self.pool.mark(nc.gpsimd.memset(self.dummy[:], 0.0))

    def sb(self, es, name, shape, dt):
        self.nsb = getattr(self, "nsb", 0) + 1
        return es.enter_context(self.nc.sbuf_tensor(f"{name}_{self.nsb}", shape, dt))

    def dram(self, name, shape, dt, kind="Internal"):
        t = self.nc.dram_tensor(name, list(shape), dt, kind=kind)
        if not hasattr(self, "named"):
            self.named = {}
        self.named[name] = (t, list(shape), dt)
        return t

    def dump(self):
        self.barrier()
        for name in self.cfg.get("dump", []):
            if name not in self.named:
                continue
            t, shape, dt = self.named[name]
            o = self.nc.dram_tensor("dbg_" + name, shape, dt, kind="ExternalOutput")
            self.ld(o.ap(), t.ap(), "dump")
        self.barrier()

    def dsem(self, key):
        if key not in self.dsems:
            self.dsems[key] = DSem(self.es.enter_context(self.nc.semaphore("d_" + key)))
        return self.dsems[key]

    def dma(self, q, out, in_, key, deps=(), slow=False):
        q.wait(*deps)
        ds = self.dsem(key)
        kw = {"allow_slow_non_contiguous": True} if slow else {}
        q.e.dma_start(out=out, in_=in_, **kw).then_inc(ds.sem, 16)
        ds.n += 16
        tok = (ds, ds.n)
        self.pending.append(tok)
        return tok

    def ld(self, out, in_, key, deps=(), slow=False):
        return self.dma(self.sp, out, in_, key, deps, slow)

    def st(self, out, in_, key, deps=()):
        return self.dma(self.pool, out, in_, key, deps)

    def barrier(self):
        toks = [(e, e.n) for e in self.engs if e.n > 0] + self.pending
        for e in self.engs:
            e.wait(*toks)
        self.pending = []

    def A(self, deps, *a, **k):
        self.act.wait(*deps)
        tok = self.act.mark(self.act.e.activation(*a, **k))
        if k.get("accum_out") is not None:
            tok = self.act.mark(self.act.e.activation(out=self.dummy[:, 1:2], in_=self.dummy[:, 0:1], func=AF.Copy))
        return tok

    def V(self, deps, fn, *a, **k):
        self.dve.wait(*deps)
        return self.dve.mark(getattr(self.dve.e, fn)(*a, **k))

    def G(self, deps, fn, *a, **k):
        self.pool.wait(*deps)
        return self.pool.mark(getattr(self.pool.e, fn)(*a, **k))

    def X(self, eng, deps, fn, *a, **k):
        eng.wait(*deps)
        return eng.mark(getattr(eng.e, fn)(*a, **k))

    def rsqrt(self, deps, out, in_, mul, add):
        a = self.A(deps, out=out, in_=in_, func=AF.Ln, scale=float(mul), bias=float(add))
        return self.A([a], out=out, in_=out, func=AF.Exp, scale=-0.5)

    def mm(self, deps, out, lhsT, rhs, start, stop, mark=False):
        self.pe.wait(*deps)
        ins = self.pe.e.matmul(out, lhsT, rhs, start=start, stop=stop)
        return self.pe.mark(ins) if mark else None

    def tr(self, deps, out, in_, ident, mark=False):
        self.pe.wait(*deps)
        ins = self.pe.e.transpose(out, in_, ident)
        return self.pe.mark(ins) if mark else None


def bcast_rows(ap_dram_1d, n, parts=128):
    return bass.AP(ap_dram_1d.tensor, ap_dram_1d.offset, [[0, parts], [1, n]])


def build(cfg):
    p = P(cfg)
    try:
        _build(cfg, p)
    except StopBuild:
        p.dump()
        return p.nc
    p.dump()
    p.es.close()
    return p.nc


def _build(cfg, p):
    NS, LS, LP, OWN = cfg["NS"], cfg["LS"], cfg["LP"], cfg["OWN"]

    def done(tag):
        if cfg.get("stop") == tag:
            raise StopBuild()
    nc = p.nc
    inp = lambda name, shape, dt=F32: nc.dram_tensor(name, list(shape), dt, kind="ExternalInput")
    xs = inp("xs", [NS * LS, D])
    xp = inp("xp", [LP, D])
    w0i, w0o = inp("w0i", [D, 7168]), inp("w0o", [D, D])
    w1i, w1o = inp("w1i", [D, 8192]), inp("w1o", [D, D])
    g0pre, g0post = inp("g0pre", [D]), inp("g0post", [D])
    g1pre, g1post = inp("g1pre", [D]), inp("g1post", [D])
    lbl = inp("lbl", [2, 3, 1024])
    ghg = inp("ghg", [128])
    lam4 = inp("lam4", [4, 128])
    gsub = inp("gsub", [256])
    ident_d = inp("ident", [128, 128], BF16)
    c128_d = inp("c128", [3, 128, 128], BF16)
    cs256_d = inp("cs256", [256, 512], BF16)
    tw_d = {L: inp(f"tw{L}", [3, 128, L // 128]) for L in sorted({LS, LP})}
    cm_d = {L: inp(f"cm{L}", [2, L // 128, L // 128], BF16) for L in sorted({LS, LP})}
    hmask_d = inp("hmask", [2, 64, 64])
    delta_d = inp("delta", [128, 256])
    dtabs_d = inp("dtabs", [128, (LS // 128) * (LS // 256)])
    dtabp_d = inp("dtabp", [128, (LP // 128) * (OWN // 256)])
    ownidx_d = inp("ownidx", [128, OWN // 128], I32)
    ys = nc.dram_tensor("ys", [NS * LS, D], F32, kind="ExternalOutput")
    yp = nc.dram_tensor("yp", [OWN, D], F32, kind="ExternalOutput")

    ges = p.es
    ident = p.sb(ges, "ident_sb", [128, 128], BF16)
    toks = [p.ld(ident[:], ident_d.ap(), "c0")]
    gpost_sb = [p.sb(ges, f"gpost{i}", [128, D], F32) for i in range(2)]
    toks.append(p.ld(gpost_sb[0][:], bcast_rows(g0post.ap(), D), "c0"))
    toks.append(p.ld(gpost_sb[1][:], bcast_rows(g1post.ap(), D), "c0"))
    gpre_sb = [p.sb(ges, f"gpre{i}", [128, KC], F32) for i in range(2)]
    toks.append(p.ld(gpre_sb[0][:], g0pre.ap().rearrange("(c p) -> p c", p=128), "c0", slow=True))
    toks.append(p.ld(gpre_sb[1][:], g1pre.ap().rearrange("(c p) -> p c", p=128), "c0", slow=True))
    p.barrier()

    def prep_w(w, ncols, gcol, name):
        wb = p.dram(name, [D, ncols], BF16)
        with contextlib.ExitStack() as es:
            CW = 1024
            NBUF = 6
            wf = [p.sb(es, f"wf{i}", [128, CW], F32) for i in range(NBUF)]
            wo = [p.sb(es, f"wo{i}", [128, CW], BF16) for i in range(NBUF)]
            cons = [None] * NBUF
            sts = [None] * NBUF
            i = 0
            for kc in range(KC):
                for c0 in range(0, ncols, CW):
                    b = i % NBUF
                    t = p.ld(wf[b][:], w.ap()[kc * 128:(kc + 1) * 128, c0:c0 + CW], f"wl{b}", deps=[cons[b]])
                    if gcol is None:
                        eng = (p.dve, p.pool, p.act)[b % 3]
                    else:
                        eng = (p.dve, p.pool, p.dve)[b % 3]
                    if gcol is None:
                        if eng is p.act:
                            cons[b] = p.A([t, sts[b]], out=wo[b][:], in_=wf[b][:], func=AF.Copy)
                        else:
                            cons[b] = p.X(eng, [t, sts[b]], "tensor_copy", wo[b][:], wf[b][:])
                    else:
                        cons[b] = p.X(eng, [t, sts[b]], "tensor_scalar", wo[b][:], wf[b][:],
                                      gcol[:, kc:kc + 1], None, ALU.mult)
                    sts[b] = p.dma(p.act, wb.ap()[kc * 128:(kc + 1) * 128, c0:c0 + CW], wo[b][:], f"ws{b}",
                                   deps=[cons[b]])
                    i += 1
        p.barrier()
        return wb

    wb0i = prep_w(w0i, 7168, gpre_sb[0], "wb0i")
    wb0o = prep_w(w0o, D, None, "wb0o")
    wb1i = prep_w(w1i, 8192, gpre_sb[1], "wb1i")
    wb1o = prep_w(w1o, D, None, "wb1o")
    done("prep")

    def norm_T(x_rows, L, hT):
        with contextlib.ExitStack() as es:
            xt = [p.sb(es, f"nx{i}", [128, D], F32) for i in range(2)]
            junk = p.sb(es, "njunk", [128, D], BF16)
            hb = [p.sb(es, f"nhb{i}", [128, D], BF16) for i in range(2)]
            ho = [p.sb(es, f"nho{i}", [128, KC, 128], BF16) for i in range(2)]
            st_ = [p.sb(es, f"nst{i}", [128, 2], F32) for i in range(2)]
            rd = [None, None]
            hbr = [None, None]
            hor = [None, None]
            for ti in range(L // 128):
                b = ti % 2
                t = p.ld(xt[b][:], x_rows[ti * 128:(ti + 1) * 128, :], f"nl{b}", deps=[rd[b]])
                a1 = p.A([t], out=junk[:], in_=xt[b][:], func=AF.Square, accum_out=st_[b][:, 0:1])
                v2 = p.rsqrt([a1], st_[b][:, 1:2], st_[b][:, 0:1], 1.0 / D, EPS)
                v3 = p.V([v2, t, hbr[b]], "tensor_scalar", hb[b][:], xt[b][:], st_[b][:, 1:2], None, ALU.mult)
                rd[b] = v3
                for kc in range(KC):
                    tk = p.tr([v3, hor[b]] if kc == 0 else [], p.pb[kc // 8][:, (kc % 8) * 128:(kc % 8 + 1) * 128],
                              hb[b][:, kc * 128:(kc + 1) * 128], ident[:], mark=(kc == KC - 1))
                hbr[b] = tk
                c1 = p.A([tk, hor[b]], out=ho[b][:, 0:8, :], in_=p.pb[0][:].rearrange("p (k t) -> p k t", k=8),
                         func=AF.Copy)
                c2 = p.V([tk, hor[b]], "tensor_copy", ho[b][:, 8:16, :],
                         p.pb[1][:].rearrange("p (k t) -> p k t", k=8))
                p.pe.wait(c1, c2)
                hor[b] = p.st(hT.ap().rearrange("(k p) t -> p k t", p=128)[:, :, ti * 128:(ti + 1) * 128],
                              ho[b][:], f"ns{b}", deps=[c1, c2])
        p.barrier()

    def proj(hT, L, wb, ncols_total, jobs):
        TB = min(L, 1024)
        with contextlib.ExitStack() as es:
            hblk = p.sb(es, "pj_h", [128, KC, TB], BF16)
            wblk = [p.sb(es, f"pj_w{i}", [128, KC, 512], BF16) for i in range(2)]
            osb = {F32: [p.sb(es, f"pj_of{i}", [128, 512], F32) for i in range(2)],
                   BF16: [p.sb(es, f"pj_ob{i}", [128, 512], BF16) for i in range(2)]}
            wread = [None, None]
            ost = {F32: [None, None], BF16: [None, None]}
            psr = [None] * 4
            wi = 0
            oi = 0
            pi = 0
            hread = None
            for tb in range(L // TB):
                th = p.ld(hblk[:], hT.ap().rearrange("(k p) t -> p k t", p=128)[:, :, tb * TB:(tb + 1) * TB],
                          "pjh", deps=[hread])
                cbs = [(j, c) for j in jobs for c in range(0, j[1], 512)]
                for (job, c) in cbs:
                    col0, ncols, mode, od, o0, scale, odt = job
                    b = wi % 2
                    wi += 1
                    tw = p.ld(wblk[b][:], wb.ap().rearrange("(k p) n -> p k n", p=128)[:, :, col0 + c:col0 + c + 512],
                              f"pjw{b}", deps=[wread[b]])
                    last = None
                    TW = min(512, TB)
                    if mode == "F":
                        subs = [(s4, t5) for s4 in range(4) for t5 in range(TB // TW)]
                    else:
                        subs = [(s4, 0) for s4 in range(TB // 128)]
                    for (s4, t5) in subs:
                        pk = pi % 4
                        pi += 1
                        W_ = TW if mode == "F" else 512
                        ps = p.ps[pk][:, 0:W_]
                        for kc in range(KC):
                            if mode == "F":
                                lhsT, rhs = wblk[b][:, kc, s4 * 128:(s4 + 1) * 128], hblk[:, kc, t5 * TW:(t5 + 1) * TW]
                            else:
                                lhsT, rhs = hblk[:, kc, s4 * 128:(s4 + 1) * 128], wblk[b][:, kc, :]
                            tk = p.mm([th, tw, psr[pk]] if kc == 0 else [], ps, lhsT, rhs, kc == 0, kc == KC - 1,
                                      mark=(kc == KC - 1))
                        last = tk
                        ob = oi % 2
                        oi += 1
                        o = osb[odt][ob][:, 0:W_]
                        if oi % 2 == 0:
                            ev = p.A([tk, ost[odt][ob]], out=o, in_=ps, func=AF.Copy, scale=float(scale))
                        else:
                            ev = p.V([tk, ost[odt][ob]], "tensor_scalar", o, ps, float(scale), None, ALU.mult)
                        psr[pk] = ev
                        if mode == "F":
                            dst = od.ap()[o0 + c + s4 * 128:o0 + c + (s4 + 1) * 128,
                                          tb * TB + t5 * TW:tb * TB + (t5 + 1) * TW]
                        else:
                            dst = od.ap()[tb * TB + s4 * 128:tb * TB + (s4 + 1) * 128, o0 + c:o0 + c + 512]
                        ost[odt][ob] = p.st(dst, o, f"pjs{ob}{'f' if odt == F32 else 'b'}", deps=[ev])
                    wread[b] = last
                    hread = last
        p.barrier()

    def fnet(uT, L, ya, tabs):
        M = L // 128
        c128, cs256, tw, cm = tabs
        Bd = p.dram(f"fn_B{p.ndram}", [128, M, 512], BF16)
        p.ndram += 1
        CP = 32
        with contextlib.ExitStack() as es:
            ug = p.sb(es, "fn_u", [128, 2, L], BF16)
            MB = min(M, 32)
            V = p.sb(es, "fn_V", [128, MB, 512], BF16)
            Bs = p.sb(es, "fn_Bs", [128, MB, 512], BF16)
            tmp = [p.sb(es, f"fn_t{i}", [128, 2, 256], F32) for i in range(2)]
            Bt = p.sb(es, "fn_Bt", [M, CP, 512], BF16)
            Y = [p.sb(es, f"fn_Y{i}", [M, 2, 256], F32) for i in range(2)]
            for g in range(4):
                tu = p.ld(ug[:], uT.ap()[g * 256:(g + 1) * 256, :].rearrange("(c p) t -> p c t", p=128), "fnu")
                ts_all = []
                for bh in range(M // MB):
                    evs = []
                    prev = [None, None]
                    for bl in range(MB):
                        b = bh * MB + bl
                        pk = b % 2
                        for ch in range(2):
                            lhsT = bass.AP(ug, ch * L + b, [[2 * L, 128], [M, 128]])
                            tk = p.mm([tu, prev[pk]] if ch == 0 else [], p.ps[pk][:], lhsT, cs256[:, ch, :],
                                      ch == 0, ch == 1, mark=(ch == 1))
                        if b % 2 == 0:
                            ev = p.A([tk], out=V[:, bl, :], in_=p.ps[pk][:], func=AF.Copy)
                        else:
                            ev = p.V([tk], "tensor_copy", V[:, bl, :], p.ps[pk][:])
                        prev[pk] = ev
                        evs.append(ev)
                    prevr = [None, None]
                    tw_tok = []
                    for bp in range(MB // 2):
                        pk = 2 + (bp % 2) * 2
                        Ar, Ai = p.ps[pk], p.ps[pk + 1]
                        b0 = 2 * bp
                        dep = [evs[b0], evs[b0 + 1], prevr[bp % 2]]
                        ar3 = Ar[:].rearrange("p (b f) -> p b f", b=2)
                        ai3 = Ai[:].rearrange("p (b f) -> p b f", b=2)
                        p.mm(dep, ar3, c128[:, 0, :], V[:, b0:b0 + 2, 0:256], True, False)
                        p.mm([], ar3, c128[:, 1, :], V[:, b0:b0 + 2, 256:512], False, True)
                        p.mm([], ai3, c128[:, 0, :], V[:, b0:b0 + 2, 256:512], True, False)
                        tk = p.mm([], ai3, c128[:, 2, :], V[:, b0:b0 + 2, 0:256], False, True, mark=True)
                        last = []
                        for j in range(2):
                            bl = b0 + j
                            b = bh * MB + bl
                            t1 = p.V([tk], "tensor_scalar", tmp[0][:, j, :], ar3[:, j, :], tw[:, 0, b:b + 1], None, ALU.mult)
                            t3 = p.V([tk], "tensor_scalar", tmp[1][:, j, :], ai3[:, j, :], tw[:, 0, b:b + 1], None, ALU.mult)
                            r1 = p.V([t1, tk], "scalar_tensor_tensor", Bs[:, bl, 0:256], ai3[:, j, :], tw[:, 1, b:b + 1],
                                     tmp[0][:, j, :], ALU.mult, ALU.add)
                            r2 = p.V([t3, tk], "scalar_tensor_tensor", Bs[:, bl, 256:512], ar3[:, j, :], tw[:, 2, b:b + 1],
                                     tmp[1][:, j, :], ALU.mult, ALU.add)
                            last = [r1, r2]
                        p.act.wait(*last)
                        prevr[bp % 2] = last[1]
                        tw_tok = last
                    ts_all.append(p.st(Bd.ap()[:, bh * MB:(bh + 1) * MB, :], Bs[:], "fnb", deps=tw_tok))
                    p.barrier()
                if True:
                    ts_ = ts_all[-1]
                    yst = [None, None]
                    evp = [None, None]
                    bt_read = None
                    for cp in range(128 // CP):
                        tl = p.ld(Bt[:], Bd.ap()[cp * CP:(cp + 1) * CP, :, :].rearrange("c b f -> b c f"), "fnbt",
                                  deps=[ts_, bt_read])
                        for c2 in range(CP // 2):
                            pk = c2 % 2
                            ps3 = p.ps[pk][0:M, :].rearrange("p (c f) -> p c f", c=2)
                            p.mm([tl, evp[pk]], ps3, cm[0:M, 0, :], Bt[:, 2 * c2:2 * c2 + 2, 0:256], True, False)
                            tk = p.mm([], ps3, cm[0:M, 1, :], Bt[:, 2 * c2:2 * c2 + 2, 256:512], False, True, mark=True)
                            if c2 % 2 == 0:
                                ev = p.A([tk, yst[pk]], out=Y[pk][:], in_=ps3, func=AF.Copy)
                            else:
                                ev = p.V([tk, yst[pk]], "tensor_copy", Y[pk][:], ps3)
                            c_abs = cp * CP + 2 * c2
                            dst = ya.ap().rearrange("(d c) f -> d c f", c=128)[:, c_abs:c_abs + 2, g * 256:(g + 1) * 256]
                            yst[pk] = p.st(dst, Y[pk][:], f"fny{pk}", deps=[ev])
                            evp[pk] = ev
                            bt_read = tk
                    p.barrier()
        p.barrier()

    def hgrn(qrT, ffT, fbT, vtok, gates, L, ymix, lbt, hm, ghg_sb):
        SEG = min(L, 2048)
        NCH = SEG // 64
        nseg = L // SEG
        NT = L // 64
        dS = p.dram(f"hg_dS{p.ndram}", [2, NT, 128, 128], F32)
        Sb = p.dram(f"hg_Sb{p.ndram}", [2, NT, 128, 128], BF16)
        qd = p.dram(f"hg_qd{p.ndram}", [2, 128, L], BF16)
        scd = p.dram(f"hg_sc{p.ndram}", [64, NT, 64], BF16)
        eld = p.dram(f"hg_el{p.ndram}", [2, 128, NT], F32)
        p.ndram += 1
        for h in range(8):
            SEG1 = min(L, 1024)
            NCH1 = SEG1 // 64
            nseg1 = L // SEG1
            with contextlib.ExitStack() as es:
                qr = p.sb(es, "h_qr", [128, SEG1], F32)
                rmask = p.sb(es, "h_rm", [128, SEG1], F32)
                qdec = [p.sb(es, f"h_qd{i}", [128, SEG1], BF16) for i in range(2)]
                vt = p.sb(es, "h_v", [64, NCH1, 128], BF16)
                sct = p.sb(es, "h_sc", [64, NCH1, 64], BF16)
                sc1 = p.sb(es, "h_sc1", [64, NCH1, 64], F32)
                el = p.sb(es, "h_el", [128, 2, NCH1], F32)
                B = []
                for d in range(2):
                    bd = {}
                    for nm_ in ("fr", "f", "g", "k", "cum", "cb", "ex", "ex2", "ex3"):
                        bd[nm_] = p.sb(es, f"h_{nm_}{d}", [128, SEG1], F32)
                    bd["kdec"] = p.sb(es, f"h_kd{d}", [128, SEG1], BF16)
                    bd["kend"] = p.sb(es, f"h_ke{d}", [128, SEG1], BF16)
                    bd["kendT"] = p.sb(es, f"h_keT{d}", [64, NCH1, 128], BF16)
                    bd["dSs"] = [p.sb(es, f"h_dS{d}{j}", [128, 4, 128], F32) for j in range(2)]
                    B.append(bd)
                m1 = p.G([], "memset", rmask[:], 1.0)
                m2 = p.G([], "memset", rmask[:].rearrange("p (n s) -> p n s", s=64)[:, :, 0:1], 0.0)
                p.barrier()
                for sg in range(nseg1):
                    t0 = sg * SEG1
                    tq = p.ld(qr[:], qrT.ap()[h * 128:(h + 1) * 128, t0:t0 + SEG1], "hq")
                    tv = p.ld(vt[:], vtok.ap()[t0:t0 + SEG1, h * 128:(h + 1) * 128].rearrange("(n s) v -> s n v", s=64),
                              "hv")
                    aq = p.A([tq], out=qr[:], in_=qr[:], func=AF.Silu)
                    shared = {}

                    def dir_gen(d):
                        bd = B[d]
                        fr, f_, g_, k_, cum, cb = bd["fr"], bd["f"], bd["g"], bd["k"], bd["cum"], bd["cb"]
                        ex, ex2, ex3, kdec, kend, kendT, dSs = (bd["ex"], bd["ex2"], bd["ex3"], bd["kdec"], bd["kend"],
                                                                bd["kendT"], bd["dSs"])
                        src = ffT if d == 0 else fbT
                        tf = p.ld(fr[:], src.ap()[h * 128:(h + 1) * 128, t0:t0 + SEG1], f"hf{d}")
                        a1 = p.A([tf], out=f_[:], in_=fr[:], func=AF.Sigmoid)
                        yield
                        v1 = p.V([a1], "tensor_scalar", f_[:], f_[:], lbt[:, 2 + d, h:h + 1], lbt[:, d, h:h + 1],
                                 ALU.mult, ALU.add)
                        yield
                        a2 = p.A([v1], out=g_[:], in_=f_[:], func=AF.Ln)
                        g1 = p.G([v1], "tensor_scalar", k_[:], f_[:], -1.0, 1.0, ALU.mult, ALU.add)
                        yield
                        v2 = p.V([a2], "tensor_tensor_scan", cum[:], rmask[:], g_[:], 0.0, ALU.mult, ALU.add)
                        yield
                        cum3 = cum[:].rearrange("p (n s) -> p n s", s=64)
                        lastb = bass.AP(cum, 63, [[SEG1, 128], [64, NCH1], [0, 64]])
                        a6 = p.A([v2], out=el[:, d, :], in_=cum3[:, :, 63], func=AF.Exp)
                        if d == 0:
                            cc = cum
                            v3 = p.V([v2], "tensor_tensor", cb[:].rearrange("p (n s) -> p n s", s=64), lastb, cum3,
                                     ALU.subtract)
                            dl = cb
                            yield
                        else:
                            v3a = p.V([v2], "tensor_tensor", cb[:].rearrange("p (n s) -> p n s", s=64), lastb, cum3,
                                      ALU.subtract)
                            yield
                            v3b = p.V([v3a], "tensor_tensor", cb[:], cb[:], g_[:], ALU.add)
                            yield
                            cc = cb
                            v3 = p.V([v3b], "tensor_tensor", g_[:], cum[:], g_[:], ALU.subtract)
                            dl = g_
                            yield
                        a3 = p.A([v3], out=ex[:], in_=cc[:], func=AF.Exp)
                        yield
                        a4 = p.A([v3], out=ex2[:], in_=cc[:], func=AF.Exp, scale=-1.0)
                        g2 = p.V([a3, aq], "tensor_tensor", qdec[d][:], qr[:], ex[:], ALU.mult)
                        yield
                        a5 = p.A([v3], out=ex3[:], in_=dl[:], func=AF.Exp)
                        g3 = p.G([a4, g1], "tensor_tensor", kdec[:], k_[:], ex2[:], ALU.mult)
                        yield
                        g4 = p.V([a5, g1], "tensor_tensor", kend[:], k_[:], ex3[:], ALU.mult)
                        p.st(qd.ap()[d, :, t0:t0 + SEG1], qdec[d][:], f"hsq{d}", deps=[g2])
                        p.st(eld.ap()[d, :, sg * NCH1:(sg + 1) * NCH1], el[:, d, :], f"hse{d}", deps=[a6])
                        yield
                        nb = min(8, NCH1)
                        NGR = NCH1 // nb
                        mk_ap = bass.AP(hm, d * 64, [[128, 64], [0, nb], [1, 64]])
                        hz = {}
                        psS, pbT = p.ps[d], p.pb[d]
                        ev = None
                        for gq in range(NGR + 1):
                            if gq < NGR:
                                n0 = gq * nb
                                for j in range(nb):
                                    sl = slice((n0 + j) * 64, (n0 + j + 1) * 64)
                                    tk = p.mm([g2, g3, hz.get("sc")] if j == 0 else [], psS[0:64, j * 64:(j + 1) * 64],
                                              kdec[:, sl], qdec[d][:, sl], True, True, mark=(j == nb - 1))
                                psv = psS[0:64, 0:nb * 64].rearrange("p (n s) -> p n s", s=64)
                                if d == 0:
                                    ev = p.V([tk], "tensor_tensor", sc1[:, n0:n0 + nb, :], psv, mk_ap, ALU.mult)
                                    shared[("sc1", gq)] = ev
                                else:
                                    ev0 = p.V([tk], "tensor_tensor", sct[:, n0:n0 + nb, :], psv, mk_ap, ALU.mult)
                                    ev = p.V([ev0, shared[("sc1", gq)]], "tensor_tensor", sct[:, n0:n0 + nb, :],
                                             sct[:, n0:n0 + nb, :], sc1[:, n0:n0 + nb, :], ALU.add)
                                hz["sc"] = ev
                                yield
                                for j in range(nb):
                                    sl = slice((n0 + j) * 64, (n0 + j + 1) * 64)
                                    tk2 = p.tr([g4, hz.get("tr")] if j == 0 else [], pbT[0:64, j * 128:(j + 1) * 128],
                                               kend[:, sl], ident[:], mark=(j == nb - 1))
                                ev2 = p.A([tk2], out=kendT[:, n0:n0 + nb, :],
                                          in_=pbT[0:64, 0:nb * 128].rearrange("p (n k) -> p n k", k=128), func=AF.Copy)
                                hz["tr"] = ev2
                                hz[("kT", gq)] = ev2
                                yield
                            if gq >= 1:
                                gprev = gq - 1
                                n0 = gprev * nb
                                for j in range(nb):
                                    i4 = j // 4
                                    bank = p.ps[2 + 2 * d + i4]
                                    tk3 = p.mm([hz[("kT", gprev)], tv, hz.get(("dsb", i4))] if j % 4 == 0 else [],
                                               bank[:, (j % 4) * 128:(j % 4 + 1) * 128], kendT[:, n0 + j, :], vt[:, n0 + j, :],
                                               True, True, mark=(j % 4 == 3 or j == nb - 1))
                                    if j % 4 == 3 or j == nb - 1:
                                        w4 = j % 4 + 1
                                        dst = dSs[i4][:, 0:w4, :]
                                        srcp = bank[:, 0:w4 * 128].rearrange("p (n v) -> p n v", v=128)
                                        if i4 == 0:
                                            ev3 = p.A([tk3, hz.get(("dss", i4))], out=dst, in_=srcp, func=AF.Copy)
                                        else:
                                            ev3 = p.V([tk3, hz.get(("dss", i4))], "tensor_copy", dst, srcp)
                                        hz[("dsb", i4)] = ev3
                                        c0 = sg * NCH1 + n0 + i4 * 4
                                        hz[("dss", i4)] = p.st(
                                            dS.ap()[d, c0:c0 + w4, :, :].rearrange("n p v -> p n v"), dst, f"hds{d}{i4}",
                                            deps=[ev3])
                                yield
                        if d == 1:
                            p.st(scd.ap()[:, sg * NCH1:(sg + 1) * NCH1, :], sct[:], "hsc", deps=[ev])

                    gens = [dir_gen(0), dir_gen(1)]
                    while gens:
                        for gg in list(gens):
                            try:
                                next(gg)
                            except StopIteration:
                                gens.remove(gg)
                    p.barrier()
            with contextlib.ExitStack() as es:
                G = min(NT, 32)
                NG = NT // G
                dsl = [p.sb(es, f"h2_ds{i}", [128, G, 128], F32) for i in range(2)]
                sall = [p.sb(es, f"h2_sa{i}", [128, G + 1, 128], F32) for i in range(2)]
                sbo = [p.sb(es, f"h2_sb{i}", [128, G, 128], BF16) for i in range(2)]
                ela = p.sb(es, "h2_el", [128, 2, NT], F32)
                te = p.ld(ela[:], eld.ap().rearrange("d p n -> p d n"), "h2e")
                z0 = p.V([], "memset", sall[0][:, 0, :], 0.0)
                z1 = p.V([], "memset", sall[1][:, G, :], 0.0)
                last = [z0, z1]
                for gi in range(NG):
                    gf, gb = gi, NG - 1 - gi
                    tl = [p.ld(dsl[0][:], dS.ap()[0, gf * G:(gf + 1) * G, :, :].rearrange("n p v -> p n v"), "h2l0"),
                          p.ld(dsl[1][:], dS.ap()[1, gb * G:(gb + 1) * G, :, :].rearrange("n p v -> p n v"), "h2l1")]
                    for i in range(G):
                        nf = gf * G + i
                        last[0] = p.V([tl[0], te, last[0]], "scalar_tensor_tensor", sall[0][:, i + 1, :], sall[0][:, i, :],
                                      ela[:, 0, nf:nf + 1], dsl[0][:, i, :], ALU.mult, ALU.add)
                        j = G - 1 - i
                        nbk = gb * G + j
                        last[1] = p.V([tl[1], te, last[1]], "scalar_tensor_tensor", sall[1][:, j, :], sall[1][:, j + 1, :],
                                      ela[:, 1, nbk:nbk + 1], dsl[1][:, j, :], ALU.mult, ALU.add)
                    c0 = p.G(last, "tensor_copy", sbo[0][:], sall[0][:, 0:G, :])
                    c1 = p.A(last, out=sbo[1][:], in_=sall[1][:, 1:G + 1, :], func=AF.Copy)
                    p.st(Sb.ap()[0, gf * G:(gf + 1) * G, :, :].rearrange("n p v -> p n v"), sbo[0][:], "h2s0", deps=[c0])
                    p.st(Sb.ap()[1, gb * G:(gb + 1) * G, :, :].rearrange("n p v -> p n v"), sbo[1][:], "h2s1", deps=[c1])
                    last[0] = p.V([c0, c1] + last, "tensor_copy", sall[0][:, 0, :], sall[0][:, G, :])
                    last[1] = p.V([last[0]], "tensor_copy", sall[1][:, G, :], sall[1][:, 0, :])
                    p.barrier()
            with contextlib.ExitStack() as es:
                G = min(NT, 16)
                NG3 = NT // G
                mk2 = lambda nm_, shp, dt: [p.sb(es, f"{nm_}{i}", shp, dt) for i in range(2)]
                qd3 = mk2("h3_qd", [128, 2, G * 64], BF16)
                sc3 = mk2("h3_sc", [64, G, 64], BF16)
                v3_ = mk2("h3_v", [64, G, 128], BF16)
                gt3 = [p.sb(es, f"h3_g{i}", [64, G, 128], F32) for i in range(3)]
                s3 = mk2("h3_s", [128, 2, G, 128], BF16)
                o3 = mk2("h3_o", [64, G, 128], F32)
                ob3 = mk2("h3_ob", [64, G, 128], BF16)
                ss = mk2("h3_ss", [64, G], F32)
                sq3 = p.sb(es, "h3_sq", [64, G, 128], F32)
                T3 = {}
                g3_ = lambda k, t: T3.get((k, t))
                nb3 = min(4, G)
                prevb = [None, None]

                def L3(gi):
                    bq = gi % 2
                    t0 = gi * G * 64
                    rdC = [g3_("pe", gi - 2)]
                    T3[("ld", gi)] = [
                        p.ld(qd3[bq][:], qd.ap()[:, :, t0:t0 + G * 64].rearrange("d p t -> p d t"), f"h3a{bq}", deps=rdC),
                        p.ld(sc3[bq][:], scd.ap()[:, gi * G:(gi + 1) * G, :], f"h3a{bq}"),
                        p.ld(v3_[bq][:], vtok.ap()[t0:t0 + G * 64, h * 128:(h + 1) * 128].rearrange("(n s) v -> s n v", s=64), f"h3a{bq}"),
                        p.ld(s3[bq][:, 0], Sb.ap()[0, gi * G:(gi + 1) * G, :, :].rearrange("n p v -> p n v"), f"h3a{bq}"),
                        p.ld(s3[bq][:, 1], Sb.ap()[1, gi * G:(gi + 1) * G, :, :].rearrange("n p v -> p n v"), f"h3a{bq}")]
                    T3[("ldg", gi)] = p.ld(
                        gt3[gi % 3][:], gates.ap()[t0:t0 + G * 64, 1024 + h * 128:1024 + (h + 1) * 128].rearrange("(n s) v -> s n v", s=64),
                        f"h3g{gi % 3}", deps=[g3_("r5", gi - 3)])

                def C3(gi):
                    bq = gi % 2
                    T3[("ag", gi)] = p.A([T3[("ldg", gi)]], out=gt3[gi % 3][:], in_=gt3[gi % 3][:], func=AF.Silu)
                    tk = None
                    ev = None
                    for j0 in range(0, G, nb3):
                        pk = (j0 // nb3) % 2
                        for jj in range(nb3):
                            j = j0 + jj
                            ps = p.ps[pk][0:64, jj * 128:(jj + 1) * 128]
                            sl = slice(j * 64, (j + 1) * 64)
                            p.mm(T3[("ld", gi)] + [prevb[pk]] if jj == 0 else [], ps, sc3[bq][:, j, :], v3_[bq][:, j, :], True, False)
                            p.mm([], ps, qd3[bq][:, 0, sl], s3[bq][:, 0, j, :], False, False)
                            tk = p.mm([], ps, qd3[bq][:, 1, sl], s3[bq][:, 1, j, :], False, True, mark=(jj == nb3 - 1))
                        ev = p.A([tk, g3_("r5", gi - 2)], out=o3[bq][:, j0:j0 + nb3, :],
                                 in_=p.ps[pk][0:64, 0:nb3 * 128].rearrange("p (n v) -> p n v", v=128), func=AF.Copy)
                        prevb[pk] = ev
                    T3[("pe", gi)] = tk
                    T3[("ev", gi)] = ev

                def E3(gi):
                    bq = gi % 2
                    t0 = gi * G * 64
                    e2 = p.A([T3[("ev", gi)], g3_("e3", gi - 1)], out=sq3[:], in_=o3[bq][:], func=AF.Square)
                    e3 = p.V([e2], "tensor_reduce", ss[bq][:], sq3[:], mybir.AxisListType.X, ALU.add)
                    T3[("e3", gi)] = e3
                    r2 = p.rsqrt([e3], ss[bq][:], ss[bq][:], 1.0 / 128, EPS)
                    ssb = bass.AP(ss[bq], 0, [[G, 64], [1, G], [0, 128]])
                    r3 = p.V([r2], "tensor_tensor", o3[bq][:], o3[bq][:], ssb, ALU.mult)
                    ghb = bass.AP(ghg_sb, 0, [[128, 64], [0, G], [1, 128]])
                    r4 = p.V([r3], "tensor_tensor", o3[bq][:], o3[bq][:], ghb, ALU.mult)
                    r5 = p.V([r4, T3[("ag", gi)], g3_("st", gi - 2)], "tensor_tensor", ob3[bq][:], o3[bq][:], gt3[gi % 3][:], ALU.mult)
                    T3[("r5", gi)] = r5
                    T3[("st", gi)] = p.st(
                        ymix.ap()[t0:t0 + G * 64, 1024 + h * 128:1024 + (h + 1) * 128].rearrange("(n s) v -> s n v", s=64),
                        ob3[bq][:], f"h3s{bq}", deps=[r5])

                L3(0)
                for gi in range(NG3 + 1):
                    if gi < NG3:
                        if gi + 1 < NG3:
                            L3(gi + 1)
                        C3(gi)
                    if gi >= 1:
                        E3(gi - 1)
                p.barrier()
        p.barrier()

    def outproj(L, ymix, ya, gates, wbo, gpost, xres, xout, hT_next):
        NTL = L // 128
        with contextlib.ExitStack() as es:
            wsb = p.sb(es, "op_w", [128, KC, D], BF16)
            tw = p.ld(wsb[:], wbo.ap().rearrange("(k p) n -> p k n", p=128), "opw")
            ym = [p.sb(es, f"op_ym{i}", [128, D], BF16) for i in range(2)]
            yaf = [p.sb(es, f"op_ya{i}", [128, 1024], F32) for i in range(2)] if ya is not None else None
            gaf = [p.sb(es, f"op_ga{i}", [128, 1024], F32) for i in range(2)] if ya is not None else None
            ymT = [p.sb(es, f"op_ymT{i}", [128, KC, 128], BF16) for i in range(2)]
            xr = [p.sb(es, f"op_x{i}", [128, D], F32) for i in range(2)]
            yo = [p.sb(es, f"op_y{i}", [128, D], F32) for i in range(2)]
            junk = p.sb(es, "op_j", [128, D], BF16)
            st_ = [p.sb(es, f"op_st{i}", [128, 4], F32) for i in range(2)]
            hb = [p.sb(es, f"op_hb{i}", [128, D], BF16) for i in range(2)]
            ho = [p.sb(es, f"op_ho{i}", [128, KC, 128], BF16) for i in range(2)]
            T = {}
            g = lambda k, t: T.get((k, t))
            pbv = [p.pb[i][:].rearrange("p (k t) -> p k t", k=8) for i in range(2)]
            def A1a(ti):
                b = ti % 2
                rows = slice(ti * 128, (ti + 1) * 128)
                T[("tx", ti)] = p.ld(xr[b][:], xres[rows, :], f"opx{b}", deps=[g("v5", ti - 2)])
                if ya is not None:
                    t1 = p.ld(ym[b][:, 1024:2048], ymix.ap()[rows, 1024:2048], f"opm{b}", deps=[g("tr", ti - 2)])
                    t2 = p.ld(yaf[b][:], ya.ap()[rows, :], f"opa{b}", deps=[g("v1", ti - 2)])
                    t3 = p.ld(gaf[b][:], gates.ap()[rows, 0:1024], f"opa{b}", deps=[g("v1", ti - 2)])
                    a1 = p.A([t3], out=gaf[b][:], in_=gaf[b][:], func=AF.Silu)
                    v1 = p.V([a1, t2, g("tr", ti - 2)], "tensor_tensor", ym[b][:, 0:1024], yaf[b][:], gaf[b][:], ALU.mult)
                    T[("v1", ti)] = v1
                    rdy = [t1, v1]
                else:
                    rdy = [p.ld(ym[b][:], ymix.ap()[rows, :], f"opm{b}", deps=[g("tr", ti - 2)])]
                for kc in range(KC):
                    tk = p.tr(rdy + [g("c1h", ti - 2), g("c2h", ti - 2)] if kc == 0 else [],
                              p.pb[kc // 8][:, (kc % 8) * 128:(kc % 8 + 1) * 128],
                              ym[b][:, kc * 128:(kc + 1) * 128], ident[:], mark=(kc == KC - 1))
                T[("tr", ti)] = tk

            def A1b(ti):
                b = ti % 2
                tk = T[("tr", ti)]
                c1 = p.A([tk, g("mm", ti - 2)], out=ymT[b][:, 0:8, :], in_=pbv[0], func=AF.Copy)
                c2 = p.V([tk, g("mm", ti - 2)], "tensor_copy", ymT[b][:, 8:16, :], pbv[1])
                T[("c1", ti)], T[("c2", ti)] = c1, c2
                for cb in range(4):
                    for kc in range(KC):
                        tk = p.mm([c1, c2, tw] + [T.get(("ev", ti - 1, i)) for i in range(4)] if (kc == 0 and cb == 0) else [],
                                  p.ps[cb][:], ymT[b][:, kc, :],
                                  wsb[:, kc, cb * 512:(cb + 1) * 512], kc == 0, kc == KC - 1, mark=(kc == KC - 1))
                T[("mm", ti)] = tk

            def A2a(ti):
                b = ti % 2
                tk = T[("mm", ti)]
                for cb in range(4):
                    dep = [tk, g("so", ti - 2), g("v8", ti - 2)]
                    if cb % 2 == 0:
                        ev = p.A(dep, out=yo[b][:, cb * 512:(cb + 1) * 512], in_=p.ps[cb][:], func=AF.Copy)
                    else:
                        ev = p.V(dep, "tensor_copy", yo[b][:, cb * 512:(cb + 1) * 512], p.ps[cb][:])
                    T[("ev", ti, cb)] = ev

            def A2b(ti):
                b = ti % 2
                rows = slice(ti * 128, (ti + 1) * 128)
                evs = [T[("ev", ti, cb)] for cb in range(4)]
                a2 = p.A(evs, out=junk[:], in_=yo[b][:], func=AF.Square, accum_out=st_[b][:, 0:1])
                v3 = p.rsqrt([a2], st_[b][:, 1:2], st_[b][:, 0:1], 1.0 / D, EPS)
                v4 = p.V([v3] + evs, "scalar_tensor_tensor", yo[b][:], yo[b][:], st_[b][:, 1:2], gpost[:], ALU.mult, ALU.mult)
                v5 = p.V([v4, T[("tx", ti)]], "tensor_tensor", yo[b][:], yo[b][:], xr[b][:], ALU.add)
                T[("v5", ti)] = v5
                T[("so", ti)] = p.st(xout[rows, :], yo[b][:], f"opo{b}", deps=[v5])
                if hT_next is not None:
                    a3 = p.A([v5], out=junk[:], in_=yo[b][:], func=AF.Square, accum_out=st_[b][:, 2:3])
                    v7 = p.rsqrt([a3], st_[b][:, 3:4], st_[b][:, 2:3], 1.0 / D, EPS)
                    T[("v8", ti)] = p.V([v7, g("trh", ti - 2)], "tensor_scalar", hb[b][:], yo[b][:], st_[b][:, 3:4], None, ALU.mult)

            def Bst(tj):
                bj = tj % 2
                rows_j = slice(tj * 128, (tj + 1) * 128)
                for kc in range(KC):
                    tk2 = p.tr([g("v8", tj), g("c1", tj + 1), g("c2", tj + 1)] if kc == 0 else [],
                               p.pb[kc // 8][:, (kc % 8) * 128:(kc % 8 + 1) * 128],
                               hb[bj][:, kc * 128:(kc + 1) * 128], ident[:], mark=(kc == KC - 1))
                T[("trh", tj)] = tk2
                c1h = p.A([tk2, g("sh", tj - 2)], out=ho[bj][:, 0:8, :], in_=pbv[0], func=AF.Copy)
                c2h = p.V([tk2, g("sh", tj - 2)], "tensor_copy", ho[bj][:, 8:16, :], pbv[1])
                T[("c1h", tj)], T[("c2h", tj)] = c1h, c2h
                T[("sh", tj)] = p.st(hT_next.ap().rearrange("(k p) t -> p k t", p=128)[:, :, rows_j], ho[bj][:],
                                     f"oph{bj}", deps=[c1h, c2h])

            for ti in range(NTL + 1):
                if ti < NTL:
                    A1a(ti)
                if ti >= 1:
                    A2a(ti - 1)
                if ti < NTL:
                    A1b(ti)
                if ti >= 1:
                    A2b(ti - 1)
                    if hT_next is not None:
                        Bst(ti - 1)
        p.barrier()

    def attention(qT, kT, vtok, gates, Lq, Lk, og, dtab, lam_sb, gsub_sb, delta, static_pos=False):
        NKT, NQB = Lk // 128, Lq // 256
        SK = 2
        with contextlib.ExitStack() as es:
            qs = p.sb(es, "at_q", [128, 2, Lq], BF16)
            ks = p.sb(es, "at_k", [128, 2, Lk], BF16)
            vs = p.sb(es, "at_v", [128, NKT, 257], BF16)
            absd = [p.sb(es, f"at_ad{i}", [128, 256], F32) for i in range(3)]
            sb_ = [p.sb(es, f"at_s{i}", [128, 256], F32) for i in range(4)]
            pt = [p.sb(es, f"at_p{i}", [128, 256], BF16) for i in range(4)]
            gt = [p.sb(es, f"at_g{i}", [128, 2, 256], F32) for i in range(2)]
            o0 = [p.sb(es, f"at_o0{i}", [128, 2, 256], F32) for i in range(2)]
            o1 = [p.sb(es, f"at_o1{i}", [128, 2, 256], F32) for i in range(2)]
            rs = [p.sb(es, f"at_rs{i}", [128, 8], F32) for i in range(2)]
            ob = [p.sb(es, f"at_ob{i}", [128, 2, 256], BF16) for i in range(2)]
            jk = p.sb(es, "at_j", [128, 256], F32)
            Sps = [p.ps[i][:, 0:256] for i in range(2)]
            acc = [[p.ps[2 + 2 * c + j] for j in range(2)] for c in range(2)]
            qbi = 0
            gt_free = [[], []]
            ob_free = [None, None]
            for h in range(8):
                slope = SLOPES[h]
                t_in = [p.ld(qs[:], qT.ap()[h * 256:(h + 1) * 256, :].rearrange("(c p) t -> p c t", p=128), "atq"),
                        p.ld(ks[:], kT.ap()[h * 256:(h + 1) * 256, :].rearrange("(c p) t -> p c t", p=128), "atq"),
                        p.ld(vs[:, :, 0:256], vtok.ap()[:, h * 256:(h + 1) * 256].rearrange("(n p) v -> p n v", p=128),
                             "atq")]
                t_in.append(p.G([], "memset", vs[:, :, 256:257], 1.0))
                acc_free = []
                for qb in range(NQB):
                    par = qbi % 2
                    qbi += 1
                    q0 = qb * 256
                    tg = p.ld(gt[par][:],
                              gates.ap()[q0:q0 + 256, h * 256:(h + 1) * 256].rearrange("(j p) v -> p j v", p=128),
                              f"atg{par}", deps=gt_free[par])
                    units = [(kt, c) for kt in range(NKT) for c in range(2)]
                    NU = len(units)
                    rd_ad = [None] * 3
                    rd_s = [None] * 4
                    rd_p = [None] * 4
                    rd_ps = [None] * 2
                    absT = {}
                    expT = {}
                    state = {"lastpv": None}

                    def do_abs(kt):
                        ab = kt % 3
                        idx = kt * NQB + qb
                        absT[kt] = p.A([rd_ad[ab]], out=absd[ab][:], in_=delta[:], func=AF.Abs, bias=dtab[:, idx:idx + 1])

                    def far(kt):
                        d0 = kt * 128 - q0
                        return static_pos and (d0 >= 256 or d0 <= -128)

                    def front(u):
                        kt, c = units[u]
                        b = u % 4
                        if c == 0:
                            if kt == 0 and not far(0):
                                do_abs(0)
                            if kt + 1 < NKT and not far(kt + 1):
                                do_abs(kt + 1)
                        b2 = u % 2
                        tk = p.mm(t_in + [rd_ps[b2]], Sps[b2], ks[:, c, kt * 128:(kt + 1) * 128], qs[:, c, q0:q0 + 256],
                                  True, True, mark=True)
                        if far(kt):
                            d0 = kt * 128 - q0
                            sgn = 1.0 if d0 > 0 else -1.0
                            v1 = p.V([tk, rd_s[b]], "scalar_tensor_tensor", sb_[b][:], delta[:], -slope * sgn,
                                     Sps[b2], ALU.mult, ALU.add)
                            rd_ps[b2] = v1
                            a1 = p.A([v1, rd_p[b]], out=pt[b][:], in_=sb_[b][:], func=AF.Exp, bias=float(-slope * abs(d0)))
                        else:
                            v1 = p.V([tk, absT[kt], rd_s[b]], "scalar_tensor_tensor", sb_[b][:], absd[kt % 3][:], -slope,
                                     Sps[b2], ALU.mult, ALU.add)
                            rd_ps[b2] = v1
                            rd_ad[kt % 3] = v1
                            a1 = p.A([v1, rd_p[b]], out=pt[b][:], in_=sb_[b][:], func=AF.Exp)
                        rd_s[b] = a1
                        expT[u] = a1

                    def back(u):
                        kt, c = units[u]
                        b = u % 4
                        for j in range(2):
                            deps = [expT[u]] if j == 0 else []
                            if kt == 0:
                                deps = deps + acc_free
                            state["lastpv"] = p.mm(deps, acc[c][j][:, 0:257], pt[b][:, j * 128:(j + 1) * 128],
                                                   vs[:, kt, :], kt == 0, kt == NKT - 1, mark=(j == 1))
                        rd_p[b] = state["lastpv"]

                    for u in range(NU + SK):
                        if u < NU:
                            front(u)
                        if u >= SK:
                            back(u - SK)
                    lastpv = state["lastpv"]
                    evs = []
                    for c in range(2):
                        for j in range(2):
                            dst = (o0[par] if c == 0 else o1[par])
                            e1 = p.V([lastpv], "reciprocal", rs[par][:, 2 * c + j:2 * c + j + 1], acc[c][j][:, 256:257])
                            if c == 0:
                                evs.append(p.V([e1], "tensor_scalar", dst[:, j, :], acc[c][j][:, 0:256],
                                               rs[par][:, 2 * c + j:2 * c + j + 1], None, ALU.mult))
                            else:
                                evs.append(p.V([e1], "tensor_scalar", dst[:, j, :], acc[c][j][:, 0:256],
                                               rs[par][:, 2 * c + j:2 * c + j + 1], lam_sb[:, 1:2], ALU.mult, ALU.mult))
                    acc_free = [evs[-1]]
                    f1 = p.V(evs, "tensor_tensor", o0[par][:], o0[par][:], o1[par][:], ALU.add)
                    for j in range(2):
                        sqt = p.A([f1], out=jk[:], in_=o0[par][:, j, :], func=AF.Square, accum_out=rs[par][:, 4 + j:5 + j])
                    f2 = p.rsqrt([sqt], rs[par][:, 4:6], rs[par][:, 4:6], 1.0 / 256, EPS)
                    f3 = p.V([f2], "tensor_scalar", rs[par][:, 4:6], rs[par][:, 4:6], (1.0 - LAMBDA_INIT), None, ALU.mult)
                    ag = p.A([tg], out=gt[par][:], in_=gt[par][:], func=AF.Silu)
                    for j in range(2):
                        f4 = p.V([f3], "scalar_tensor_tensor", o0[par][:, j, :], o0[par][:, j, :], rs[par][:, 4 + j:5 + j],
                                 gsub_sb[:], ALU.mult, ALU.mult)
                    f5 = p.V([f4, ag, ob_free[par]], "tensor_tensor", ob[par][:], o0[par][:], gt[par][:], ALU.mult)
                    ob_free[par] = p.st(og.ap()[q0:q0 + 256, h * 256:(h + 1) * 256].rearrange("(j p) v -> p j v", p=128),
                                        ob[par][:], f"ato{par}", deps=[f5])
                    gt_free[par] = [f5, ag]
                p.barrier()
                gt_free = [[], []]
                ob_free = [None, None]
        p.barrier()

    c128 = p.sb(ges, "c128_sb", [128, 3, 128], BF16)
    p.ld(c128[:], c128_d.ap().rearrange("k p c -> p k c"), "c0")
    cs256 = p.sb(ges, "cs256_sb", [128, 2, 512], BF16)
    p.ld(cs256[:], cs256_d.ap().rearrange("(c p) n -> p c n", p=128), "c0")
    tw_sb, cm_sb = {}, {}
    for L in tw_d:
        M = L // 128
        tw_sb[L] = p.sb(ges, f"tw{L}_sb", [128, 3, M], F32)
        p.ld(tw_sb[L][:], tw_d[L].ap().rearrange("k p m -> p k m"), "c0")
        cm_sb[L] = p.sb(ges, f"cm{L}_sb", [M, 2, M], BF16)
        p.ld(cm_sb[L][:], cm_d[L].ap().rearrange("k b d -> b k d"), "c0")
    hm = p.sb(ges, "hm_sb", [64, 2, 64], F32)
    p.ld(hm[:], hmask_d.ap().rearrange("k s t -> s k t"), "c0")
    delta = p.sb(ges, "delta_sb", [128, 256], F32)
    p.ld(delta[:], delta_d.ap(), "c0")
    dtabs = p.sb(ges, "dtabs_sb", [128, (LS // 128) * (LS // 256)], F32)
    p.ld(dtabs[:], dtabs_d.ap(), "c0")
    dtabp = p.sb(ges, "dtabp_sb", [128, (LP // 128) * (OWN // 256)], F32)
    p.ld(dtabp[:], dtabp_d.ap(), "c0")
    ownidx = p.sb(ges, "ownidx_sb", [128, OWN // 128], I32)
    p.ld(ownidx[:], ownidx_d.ap(), "c0")
    ghg_sb = p.sb(ges, "ghg_sb", [64, 128], F32)
    p.ld(ghg_sb[:], bcast_rows(ghg.ap(), 128, 64), "c0")
    gsub_sb = p.sb(ges, "gsub_sb", [128, 256], F32)
    p.ld(gsub_sb[:], bcast_rows(gsub.ap(), 256), "c0")
    lraw = p.sb(ges, "lraw_sb", [128, 2, 3, 8], F32)
    p.ld(lraw[:], lbl.ap().rearrange("d s (h k) -> k d s h", k=128), "c0", slow=True)
    lbt = p.sb(ges, "lbt_sb", [128, 4, 8], F32)
    lsum = p.sb(ges, "lsum_sb", [128, 2, 8], F32)
    lamv = p.sb(ges, "lamv_sb", [128, 4, 128], F32)
    p.ld(lamv[:], bass.AP(lam4.ap().tensor, 0, [[0, 128], [128, 4], [1, 128]]), "c0")
    lam_sb = p.sb(ges, "lam_sb", [128, 4], F32)
    ljunk = p.sb(ges, "ljunk_sb", [128, 128], F32)
    p.barrier()
    a = p.A([], out=lraw[:], in_=lraw[:], func=AF.Exp)
    v = p.V([a], "tensor_tensor", lsum[:], lraw[:, :, 0, :], lraw[:, :, 1, :], ALU.add)
    v = p.V([v], "tensor_tensor", lsum[:], lsum[:], lraw[:, :, 2, :], ALU.add)
    v = p.V([v], "reciprocal", lsum[:], lsum[:])
    v = p.V([v], "tensor_tensor", lbt[:, 0:2, :], lraw[:, :, 0, :], lsum[:], ALU.mult)
    v = p.V([v], "tensor_scalar", lbt[:, 2:4, :], lbt[:, 0:2, :], -1.0, 1.0, ALU.mult, ALU.add)
    v = p.V([v], "tensor_tensor", ljunk[:], lamv[:, 0, :], lamv[:, 1, :], ALU.mult)
    v = p.V([v], "tensor_reduce", lam_sb[:, 2:3], ljunk[:], mybir.AxisListType.X, ALU.add)
    v = p.V([v], "tensor_tensor", ljunk[:], lamv[:, 2, :], lamv[:, 3, :], ALU.mult)
    v = p.V([v], "tensor_reduce", lam_sb[:, 3:4], ljunk[:], mybir.AxisListType.X, ALU.add)
    a = p.A([v], out=lam_sb[:, 2:4], in_=lam_sb[:, 2:4], func=AF.Exp)
    v = p.V([a], "tensor_tensor", lam_sb[:, 0:1], lam_sb[:, 2:3], lam_sb[:, 3:4], ALU.subtract)
    v = p.V([v], "tensor_scalar", lam_sb[:, 1:2], lam_sb[:, 0:1], LAMBDA_INIT, -1.0, ALU.add, ALU.mult)
    p.barrier()

    seqs = [("s%d" % i, LS, xs.ap()[i * LS:(i + 1) * LS, :], ys.ap()[i * LS:(i + 1) * LS, :]) for i in range(NS)]
    seqs.append(("p", LP, xp.ap(), None))
    scale_q = 128 ** -0.5
    for (nm, L, xin, yout) in seqs:
        isP = yout is None
        hT = p.dram(f"hT_{nm}", [D, L], BF16)
        norm_T(xin, L, hT)
        done("norm")
        uT = p.dram(f"uT_{nm}", [1024, L], BF16)
        gat0 = p.dram(f"g0_{nm}", [L, 2048], F32)
        qrT = p.dram(f"qr_{nm}", [1024, L], F32)
        ffT = p.dram(f"ff_{nm}", [1024, L], F32)
        fbT = p.dram(f"fb_{nm}", [1024, L], F32)
        vtk = p.dram(f"vt_{nm}", [L, 1024], BF16)
        proj(hT, L, wb0i, 7168, [
            (0, 1024, "F", uT, 0, 1.0, BF16),
            (1024, 1024, "T", gat0, 0, 1.0, F32),
            (2048, 1024, "F", qrT, 0, 1.0, F32),
            (3072, 1024, "T", vtk, 0, 1.0, BF16),
            (4096, 1024, "F", ffT, 0, 1.0, F32),
            (5120, 1024, "F", fbT, 0, 1.0, F32),
            (6144, 1024, "T", gat0, 1024, 1.0, F32),
        ])
        done("proj0")
        ya = p.dram(f"ya_{nm}", [L, 1024], F32)
        fnet(uT, L, ya, (c128, cs256, tw_sb[L], cm_sb[L]))
        done("fnet")
        ymix = p.dram(f"ym_{nm}", [L, 2048], BF16)
        hgrn(qrT, ffT, fbT, vtk, gat0, L, ymix, lbt, hm, ghg_sb)
        done("hgrn")
        x1 = p.dram(f"x1_{nm}", [L, D], F32)
        h1T = p.dram(f"h1T_{nm}", [D, L], BF16)
        outproj(L, ymix, ya, gat0, wb0o, gpost_sb[0], xin, x1.ap(), h1T)
        done("out0")
        kT = p.dram(f"kT_{nm}", [2048, L], BF16)
        v1t = p.dram(f"v1_{nm}", [L, 2048], BF16)
        if not isP:
            Lq = L
            qT = p.dram(f"qT_{nm}", [2048, Lq], BF16)
            gat1 = p.dram(f"g1_{nm}", [Lq, 2048], F32)
            proj(h1T, L, wb1i, 8192, [
                (0, 2048, "F", qT, 0, scale_q, BF16),
                (2048, 2048, "F", kT, 0, 1.0, BF16),
                (4096, 2048, "T", v1t, 0, 1.0, BF16),
                (6144, 2048, "T", gat1, 0, 1.0, F32),
            ])
            xres1 = x1.ap()
            dtab = dtabs
        else:
            Lq = OWN
            proj(h1T, L, wb1i, 8192, [
                (2048, 2048, "F", kT, 0, 1.0, BF16),
                (4096, 2048, "T", v1t, 0, 1.0, BF16),
            ])
            x1own = p.dram("x1own", [OWN, D], F32)
            with contextlib.ExitStack() as es:
                gx = p.sb(es, "gx", [128, D], F32)
                prev = None
                for ti in range(OWN // 128):
                    p.pool.wait(prev)
                    ds = p.dsem("gath")
                    p.nc.gpsimd.indirect_dma_start(
                        out=gx[:], out_offset=None, in_=x1.ap(),
                        in_offset=bass.IndirectOffsetOnAxis(ap=ownidx[:, ti:ti + 1], axis=0),
                    ).then_inc(ds.sem, 16)
                    ds.n += 16
                    tok = (ds, ds.n)
                    p.pending.append(tok)
                    prev = p.st(x1own.ap()[ti * 128:(ti + 1) * 128, :], gx[:], "gaths", deps=[tok])
            p.barrier()
            h1To = p.dram("h1To", [D, OWN], BF16)
            norm_T(x1own.ap(), OWN, h1To)
            qT = p.dram(f"qT_{nm}", [2048, Lq], BF16)
            gat1 = p.dram(f"g1_{nm}", [Lq, 2048], F32)
            proj(h1To, OWN, wb1i, 8192, [
                (0, 2048, "F", qT, 0, scale_q, BF16),
                (6144, 2048, "T", gat1, 0, 1.0, F32),
            ])
            xres1 = x1own.ap()
            yout = yp.ap()
            dtab = dtabp
        done("proj1")
        og = p.dram(f"og_{nm}", [Lq, 2048], BF16)
        attention(qT, kT, v1t, gat1, Lq, L, og, dtab, lam_sb, gsub_sb, delta, static_pos=(not isP))
        done("attn")
        outproj(Lq, og, None, None, wb1o, gpost_sb[1], xres1, yout, None)
        done("seq0")


def dft_cs(n):
    j = np.arange(n)
    ang = 2 * np.pi * np.outer(j, j) / n
    return np.cos(ang), np.sin(ang)


def host_tables(cfg, core):
    NS, LS, LP, OWN = cfg["NS"], cfg["LS"], cfg["LP"], cfg["OWN"]
    t = {}
    t["ident"] = np.eye(128, dtype=np.float32).astype(NPBF)
    c, s = dft_cs(128)
    t["c128"] = np.stack([c, s, -s]).astype(np.float32).astype(NPBF)
    c, s = dft_cs(256)
    t["cs256"] = np.concatenate([c, -s], axis=1).astype(np.float32).astype(NPBF)
    for L in sorted({LS, LP}):
        M = L // 128
        ang = 2 * np.pi * np.outer(np.arange(128), np.arange(M)) / L
        t[f"tw{L}"] = np.stack([np.cos(ang), np.sin(ang), -np.sin(ang)]).astype(np.float32)
        cm, sm = dft_cs(M)
        sc = 1.0 / math.sqrt(L * 256)
        t[f"cm{L}"] = np.stack([cm * sc, sm * sc]).astype(np.float32).astype(NPBF)
    s_, t_ = np.meshgrid(np.arange(64), np.arange(64), indexing="ij")
    t["hmask"] = np.stack([(s_ <= t_), (s_ >= t_)]).astype(np.float32)
    t["delta"] = (np.arange(128)[:, None] - np.arange(256)[None, :]).astype(np.float32)
    kt, qb = np.meshgrid(np.arange(LS // 128), np.arange(LS // 256), indexing="ij")
    t["dtabs"] = np.broadcast_to((kt * 128 - qb * 256).reshape(1, -1), (128, kt.size)).astype(np.float32).copy()
    kt, qb = np.meshgrid(np.arange(LP // 128), np.arange(OWN // 256), indexing="ij")
    t["dtabp"] = np.broadcast_to((kt * 128 - (core * OWN + qb * 256)).reshape(1, -1), (128, kt.size)).astype(np.float32).copy()
    t["ownidx"] = (core * OWN + np.arange(OWN)).reshape(OWN // 128, 128).T.astype(np.int32).copy()
    return t


def run(inputs, cfg, ncores):
    NS, LS, LP, OWN = cfg["NS"], cfg["LS"], cfg["LP"], cfg["OWN"]
    f = lambda a: np.ascontiguousarray(np.asarray(a, dtype=np.float32))
    xsamp = f(inputs["x_sample"])
    shared = {
        "xp": f(inputs["x_prompt"])[0],
        "w0i": f(inputs["ev_w_in"])[0], "w0o": f(inputs["ev_w_out"])[0],
        "w1i": f(inputs["od_w_in"])[0], "w1o": f(inputs["od_w_out"])[0],
        "g0pre": f(inputs["ev_norm_pre"])[0], "g0post": f(inputs["ev_norm_post"])[0],
        "g1pre": f(inputs["od_norm_pre"])[0], "g1post": f(inputs["od_norm_post"])[0],
        "lbl": f(inputs["hgrn_lb_logits"]), "ghg": f(inputs["hgrn_norm"])[0],
        "lam4": np.stack([f(inputs["lambda_q1"])[0], f(inputs["lambda_k1"])[0],
                          f(inputs["lambda_q2"])[0], f(inputs["lambda_k2"])[0]]),
        "gsub": f(inputs["subln"])[0],
    }
    nc = build(cfg)
    in_maps = []
    for c in range(ncores):
        m = dict(shared)
        m["xs"] = xsamp[c * NS:(c + 1) * NS].reshape(NS * LS, D)
        m.update(host_tables(cfg, c))
        in_maps.append(m)
    res = run_bass_kernel_spmd(nc, in_maps, core_ids=list(range(ncores)))
    global LAST_RES
    LAST_RES = res.results
    y_s = np.concatenate([r["ys"].reshape(NS, LS, D) for r in res.results], axis=0)
    y_p = np.concatenate([r["yp"] for r in res.results], axis=0)[None]
    return y_p.astype(np.float32), y_s.astype(np.float32)


def kernel(**inputs):
    import os
    cfg = {"NS": 2, "LS": 2048, "LP": 8192, "OWN": 1024}
    if os.environ.get("K_STOP"):
        cfg["stop"] = os.environ["K_STOP"]
    return run(inputs, cfg, 8)
```
